# Optimizing a Trainium2 kernel written in Bass

```python
import math
import jax, jax.numpy as jnp
from jax import lax
import numpy as np

D_MODEL = 1024
BATCH = 8
SEQ = 2048
DEPTH = 1
DEC_BATCH = 128
DEC_SEQ = 1
PAST_LEN = 16384
PAGE_SIZE = 128

LRU_WIDTH = 768
N_LRU_BLOCKS = 6
LRU_BLOCK = LRU_WIDTH // N_LRU_BLOCKS
LRU_CONV = 4
LRU_C = 8.0
SC_WIDTH = 768
SC_GROUPS = 6
SC_CONV = 3
N_MEM = 256
XA_HEADS = 4
XA_HEAD_DIM = 128
XA_WIDTH = XA_HEADS * XA_HEAD_DIM
N_BRANCH = 3
EPS = 1e-6

IN_COLS = 2 * LRU_WIDTH + 4 * SC_WIDTH + 2 * XA_WIDTH + N_BRANCH * D_MODEL
SPLIT_POINTS = (
    LRU_WIDTH,
    2 * LRU_WIDTH,
    2 * LRU_WIDTH + SC_WIDTH,
    2 * LRU_WIDTH + 2 * SC_WIDTH,
    2 * LRU_WIDTH + 3 * SC_WIDTH,
    2 * LRU_WIDTH + 4 * SC_WIDTH,
    2 * LRU_WIDTH + 4 * SC_WIDTH + XA_WIDTH,
    2 * LRU_WIDTH + 4 * SC_WIDTH + 2 * XA_WIDTH,
)

kernel_name = "hawk_shortconv_memxattn_step"


def rmsnorm(x, g):
    xf = x.astype(jnp.float32)
    ms = jnp.mean(xf * xf, axis=-1, keepdims=True)
    return (xf * lax.rsqrt(ms + EPS)).astype(x.dtype) * g


def causal_dwconv(x, buf, w):
    width = w.shape[0]
    t = x.shape[1]
    xp = jnp.concatenate([buf.astype(x.dtype), x], axis=1)
    y = xp[:, 0:t] * w[0]
    for k in range(1, width):
        y = y + xp[:, k:k + t] * w[k]
    return y, xp[:, t:]


def rglru(xc, h0, wa, ba, wx, bx, lam):
    bn, t, c = xc.shape
    xb = xc.reshape(bn, t, N_LRU_BLOCKS, LRU_BLOCK)
    r = jax.nn.sigmoid(jnp.einsum('btnc,ncd->btnd', xb, wa).reshape(bn, t, c) + ba)
    i = jax.nn.sigmoid(jnp.einsum('btnc,ncd->btnd', xb, wx).reshape(bn, t, c) + bx)
    log_a = -LRU_C * r.astype(jnp.float32) * jax.nn.softplus(-lam.astype(jnp.float32))
    a = jnp.exp(log_a)
    mult = jnp.sqrt(jnp.maximum(-jnp.expm1(2.0 * log_a), 0.0))
    b_in = mult * (i * xc).astype(jnp.float32)

    def step(h, inp):
        a_t, b_t = inp
        h = a_t * h + b_t
        return h, h

    h_last, hs = lax.scan(step, h0.astype(jnp.float32),
                          (jnp.swapaxes(a, 0, 1), jnp.swapaxes(b_in, 0, 1)))
    return jnp.swapaxes(hs, 0, 1).astype(xc.dtype), h_last.astype(h0.dtype)


def memory_kv(mem, mem_norm_g, xa_wk, xa_wv):
    m = rmsnorm(mem, mem_norm_g)
    bn = mem.shape[0]
    k = (m @ xa_wk).reshape(bn, N_MEM, XA_HEADS, XA_HEAD_DIM)
    v = (m @ xa_wv).reshape(bn, N_MEM, XA_HEADS, XA_HEAD_DIM)
    return k, v


def hybrid_layer(x, mem_k, mem_v, lru_h, lru_buf, sc_buf,
                 norm_g, w_in, lru_conv_w, lru_conv_b, lru_wa, lru_ba, lru_wx, lru_bx,
                 lru_lambda, lru_wo, sconv_w, sconv_wo, xa_wo, w_out):
    bn, t, _ = x.shape
    u = rmsnorm(x, norm_g)
    proj = u @ w_in
    lx, lg, sb, scg, sh, sg, q, qg, mg = jnp.split(proj, SPLIT_POINTS, axis=-1)

    xc, new_lru_buf = causal_dwconv(lx, lru_buf, lru_conv_w)
    xc = xc + lru_conv_b
    hs, new_h = rglru(xc, lru_h, lru_wa, lru_ba, lru_wx, lru_bx, lru_lambda)
    z_a = (hs * jax.nn.silu(lg)) @ lru_wo

    cin = scg * sh
    cy, new_sc_buf = causal_dwconv(cin, sc_buf, sconv_w)
    z_b = (jax.nn.silu(sg) * sb * cy) @ sconv_wo

    qh = q.reshape(bn, t, XA_HEADS, XA_HEAD_DIM)
    s = jnp.einsum('bthd,bmhd->bhtm', qh, mem_k.astype(qh.dtype)).astype(jnp.float32)
    p = jax.nn.softmax(s * (1.0 / math.sqrt(XA_HEAD_DIM)), axis=-1).astype(x.dtype)
    o = jnp.einsum('bhtm,bmhd->bthd', p, mem_v.astype(x.dtype)).reshape(bn, t, XA_WIDTH)
    z_c = (o * jax.nn.silu(qg)) @ xa_wo

    g = jax.nn.sigmoid(mg)
    g_a, g_b, g_c = jnp.split(g, N_BRANCH, axis=-1)
    y = (g_a * z_a + g_b * z_b + g_c * z_c) @ w_out
    return x + y, new_h, new_lru_buf, new_sc_buf


def setup_inputs(seed: int = 0) -> dict:
    key = jax.random.key(seed)
    ks = jax.random.split(key, 32)
    f32 = jnp.float32
    nrm = lambda k, shape, s: (jax.random.normal(k, shape, f32) * s)

    x_prompt = nrm(ks[0], (BATCH, SEQ, D_MODEL), 1.0)
    x_sample = nrm(ks[1], (DEC_BATCH, DEC_SEQ, D_MODEL), 1.0)
    cache_mem_k = nrm(ks[2], (DEPTH, DEC_BATCH, N_MEM, XA_HEADS, XA_HEAD_DIM), 1.0)
    cache_mem_v = nrm(ks[3], (DEPTH, DEC_BATCH, N_MEM, XA_HEADS, XA_HEAD_DIM), 1.0)
    state_lru_h = nrm(ks[4], (DEPTH, DEC_BATCH, LRU_WIDTH), 0.5)
    state_lru_conv = nrm(ks[5], (DEPTH, DEC_BATCH, LRU_CONV - 1, LRU_WIDTH), 1.0)
    state_sconv = nrm(ks[6], (DEPTH, DEC_BATCH, SC_CONV - 1, SC_WIDTH), 1.0)
    mem_prompt = nrm(ks[7], (BATCH, N_MEM, D_MODEL), 1.0)

    norm_g = 1.0 + nrm(ks[8], (DEPTH, D_MODEL), 0.02)
    mem_norm_g = 1.0 + nrm(ks[9], (DEPTH, D_MODEL), 0.02)
    w_in = nrm(ks[10], (DEPTH, D_MODEL, IN_COLS), D_MODEL ** -0.5)
    lru_conv_w = nrm(ks[11], (DEPTH, LRU_CONV, LRU_WIDTH), LRU_CONV ** -0.5)
    lru_conv_b = nrm(ks[12], (DEPTH, LRU_WIDTH), 0.01)
    lru_wa = nrm(ks[13], (DEPTH, N_LRU_BLOCKS, LRU_BLOCK, LRU_BLOCK), LRU_BLOCK ** -0.5)
    lru_ba = nrm(ks[14], (DEPTH, LRU_WIDTH), 0.01)
    lru_wx = nrm(ks[15], (DEPTH, N_LRU_BLOCKS, LRU_BLOCK, LRU_BLOCK), LRU_BLOCK ** -0.5)
    lru_bx = nrm(ks[16], (DEPTH, LRU_WIDTH), 0.01)
    a_c = jax.random.uniform(ks[17], (DEPTH, LRU_WIDTH), f32, 0.9, 0.999)
    a0 = a_c ** (1.0 / LRU_C)
    lru_lambda = jnp.log(a0) - jnp.log1p(-a0)
    lru_wo = nrm(ks[18], (DEPTH, LRU_WIDTH, D_MODEL), LRU_WIDTH ** -0.5)
    sconv_w = nrm(ks[19], (DEPTH, SC_CONV, SC_WIDTH), SC_CONV ** -0.5)
    sconv_wo = nrm(ks[20], (DEPTH, SC_WIDTH, D_MODEL), SC_WIDTH ** -0.5)
    xa_wk = nrm(ks[21], (DEPTH, D_MODEL, XA_WIDTH), D_MODEL ** -0.5)
    xa_wv = nrm(ks[22], (DEPTH, D_MODEL, XA_WIDTH), D_MODEL ** -0.5)
    xa_wo = nrm(ks[23], (DEPTH, XA_WIDTH, D_MODEL), XA_WIDTH ** -0.5)
    w_out = nrm(ks[24], (DEPTH, D_MODEL, D_MODEL), D_MODEL ** -0.5)
    final_norm_g = 1.0 + nrm(ks[25], (D_MODEL,), 0.02)

    return {
        "x_prompt": x_prompt, "x_sample": x_sample,
        "cache_mem_k": cache_mem_k, "cache_mem_v": cache_mem_v,
        "state_lru_h": state_lru_h, "state_lru_conv": state_lru_conv,
        "state_sconv": state_sconv, "mem_prompt": mem_prompt,
        "norm_g": norm_g, "mem_norm_g": mem_norm_g, "w_in": w_in,
        "lru_conv_w": lru_conv_w, "lru_conv_b": lru_conv_b,
        "lru_wa": lru_wa, "lru_ba": lru_ba, "lru_wx": lru_wx, "lru_bx": lru_bx,
        "lru_lambda": lru_lambda, "lru_wo": lru_wo,
        "sconv_w": sconv_w, "sconv_wo": sconv_wo,
        "xa_wk": xa_wk, "xa_wv": xa_wv, "xa_wo": xa_wo,
        "w_out": w_out, "final_norm_g": final_norm_g,
    }


def reference(x_prompt, x_sample, cache_mem_k, cache_mem_v, state_lru_h, state_lru_conv,
              state_sconv, mem_prompt, norm_g, mem_norm_g, w_in, lru_conv_w, lru_conv_b,
              lru_wa, lru_ba, lru_wx, lru_bx, lru_lambda, lru_wo, sconv_w, sconv_wo,
              xa_wk, xa_wv, xa_wo, w_out, final_norm_g):
    xp = x_prompt
    xs = x_sample
    bp = x_prompt.shape[0]
    p_mk, p_mv, p_h, p_lb, p_sb = [], [], [], [], []
    s_h, s_lb, s_sb = [], [], []
    for l in range(DEPTH):
        lw = (norm_g[l], w_in[l], lru_conv_w[l], lru_conv_b[l], lru_wa[l], lru_ba[l],
              lru_wx[l], lru_bx[l], lru_lambda[l], lru_wo[l], sconv_w[l], sconv_wo[l],
              xa_wo[l], w_out[l])
        mk, mv = memory_kv(mem_prompt, mem_norm_g[l], xa_wk[l], xa_wv[l])
        h0 = jnp.zeros((bp, LRU_WIDTH), xp.dtype)
        lb0 = jnp.zeros((bp, LRU_CONV - 1, LRU_WIDTH), xp.dtype)
        sb0 = jnp.zeros((bp, SC_CONV - 1, SC_WIDTH), xp.dtype)
        xp, ph, plb, psb = hybrid_layer(xp, mk, mv, h0, lb0, sb0, *lw)
        p_mk.append(mk); p_mv.append(mv); p_h.append(ph); p_lb.append(plb); p_sb.append(psb)
        xs, sh, slb, ssb = hybrid_layer(xs, cache_mem_k[l], cache_mem_v[l], state_lru_h[l],
                                        state_lru_conv[l], state_sconv[l], *lw)
        s_h.append(sh); s_lb.append(slb); s_sb.append(ssb)
    y_prompt = rmsnorm(xp, final_norm_g)
    y_sample = rmsnorm(xs, final_norm_g)
    return (y_prompt, y_sample,
            jnp.stack(p_mk), jnp.stack(p_mv), jnp.stack(p_h), jnp.stack(p_lb), jnp.stack(p_sb),
            jnp.stack(s_h), jnp.stack(s_lb), jnp.stack(s_sb))
```

```python
import math
from contextlib import ExitStack

import numpy as np
import concourse.bass as bass
import concourse.mybir as mybir
from concourse.bass_utils import run_bass_kernel_spmd

F32 = mybir.dt.float32
BF16 = mybir.dt.bfloat16
AF = mybir.ActivationFunctionType
ALU = mybir.AluOpType
AX = mybir.AxisListType

NCORES = 8
T = 2048
TS = 16
TT = T + TS
D = 1024
KC = 8
W = 768
NCH = 6
NM = 256
XW = 512
IN_COLS = 8704
EPS = 1e-6
SCALE = 1.0 / math.sqrt(128.0)
STOP_AFTER = None

C_LX, C_LG, C_SB, C_SCG, C_SH, C_SG, C_Q, C_QG, C_MG = 0, 768, 1536, 2304, 3072, 3840, 4608, 5120, 5632


class Tok:
    __slots__ = ("name", "w", "r")

    def __init__(self, name=""):
        self.name = name
        self.w = None
        self.r = {}


class Stream:
    def __init__(self, key, sem):
        self.key = key
        self.sem = sem
        self.cnt = 0
        self.seen = {}
        self.ops = []
        self.pending = False


class Sched:
    def __init__(self, nc):
        self.nc = nc
        self.sems = {}
        self.streams = {}
        self.dma_cnt = {}

    def add_stream(self, key, sem):
        self.sems[key] = sem
        self.streams[key] = Stream(key, sem)

    def add_dma_sem(self, key, sem):
        self.sems[key] = sem
        self.dma_cnt[key] = 0

    def _needs(self, st, reads, writes):
        needs = {}

        def need(ev):
            if ev is None:
                return
            k, v = ev
            if needs.get(k, 0) < v:
                needs[k] = v
        for t in reads:
            need(t.w)
        for t in writes:
            need(t.w)
            for k, v in t.r.items():
                if k == st.key:
                    continue
                need((k, v))
        out = []
        for k, v in needs.items():
            if k == st.key and k == "pe":
                continue
            if st.seen.get(k, 0) < v:
                st.seen[k] = v
                out.append((k, v))
        return out

    def op(self, key, fn, reads=(), writes=(), inc=True):
        st = self.streams[key]
        waits = self._needs(st, reads, writes)
        if inc:
            st.cnt += 1
            st.pending = False
            ev = (key, st.cnt)
        else:
            st.pending = True
            ev = (key, st.cnt + 1)
        sems = self.sems
        sem = st.sem

        def run(eng, waits=waits, fn=fn, inc=inc):
            for k, v in waits:
                eng.wait_ge(sems[k], v)
            ins = fn(eng)
            if inc:
                ins.then_inc(sem, 1)
        st.ops.append(run)
        for t in writes:
            t.w = ev
            t.r = {}
        for t in reads:
            if t.r.get(key, 0) < ev[1]:
                t.r[key] = ev[1]

    def dma(self, key, semkey, out, in_, reads=(), writes=(), nowait=False, **kw):
        st = self.streams[key]
        waits = [] if nowait else self._needs(st, reads, writes)
        self.dma_cnt[semkey] += 16
        ev = (semkey, self.dma_cnt[semkey])
        sems = self.sems

        def run(eng, waits=waits):
            for k, v in waits:
                eng.wait_ge(sems[k], v)
            eng.dma_start(out=out, in_=in_, **kw).then_inc(sems[semkey], 16)
        st.ops.append(run)
        for t in writes:
            t.w = ev
            t.r = {}
        for t in reads:
            if t.r.get(semkey, 0) < ev[1]:
                t.r[semkey] = ev[1]

    def wait_dma(self, keys, semkey):
        v = self.dma_cnt[semkey]
        sems = self.sems
        for k in keys:
            st = self.streams[k]
            if v and st.seen.get(semkey, 0) < v:
                st.seen[semkey] = v
                st.ops.append(lambda eng, v=v: eng.wait_ge(sems[semkey], v))

    def barrier(self, skip=(), no_wait=()):
        targets = {}
        for k, st in self.streams.items():
            assert not st.pending
            if st.cnt:
                targets[k] = st.cnt
        for k, v in self.dma_cnt.items():
            if v and k not in skip:
                targets[k] = v
        sems = self.sems
        for k, st in self.streams.items():
            if k in no_wait:
                continue
            waits = []
            for tk, tv in targets.items():
                if st.seen.get(tk, 0) < tv:
                    st.seen[tk] = tv
                    waits.append((tk, tv))

            def run(eng, waits=waits):
                for kk, v in waits:
                    eng.wait_ge(sems[kk], v)
            st.ops.append(run)

    def finish(self, block):
        for key, st in self.streams.items():
            assert not st.pending, key
        ss = self.streams

        def mk(key):
            def body(eng):
                for o in ss[key].ops:
                    o(eng)
            return body
        block.gpsimd(mk("pool"))
        block.tensor(mk("pe"))
        block.scalar(mk("act"))
        block.vector(mk("dve"))
        block.sync(mk("sp"))


def build_program():
    nc = bass.Bass("TRN2", target_bir_lowering=False)

    def din(name, shape):
        return nc.dram_tensor(name, shape, F32, kind="ExternalInput").ap()

    def dout(name, shape):
        return nc.dram_tensor(name, shape, F32, kind="ExternalOutput").ap()

    xp = din("xp", [T, D])
    xs = din("xs", [TS, D])
    memp = din("memp", [NM, D])
    ck = din("ck", [TS, NM, XW])
    cv = din("cv", [TS, NM, XW])
    st_h = din("st_h", [TS, W])
    st_lc = din("st_lc", [TS, 3 * W])
    st_sc = din("st_sc", [TS, 2 * W])
    norm_g = din("norm_g", [D])
    mem_norm_g = din("mem_norm_g", [D])
    w_in = din("w_in", [D, IN_COLS])
    lru_conv_w = din("lru_conv_w", [4, W])
    lru_conv_b = din("lru_conv_b", [W])
    lru_wa = din("lru_wa", [NCH, 128, 128])
    lru_ba = din("lru_ba", [W])
    lru_wx = din("lru_wx", [NCH, 128, 128])
    lru_bx = din("lru_bx", [W])
    lru_lambda = din("lru_lambda", [W])
    lru_wo = din("lru_wo", [W, D])
    sconv_w = din("sconv_w", [3, W])
    sconv_wo = din("sconv_wo", [W, D])
    xa_wk = din("xa_wk", [D, XW])
    xa_wv = din("xa_wv", [D, XW])
    xa_wo = din("xa_wo", [XW, D])
    w_out = din("w_out", [D, D])
    final_norm_g = din("final_norm_g", [D])

    y_p = dout("y_p", [T, D])
    y_s = dout("y_s", [TS, D])
    o_pk = dout("o_pk", [NM, XW])
    o_pv = dout("o_pv", [NM, XW])
    o_ph = dout("o_ph", [1, W])
    o_plc = dout("o_plc", [3, W])
    o_psc = dout("o_psc", [2, W])
    o_sh = dout("o_sh", [TS, W])
    o_slc = dout("o_slc", [TS, 3 * W])
    o_ssc = dout("o_ssc", [TS, 2 * W])

    dbg = dout("dbg", [128, 4096]) if STOP_AFTER else None
    es = ExitStack()
    with es:
        def sb(name, shape, dt):
            return es.enter_context(nc.sbuf_tensor(name, shape, dt))

        uT_t = sb("uT", [128, KC * TT], BF16)
        uT = uT_t[:].rearrange("p (k t) -> p k t", k=KC)
        gA_t = sb("gA", [128, NCH * TT], BF16)
        gA = gA_t[:].rearrange("p (k t) -> p k t", k=NCH)
        gB_t = sb("gB", [128, NCH * TT], BF16)
        gB = gB_t[:].rearrange("p (k t) -> p k t", k=NCH)
        og_t = sb("og", [128, 4 * TT], BF16)
        og = og_t[:].rearrange("p (k t) -> p k t", k=4)
        NSLOT = 5
        SLOT_E = 3072
        slots = [sb(f"wslot{i}", [128, SLOT_E], BF16) for i in range(NSLOT)]
        ident = sb("ident", [128, 128], F32)
        identb = sb("identb", [128, 128], BF16)
        onesb = sb("onesb", [128, 128], BF16)
        pT = sb("pT", [128, 82], F32)
        dv = sb("dv", [128, 24], F32)
        SIT_t = sb("SIT", [128, 36 * TS], F32)
        SIT = SIT_t[:].rearrange("p (a b) -> p a b", a=36)
        SO_t = sb("SO", [128, NCH * 54], F32)
        SO = SO_t[:].rearrange("p (a b) -> p a b", a=NCH)
        stat = sb("stat", [128, 64], F32)
        stat2 = sb("stat2", [128, 64], F32)
        epst = sb("epst", [128, 1], F32)
        q25 = sb("q25", [128, 1], F32)
        RW = 18816
        R = sb("R", [128, RW], F32)

        def carve(off_b, nelem, dt):
            assert off_b % 4 == 0
            if dt == F32:
                assert off_b // 4 + nelem <= RW, (off_b, nelem)
                return R[:, off_b // 4: off_b // 4 + nelem]
            assert nelem % 2 == 0 and off_b // 4 + nelem // 2 <= RW, (off_b, nelem)
            return R[:, off_b // 4: off_b // 4 + nelem // 2].bitcast(BF16)

        PP = [es.enter_context(nc.psum_tensor(f"PP{i}", [128, 1024], F32)) for i in range(3)]
        PS = es.enter_context(nc.psum_tensor("PS", [128, 512], F32))
        PX = es.enter_context(nc.psum_tensor("PX", [128, 512], F32))
        tPP = [Tok(f"PP{i}") for i in range(3)]
        tPS = [Tok(f"PS{i}") for i in range(8)]
        tPX = Tok("PX")
        PSb = PS[:].bitcast(BF16)
        PXb = PX[:].bitcast(BF16)

        S = Sched(nc)
        for k in ["pe", "act", "dve", "pool", "sp"]:
            S.add_stream(k, es.enter_context(nc.semaphore("s_" + k)))
        dsem_names = (["cst", "cs2", "cs3", "cs4", "cs5", "sva", "svb", "fxa", "fxb", "out", "xa", "xb", "xc", "ya", "yb", "ka", "kb", "va", "vb"]
                      + [f"ws{i}" for i in range(NSLOT)])
        for k in dsem_names:
            S.add_dma_sem(k, es.enter_context(nc.semaphore("d_" + k)))
        block = es.enter_context(nc.Block())

        state = {"pp": 0, "ps": 0}

        def next_pp():
            i = state["pp"] % 3
            state["pp"] += 1
            return PP[i], tPP[i]

        def next_ps():
            i = state["ps"] % 8
            state["ps"] += 1
            return i, tPS[i]

        wtiles = []
        wstate = {"issued": 0}
        tslot = [Tok(f"slot{i}") for i in range(NSLOT)]

        def w_in_cols(c0, ncols):
            return w_in.rearrange("(k p) c -> p k c", p=128)[:, :, c0:c0 + ncols]

        def add_wtile(pieces):
            off = 0
            lst = []
            for ap in pieces:
                kc, ncols = ap.shape[1], ap.shape[2]
                lst.append((ap, off, kc, ncols))
                off += kc * ncols
            assert off <= SLOT_E, off
            wtiles.append(lst)
            return len(wtiles) - 1

        def w_issue_upto(i):
            while wstate["issued"] <= min(i, len(wtiles) - 1):
                j = wstate["issued"]
                sl = j % NSLOT
                for pi, (ap, off, kc, ncols) in enumerate(wtiles[j]):
                    dst = slots[sl][:, off:off + kc * ncols].rearrange("p (k c) -> p k c", k=kc)
                    S.dma("pool", f"ws{sl}", dst, ap, writes=[tslot[sl]], nowait=(pi > 0))
                wstate["issued"] += 1

        def w_get(i):
            w_issue_upto(i + NSLOT - 2)
            sl = i % NSLOT
            views = []
            for (ap, off, kc, ncols) in wtiles[i]:
                views.append(slots[sl][:, off:off + kc * ncols].rearrange("p (k c) -> p k c", k=kc))
            return views, tslot[sl]

        WT = {}
        WT["wk0"] = add_wtile([xa_wk.rearrange("(k p) c -> p k c", p=128)[:, :, 0:256]])
        WT["wk1"] = add_wtile([xa_wk.rearrange("(k p) c -> p k c", p=128)[:, :, 256:512]])
        WT["wv0"] = add_wtile([xa_wv.rearrange("(k p) c -> p k c", p=128)[:, :, 0:256]])
        WT["wv1"] = add_wtile([xa_wv.rearrange("(k p) c -> p k c", p=128)[:, :, 256:512]])
        for hh in range(2):
            WT[f"q{hh}"] = add_wtile([w_in_cols(C_Q + hh * 256, 256)])
        for hh in range(2):
            WT[f"qg{hh}"] = add_wtile([w_in_cols(C_QG + hh * 256, 256)])
        for n in range(NCH + 1):
            if n < NCH:
                WT[f"A{n}"] = add_wtile([w_in_cols(C_LG + n * 128, 128), w_in_cols(C_LX + n * 128, 128)])
            if n >= 1:
                m_ = n - 1
                WT[f"B{m_}a"] = add_wtile([w_in_cols(C_SG + m_ * 128, 128), w_in_cols(C_SB + m_ * 128, 128)])
                WT[f"B{m_}b"] = add_wtile([w_in_cols(C_SCG + m_ * 128, 128), w_in_cols(C_SH + m_ * 128, 128)])
        lwo = lru_wo.rearrange("(k p) c -> p k c", p=128)
        swo = sconv_wo.rearrange("(k p) c -> p k c", p=128)
        xwo = xa_wo.rearrange("(k p) c -> p k c", p=128)
        for j in range(8):
            WT[f"Mg{j}"] = add_wtile([w_in_cols(C_MG + x * 1024 + j * 128, 128) for x in range(3)])
            WT[f"Mo{j}"] = add_wtile([lwo[:, :, j * 128:(j + 1) * 128], swo[:, :, j * 128:(j + 1) * 128],
                                      xwo[:, :, j * 128:(j + 1) * 128]])

        def mm_group(out_ap, pairs, reads, wtok, last_inc=True):
            n = len(pairs)
            for i, (l, r) in enumerate(pairs):
                S.op("pe", lambda e, l=l, r=r, i=i: e.matmul(out_ap, l, r, start=(i == 0), stop=(i == n - 1)),
                     reads=reads, writes=[wtok], inc=(last_inc and i == n - 1))

        def unit_half(lhs_list, rhs_arr, lo, reads):
            pp, tk = next_pp()
            nk = len(lhs_list)
            for b in range(2):
                pairs = [(lhs_list[k], rhs_arr[:, k, lo + b * 512: lo + (b + 1) * 512]) for k in range(nk)]
                mm_group(pp[:, b * 512:(b + 1) * 512], pairs, reads, tk, last_inc=(b == 1))
            return pp, tk

        tPSb = Tok("PSbank")
        tPS2 = [tPSb, tPX]

        def unit_samp(lhs_list, rhs_arr, reads):
            i = state.setdefault("ps2", 0) % 2
            state["ps2"] += 1
            tk = tPS2[i]
            nk = len(lhs_list)
            bank = PS if i == 0 else PX
            out_ap = bank[:, 0:TS]
            pairs = [(lhs_list[k], rhs_arr[:, k, T:TT]) for k in range(nk)]
            mm_group(out_ap, pairs, reads, tk)
            return out_ap, tk

        HALVES = [(0, 1024), (1024, 2048)]

        def dump(items, off_b=56320):
            dstage = carve(off_b, 4096, F32)
            S.barrier()
            S.op("dve", lambda e: e.memset(dstage[:], 0.0))
            S.barrier()
            off = 0
            for ap in items:
                P_, n_ = ap.shape[0], ap.shape[1]
                S.op("dve", lambda e, ap=ap, off=off, P_=P_, n_=n_: e.tensor_copy(dstage[0:P_, off:off + n_], ap))
                off += n_
            S.barrier()
            S.dma("sp", "out", dbg[:, :], dstage[:])
            S.barrier()

        t_ident = Tok("ident")
        S.op("pool", lambda e: e.memset(ident[:], 0.0), writes=[t_ident])
        S.op("pool", lambda e: e.affine_select(ident[:], ident[:], pattern=[[-1, 128]], compare_op=ALU.not_equal,
                                               fill=1.0, base=0, channel_multiplier=1),
             reads=[t_ident], writes=[t_ident])
        t_identb = Tok("identb")
        S.op("dve", lambda e: e.tensor_copy(identb[:], ident[:]), reads=[t_ident], writes=[t_identb])
        t_stat = Tok("stat")
        t_sm = Tok("sm")
        t_st3 = t_sm
        S.op("pool", lambda e: e.memset(stat[:], 0.0), writes=[t_stat])
        S.op("pool", lambda e: e.memset(stat2[:], 0.0), writes=[t_sm])
        t_ones = Tok("ones")
        S.op("pool", lambda e: e.memset(onesb[:], 1.0), writes=[t_ones])
        t_eps = Tok("eps")
        S.op("pool", lambda e: e.memset(epst[:], EPS), writes=[t_eps])
        S.op("pool", lambda e: e.memset(q25[:], 0.25), writes=[t_eps])

        prt = carve(0, 128, F32)[0:82, :]
        sin = carve(512, 4608, F32)
        NXB = 8
        xbuf = [carve(18944 + i * 4096, 1024, F32) for i in range(3)] + [carve(53760 + i * 4096, 1024, F32) for i in range(5)]
        xnb = [carve(31232 + i * 2048, 1024, BF16) for i in range(2)]
        junk = carve(35328, 1024, BF16)
        mnT_t = carve(37376, KC * NM, BF16)
        mnT = mnT_t.rearrange("p (k t) -> p k t", k=KC)
        kT_t = carve(41472, 4 * NM, BF16)
        kT = kT_t.rearrange("p (k t) -> p k t", k=4)
        vb_t = carve(43520, 2 * XW, BF16)
        vb = vb_t.rearrange("p (k t) -> p k t", k=2)
        kvout = [carve(45568 + i * 4096, 2 * XW, F32).rearrange("p (k t) -> p k t", k=2) for i in range(2)]

        t_prt = Tok("prt")

        def rows(v, r):
            return v.rearrange("(r c) -> r c", c=128)

        plist = [(norm_g, 0, 8, None), (mem_norm_g, 8, 8, None),
                 (lru_conv_w, 16, 24, "k (n c) -> (k n) c"), (lru_conv_b, 40, 6, None), (lru_ba, 46, 6, None),
                 (lru_bx, 52, 6, None), (lru_lambda, 58, 6, None), (sconv_w, 64, 18, "k (n c) -> (k n) c")]
        for (v, r0, nr, pat) in plist:
            src = v.rearrange(pat, c=128) if pat else v.rearrange("(r c) -> r c", c=128)
            S.dma("sp", "cst", prt[r0:r0 + nr, :], src, writes=[t_prt], nowait=True)
        t_sin = Tok("sin")
        S.dma("sp", "cs2", sin[0:TS, 0:2304], st_lc[:, :], writes=[t_sin], nowait=True)
        S.dma("sp", "cs2", sin[0:TS, 2304:3072], st_h[:, :], writes=[t_sin], nowait=True)
        S.dma("sp", "cs2", sin[0:TS, 3072:4608], st_sc[:, :], writes=[t_sin], nowait=True)
        t_out = Tok("out")
        o_slc3 = o_slc.rearrange("b (k c) -> b k c", k=3)
        st_lc3 = st_lc.rearrange("b (k c) -> b k c", k=3)
        S.dma("sp", "out", o_slc3[:, 0:2, :], st_lc3[:, 1:3, :])
        o_ssc3 = o_ssc.rearrange("b (k c) -> b k c", k=2)
        st_sc3 = st_sc.rearrange("b (k c) -> b k c", k=2)
        S.dma("sp", "out", o_ssc3[:, 0:1, :], st_sc3[:, 1:2, :])

        if STOP_AFTER == "P0dma":
            S.barrier()
            S.finish(block)
            return nc
        t_pT = Tok("pT")
        S.op("pe", lambda e: e.transpose(PX[:, 0:82], prt, ident[0:82, 0:82]), reads=[t_prt, t_ident], writes=[tPX])
        S.op("act", lambda e: e.copy(out=pT[:], in_=PX[:, 0:82]), reads=[tPX], writes=[t_pT])
        if STOP_AFTER == "P0b":
            S.barrier()
            S.finish(block)
            return nc
        t_x = [Tok(f"x{i}") for i in range(NXB)]
        t_xn = [Tok(f"xn{i}") for i in range(2)]
        t_junk = Tok("junk")
        t_uT = Tok("uT")
        t_mnT = Tok("mnT")
        tiles = [(memp[i * 128:(i + 1) * 128, :], 128, "m", i * 128) for i in range(2)]
        tiles += [(xp[i * 128:(i + 1) * 128, :], 128, "u", i * 128) for i in range(16)]
        tiles.append((xs[:, :], TS, "u", T))
        xsem = ["xa", "xb", "xc", "ya", "yb", "ka", "kb", "va"]
        tPXh = [Tok("PXh0"), Tok("PXh1")]
        t_st0 = [Tok(f"st0_{i}") for i in range(len(tiles))]

        def p0_stageA(ti):
            src, nr, kind, c0 = tiles[ti]
            xb_, tx = xbuf[ti % NXB], t_x[ti % NXB]
            S.dma("sp", xsem[ti % NXB], xb_[0:nr, :], src, writes=[tx])
            S.op("act", lambda e, xb_=xb_, nr=nr, ti=ti: e.activation(out=junk[0:nr, :], in_=xb_[0:nr, :], func=AF.Square,
                                                                      accum_out=stat[0:nr, ti:ti + 1]),
                 reads=[tx, t_stat], writes=[t_st0[ti], t_junk])

        def p0_stageB(ti):
            src, nr, kind, c0 = tiles[ti]
            xb_, tx = xbuf[ti % NXB], t_x[ti % NXB]
            xn_, txn = xnb[ti % 2], t_xn[ti % 2]
            S.op("act", lambda e, nr=nr, ti=ti: e.activation(out=stat[0:nr, ti:ti + 1], in_=stat[0:nr, ti:ti + 1], func=AF.Sqrt,
                                                             scale=1.0 / D, bias=epst[0:nr, 0:1]),
                 reads=[t_st0[ti], t_eps], writes=[t_st0[ti]])
            S.op("dve", lambda e, nr=nr, ti=ti: e.reciprocal(out=stat[0:nr, ti:ti + 1], in_=stat[0:nr, ti:ti + 1]),
                 reads=[t_st0[ti]], writes=[t_st0[ti]])
            S.op("dve", lambda e, xb_=xb_, xn_=xn_, nr=nr, ti=ti: e.tensor_scalar(out=xn_[0:nr, :], in0=xb_[0:nr, :],
                                                                                scalar1=stat[0:nr, ti:ti + 1], scalar2=None,
                                                                                op0=ALU.mult),
                 reads=[tx, t_st0[ti]], writes=[txn])

        def p0_stageC(ti):
            src, nr, kind, c0 = tiles[ti]
            xn_, txn = xnb[ti % 2], t_xn[ti % 2]
            bankb, tbank = (PXb, tPX) if ti % 2 == 0 else (PSb, tPSb)
            for kc in range(KC):
                S.op("pe", lambda e, xn_=xn_, nr=nr, kc=kc, bankb=bankb: e.transpose(bankb[:, kc * 128: kc * 128 + nr],
                                                                        xn_[0:nr, kc * 128:(kc + 1) * 128],
                                                                        identb[0:nr, 0:nr]),
                     reads=[txn, t_identb], writes=[tbank], inc=(kc == KC - 1))
            pview = bankb.rearrange("p (k t) -> p k t", k=KC)[:, :, 0:nr]
            if kind == "u":
                dst = uT[:, :, c0:c0 + nr]
                gcols = pT[:, 0:8]
                tdst = t_uTi[ti]
            else:
                dst = mnT[:, :, c0:c0 + nr]
                gcols = pT[:, 8:16]
                tdst = t_mnT
            S.op("dve", lambda e, dst=dst, pview=pview, gcols=gcols, nr=nr: e.tensor_tensor(
                out=dst, in0=pview, in1=gcols.unsqueeze(2).to_broadcast([128, KC, nr]), op=ALU.mult),
                reads=[tbank, t_pT], writes=[tdst])

        t_kT = Tok("kT")
        t_vb = Tok("vb")
        t_kvout = [Tok("kvo0"), Tok("kvo1")]
        kvst = {}

        def kv_mm():
            for which in range(2):
                wv_, wt_ = [], []
                for hh in range(2):
                    vws, tk = w_get(WT[("wk" if which == 0 else "wv") + str(hh)])
                    wv_.append(vws[0])
                    wt_.append(tk)
                pp, tk = next_pp()
                for mc in range(2):
                    for hh in range(2):
                        pairs = [(mnT[:, k, mc * 128:(mc + 1) * 128], wv_[hh][:, k, :]) for k in range(KC)]
                        mm_group(pp[:, mc * 512 + hh * 256: mc * 512 + (hh + 1) * 256], pairs, [t_mnT, wt_[hh]], tk,
                                 last_inc=(mc == 1 and hh == 1))
                kvst[which] = (pp, tk)
                if which == 0:
                    pp2, tk2 = next_pp()
                    for dc in range(4):
                        hh, off = dc // 2, (dc % 2) * 128
                        pairs = [(wv_[hh][:, k, off:off + 128], mnT[:, k, :]) for k in range(KC)]
                        mm_group(pp2[:, dc * 256:(dc + 1) * 256], pairs, [t_mnT, wt_[hh]], tk2, last_inc=(dc == 3))
                    kvst["kT"] = (pp2, tk2)

        def kv_evac():
            for which in range(2):
                pp, tk = kvst[which]
                ko = kvout[which]
                S.op("act", lambda e, ko=ko, pp=pp: e.copy(out=ko, in_=pp[:].rearrange("p (k t) -> p k t", k=2)),
                     reads=[tk], writes=[t_kvout[which]])
                dsto = (o_pk if which == 0 else o_pv).rearrange("(k p) c -> p k c", p=128)
                S.dma("sp", "out", dsto, ko, reads=[t_kvout[which]])
            pp, tk = kvst[1]
            S.op("act", lambda e, pp=pp: e.copy(out=vb, in_=pp[:].rearrange("p (k t) -> p k t", k=2)),
                 reads=[tk], writes=[t_vb])
            pp2, tk2 = kvst["kT"]
            S.op("dve", lambda e, pp2=pp2: e.tensor_copy(kT, pp2[:].rearrange("p (k t) -> p k t", k=4)),
                 reads=[tk2], writes=[t_kT])

        t_uTi = [Tok(f"uT{i}") for i in range(len(tiles))]
        PD = 5
        for ti in range(PD):
            p0_stageA(ti)
        p0_stageB(0)
        for ti in range(len(tiles)):
            if ti + PD < len(tiles):
                p0_stageA(ti + PD)
            if ti + 1 < len(tiles):
                p0_stageB(ti + 1)
            p0_stageC(ti)
            if ti == 1:
                kv_mm()
            if ti == 5:
                kv_evac()

        t_dv = Tok("dv")
        S.op("act", lambda e: e.activation(out=dv[:, 0:6], in_=pT[:, 58:64], func=AF.Exp, scale=-1.0),
             reads=[t_pT], writes=[t_dv])
        S.op("act", lambda e: e.activation(out=dv[:, 6:12], in_=dv[:, 0:6], func=AF.Ln, bias=1.0, scale=1.0),
             reads=[t_dv], writes=[t_dv])
        S.op("dve", lambda e: e.tensor_scalar(out=dv[:, 0:6], in0=dv[:, 6:12], scalar1=-4.0, scalar2=None, op0=ALU.mult),
             reads=[t_dv], writes=[t_dv])
        S.op("dve", lambda e: e.tensor_scalar(out=dv[:, 6:12], in0=dv[:, 6:12], scalar1=-8.0, scalar2=None, op0=ALU.mult),
             reads=[t_dv], writes=[t_dv])
        S.op("dve", lambda e: e.tensor_scalar(out=dv[:, 12:24], in0=pT[:, 46:58], scalar1=0.5, scalar2=None, op0=ALU.mult),
             reads=[t_pT, t_dv], writes=[t_dv])
        if STOP_AFTER == "P0a":
            S.barrier()
            S.finish(block)
            return nc
        t_SIT = Tok("SIT")
        blocks_ = []
        for n in range(NCH):
            for k in range(3):
                blocks_.append((n * 6 + k, k * W + n * 128))
            blocks_.append((n * 6 + 3, 2304 + n * 128))
            for k in range(2):
                blocks_.append((n * 6 + 4 + k, 3072 + k * W + n * 128))
        for g0 in range(0, 36, 18):
            grp = blocks_[g0:g0 + 18]
            for gi, (slot_i, c0) in enumerate(grp):
                S.op("pe", lambda e, gi=gi, c0=c0: e.transpose(PX[:, gi * TS:(gi + 1) * TS], sin[0:TS, c0:c0 + 128],
                                                              ident[0:TS, 0:TS]),
                     reads=[t_sin, t_ident], writes=[tPX], inc=(gi == len(grp) - 1))
            for gi, (slot_i, c0) in enumerate(grp):
                S.op("act", lambda e, gi=gi, slot_i=slot_i: e.copy(out=SIT[:, slot_i, :], in_=PX[:, gi * TS:(gi + 1) * TS]),
                     reads=[tPX], writes=[t_SIT])

        if STOP_AFTER == "P0c":
            dump([pT[:], stat[:, 0:19], uT[:, 0, 0:256], uT[:, 7, T - 128:TT], mnT[:, 0, :], mnT[:, 7, :]])
            S.barrier()
            S.finish(block)
            return nc
        S.barrier(skip=("out",))
        if STOP_AFTER == "KV":
            S.finish(block)
            return nc

        qT_t = carve(0, 4 * T, BF16)
        qT = qT_t.rearrange("p (k t) -> p k t", k=4)
        qs_tok = carve(16384, XW, BF16)
        sqg_tok = carve(17408, XW, F32)
        pTe = [carve(19456 + i * 2048, 1024, BF16).rearrange("p (k t) -> p k t", k=2) for i in range(2)]
        rden = [carve(23552 + i * 2048, 512, F32) for i in range(2)]
        o1b = [carve(27648 + i * 2048, 512, F32) for i in range(2)]
        selb_t = carve(31744, TS * 128, BF16)
        selb = selb_t.rearrange("p (b c) -> p b c", b=TS)
        eye16_t = carve(35840, 256, F32)
        eye16 = eye16_t.rearrange("p (a b) -> p a b", a=16)
        Sall_t = carve(45568, 128, F32)
        Sall = Sall_t.rearrange("p (c b h) -> p c b h", c=2, b=TS)
        Esm = carve(46080, 256, F32)
        Psm = carve(47104, 256, F32)
        Mk_t = carve(48128, 2 * 4 * 16 * 16, BF16)
        Mk = Mk_t.rearrange("p (c h b q) -> p c h b q", c=2, h=4, b=16)
        qb_sb = [carve(56320 + i * 2048, 512, F32) for i in range(2)]
        Kb = [carve(60416 + i * 4096, 1024, F32).rearrange("p (c f) -> p c f", c=2) for i in range(2)]
        prod = carve(68608, 1024, F32).rearrange("p (c f) -> p c f", c=2)

        t_qT = [Tok(f"qT{h}") for h in range(4)]
        t_og = [Tok(f"og{h}") for h in range(4)]
        t_qs = Tok("qs")
        t_sqg = Tok("sqg")
        def next_pp_c():
            while True:
                i = state["pp"] % 3
                state["pp"] += 1
                if i != state.get("pp_excl", -1):
                    return PP[i], tPP[i]

        def unit_half_c(lhs_list, rhs_arr, lo, reads):
            pp, tk = next_pp_c()
            nk = len(lhs_list)
            for b in range(2):
                pairs = [(lhs_list[k], rhs_arr[:, k, lo + b * 512: lo + (b + 1) * 512]) for k in range(nk)]
                mm_group(pp[:, b * 512:(b + 1) * 512], pairs, reads, tk, last_inc=(b == 1))
            return pp, tk

        t_pTe = [Tok("pTe0"), Tok("pTe1")]
        t_rden = [Tok("rden0"), Tok("rden1")]
        t_o1 = [Tok("o10"), Tok("o11")]

        def gen_C_prompt():
            for hh in range(2):
                vws, wtk = w_get(WT[f"q{hh}"])
                wq = vws[0]
                for hl in range(2):
                    h = hh * 2 + hl
                    lhs = [wq[:, k, hl * 128:(hl + 1) * 128] for k in range(KC)]
                    for (lo, hi) in HALVES:
                        pp, tk = unit_half_c(lhs, uT, lo, [wtk])
                        S.op("act", lambda e, pp=pp, h=h, lo=lo, hi=hi: e.copy(out=qT[:, h, lo:hi], in_=pp[:]),
                             reads=[tk], writes=[t_qT[h]])
                        yield
                pairs = [(uT[:, k, T:TT], wq[:, k, :]) for k in range(KC)]
                mm_group(PS[0:TS, 0:256], pairs, [wtk], tPSb)
                S.op("act", lambda e, hh=hh: e.copy(out=qs_tok[0:TS, hh * 256:(hh + 1) * 256], in_=PS[0:TS, 0:256]),
                     reads=[tPSb], writes=[t_qs])
                yield
            for hh in range(2):
                vws, wtk = w_get(WT[f"qg{hh}"])
                wq = vws[0]
                for hl in range(2):
                    h = hh * 2 + hl
                    lhs = [wq[:, k, hl * 128:(hl + 1) * 128] for k in range(KC)]
                    for (lo, hi) in HALVES:
                        pp, tk = unit_half_c(lhs, uT, lo, [wtk])
                        S.op("act", lambda e, pp=pp, h=h, lo=lo, hi=hi: e.activation(out=og[:, h, lo:hi], in_=pp[:], func=AF.Silu),
                             reads=[tk], writes=[t_og[h]])
                        yield
                pairs = [(uT[:, k, T:TT], wq[:, k, :]) for k in range(KC)]
                mm_group(PS[0:TS, 0:256], pairs, [wtk], tPSb)
                S.op("act", lambda e, hh=hh: e.activation(out=sqg_tok[0:TS, hh * 256:(hh + 1) * 256],
                                                         in_=PS[0:TS, 0:256], func=AF.Silu),
                     reads=[tPSb], writes=[t_sqg])
                yield
            iters = [(tb, h) for tb in range(4) for h in range(4)]
            stage1 = {}

            def att_s1(i):
                tb, h = iters[i]
                c0 = tb * 512
                pe_, tpe = pTe[i % 2], t_pTe[i % 2]
                pp, tk = next_pp_c()
                for mc in range(2):
                    mm_group(pp[:, mc * 512:(mc + 1) * 512], [(kT[:, h, mc * 128:(mc + 1) * 128], qT[:, h, c0:c0 + 512])],
                             [t_kT, t_qT[h]], tk, last_inc=(mc == 1))
                S.op("act", lambda e, pp=pp, pe_=pe_: e.activation(out=pe_, in_=pp[:].rearrange("p (k t) -> p k t", k=2),
                                                                  func=AF.Exp, scale=SCALE),
                     reads=[tk], writes=[tpe])

            def att_s2(i):
                tb, h = iters[i]
                c0 = tb * 512
                pe_, tpe = pTe[i % 2], t_pTe[i % 2]
                rd, trd = rden[i % 2], t_rden[i % 2]
                o1, to1 = o1b[i % 2], t_o1[i % 2]
                pp2, tk2 = next_pp_c()
                mm_group(pp2[:, 0:512], [(vb[:, mc, h * 128:(h + 1) * 128], pe_[:, mc, :]) for mc in range(2)],
                         [t_vb, tpe], tk2, last_inc=False)
                mm_group(pp2[:, 512:1024], [(onesb[:], pe_[:, mc, :]) for mc in range(2)], [t_ones, tpe], tk2)
                S.op("act", lambda e, pp2=pp2, rd=rd: e.activation(out=rd, in_=pp2[:, 512:1024], func=AF.Ln), reads=[tk2], writes=[trd])
                S.op("act", lambda e, rd=rd: e.activation(out=rd, in_=rd, func=AF.Exp, scale=-1.0), reads=[trd], writes=[trd])
                S.op("dve", lambda e, pp2=pp2, rd=rd, o1=o1: e.tensor_tensor(out=o1, in0=pp2[:, 0:512], in1=rd, op=ALU.mult),
                     reads=[tk2, trd], writes=[to1])
                S.op("dve", lambda e, o1=o1, h=h, c0=c0: e.tensor_tensor(out=og[:, h, c0:c0 + 512], in0=o1,
                                                                        in1=og[:, h, c0:c0 + 512], op=ALU.mult),
                     reads=[to1, t_og[h]], writes=[t_og[h]])

            att_s1(0)
            for i in range(len(iters)):
                if i + 1 < len(iters):
                    att_s1(i + 1)
                att_s2(i)
                yield

        t_selb = Tok("selb")
        t_eye = Tok("eye")
        t_qb = [Tok("qb0"), Tok("qb1")]
        t_Kb = [Tok("Kb0"), Tok("Kb1")]
        t_prod = Tok("prod")
        t_Sall = Tok("Sall")
        t_E = Tok("E")
        t_P = Tok("P")
        t_Mk = Tok("Mk")
        ksem = ["ka", "kb"]
        vsem = ["sva", "svb"]

        def gen_C_sample():
            S.wait_dma(["dve", "act", "pool", "pe"], "out")
            S.op("dve", lambda e: e.tensor_copy(selb[0:TS, :, :], identb[0:TS, 0:TS].unsqueeze(2).to_broadcast([TS, TS, 128])),
                 reads=[t_identb], writes=[t_selb])
            S.op("pool", lambda e: e.memset(eye16_t, 0.0), writes=[t_eye])
            S.op("pool", lambda e: e.affine_select(eye16, eye16, pattern=[[1, 16], [-1, 16]], compare_op=ALU.not_equal,
                                                   fill=1.0, base=0, channel_multiplier=0),
                 reads=[t_eye], writes=[t_eye])
            yield
            for b in range(TS):
                kb_, tkb = Kb[b % 2], t_Kb[b % 2]
                qb_, tqb = qb_sb[b % 2], t_qb[b % 2]
                S.dma("sp", ksem[b % 2], kb_, ck[b].rearrange("(c m) f -> m c f", c=2), writes=[tkb])
                S.op("pe", lambda e, b=b: e.matmul(PX[:, 0:512], selb[0:TS, b, :], qs_tok[0:TS, :], start=True, stop=True),
                     reads=[t_selb, t_qs], writes=[tPX])
                S.op("act", lambda e, qb_=qb_: e.copy(out=qb_, in_=PX[:, 0:512]), reads=[tPX], writes=[tqb])
                S.op("pool", lambda e, kb_=kb_, qb_=qb_: e.tensor_tensor(out=prod, in0=kb_,
                                                                         in1=qb_.unsqueeze(1).to_broadcast([128, 2, 512]), op=ALU.mult),
                     reads=[tkb, tqb], writes=[t_prod])
                S.op("dve", lambda e, b=b: e.tensor_reduce(out=Sall[:, :, b, :],
                                                           in_=prod.rearrange("p c (h d) -> p c h d", h=4), axis=AX.X, op=ALU.add),
                     reads=[t_prod], writes=[t_Sall])
                yield
            for c in range(2):
                S.op("pe", lambda e, c=c: e.transpose(PX[0:64, c * 128:(c + 1) * 128],
                                                      Sall_t[:, c * 64:(c + 1) * 64], ident[:, :]),
                     reads=[t_Sall, t_ident], writes=[tPX], inc=(c == 1))
            S.op("dve", lambda e: e.tensor_reduce(out=stat2[0:64, 0:1], in_=PX[0:64, 0:256], axis=AX.X, op=ALU.max),
                 reads=[tPX], writes=[t_sm])
            S.op("dve", lambda e: e.tensor_scalar(out=stat2[0:64, 0:1], in0=stat2[0:64, 0:1], scalar1=-SCALE, scalar2=None, op0=ALU.mult),
                 reads=[t_sm], writes=[t_sm])
            S.op("act", lambda e: e.activation(out=Esm[0:64, :], in_=PX[0:64, 0:256], func=AF.Exp, scale=SCALE,
                                               bias=stat2[0:64, 0:1], accum_out=stat2[0:64, 1:2]),
                 reads=[tPX, t_sm], writes=[t_E, t_sm])
            S.op("dve", lambda e: e.reciprocal(out=stat2[0:64, 1:2], in_=stat2[0:64, 1:2]), reads=[t_sm], writes=[t_sm])
            S.op("dve", lambda e: e.tensor_scalar(out=Psm[0:64, :], in0=Esm[0:64, :], scalar1=stat2[0:64, 1:2], scalar2=None, op0=ALU.mult),
                 reads=[t_E, t_sm], writes=[t_P])
            for c in range(2):
                S.op("pe", lambda e, c=c: e.transpose(PX[:, c * 64:(c + 1) * 64], Psm[0:64, c * 128:(c + 1) * 128], ident[0:64, 0:64]),
                     reads=[t_P, t_ident], writes=[tPX], inc=(c == 1))
            for c in range(2):
                S.op("dve", lambda e, c=c: e.tensor_tensor(
                    out=Mk[:, c],
                    in0=PX[:, c * 64:(c + 1) * 64].rearrange("p (b h) -> p h b", h=4).unsqueeze(3).to_broadcast([128, 4, 16, 16]),
                    in1=eye16.unsqueeze(1).to_broadcast([128, 4, 16, 16]), op=ALU.mult),
                    reads=[tPX, t_eye], writes=[t_Mk])
            yield
            ppa, tka = next_pp_c()
            state["pp_excl"] = PP.index(ppa)
            hbank = [(PS, 0, tPSb), (PX, 0, tPX), (ppa, 0, tka), (ppa, 512, tka)]
            Vb = [carve(68608 + i * 2048, 1024, BF16).rearrange("p (c f) -> p c f", c=2) for i in range(2)]
            t_Vb = [Tok("Vb0"), Tok("Vb1")]
            for b in range(TS):
                vb_, tvb = Vb[b % 2], t_Vb[b % 2]
                S.dma("pool", vsem[b % 2], vb_, cv[b].rearrange("(c m) f -> m c f", c=2), writes=([tvb, t_prod] if b < 2 else [tvb]))
                for h in range(4):
                    pph, coff, tkh = hbank[h]
                    for c in range(2):
                        first = (b == 0 and c == 0)
                        last = (b == TS - 1 and c == 1)
                        S.op("pe", lambda e, vb_=vb_, b=b, h=h, c=c, first=first, last=last, pph=pph, coff=coff: e.matmul(
                            pph[0:TS, coff:coff + 128], Mk[:, c, h, b, :], vb_[:, c, h * 128:(h + 1) * 128],
                            start=first, stop=last, skip_group_check=True),
                            reads=[t_Mk, tvb], writes=[tkh], inc=(h == 3 and c == 1))
                yield
            ogs_tok = qs_tok
            for h in range(4):
                pph, coff, tkh = hbank[h]
                S.op("dve", lambda e, h=h, pph=pph, coff=coff: e.tensor_tensor(
                    out=ogs_tok[0:TS, h * 128:(h + 1) * 128], in0=pph[0:TS, coff:coff + 128],
                    in1=sqg_tok[0:TS, h * 128:(h + 1) * 128], op=ALU.mult),
                    reads=[tkh, t_sqg, t_qs], writes=[t_qs])
            state["pp_excl"] = -1
            for h in range(4):
                S.op("pe", lambda e, h=h: e.transpose(PXb[:, h * TS:(h + 1) * TS], ogs_tok[0:TS, h * 128:(h + 1) * 128],
                                                      identb[0:TS, 0:TS]),
                     reads=[t_qs, t_identb], writes=[tPX], inc=(h == 3))
            S.op("act", lambda e: e.copy(out=og[:, :, T:TT], in_=PXb[:, 0:4 * TS].rearrange("p (h b) -> p h b", h=4)),
                 reads=[tPX], writes=t_og)
            yield

        gp, gs = gen_C_prompt(), gen_C_sample()
        for _ in range(10):
            next(gp)
        alive = [gp, gs]
        while alive:
            for g in list(alive):
                try:
                    next(g)
                except StopIteration:
                    alive.remove(g)
        if STOP_AFTER == "C":
            dump([og[:, 0, 0:512], og[:, 3, T - 512:T], og[:, 0, T:TT], og[:, 1, T:TT], og[:, 2, T:TT], og[:, 3, T:TT], qT[:, 0, 0:256]], off_b=0)
        S.barrier(no_wait=("pe",))
        if STOP_AFTER == "C":
            S.finish(block)
            return nc

        wab_t = carve(0, 2 * NCH * 128, BF16)
        wab = wab_t.rearrange("p (a n d) -> p a n d", a=2, n=NCH)
        lxp = carve(3072, 3 + T, F32)
        lxs_t = carve(11280, 4 * TS, F32)
        lxs = lxs_t.rearrange("p (k b) -> p k b", k=4)
        xc = carve(11536, TT, F32)
        xcb = carve(19792, TT, BF16)
        thr = carve(23920, TT, F32)
        a2b = carve(32176, TT, F32)
        thi = carve(40432, TT, F32)
        scg_sb = carve(48752, TT, F32)
        cy = scg_sb
        cinp = carve(57008, 2 + T, F32)
        cins_t = carve(65208, 3 * TS, F32)
        cins = cins_t.rearrange("p (k b) -> p k b", k=3)
        so_tok = carve(65400, W, F32)
        t_wab = Tok("wab")
        S.dma("pool", "cs3", wab[:, 0], lru_wa.rearrange("n c d -> c n d"), writes=[t_wab])
        S.dma("pool", "cs3", wab[:, 1], lru_wx.rearrange("n c d -> c n d"), writes=[t_wab], nowait=True)
        t_lxp, t_lxs, t_xc, t_xcb, t_thr, t_a2, t_thi = (Tok("lxp"), Tok("lxs"), Tok("xc"), Tok("xcb"), Tok("thr"),
                                                        Tok("a2"), Tok("thi"))
        t_gA = [Tok(f"gA{n}") for n in range(NCH)]
        t_SO = Tok("SO")
        t_scg, t_cinp, t_cins = Tok("scg"), Tok("cinp"), Tok("cins")
        t_cy = t_scg
        t_gB = [Tok(f"gB{n}") for n in range(NCH)]
        S.op("pool", lambda e: e.memset(lxp[:, 0:3], 0.0), writes=[t_lxp])
        S.op("pool", lambda e: e.memset(cinp[:, 0:2], 0.0), writes=[t_cinp])

        def gen_A(n):
            vws, wtk = w_get(WT[f"A{n}"])
            wlg, wlx = vws
            lhs_lg = [wlg[:, k, :] for k in range(KC)]
            lhs_lx = [wlx[:, k, :] for k in range(KC)]
            for (lo, hi) in HALVES:
                pp, tk = unit_half(lhs_lg, uT, lo, [wtk])
                S.op("act", lambda e, pp=pp, n=n, lo=lo, hi=hi: e.activation(out=gA[:, n, lo:hi], in_=pp[:], func=AF.Silu),
                     reads=[tk], writes=[t_gA[n]])
            sp_, tk = unit_samp(lhs_lg, uT, [wtk])
            S.op("act", lambda e, sp_=sp_, n=n: e.activation(out=gA[:, n, T:TT], in_=sp_, func=AF.Silu),
                 reads=[tk], writes=[t_gA[n]])
            yield
            for (lo, hi) in HALVES:
                pp, tk = unit_half(lhs_lx, uT, lo, [wtk])
                S.op("dve", lambda e, pp=pp, lo=lo, hi=hi: e.tensor_copy(lxp[:, 3 + lo:3 + hi], pp[:]),
                     reads=[tk], writes=[t_lxp])
            sp_, tk = unit_samp(lhs_lx, uT, [wtk])
            S.op("dve", lambda e, n=n: e.tensor_copy(lxs[:, 0:3, :], SIT[:, n * 6:n * 6 + 3, :]), reads=[t_SIT], writes=[t_lxs])
            S.op("dve", lambda e, sp_=sp_: e.tensor_copy(lxs[:, 3, :], sp_), reads=[tk], writes=[t_lxs])
            yield
            cw = lambda k, n=n: pT[:, 16 + k * 6 + n: 17 + k * 6 + n]
            cbias = pT[:, 40 + n:41 + n]
            S.op("act", lambda e, cw=cw, cbias=cbias: e.activation(out=xc[:, 0:T], in_=lxp[:, 0:T], func=AF.Identity,
                                                                   scale=cw(0), bias=cbias),
                 reads=[t_lxp, t_pT], writes=[t_xc])
            S.op("act", lambda e, cw=cw, cbias=cbias: e.activation(out=xc[:, T:TT], in_=lxs[:, 0, :], func=AF.Identity,
                                                                   scale=cw(0), bias=cbias),
                 reads=[t_lxs, t_pT], writes=[t_xc])
            for k in range(1, 4):
                S.op("dve", lambda e, k=k, cw=cw: e.scalar_tensor_tensor(out=xc[:, 0:T], in0=lxp[:, k:k + T], scalar=cw(k),
                                                                         in1=xc[:, 0:T], op0=ALU.mult, op1=ALU.add),
                     reads=[t_lxp, t_xc], writes=[t_xc])
                S.op("dve", lambda e, k=k, cw=cw: e.scalar_tensor_tensor(out=xc[:, T:TT], in0=lxs[:, k, :], scalar=cw(k),
                                                                         in1=xc[:, T:TT], op0=ALU.mult, op1=ALU.add),
                     reads=[t_lxs, t_xc], writes=[t_xc])
            S.op("pool", lambda e, n=n: e.tensor_copy(SO[:, n, 0:3], lxp[:, T:T + 3]), reads=[t_lxp], writes=[t_SO])
            S.op("pool", lambda e, n=n: e.tensor_copy(SO[:, n, 6:22], lxs[:, 3, :]), reads=[t_lxs], writes=[t_SO])
            yield
            S.op("act", lambda e: e.copy(out=xcb, in_=xc), reads=[t_xc], writes=[t_xcb])
            yield
            for gi, (dst, tdst, bcol) in enumerate([(thr, t_thr, 12 + n), (thi, t_thi, 18 + n)]):
                lhs = [wab[:, gi, n, :]]
                xcb3 = xcb.unsqueeze(1)
                for (lo, hi) in HALVES:
                    pp, tk = unit_half(lhs, xcb3, lo, [t_wab, t_xcb])
                    S.op("act", lambda e, pp=pp, dst=dst, lo=lo, hi=hi, bcol=bcol: e.activation(
                        out=dst[:, lo:hi], in_=pp[:], func=AF.Tanh, scale=0.5, bias=dv[:, bcol:bcol + 1]),
                        reads=[tk, t_dv], writes=[tdst])
                sp_, tk = unit_samp(lhs, xcb3, [t_wab, t_xcb])
                S.op("act", lambda e, sp_=sp_, dst=dst, bcol=bcol: e.activation(
                    out=dst[:, T:TT], in_=sp_, func=AF.Tanh, scale=0.5, bias=dv[:, bcol:bcol + 1]),
                    reads=[tk, t_dv], writes=[tdst])
                yield
            S.op("act", lambda e, n=n: e.activation(out=a2b, in_=thr, func=AF.Exp, scale=dv[:, 6 + n:7 + n], bias=dv[:, 6 + n:7 + n]),
                 reads=[t_thr, t_dv], writes=[t_a2])
            S.op("act", lambda e, n=n: e.activation(out=thr, in_=thr, func=AF.Exp, scale=dv[:, n:n + 1], bias=dv[:, n:n + 1]),
                 reads=[t_thr, t_dv], writes=[t_thr])
            S.op("dve", lambda e: e.tensor_scalar(out=a2b, in0=a2b, scalar1=1.0, scalar2=-1.0, op0=ALU.min, op1=ALU.mult),
                 reads=[t_a2], writes=[t_a2])
            yield
            S.op("act", lambda e: e.activation(out=a2b, in_=a2b, func=AF.Sqrt, bias=1.0, scale=1.0), reads=[t_a2], writes=[t_a2])
            S.op("dve", lambda e: e.scalar_tensor_tensor(out=a2b, in0=a2b, scalar=0.5, in1=xc, op0=ALU.mult, op1=ALU.mult),
                 reads=[t_a2, t_xc], writes=[t_a2])
            S.op("dve", lambda e: e.scalar_tensor_tensor(out=thi, in0=thi, scalar=1.0, in1=a2b, op0=ALU.add, op1=ALU.mult),
                 reads=[t_thi, t_a2], writes=[t_thi])
            yield
            S.op("dve", lambda e: e.tensor_tensor_scan(out=xc[:, 0:T], data0=thr[:, 0:T], data1=thi[:, 0:T], initial=0.0,
                                                       op0=ALU.mult, op1=ALU.add),
                 reads=[t_thr, t_thi, t_a2], writes=[t_xc])
            S.op("dve", lambda e, n=n: e.tensor_tensor(out=xc[:, T:TT], in0=thr[:, T:TT], in1=SIT[:, n * 6 + 3, :], op=ALU.mult),
                 reads=[t_thr, t_SIT, t_a2], writes=[t_xc])
            S.op("dve", lambda e: e.tensor_tensor(out=xc[:, T:TT], in0=xc[:, T:TT], in1=thi[:, T:TT], op=ALU.add),
                 reads=[t_xc, t_thi], writes=[t_xc])
            S.op("pool", lambda e, n=n: e.tensor_copy(SO[:, n, 3:4], xc[:, T - 1:T]), reads=[t_xc], writes=[t_SO])
            S.op("pool", lambda e, n=n: e.tensor_copy(SO[:, n, 22:38], xc[:, T:TT]), reads=[t_xc], writes=[t_SO])
            S.op("dve", lambda e, n=n: e.tensor_tensor(out=gA[:, n, :], in0=xc, in1=gA[:, n, :], op=ALU.mult),
                 reads=[t_xc, t_gA[n]], writes=[t_gA[n]])
            yield

        def gen_B(n):
            vws, wtk = w_get(WT[f"B{n}a"])
            wsg, wsb = vws
            lhs_sg = [wsg[:, k, :] for k in range(KC)]
            lhs_sb = [wsb[:, k, :] for k in range(KC)]
            for (lo, hi) in HALVES:
                pp, tk = unit_half(lhs_sg, uT, lo, [wtk])
                S.op("act", lambda e, pp=pp, n=n, lo=lo, hi=hi: e.activation(out=gB[:, n, lo:hi], in_=pp[:], func=AF.Silu),
                     reads=[tk], writes=[t_gB[n]])
            sp_, tk = unit_samp(lhs_sg, uT, [wtk])
            S.op("act", lambda e, sp_=sp_, n=n: e.activation(out=gB[:, n, T:TT], in_=sp_, func=AF.Silu),
                 reads=[tk], writes=[t_gB[n]])
            yield
            for (lo, hi) in HALVES:
                pp, tk = unit_half(lhs_sb, uT, lo, [wtk])
                S.op("dve", lambda e, pp=pp, n=n, lo=lo, hi=hi: e.tensor_tensor(out=gB[:, n, lo:hi], in0=pp[:], in1=gB[:, n, lo:hi],
                                                                                op=ALU.mult),
                     reads=[tk, t_gB[n]], writes=[t_gB[n]])
            sp_, tk = unit_samp(lhs_sb, uT, [wtk])
            S.op("dve", lambda e, sp_=sp_, n=n: e.tensor_tensor(out=gB[:, n, T:TT], in0=sp_, in1=gB[:, n, T:TT], op=ALU.mult),
                 reads=[tk, t_gB[n]], writes=[t_gB[n]])
            yield
            vws, wtk = w_get(WT[f"B{n}b"])
            wscg, wsh = vws
            lhs_scg = [wscg[:, k, :] for k in range(KC)]
            lhs_sh = [wsh[:, k, :] for k in range(KC)]
            for (lo, hi) in HALVES:
                pp, tk = unit_half(lhs_scg, uT, lo, [wtk])
                S.op("act", lambda e, pp=pp, lo=lo, hi=hi: e.copy(out=scg_sb[:, lo:hi], in_=pp[:]), reads=[tk], writes=[t_scg])
            sp_, tk = unit_samp(lhs_scg, uT, [wtk])
            S.op("act", lambda e, sp_=sp_: e.copy(out=scg_sb[:, T:TT], in_=sp_), reads=[tk], writes=[t_scg])
            yield
            for (lo, hi) in HALVES:
                pp, tk = unit_half(lhs_sh, uT, lo, [wtk])
                S.op("dve", lambda e, pp=pp, lo=lo, hi=hi: e.tensor_tensor(out=cinp[:, 2 + lo:2 + hi], in0=pp[:],
                                                                           in1=scg_sb[:, lo:hi], op=ALU.mult),
                     reads=[tk, t_scg], writes=[t_cinp])
            sp_, tk = unit_samp(lhs_sh, uT, [wtk])
            S.op("dve", lambda e, n=n: e.tensor_copy(cins[:, 0:2, :], SIT[:, n * 6 + 4:n * 6 + 6, :]), reads=[t_SIT], writes=[t_cins])
            S.op("dve", lambda e, sp_=sp_: e.tensor_tensor(out=cins[:, 2, :], in0=sp_, in1=scg_sb[:, T:TT], op=ALU.mult),
                 reads=[tk, t_scg], writes=[t_cins])
            yield
            sw = lambda k, n=n: pT[:, 64 + k * 6 + n: 65 + k * 6 + n]
            S.op("act", lambda e, sw=sw: e.activation(out=cy[:, 0:T], in_=cinp[:, 0:T], func=AF.Identity, scale=sw(0)),
                 reads=[t_cinp, t_pT], writes=[t_cy])
            S.op("act", lambda e, sw=sw: e.activation(out=cy[:, T:TT], in_=cins[:, 0, :], func=AF.Identity, scale=sw(0)),
                 reads=[t_cins, t_pT], writes=[t_cy])
            for k in range(1, 3):
                S.op("dve", lambda e, k=k, sw=sw: e.scalar_tensor_tensor(out=cy[:, 0:T], in0=cinp[:, k:k + T], scalar=sw(k),
                                                                         in1=cy[:, 0:T], op0=ALU.mult, op1=ALU.add),
                     reads=[t_cinp, t_cy], writes=[t_cy])
                S.op("dve", lambda e, k=k, sw=sw: e.scalar_tensor_tensor(out=cy[:, T:TT], in0=cins[:, k, :], scalar=sw(k),
                                                                         in1=cy[:, T:TT], op0=ALU.mult, op1=ALU.add),
                     reads=[t_cins, t_cy], writes=[t_cy])
            S.op("pool", lambda e, n=n: e.tensor_copy(SO[:, n, 4:6], cinp[:, T:T + 2]), reads=[t_cinp], writes=[t_SO])
            S.op("pool", lambda e, n=n: e.tensor_copy(SO[:, n, 38:54], cins[:, 2, :]), reads=[t_cins], writes=[t_SO])
            S.op("dve", lambda e, n=n: e.tensor_tensor(out=gB[:, n, :], in0=cy, in1=gB[:, n, :], op=ALU.mult),
                 reads=[t_cy, t_gB[n]], writes=[t_gB[n]])
            yield

        def interleave(*gens):
            gens = list(gens)
            while gens:
                for g in list(gens):
                    try:
                        next(g)
                    except StopIteration:
                        gens.remove(g)

        gens = {}

        def adv(kind, n):
            g = gens.get((kind, n))
            if g is None:
                return
            try:
                next(g)
            except StopIteration:
                pass

        for n in range(NCH + 1):
            if n < NCH:
                gens[("A", n)] = gen_A(n)
            if n >= 1:
                gens[("B", n - 1)] = gen_B(n - 1)
            adv("A", n)
            adv("B", n - 1)
            adv("A", n - 1)
            adv("A", n)
            adv("B", n - 1)
            adv("A", n - 1)
            adv("A", n)
            adv("B", n - 1)
            adv("B", n - 1)
            adv("A", n)
            adv("A", n)
            adv("A", n)
            adv("B", n - 1)
            adv("A", n)
        for g in gens.values():
            for _ in g:
                pass
        if STOP_AFTER == "B":
            dump([gB[:, 0, 0:512], gB[:, 5, T - 512:T], gB[:, 0, T:TT], gB[:, 5, T:TT], gB[:, 2, 1024:1536]], off_b=0)
        t_sot = Tok("sot")
        for g0 in range(0, NCH, 3):
            for n in range(g0, g0 + 3):
                S.op("pe", lambda e, n=n, g0=g0: e.transpose(PX[0:54, (n - g0) * 128:(n - g0 + 1) * 128], SO[:, n, :], ident[:, :]),
                     reads=[t_SO, t_ident], writes=[tPX], inc=(n == g0 + 2))
            S.op("act", lambda e, g0=g0: e.copy(out=so_tok[0:54, g0 * 128:(g0 + 3) * 128], in_=PX[0:54, 0:384]),
                 reads=[tPX], writes=[t_sot])
        S.dma("sp", "out", o_plc[:, :], so_tok[0:3, :], reads=[t_sot])
        S.dma("sp", "out", o_ph[:, :], so_tok[3:4, :], reads=[t_sot])
        S.dma("sp", "out", o_psc[:, :], so_tok[4:6, :], reads=[t_sot])
        S.dma("sp", "out", o_slc3[:, 2, :], so_tok[6:22, :], reads=[t_sot])
        S.dma("sp", "out", o_sh[:, :], so_tok[22:38, :], reads=[t_sot])
        S.dma("sp", "out", o_ssc3[:, 1, :], so_tok[38:54, :], reads=[t_sot])
        if STOP_AFTER == "B":
            dump([gB[:, 0, 0:512], gB[:, 5, T - 512:T], gB[:, 0, T:TT], gB[:, 5, T:TT], gB[:, 2, 1024:1536]], off_b=56320)
        S.barrier(no_wait=("pe",))
        if STOP_AFTER == "B":
            S.finish(block)
            return nc

        NTH = 3
        thb = [carve(i * 4096, 1024, F32) for i in range(NTH)]
        tacc = [carve(12288 + i * 4096, 1024, F32) for i in range(2)]
        mT_t = carve(20480, KC * TT, BF16)
        mT = mT_t.rearrange("p (k t) -> p k t", k=KC)
        wo_t = carve(53504, KC * D, BF16)
        wo = wo_t.rearrange("p (k c) -> p k c", k=KC)
        t_wo = Tok("wo")
        if STOP_AFTER == "M0":
            S.barrier()
            S.finish(block)
            return nc
        t_th = [Tok(f"th{i}") for i in range(NTH)]
        t_acc = [Tok("acc0"), Tok("acc1")]
        t_mT = Tok("mT")
        thc = {"i": 0, "a": 0}
        gsrc = [(gA, NCH), (gB, NCH), (og, 4)]
        gtok = [t_gA, t_gB, t_og]
        ths_f = carve(69888, 3 * TS, F32)
        tzs_f = carve(70080, 3 * TS, F32)
        accs_f = carve(70272, TS, F32)
        t_accs2 = Tok("accs2")
        t_ths = Tok("ths")
        t_accs = Tok("accs")
        for j in range(8):
            vg, wtkg = w_get(WT[f"Mg{j}"])
            vo, wtko = w_get(WT[f"Mo{j}"])
            S.dma("pool", "cs4", wo[:, j, :], w_out[j * 128:(j + 1) * 128, :], writes=[t_wo], nowait=True)
            tMs_g, tMs_o = tPSb, tPX
            for x in range(3):
                lhs_g = [vg[x][:, k, :] for k in range(KC)]
                mm_group(PS[:, x * TS:(x + 1) * TS], [(lhs_g[k], uT[:, k, T:TT]) for k in range(KC)], [wtkg], tMs_g)
            S.op("act", lambda e: e.activation(out=ths_f, in_=PS[:, 0:3 * TS], func=AF.Tanh, scale=0.5),
                 reads=[tMs_g], writes=[t_ths])
            for x in range(3):
                garr, nk = gsrc[x]
                lhs_o = [vo[x][:, k, :] for k in range(nk)]
                mm_group(PX[:, x * TS:(x + 1) * TS], [(lhs_o[k], garr[:, k, T:TT]) for k in range(nk)], [wtko] + gtok[x], tMs_o)
            S.op("dve", lambda e: e.scalar_tensor_tensor(out=tzs_f, in0=ths_f, scalar=1.0, in1=PX[:, 0:3 * TS],
                                                         op0=ALU.add, op1=ALU.mult),
                 reads=[t_ths, tMs_o], writes=[t_accs])
            S.op("dve", lambda e: e.tensor_reduce(out=accs_f, in_=tzs_f.rearrange("p (x b) -> p b x", x=3),
                                                  axis=AX.X, op=ALU.add),
                 reads=[t_accs], writes=[t_accs2])
            S.op("dve", lambda e, j=j: e.tensor_copy(mT[:, j, T:TT], accs_f), reads=[t_accs2], writes=[t_mT])
            for (lo, hi) in HALVES:
                acc, tacc_ = tacc[thc["a"] % 2], t_acc[thc["a"] % 2]
                thc["a"] += 1
                for x in range(3):
                    garr, nk = gsrc[x]
                    lhs_g = [vg[x][:, k, :] for k in range(KC)]
                    lhs_o = [vo[x][:, k, :] for k in range(nk)]
                    th_, tth = thb[thc["i"] % NTH], t_th[thc["i"] % NTH]
                    thc["i"] += 1
                    pp, tk = unit_half(lhs_g, uT, lo, [wtkg])
                    S.op("act", lambda e, pp=pp, th_=th_: e.activation(out=th_, in_=pp[:], func=AF.Tanh, scale=0.5),
                         reads=[tk], writes=[tth])
                    zp_, tkz = unit_half(lhs_o, garr, lo, [wtko] + gtok[x])
                    zp = zp_[:]
                    if x == 0:
                        S.op("dve", lambda e, acc=acc, th_=th_, zp=zp: e.scalar_tensor_tensor(out=acc, in0=th_, scalar=1.0, in1=zp,
                                                                                           op0=ALU.add, op1=ALU.mult),
                             reads=[tth, tkz], writes=[tacc_])
                    else:
                        S.op("dve", lambda e, th_=th_, zp=zp: e.scalar_tensor_tensor(out=th_, in0=th_, scalar=1.0, in1=zp,
                                                                                    op0=ALU.add, op1=ALU.mult),
                             reads=[tth, tkz], writes=[tth])
                        if x == 1:
                            S.op("pool", lambda e, acc=acc, th_=th_: e.tensor_tensor(out=acc, in0=acc, in1=th_, op=ALU.add),
                                 reads=[tth, tacc_], writes=[tacc_])
                        else:
                            S.op("pool", lambda e, acc=acc, th_=th_, j=j, lo=lo, hi=hi: e.tensor_tensor(
                                out=mT[:, j, lo:hi], in0=acc, in1=th_, op=ALU.add),
                                reads=[tth, tacc_], writes=[t_mT])
            if STOP_AFTER == "M5":
                S.barrier()
                S.finish(block)
                return nc
        S.barrier(no_wait=("pe",))
        if STOP_AFTER == "M":
            S.finish(block)
            return nc

        xr = [carve(i * 4096, 1024, F32) for i in range(2)]
        yr = [carve(8192 + i * 4096, 1024, F32) for i in range(2)] + [carve(69888, 1024, F32)]
        fgb = carve(16384, 1024, F32)
        t_fgb = Tok("fgb")
        S.dma("sp", "cs5", fgb, final_norm_g.partition_broadcast(128), writes=[t_fgb])
        t_xr = [Tok("xr0"), Tok("xr1")]
        t_yr = [Tok("yr0"), Tok("yr1"), Tok("yr2")]
        ftiles = [(xp[i * 128:(i + 1) * 128, :], y_p[i * 128:(i + 1) * 128, :], 128, i * 128) for i in range(16)]
        ftiles.append((xs[:, :], y_s[:, :], TS, T))
        xsem2 = ["fxa", "fxb"]
        ysem = ["ya", "yb", "xc"]
        t_stF = [Tok(f"stF{i}") for i in range(len(ftiles))]

        def f_stageA(ti):
            src, dst, nr, c0 = ftiles[ti]
            xr_, txr = xr[ti % 2], t_xr[ti % 2]
            yr_, tyr = yr[ti % 3], t_yr[ti % 3]
            S.dma("pool", xsem2[ti % 2], xr_[0:nr, :], src, writes=[txr])
            pp, tk = next_pp()
            for b in range(2):
                pairs = [(mT[:, k, c0:c0 + nr], wo[:, k, b * 512:(b + 1) * 512]) for k in range(KC)]
                mm_group(pp[0:nr, b * 512:(b + 1) * 512], pairs, [t_mT, t_wo], tk, last_inc=(b == 1))
            S.op("dve", lambda e, pp=pp, yr_=yr_, xr_=xr_, nr=nr: e.scalar_tensor_tensor(
                out=yr_[0:nr, :], in0=pp[0:nr, :], scalar=0.5, in1=xr_[0:nr, :], op0=ALU.mult, op1=ALU.add),
                reads=[tk, txr], writes=[tyr])

        def f_stageA2(ti):
            src, dst, nr, c0 = ftiles[ti]
            xr_, txr = xr[ti % 2], t_xr[ti % 2]
            yr_, tyr = yr[ti % 3], t_yr[ti % 3]
            col = 32 + ti
            S.op("act", lambda e, xr_=xr_, yr_=yr_, nr=nr, col=col: e.activation(out=xr_[0:nr, :], in_=yr_[0:nr, :], func=AF.Square,
                                                                                accum_out=stat2[0:nr, col:col + 1]),
                 reads=[tyr, t_sm], writes=[txr, t_stF[ti]])

        def f_stageB1(ti):
            src, dst, nr, c0 = ftiles[ti]
            col = 32 + ti
            S.op("act", lambda e, nr=nr, col=col: e.activation(out=stat2[0:nr, col:col + 1], in_=stat2[0:nr, col:col + 1], func=AF.Sqrt,
                                                               scale=1.0 / D, bias=epst[0:nr, 0:1]),
                 reads=[t_stF[ti], t_eps], writes=[t_stF[ti]])

        def f_stageB(ti):
            src, dst, nr, c0 = ftiles[ti]
            yr_, tyr = yr[ti % 3], t_yr[ti % 3]
            col = 32 + ti
            S.op("dve", lambda e, nr=nr, col=col: e.reciprocal(out=stat2[0:nr, col:col + 1], in_=stat2[0:nr, col:col + 1]),
                 reads=[t_stF[ti]], writes=[t_stF[ti]])
            S.op("dve", lambda e, yr_=yr_, nr=nr, col=col: e.scalar_tensor_tensor(
                out=yr_[0:nr, :], in0=yr_[0:nr, :], scalar=stat2[0:nr, col:col + 1], in1=fgb[0:nr, :], op0=ALU.mult, op1=ALU.mult),
                reads=[tyr, t_stF[ti], t_fgb], writes=[tyr])
            S.dma("sp", ysem[ti % 3], dst, yr_[0:nr, :], reads=[tyr])

        f_stageA(0)
        f_stageA2(0)
        for ti in range(len(ftiles)):
            if ti + 1 < len(ftiles):
                f_stageA(ti + 1)
            f_stageB1(ti)
            if ti + 1 < len(ftiles):
                f_stageA2(ti + 1)
            f_stageB(ti)
        S.barrier()
        S.finish(block)
    return nc


_CACHE = {}


def _get_program():
    if "nc" not in _CACHE:
        _CACHE["nc"] = build_program()
    return _CACHE["nc"]


def kernel(x_prompt, x_sample, cache_mem_k, cache_mem_v, state_lru_h, state_lru_conv, state_sconv, mem_prompt,
           norm_g, mem_norm_g, w_in, lru_conv_w, lru_conv_b, lru_wa, lru_ba, lru_wx, lru_bx, lru_lambda, lru_wo,
           sconv_w, sconv_wo, xa_wk, xa_wv, xa_wo, w_out, final_norm_g):
    f = lambda a: np.ascontiguousarray(np.asarray(a, dtype=np.float32))
    shared = {
        "norm_g": f(norm_g[0]), "mem_norm_g": f(mem_norm_g[0]), "w_in": f(w_in[0]),
        "lru_conv_w": f(lru_conv_w[0]), "lru_conv_b": f(lru_conv_b[0]), "lru_wa": f(lru_wa[0]),
        "lru_ba": f(lru_ba[0]), "lru_wx": f(lru_wx[0]), "lru_bx": f(lru_bx[0]), "lru_lambda": f(lru_lambda[0]),
        "lru_wo": f(lru_wo[0]), "sconv_w": f(sconv_w[0]), "sconv_wo": f(sconv_wo[0]), "xa_wk": f(xa_wk[0]),
        "xa_wv": f(xa_wv[0]), "xa_wo": f(xa_wo[0]), "w_out": f(w_out[0]), "final_norm_g": f(final_norm_g),
    }
    in_maps = []
    for c in range(NCORES):
        sl = slice(c * TS, (c + 1) * TS)
        m = dict(shared)
        m["xp"] = f(x_prompt[c])
        m["xs"] = f(np.asarray(x_sample)[sl, 0, :])
        m["memp"] = f(mem_prompt[c])
        m["ck"] = f(np.asarray(cache_mem_k)[0, sl].reshape(TS, NM, XW))
        m["cv"] = f(np.asarray(cache_mem_v)[0, sl].reshape(TS, NM, XW))
        m["st_h"] = f(np.asarray(state_lru_h)[0, sl])
        m["st_lc"] = f(np.asarray(state_lru_conv)[0, sl].reshape(TS, 3 * W))
        m["st_sc"] = f(np.asarray(state_sconv)[0, sl].reshape(TS, 2 * W))
        in_maps.append(m)
    nc = _get_program()
    res = run_bass_kernel_spmd(nc, in_maps, core_ids=list(range(NCORES)))
    rs = res.results
    cat = lambda k: np.concatenate([np.asarray(r[k]) for r in rs], axis=0)
    y_prompt = np.stack([np.asarray(r["y_p"]) for r in rs], axis=0).astype(np.float32)
    y_sample = cat("y_s").reshape(NCORES * TS, 1, D).astype(np.float32)
    p_mk = np.stack([np.asarray(r["o_pk"]) for r in rs], axis=0).reshape(1, NCORES, NM, 4, 128).astype(np.float32)
    p_mv = np.stack([np.asarray(r["o_pv"]) for r in rs], axis=0).reshape(1, NCORES, NM, 4, 128).astype(np.float32)
    p_h = cat("o_ph").reshape(1, NCORES, W).astype(np.float32)
    p_lc = np.stack([np.asarray(r["o_plc"]) for r in rs], axis=0).reshape(1, NCORES, 3, W).astype(np.float32)
    p_sc = np.stack([np.asarray(r["o_psc"]) for r in rs], axis=0).reshape(1, NCORES, 2, W).astype(np.float32)
    s_h = cat("o_sh").reshape(1, NCORES * TS, W).astype(np.float32)
    s_lc = cat("o_slc").reshape(1, NCORES * TS, 3, W).astype(np.float32)
    s_sc = cat("o_ssc").reshape(1, NCORES * TS, 2, W).astype(np.float32)
    return (y_prompt, y_sample, p_mk, p_mv, p_h, p_lc, p_sc, s_h, s_lc, s_sc)
```

```python
import math
from contextlib import ExitStack

import numpy as np
import concourse.bass as bass
import concourse.mybir as mybir
from concourse.bass_utils import run_bass_kernel_spmd

F32 = mybir.dt.float32
BF16 = mybir.dt.bfloat16
AF = mybir.ActivationFunctionType
ALU = mybir.AluOpType
AX = mybir.AxisListType

NCORES = 8
T = 2048
TS = 16
TT = T + TS
D = 1024
KC = 8
W = 768
NCH = 6
NM = 256
XW = 512
IN_COLS = 8704
EPS = 1e-6
SCALE = 1.0 / math.sqrt(128.0)
STOP_AFTER = None

C_LX, C_LG, C_SB, C_SCG, C_SH, C_SG, C_Q, C_QG, C_MG = 0, 768, 1536, 2304, 3072, 3840, 4608, 5120, 5632


class Tok:
    __slots__ = ("name", "w", "r")

    def __init__(self, name=""):
        self.name = name
        self.w = None
        self.r = {}


class Stream:
    def __init__(self, key, sem):
        self.key = key
        self.sem = sem
        self.cnt = 0
        self.seen = {}
        self.ops = []
        self.pending = False


class Sched:
    def __init__(self, nc):
        self.nc = nc
        self.sems = {}
        self.streams = {}
        self.dma_cnt = {}

    def add_stream(self, key, sem):
        self.sems[key] = sem
        self.streams[key] = Stream(key, sem)

    def add_dma_sem(self, key, sem):
        self.sems[key] = sem
        self.dma_cnt[key] = 0

    def _needs(self, st, reads, writes):
        needs = {}

        def need(ev):
            if ev is None:
                return
            k, v = ev
            if needs.get(k, 0) < v:
                needs[k] = v
        for t in reads:
            need(t.w)
        for t in writes:
            need(t.w)
            for k, v in t.r.items():
                if k == st.key:
                    continue
                need((k, v))
        out = []
        for k, v in needs.items():
            if k == st.key and k == "pe":
                continue
            if st.seen.get(k, 0) < v:
                st.seen[k] = v
                out.append((k, v))
        return out

    def op(self, key, fn, reads=(), writes=(), inc=True):
        st = self.streams[key]
        waits = self._needs(st, reads, writes)
        if inc:
            st.cnt += 1
            st.pending = False
            ev = (key, st.cnt)
        else:
            st.pending = True
            ev = (key, st.cnt + 1)
        sems = self.sems
        sem = st.sem

        def run(eng, waits=waits, fn=fn, inc=inc):
            for k, v in waits:
                eng.wait_ge(sems[k], v)
            ins = fn(eng)
            if inc:
                ins.then_inc(sem, 1)
        st.ops.append(run)
        for t in writes:
            t.w = ev
            t.r = {}
        for t in reads:
            if t.r.get(key, 0) < ev[1]:
                t.r[key] = ev[1]

    def dma(self, key, semkey, out, in_, reads=(), writes=(), nowait=False, **kw):
        st = self.streams[key]
        waits = [] if nowait else self._needs(st, reads, writes)
        self.dma_cnt[semkey] += 16
        ev = (semkey, self.dma_cnt[semkey])
        sems = self.sems

        def run(eng, waits=waits):
            for k, v in waits:
                eng.wait_ge(sems[k], v)
            eng.dma_start(out=out, in_=in_, **kw).then_inc(sems[semkey], 16)
        st.ops.append(run)
        for t in writes:
            t.w = ev
            t.r = {}
        for t in reads:
            if t.r.get(semkey, 0) < ev[1]:
                t.r[semkey] = ev[1]

    def wait_dma(self, keys, semkey):
        v = self.dma_cnt[semkey]
        sems = self.sems
        for k in keys:
            st = self.streams[k]
            if v and st.seen.get(semkey, 0) < v:
                st.seen[semkey] = v
                st.ops.append(lambda eng, v=v: eng.wait_ge(sems[semkey], v))

    def barrier(self, skip=(), no_wait=()):
        targets = {}
        for k, st in self.streams.items():
            assert not st.pending
            if st.cnt:
                targets[k] = st.cnt
        for k, v in self.dma_cnt.items():
            if v and k not in skip:
                targets[k] = v
        sems = self.sems
        for k, st in self.streams.items():
            if k in no_wait:
                continue
            waits = []
            for tk, tv in targets.items():
                if st.seen.get(tk, 0) < tv:
                    st.seen[tk] = tv
                    waits.append((tk, tv))

            def run(eng, waits=waits):
                for kk, v in waits:
                    eng.wait_ge(sems[kk], v)
            st.ops.append(run)

    def finish(self, block):
        for key, st in self.streams.items():
            assert not st.pending, key
        ss = self.streams

        def mk(key):
            def body(eng):
                for o in ss[key].ops:
                    o(eng)
            return body
        block.gpsimd(mk("pool"))
        block.tensor(mk("pe"))
        block.scalar(mk("act"))
        block.vector(mk("dve"))
        block.sync(mk("sp"))


def build_program():
    nc = bass.Bass("TRN2", target_bir_lowering=False)

    def din(name, shape):
        return nc.dram_tensor(name, shape, F32, kind="ExternalInput").ap()

    def dout(name, shape):
        return nc.dram_tensor(name, shape, F32, kind="ExternalOutput").ap()

    xp = din("xp", [T, D])
    xs = din("xs", [TS, D])
    memp = din("memp", [NM, D])
    ck = din("ck", [TS, NM, XW])
    cv = din("cv", [TS, NM, XW])
    st_h = din("st_h", [TS, W])
    st_lc = din("st_lc", [TS, 3 * W])
    st_sc = din("st_sc", [TS, 2 * W])
    norm_g = din("norm_g", [D])
    mem_norm_g = din("mem_norm_g", [D])
    w_in = din("w_in", [D, IN_COLS])
    lru_conv_w = din("lru_conv_w", [4, W])
    lru_conv_b = din("lru_conv_b", [W])
    lru_wa = din("lru_wa", [NCH, 128, 128])
    lru_ba = din("lru_ba", [W])
    lru_wx = din("lru_wx", [NCH, 128, 128])
    lru_bx = din("lru_bx", [W])
    lru_lambda = din("lru_lambda", [W])
    lru_wo = din("lru_wo", [W, D])
    sconv_w = din("sconv_w", [3, W])
    sconv_wo = din("sconv_wo", [W, D])
    xa_wk = din("xa_wk", [D, XW])
    xa_wv = din("xa_wv", [D, XW])
    xa_wo = din("xa_wo", [XW, D])
    w_out = din("w_out", [D, D])
    final_norm_g = din("final_norm_g", [D])

    y_p = dout("y_p", [T, D])
    y_s = dout("y_s", [TS, D])
    o_pk = dout("o_pk", [NM, XW])
    o_pv = dout("o_pv", [NM, XW])
    o_ph = dout("o_ph", [1, W])
    o_plc = dout("o_plc", [3, W])
    o_psc = dout("o_psc", [2, W])
    o_sh = dout("o_sh", [TS, W])
    o_slc = dout("o_slc", [TS, 3 * W])
    o_ssc = dout("o_ssc", [TS, 2 * W])

    dbg = dout("dbg", [128, 4096]) if STOP_AFTER else None
    es = ExitStack()
    with es:
        def sb(name, shape, dt):
            return es.enter_context(nc.sbuf_tensor(name, shape, dt))

        uT_t = sb("uT", [128, KC * TT], BF16)
        uT = uT_t[:].rearrange("p (k t) -> p k t", k=KC)
        gA_t = sb("gA", [128, NCH * TT], BF16)
        gA = gA_t[:].rearrange("p (k t) -> p k t", k=NCH)
        gB_t = sb("gB", [128, NCH * TT], BF16)
        gB = gB_t[:].rearrange("p (k t) -> p k t", k=NCH)
        og_t = sb("og", [128, 4 * TT], BF16)
        og = og_t[:].rearrange("p (k t) -> p k t", k=4)
        NSLOT = 5
        SLOT_E = 3072
        slots = [sb(f"wslot{i}", [128, SLOT_E], BF16) for i in range(NSLOT)]
        ident = sb("ident", [128, 128], F32)
        identb = sb("identb", [128, 128], BF16)
        onesb = sb("onesb", [128, 128], BF16)
        pT = sb("pT", [128, 82], F32)
        dv = sb("dv", [128, 24], F32)
        SIT_t = sb("SIT", [128, 36 * TS], F32)
        SIT = SIT_t[:].rearrange("p (a b) -> p a b", a=36)
        SO_t = sb("SO", [128, NCH * 54], F32)
        SO = SO_t[:].rearrange("p (a b) -> p a b", a=NCH)
        stat = sb("stat", [128, 64], F32)
        stat2 = sb("stat2", [128, 64], F32)
        epst = sb("epst", [128, 1], F32)
        q25 = sb("q25", [128, 1], F32)
        RW = 18816
        R = sb("R", [128, RW], F32)

        def carve(off_b, nelem, dt):
            assert off_b % 4 == 0
            if dt == F32:
                assert off_b // 4 + nelem <= RW, (off_b, nelem)
                return R[:, off_b // 4: off_b // 4 + nelem]
            assert nelem % 2 == 0 and off_b // 4 + nelem // 2 <= RW, (off_b, nelem)
            return R[:, off_b // 4: off_b // 4 + nelem // 2].bitcast(BF16)

        PP = [es.enter_context(nc.psum_tensor(f"PP{i}", [128, 1024], F32)) for i in range(3)]
        PS = es.enter_context(nc.psum_tensor("PS", [128, 512], F32))
        PX = es.enter_context(nc.psum_tensor("PX", [128, 512], F32))
        tPP = [Tok(f"PP{i}") for i in range(3)]
        tPS = [Tok(f"PS{i}") for i in range(8)]
        tPX = Tok("PX")
        PSb = PS[:].bitcast(BF16)
        PXb = PX[:].bitcast(BF16)

        S = Sched(nc)
        for k in ["pe", "act", "dve", "pool", "sp"]:
            S.add_stream(k, es.enter_context(nc.semaphore("s_" + k)))
        dsem_names = (["cst", "cs2", "cs3", "cs4", "cs5", "sva", "svb", "fxa", "fxb", "out", "xa", "xb", "xc", "ya", "yb", "ka", "kb", "va", "vb"]
                      + [f"ws{i}" for i in range(NSLOT)])
        for k in dsem_names:
            S.add_dma_sem(k, es.enter_context(nc.semaphore("d_" + k)))
        block = es.enter_context(nc.Block())

        state = {"pp": 0, "ps": 0}

        def next_pp():
            i = state["pp"] % 3
            state["pp"] += 1
            return PP[i], tPP[i]

        def next_ps():
            i = state["ps"] % 8
            state["ps"] += 1
            return i, tPS[i]

        wtiles = []
        wstate = {"issued": 0}
        tslot = [Tok(f"slot{i}") for i in range(NSLOT)]

        def w_in_cols(c0, ncols):
            return w_in.rearrange("(k p) c -> p k c", p=128)[:, :, c0:c0 + ncols]

        def add_wtile(pieces):
            off = 0
            lst = []
            for ap in pieces:
                kc, ncols = ap.shape[1], ap.shape[2]
                lst.append((ap, off, kc, ncols))
                off += kc * ncols
            assert off <= SLOT_E, off
            wtiles.append(lst)
            return len(wtiles) - 1

        def w_issue_upto(i):
            while wstate["issued"] <= min(i, len(wtiles) - 1):
                j = wstate["issued"]
                sl = j % NSLOT
                for pi, (ap, off, kc, ncols) in enumerate(wtiles[j]):
                    dst = slots[sl][:, off:off + kc * ncols].rearrange("p (k c) -> p k c", k=kc)
                    S.dma("pool", f"ws{sl}", dst, ap, writes=[tslot[sl]], nowait=(pi > 0))
                wstate["issued"] += 1

        def w_get(i):
            w_issue_upto(i + NSLOT - 2)
            sl = i % NSLOT
            views = []
            for (ap, off, kc, ncols) in wtiles[i]:
                views.append(slots[sl][:, off:off + kc * ncols].rearrange("p (k c) -> p k c", k=kc))
            return views, tslot[sl]

        WT = {}
        WT["wk0"] = add_wtile([xa_wk.rearrange("(k p) c -> p k c", p=128)[:, :, 0:256]])
        WT["wk1"] = add_wtile([xa_wk.rearrange("(k p) c -> p k c", p=128)[:, :, 256:512]])
        WT["wv0"] = add_wtile([xa_wv.rearrange("(k p) c -> p k c", p=128)[:, :, 0:256]])
        WT["wv1"] = add_wtile([xa_wv.rearrange("(k p) c -> p k c", p=128)[:, :, 256:512]])
        for hh in range(2):
            WT[f"q{hh}"] = add_wtile([w_in_cols(C_Q + hh * 256, 256)])
        for hh in range(2):
            WT[f"qg{hh}"] = add_wtile([w_in_cols(C_QG + hh * 256, 256)])
        for n in range(NCH + 1):
            if n < NCH:
                WT[f"A{n}"] = add_wtile([w_in_cols(C_LG + n * 128, 128), w_in_cols(C_LX + n * 128, 128)])
            if n >= 1:
                m_ = n - 1
                WT[f"B{m_}a"] = add_wtile([w_in_cols(C_SG + m_ * 128, 128), w_in_cols(C_SB + m_ * 128, 128)])
                WT[f"B{m_}b"] = add_wtile([w_in_cols(C_SCG + m_ * 128, 128), w_in_cols(C_SH + m_ * 128, 128)])
        lwo = lru_wo.rearrange("(k p) c -> p k c", p=128)
        swo = sconv_wo.rearrange("(k p) c -> p k c", p=128)
        xwo = xa_wo.rearrange("(k p) c -> p k c", p=128)
        for j in range(8):
            WT[f"Mg{j}"] = add_wtile([w_in_cols(C_MG + x * 1024 + j * 128, 128) for x in range(3)])
            WT[f"Mo{j}"] = add_wtile([lwo[:, :, j * 128:(j + 1) * 128], swo[:, :, j * 128:(j + 1) * 128],
                                      xwo[:, :, j * 128:(j + 1) * 128]])

        def mm_group(out_ap, pairs, reads, wtok, last_inc=True):
            n = len(pairs)
            for i, (l, r) in enumerate(pairs):
                S.op("pe", lambda e, l=l, r=r, i=i: e.matmul(out_ap, l, r, start=(i == 0), stop=(i == n - 1)),
                     reads=reads, writes=[wtok], inc=(last_inc and i == n - 1))

        def unit_half(lhs_list, rhs_arr, lo, reads):
            pp, tk = next_pp()
            nk = len(lhs_list)
            for b in range(2):
                pairs = [(lhs_list[k], rhs_arr[:, k, lo + b * 512: lo + (b + 1) * 512]) for k in range(nk)]
                mm_group(pp[:, b * 512:(b + 1) * 512], pairs, reads, tk, last_inc=(b == 1))
            return pp, tk

        tPSb = Tok("PSbank")
        tPS2 = [tPSb, tPX]

        def unit_samp(lhs_list, rhs_arr, reads):
            i = state.setdefault("ps2", 0) % 2
            state["ps2"] += 1
            tk = tPS2[i]
            nk = len(lhs_list)
            bank = PS if i == 0 else PX
            out_ap = bank[:, 0:TS]
            pairs = [(lhs_list[k], rhs_arr[:, k, T:TT]) for k in range(nk)]
            mm_group(out_ap, pairs, reads, tk)
            return out_ap, tk

        HALVES = [(0, 1024), (1024, 2048)]

        def dump(items, off_b=56320):
            dstage = carve(off_b, 4096, F32)
            S.barrier()
            S.op("dve", lambda e: e.memset(dstage[:], 0.0))
            S.barrier()
            off = 0
            for ap in items:
                P_, n_ = ap.shape[0], ap.shape[1]
                S.op("dve", lambda e, ap=ap, off=off, P_=P_, n_=n_: e.tensor_copy(dstage[0:P_, off:off + n_], ap))
                off += n_
            S.barrier()
            S.dma("sp", "out", dbg[:, :], dstage[:])
            S.barrier()

        t_ident = Tok("ident")
        S.op("pool", lambda e: e.memset(ident[:], 0.0), writes=[t_ident])
        S.op("pool", lambda e: e.affine_select(ident[:], ident[:], pattern=[[-1, 128]], compare_op=ALU.not_equal,
                                               fill=1.0, base=0, channel_multiplier=1),
             reads=[t_ident], writes=[t_ident])
        t_identb = Tok("identb")
        S.op("dve", lambda e: e.tensor_copy(identb[:], ident[:]), reads=[t_ident], writes=[t_identb])
        t_stat = Tok("stat")
        t_sm = Tok("sm")
        t_st3 = t_sm
        S.op("pool", lambda e: e.memset(stat[:], 0.0), writes=[t_stat])
        S.op("pool", lambda e: e.memset(stat2[:], 0.0), writes=[t_sm])
        t_ones = Tok("ones")
        S.op("pool", lambda e: e.memset(onesb[:], 1.0), writes=[t_ones])
        t_eps = Tok("eps")
        S.op("pool", lambda e: e.memset(epst[:], EPS), writes=[t_eps])
        S.op("pool", lambda e: e.memset(q25[:], 0.25), writes=[t_eps])

        prt = carve(0, 128, F32)[0:82, :]
        sin = carve(512, 4608, F32)
        NXB = 8
        xbuf = [carve(18944 + i * 4096, 1024, F32) for i in range(3)] + [carve(53760 + i * 4096, 1024, F32) for i in range(5)]
        xnb = [carve(31232 + i * 2048, 1024, BF16) for i in range(2)]
        junk = carve(35328, 1024, BF16)
        mnT_t = carve(37376, KC * NM, BF16)
        mnT = mnT_t.rearrange("p (k t) -> p k t", k=KC)
        kT_t = carve(41472, 4 * NM, BF16)
        kT = kT_t.rearrange("p (k t) -> p k t", k=4)
        vb_t = carve(43520, 2 * XW, BF16)
        vb = vb_t.rearrange("p (k t) -> p k t", k=2)
        kvout = [carve(45568 + i * 4096, 2 * XW, F32).rearrange("p (k t) -> p k t", k=2) for i in range(2)]

        t_prt = Tok("prt")

        def rows(v, r):
            return v.rearrange("(r c) -> r c", c=128)

        plist = [(norm_g, 0, 8, None), (mem_norm_g, 8, 8, None),
                 (lru_conv_w, 16, 24, "k (n c) -> (k n) c"), (lru_conv_b, 40, 6, None), (lru_ba, 46, 6, None),
                 (lru_bx, 52, 6, None), (lru_lambda, 58, 6, None), (sconv_w, 64, 18, "k (n c) -> (k n) c")]
        for (v, r0, nr, pat) in plist:
            src = v.rearrange(pat, c=128) if pat else v.rearrange("(r c) -> r c", c=128)
            S.dma("sp", "cst", prt[r0:r0 + nr, :], src, writes=[t_prt], nowait=True)
        t_sin = Tok("sin")
        S.dma("sp", "cs2", sin[0:TS, 0:2304], st_lc[:, :], writes=[t_sin], nowait=True)
        S.dma("sp", "cs2", sin[0:TS, 2304:3072], st_h[:, :], writes=[t_sin], nowait=True)
        S.dma("sp", "cs2", sin[0:TS, 3072:4608], st_sc[:, :], writes=[t_sin], nowait=True)
        t_out = Tok("out")
        o_slc3 = o_slc.rearrange("b (k c) -> b k c", k=3)
        st_lc3 = st_lc.rearrange("b (k c) -> b k c", k=3)
        S.dma("sp", "out", o_slc3[:, 0:2, :], st_lc3[:, 1:3, :])
        o_ssc3 = o_ssc.rearrange("b (k c) -> b k c", k=2)
        st_sc3 = st_sc.rearrange("b (k c) -> b k c", k=2)
        S.dma("sp", "out", o_ssc3[:, 0:1, :], st_sc3[:, 1:2, :])

        if STOP_AFTER == "P0dma":
            S.barrier()
            S.finish(block)
            return nc
        t_pT = Tok("pT")
        S.op("pe", lambda e: e.transpose(PX[:, 0:82], prt, ident[0:82, 0:82]), reads=[t_prt, t_ident], writes=[tPX])
        S.op("act", lambda e: e.copy(out=pT[:], in_=PX[:, 0:82]), reads=[tPX], writes=[t_pT])
        if STOP_AFTER == "P0b":
            S.barrier()
            S.finish(block)
            return nc
        t_x = [Tok(f"x{i}") for i in range(NXB)]
        t_xn = [Tok(f"xn{i}") for i in range(2)]
        t_junk = Tok("junk")
        t_uT = Tok("uT")
        t_mnT = Tok("mnT")
        tiles = [(memp[i * 128:(i + 1) * 128, :], 128, "m", i * 128) for i in range(2)]
        tiles += [(xp[i * 128:(i + 1) * 128, :], 128, "u", i * 128) for i in range(16)]
        tiles.append((xs[:, :], TS, "u", T))
        xsem = ["xa", "xb", "xc", "ya", "yb", "ka", "kb", "va"]
        tPXh = [Tok("PXh0"), Tok("PXh1")]
        t_st0 = [Tok(f"st0_{i}") for i in range(len(tiles))]

        def p0_stageA(ti):
            src, nr, kind, c0 = tiles[ti]
            xb_, tx = xbuf[ti % NXB], t_x[ti % NXB]
            S.dma("sp", xsem[ti % NXB], xb_[0:nr, :], src, writes=[tx])
            S.op("act", lambda e, xb_=xb_, nr=nr, ti=ti: e.activation(out=junk[0:nr, :], in_=xb_[0:nr, :], func=AF.Square,
                                                                      accum_out=stat[0:nr, ti:ti + 1]),
                 reads=[tx, t_stat], writes=[t_st0[ti], t_junk])

        def p0_stageB(ti):
            src, nr, kind, c0 = tiles[ti]
            xb_, tx = xbuf[ti % NXB], t_x[ti % NXB]
            xn_, txn = xnb[ti % 2], t_xn[ti % 2]
            S.op("act", lambda e, nr=nr, ti=ti: e.activation(out=stat[0:nr, ti:ti + 1], in_=stat[0:nr, ti:ti + 1], func=AF.Sqrt,
                                                             scale=1.0 / D, bias=epst[0:nr, 0:1]),
                 reads=[t_st0[ti], t_eps], writes=[t_st0[ti]])
            S.op("dve", lambda e, nr=nr, ti=ti: e.reciprocal(out=stat[0:nr, ti:ti + 1], in_=stat[0:nr, ti:ti + 1]),
                 reads=[t_st0[ti]], writes=[t_st0[ti]])
            S.op("dve", lambda e, xb_=xb_, xn_=xn_, nr=nr, ti=ti: e.tensor_scalar(out=xn_[0:nr, :], in0=xb_[0:nr, :],
                                                                                scalar1=stat[0:nr, ti:ti + 1], scalar2=None,
                                                                                op0=ALU.mult),
                 reads=[tx, t_st0[ti]], writes=[txn])

        def p0_stageC(ti):
            src, nr, kind, c0 = tiles[ti]
            xn_, txn = xnb[ti % 2], t_xn[ti % 2]
            bankb, tbank = (PXb, tPX) if ti % 2 == 0 else (PSb, tPSb)
            for kc in range(KC):
                S.op("pe", lambda e, xn_=xn_, nr=nr, kc=kc, bankb=bankb: e.transpose(bankb[:, kc * 128: kc * 128 + nr],
                                                                        xn_[0:nr, kc * 128:(kc + 1) * 128],
                                                                        identb[0:nr, 0:nr]),
                     reads=[txn, t_identb], writes=[tbank], inc=(kc == KC - 1))
            pview = bankb.rearrange("p (k t) -> p k t", k=KC)[:, :, 0:nr]
            if kind == "u":
                dst = uT[:, :, c0:c0 + nr]
                gcols = pT[:, 0:8]
                tdst = t_uTi[ti]
            else:
                dst = mnT[:, :, c0:c0 + nr]
                gcols = pT[:, 8:16]
                tdst = t_mnT
            S.op("dve", lambda e, dst=dst, pview=pview, gcols=gcols, nr=nr: e.tensor_tensor(
                out=dst, in0=pview, in1=gcols.unsqueeze(2).to_broadcast([128, KC, nr]), op=ALU.mult),
                reads=[tbank, t_pT], writes=[tdst])

        t_kT = Tok("kT")
        t_vb = Tok("vb")
        t_kvout = [Tok("kvo0"), Tok("kvo1")]
        kvst = {}

        def kv_mm():
            for which in range(2):
                wv_, wt_ = [], []
                for hh in range(2):
                    vws, tk = w_get(WT[("wk" if which == 0 else "wv") + str(hh)])
                    wv_.append(vws[0])
                    wt_.append(tk)
                pp, tk = next_pp()
                for mc in range(2):
                    for hh in range(2):
                        pairs = [(mnT[:, k, mc * 128:(mc + 1) * 128], wv_[hh][:, k, :]) for k in range(KC)]
                        mm_group(pp[:, mc * 512 + hh * 256: mc * 512 + (hh + 1) * 256], pairs, [t_mnT, wt_[hh]], tk,
                                 last_inc=(mc == 1 and hh == 1))
                kvst[which] = (pp, tk)
                if which == 0:
                    pp2, tk2 = next_pp()
                    for dc in range(4):
                        hh, off = dc // 2, (dc % 2) * 128
                        pairs = [(wv_[hh][:, k, off:off + 128], mnT[:, k, :]) for k in range(KC)]
                        mm_group(pp2[:, dc * 256:(dc + 1) * 256], pairs, [t_mnT, wt_[hh]], tk2, last_inc=(dc == 3))
                    kvst["kT"] = (pp2, tk2)

        def kv_evac():
            for which in range(2):
                pp, tk = kvst[which]
                ko = kvout[which]
                S.op("act", lambda e, ko=ko, pp=pp: e.copy(out=ko, in_=pp[:].rearrange("p (k t) -> p k t", k=2)),
                     reads=[tk], writes=[t_kvout[which]])
                dsto = (o_pk if which == 0 else o_pv).rearrange("(k p) c -> p k c", p=128)
                S.dma("sp", "out", dsto, ko, reads=[t_kvout[which]])
            pp, tk = kvst[1]
            S.op("act", lambda e, pp=pp: e.copy(out=vb, in_=pp[:].rearrange("p (k t) -> p k t", k=2)),
                 reads=[tk], writes=[t_vb])
            pp2, tk2 = kvst["kT"]
            S.op("dve", lambda e, pp2=pp2: e.tensor_copy(kT, pp2[:].rearrange("p (k t) -> p k t", k=4)),
                 reads=[tk2], writes=[t_kT])

        t_uTi = [Tok(f"uT{i}") for i in range(len(tiles))]
        PD = 5
        for ti in range(PD):
            p0_stageA(ti)
        p0_stageB(0)
        for ti in range(len(tiles)):
            if ti + PD < len(tiles):
                p0_stageA(ti + PD)
            if ti + 1 < len(tiles):
                p0_stageB(ti + 1)
            p0_stageC(ti)
            if ti == 1:
                kv_mm()
            if ti == 5:
                kv_evac()

        t_dv = Tok("dv")
        S.op("act", lambda e: e.activation(out=dv[:, 0:6], in_=pT[:, 58:64], func=AF.Exp, scale=-1.0),
             reads=[t_pT], writes=[t_dv])
        S.op("act", lambda e: e.activation(out=dv[:, 6:12], in_=dv[:, 0:6], func=AF.Ln, bias=1.0, scale=1.0),
             reads=[t_dv], writes=[t_dv])
        S.op("dve", lambda e: e.tensor_scalar(out=dv[:, 0:6], in0=dv[:, 6:12], scalar1=-4.0, scalar2=None, op0=ALU.mult),
             reads=[t_dv], writes=[t_dv])
        S.op("dve", lambda e: e.tensor_scalar(out=dv[:, 6:12], in0=dv[:, 6:12], scalar1=-8.0, scalar2=None, op0=ALU.mult),
             reads=[t_dv], writes=[t_dv])
        S.op("dve", lambda e: e.tensor_scalar(out=dv[:, 12:24], in0=pT[:, 46:58], scalar1=0.5, scalar2=None, op0=ALU.mult),
             reads=[t_pT, t_dv], writes=[t_dv])
        if STOP_AFTER == "P0a":
            S.barrier()
            S.finish(block)
            return nc
        t_SIT = Tok("SIT")
        blocks_ = []
        for n in range(NCH):
            for k in range(3):
                blocks_.append((n * 6 + k, k * W + n * 128))
            blocks_.append((n * 6 + 3, 2304 + n * 128))
            for k in range(2):
                blocks_.append((n * 6 + 4 + k, 3072 + k * W + n * 128))
        for g0 in range(0, 36, 18):
            grp = blocks_[g0:g0 + 18]
            for gi, (slot_i, c0) in enumerate(grp):
                S.op("pe", lambda e, gi=gi, c0=c0: e.transpose(PX[:, gi * TS:(gi + 1) * TS], sin[0:TS, c0:c0 + 128],
                                                              ident[0:TS, 0:TS]),
                     reads=[t_sin, t_ident], writes=[tPX], inc=(gi == len(grp) - 1))
            for gi, (slot_i, c0) in enumerate(grp):
                S.op("act", lambda e, gi=gi, slot_i=slot_i: e.copy(out=SIT[:, slot_i, :], in_=PX[:, gi * TS:(gi + 1) * TS]),
                     reads=[tPX], writes=[t_SIT])

        if STOP_AFTER == "P0c":
            dump([pT[:], stat[:, 0:19], uT[:, 0, 0:256], uT[:, 7, T - 128:TT], mnT[:, 0, :], mnT[:, 7, :]])
            S.barrier()
            S.finish(block)
            return nc
        S.barrier(skip=("out",))
        if STOP_AFTER == "KV":
            S.finish(block)
            return nc

        qT_t = carve(0, 4 * T, BF16)
        qT = qT_t.rearrange("p (k t) -> p k t", k=4)
        qs_tok = carve(16384, XW, BF16)
        sqg_tok = carve(17408, XW, F32)
        pTe = [carve(19456 + i * 2048, 1024, BF16).rearrange("p (k t) -> p k t", k=2) for i in range(2)]
        rden = [carve(23552 + i * 2048, 512, F32) for i in range(2)]
        o1b = [carve(27648 + i * 2048, 512, F32) for i in range(2)]
        selb_t = carve(31744, TS * 128, BF16)
        selb = selb_t.rearrange("p (b c) -> p b c", b=TS)
        eye16_t = carve(35840, 256, F32)
        eye16 = eye16_t.rearrange("p (a b) -> p a b", a=16)
        Sall_t = carve(45568, 128, F32)
        Sall = Sall_t.rearrange("p (c b h) -> p c b h", c=2, b=TS)
        Esm = carve(46080, 256, F32)
        Psm = carve(47104, 256, F32)
        Mk_t = carve(48128, 2 * 4 * 16 * 16, BF16)
        Mk = Mk_t.rearrange("p (c h b q) -> p c h b q", c=2, h=4, b=16)
        qb_sb = [carve(56320 + i * 2048, 512, F32) for i in range(2)]
        Kb = [carve(60416 + i * 4096, 1024, F32).rearrange("p (c f) -> p c f", c=2) for i in range(2)]
        prod = carve(68608, 1024, F32).rearrange("p (c f) -> p c f", c=2)

        t_qT = [Tok(f"qT{h}") for h in range(4)]
        t_og = [Tok(f"og{h}") for h in range(4)]
        t_qs = Tok("qs")
        t_sqg = Tok("sqg")
        def next_pp_c():
            while True:
                i = state["pp"] % 3
                state["pp"] += 1
                if i != state.get("pp_excl", -1):
                    return PP[i], tPP[i]

        def unit_half_c(lhs_list, rhs_arr, lo, reads):
            pp, tk = next_pp_c()
            nk = len(lhs_list)
            for b in range(2):
                pairs = [(lhs_list[k], rhs_arr[:, k, lo + b * 512: lo + (b + 1) * 512]) for k in range(nk)]
                mm_group(pp[:, b * 512:(b + 1) * 512], pairs, reads, tk, last_inc=(b == 1))
            return pp, tk

        t_pTe = [Tok("pTe0"), Tok("pTe1")]
        t_rden = [Tok("rden0"), Tok("rden1")]
        t_o1 = [Tok("o10"), Tok("o11")]

        def gen_C_prompt():
            for hh in range(2):
                vws, wtk = w_get(WT[f"q{hh}"])
                wq = vws[0]
                for hl in range(2):
                    h = hh * 2 + hl
                    lhs = [wq[:, k, hl * 128:(hl + 1) * 128] for k in range(KC)]
                    for (lo, hi) in HALVES:
                        pp, tk = unit_half_c(lhs, uT, lo, [wtk])
                        S.op("act", lambda e, pp=pp, h=h, lo=lo, hi=hi: e.copy(out=qT[:, h, lo:hi], in_=pp[:]),
                             reads=[tk], writes=[t_qT[h]])
                        yield
                pairs = [(uT[:, k, T:TT], wq[:, k, :]) for k in range(KC)]
                mm_group(PS[0:TS, 0:256], pairs, [wtk], tPSb)
                S.op("act", lambda e, hh=hh: e.copy(out=qs_tok[0:TS, hh * 256:(hh + 1) * 256], in_=PS[0:TS, 0:256]),
                     reads=[tPSb], writes=[t_qs])
                yield
            for hh in range(2):
                vws, wtk = w_get(WT[f"qg{hh}"])
                wq = vws[0]
                for hl in range(2):
                    h = hh * 2 + hl
                    lhs = [wq[:, k, hl * 128:(hl + 1) * 128] for k in range(KC)]
                    for (lo, hi) in HALVES:
                        pp, tk = unit_half_c(lhs, uT, lo, [wtk])
                        S.op("act", lambda e, pp=pp, h=h, lo=lo, hi=hi: e.activation(out=og[:, h, lo:hi], in_=pp[:], func=AF.Silu),
                             reads=[tk], writes=[t_og[h]])
                        yield
                pairs = [(uT[:, k, T:TT], wq[:, k, :]) for k in range(KC)]
                mm_group(PS[0:TS, 0:256], pairs, [wtk], tPSb)
                S.op("act", lambda e, hh=hh: e.activation(out=sqg_tok[0:TS, hh * 256:(hh + 1) * 256],
                                                         in_=PS[0:TS, 0:256], func=AF.Silu),
                     reads=[tPSb], writes=[t_sqg])
                yield
            iters = [(tb, h) for tb in range(4) for h in range(4)]
            stage1 = {}

            def att_s1(i):
                tb, h = iters[i]
                c0 = tb * 512
                pe_, tpe = pTe[i % 2], t_pTe[i % 2]
                pp, tk = next_pp_c()
                for mc in range(2):
                    mm_group(pp[:, mc * 512:(mc + 1) * 512], [(kT[:, h, mc * 128:(mc + 1) * 128], qT[:, h, c0:c0 + 512])],
                             [t_kT, t_qT[h]], tk, last_inc=(mc == 1))
                S.op("act", lambda e, pp=pp, pe_=pe_: e.activation(out=pe_, in_=pp[:].rearrange("p (k t) -> p k t", k=2),
                                                                  func=AF.Exp, scale=SCALE),
                     reads=[tk], writes=[tpe])

            def att_s2(i):
                tb, h = iters[i]
                c0 = tb * 512
                pe_, tpe = pTe[i % 2], t_pTe[i % 2]
                rd, trd = rden[i % 2], t_rden[i % 2]
                o1, to1 = o1b[i % 2], t_o1[i % 2]
                pp2, tk2 = next_pp_c()
                mm_group(pp2[:, 0:512], [(vb[:, mc, h * 128:(h + 1) * 128], pe_[:, mc, :]) for mc in range(2)],
                         [t_vb, tpe], tk2, last_inc=False)
                mm_group(pp2[:, 512:1024], [(onesb[:], pe_[:, mc, :]) for mc in range(2)], [t_ones, tpe], tk2)
                S.op("act", lambda e, pp2=pp2, rd=rd: e.activation(out=rd, in_=pp2[:, 512:1024], func=AF.Ln), reads=[tk2], writes=[trd])
                S.op("act", lambda e, rd=rd: e.activation(out=rd, in_=rd, func=AF.Exp, scale=-1.0), reads=[trd], writes=[trd])
                S.op("dve", lambda e, pp2=pp2, rd=rd, o1=o1: e.tensor_tensor(out=o1, in0=pp2[:, 0:512], in1=rd, op=ALU.mult),
                     reads=[tk2, trd], writes=[to1])
                S.op("dve", lambda e, o1=o1, h=h, c0=c0: e.tensor_tensor(out=og[:, h, c0:c0 + 512], in0=o1,
                                                                        in1=og[:, h, c0:c0 + 512], op=ALU.mult),
                     reads=[to1, t_og[h]], writes=[t_og[h]])

            att_s1(0)
            for i in range(len(iters)):
                if i + 1 < len(iters):
                    att_s1(i + 1)
                att_s2(i)
                yield

        t_selb = Tok("selb")
        t_eye = Tok("eye")
        t_qb = [Tok("qb0"), Tok("qb1")]
        t_Kb = [Tok("Kb0"), Tok("Kb1")]
        t_prod = Tok("prod")
        t_Sall = Tok("Sall")
        t_E = Tok("E")
        t_P = Tok("P")
        t_Mk = Tok("Mk")
        ksem = ["ka", "kb"]
        vsem = ["sva", "svb"]

        def gen_C_sample():
            S.wait_dma(["dve", "act", "pool", "pe"], "out")
            S.op("dve", lambda e: e.tensor_copy(selb[0:TS, :, :], identb[0:TS, 0:TS].unsqueeze(2).to_broadcast([TS, TS, 128])),
                 reads=[t_identb], writes=[t_selb])
            S.op("pool", lambda e: e.memset(eye16_t, 0.0), writes=[t_eye])
            S.op("pool", lambda e: e.affine_select(eye16, eye16, pattern=[[1, 16], [-1, 16]], compare_op=ALU.not_equal,
                                                   fill=1.0, base=0, channel_multiplier=0),
                 reads=[t_eye], writes=[t_eye])
            yield
            for b in range(TS):
                kb_, tkb = Kb[b % 2], t_Kb[b % 2]
                qb_, tqb = qb_sb[b % 2], t_qb[b % 2]
                S.dma("sp", ksem[b % 2], kb_, ck[b].rearrange("(c m) f -> m c f", c=2), writes=[tkb])
                S.op("pe", lambda e, b=b: e.matmul(PX[:, 0:512], selb[0:TS, b, :], qs_tok[0:TS, :], start=True, stop=True),
                     reads=[t_selb, t_qs], writes=[tPX])
                S.op("act", lambda e, qb_=qb_: e.copy(out=qb_, in_=PX[:, 0:512]), reads=[tPX], writes=[tqb])
                S.op("pool", lambda e, kb_=kb_, qb_=qb_: e.tensor_tensor(out=prod, in0=kb_,
                                                                         in1=qb_.unsqueeze(1).to_broadcast([128, 2, 512]), op=ALU.mult),
                     reads=[tkb, tqb], writes=[t_prod])
                S.op("dve", lambda e, b=b: e.tensor_reduce(out=Sall[:, :, b, :],
                                                           in_=prod.rearrange("p c (h d) -> p c h d", h=4), axis=AX.X, op=ALU.add),
                     reads=[t_prod], writes=[t_Sall])
                yield
            for c in range(2):
                S.op("pe", lambda e, c=c: e.transpose(PX[0:64, c * 128:(c + 1) * 128],
                                                      Sall_t[:, c * 64:(c + 1) * 64], ident[:, :]),
                     reads=[t_Sall, t_ident], writes=[tPX], inc=(c == 1))
            S.op("dve", lambda e: e.tensor_reduce(out=stat2[0:64, 0:1], in_=PX[0:64, 0:256], axis=AX.X, op=ALU.max),
                 reads=[tPX], writes=[t_sm])
            S.op("dve", lambda e: e.tensor_scalar(out=stat2[0:64, 0:1], in0=stat2[0:64, 0:1], scalar1=-SCALE, scalar2=None, op0=ALU.mult),
                 reads=[t_sm], writes=[t_sm])
            S.op("act", lambda e: e.activation(out=Esm[0:64, :], in_=PX[0:64, 0:256], func=AF.Exp, scale=SCALE,
                                               bias=stat2[0:64, 0:1], accum_out=stat2[0:64, 1:2]),
                 reads=[tPX, t_sm], writes=[t_E, t_sm])
            S.op("dve", lambda e: e.reciprocal(out=stat2[0:64, 1:2], in_=stat2[0:64, 1:2]), reads=[t_sm], writes=[t_sm])
            S.op("dve", lambda e: e.tensor_scalar(out=Psm[0:64, :], in0=Esm[0:64, :], scalar1=stat2[0:64, 1:2], scalar2=None, op0=ALU.mult),
                 reads=[t_E, t_sm], writes=[t_P])
            for c in range(2):
                S.op("pe", lambda e, c=c: e.transpose(PX[:, c * 64:(c + 1) * 64], Psm[0:64, c * 128:(c + 1) * 128], ident[0:64, 0:64]),
                     reads=[t_P, t_ident], writes=[tPX], inc=(c == 1))
            for c in range(2):
                S.op("dve", lambda e, c=c: e.tensor_tensor(
                    out=Mk[:, c],
                    in0=PX[:, c * 64:(c + 1) * 64].rearrange("p (b h) -> p h b", h=4).unsqueeze(3).to_broadcast([128, 4, 16, 16]),
                    in1=eye16.unsqueeze(1).to_broadcast([128, 4, 16, 16]), op=ALU.mult),
                    reads=[tPX, t_eye], writes=[t_Mk])
            yield
            ppa, tka = next_pp_c()
            state["pp_excl"] = PP.index(ppa)
            hbank = [(PS, 0, tPSb), (PX, 0, tPX), (ppa, 0, tka), (ppa, 512, tka)]
            Vb = [carve(68608 + i * 2048, 1024, BF16).rearrange("p (c f) -> p c f", c=2) for i in range(2)]
            t_Vb = [Tok("Vb0"), Tok("Vb1")]
            for b in range(TS):
                vb_, tvb = Vb[b % 2], t_Vb[b % 2]
                S.dma("pool", vsem[b % 2], vb_, cv[b].rearrange("(c m) f -> m c f", c=2), writes=([tvb, t_prod] if b < 2 else [tvb]))
                for h in range(4):
                    pph, coff, tkh = hbank[h]
                    for c in range(2):
                        first = (b == 0 and c == 0)
                        last = (b == TS - 1 and c == 1)
                        S.op("pe", lambda e, vb_=vb_, b=b, h=h, c=c, first=first, last=last, pph=pph, coff=coff: e.matmul(
                            pph[0:TS, coff:coff + 128], Mk[:, c, h, b, :], vb_[:, c, h * 128:(h + 1) * 128],
                            start=first, stop=last, skip_group_check=True),
                            reads=[t_Mk, tvb], writes=[tkh], inc=(h == 3 and c == 1))
                yield
            ogs_tok = qs_tok
            for h in range(4):
                pph, coff, tkh = hbank[h]
                S.op("dve", lambda e, h=h, pph=pph, coff=coff: e.tensor_tensor(
                    out=ogs_tok[0:TS, h * 128:(h + 1) * 128], in0=pph[0:TS, coff:coff + 128],
                    in1=sqg_tok[0:TS, h * 128:(h + 1) * 128], op=ALU.mult),
                    reads=[tkh, t_sqg, t_qs], writes=[t_qs])
            state["pp_excl"] = -1
            for h in range(4):
                S.op("pe", lambda e, h=h: e.transpose(PXb[:, h * TS:(h + 1) * TS], ogs_tok[0:TS, h * 128:(h + 1) * 128],
                                                      identb[0:TS, 0:TS]),
                     reads=[t_qs, t_identb], writes=[tPX], inc=(h == 3))
            S.op("act", lambda e: e.copy(out=og[:, :, T:TT], in_=PXb[:, 0:4 * TS].rearrange("p (h b) -> p h b", h=4)),
                 reads=[tPX], writes=t_og)
            yield

        gp, gs = gen_C_prompt(), gen_C_sample()
        for _ in range(10):
            next(gp)
        alive = [gp, gs]
        while alive:
            for g in list(alive):
                try:
                    next(g)
                except StopIteration:
                    alive.remove(g)
        if STOP_AFTER == "C":
            dump([og[:, 0, 0:512], og[:, 3, T - 512:T], og[:, 0, T:TT], og[:, 1, T:TT], og[:, 2, T:TT], og[:, 3, T:TT], qT[:, 0, 0:256]], off_b=0)
        S.barrier(no_wait=("pe",))
        if STOP_AFTER == "C":
            S.finish(block)
            return nc

        wab_t = carve(0, 2 * NCH * 128, BF16)
        wab = wab_t.rearrange("p (a n d) -> p a n d", a=2, n=NCH)
        lxp = carve(3072, 3 + T, F32)
        lxs_t = carve(11280, 4 * TS, F32)
        lxs = lxs_t.rearrange("p (k b) -> p k b", k=4)
        xc = carve(11536, TT, F32)
        xcb = carve(19792, TT, BF16)
        thr = carve(23920, TT, F32)
        a2b = carve(32176, TT, F32)
        thi = carve(40432, TT, F32)
        scg_sb = carve(48752, TT, F32)
        cy = scg_sb
        cinp = carve(57008, 2 + T, F32)
        cins_t = carve(65208, 3 * TS, F32)
        cins = cins_t.rearrange("p (k b) -> p k b", k=3)
        so_tok = carve(65400, W, F32)
        t_wab = Tok("wab")
        S.dma("pool", "cs3", wab[:, 0], lru_wa.rearrange("n c d -> c n d"), writes=[t_wab])
        S.dma("pool", "cs3", wab[:, 1], lru_wx.rearrange("n c d -> c n d"), writes=[t_wab], nowait=True)
        t_lxp, t_lxs, t_xc, t_xcb, t_thr, t_a2, t_thi = (Tok("lxp"), Tok("lxs"), Tok("xc"), Tok("xcb"), Tok("thr"),
                                                        Tok("a2"), Tok("thi"))
        t_gA = [Tok(f"gA{n}") for n in range(NCH)]
        t_SO = Tok("SO")
        t_scg, t_cinp, t_cins = Tok("scg"), Tok("cinp"), Tok("cins")
        t_cy = t_scg
        t_gB = [Tok(f"gB{n}") for n in range(NCH)]
        S.op("pool", lambda e: e.memset(lxp[:, 0:3], 0.0), writes=[t_lxp])
        S.op("pool", lambda e: e.memset(cinp[:, 0:2], 0.0), writes=[t_cinp])

        def gen_A(n):
            vws, wtk = w_get(WT[f"A{n}"])
            wlg, wlx = vws
            lhs_lg = [wlg[:, k, :] for k in range(KC)]
            lhs_lx = [wlx[:, k, :] for k in range(KC)]
            for (lo, hi) in HALVES:
                pp, tk = unit_half(lhs_lg, uT, lo, [wtk])
                S.op("act", lambda e, pp=pp, n=n, lo=lo, hi=hi: e.activation(out=gA[:, n, lo:hi], in_=pp[:], func=AF.Silu),
                     reads=[tk], writes=[t_gA[n]])
            sp_, tk = unit_samp(lhs_lg, uT, [wtk])
            S.op("act", lambda e, sp_=sp_, n=n: e.activation(out=gA[:, n, T:TT], in_=sp_, func=AF.Silu),
                 reads=[tk], writes=[t_gA[n]])
            yield
            for (lo, hi) in HALVES:
                pp, tk = unit_half(lhs_lx, uT, lo, [wtk])
                S.op("dve", lambda e, pp=pp, lo=lo, hi=hi: e.tensor_copy(lxp[:, 3 + lo:3 + hi], pp[:]),
                     reads=[tk], writes=[t_lxp])
            sp_, tk = unit_samp(lhs_lx, uT, [wtk])
            S.op("dve", lambda e, n=n: e.tensor_copy(lxs[:, 0:3, :], SIT[:, n * 6:n * 6 + 3, :]), reads=[t_SIT], writes=[t_lxs])
            S.op("dve", lambda e, sp_=sp_: e.tensor_copy(lxs[:, 3, :], sp_), reads=[tk], writes=[t_lxs])
            yield
            cw = lambda k, n=n: pT[:, 16 + k * 6 + n: 17 + k * 6 + n]
            cbias = pT[:, 40 + n:41 + n]
            S.op("act", lambda e, cw=cw, cbias=cbias: e.activation(out=xc[:, 0:T], in_=lxp[:, 0:T], func=AF.Identity,
                                                                   scale=cw(0), bias=cbias),
                 reads=[t_lxp, t_pT], writes=[t_xc])
            S.op("act", lambda e, cw=cw, cbias=cbias: e.activation(out=xc[:, T:TT], in_=lxs[:, 0, :], func=AF.Identity,
                                                                   scale=cw(0), bias=cbias),
                 reads=[t_lxs, t_pT], writes=[t_xc])
            for k in range(1, 4):
                S.op("dve", lambda e, k=k, cw=cw: e.scalar_tensor_tensor(out=xc[:, 0:T], in0=lxp[:, k:k + T], scalar=cw(k),
                                                                         in1=xc[:, 0:T], op0=ALU.mult, op1=ALU.add),
                     reads=[t_lxp, t_xc], writes=[t_xc])
                S.op("dve", lambda e, k=k, cw=cw: e.scalar_tensor_tensor(out=xc[:, T:TT], in0=lxs[:, k, :], scalar=cw(k),
                                                                         in1=xc[:, T:TT], op0=ALU.mult, op1=ALU.add),
                     reads=[t_lxs, t_xc], writes=[t_xc])
            S.op("pool", lambda e, n=n: e.tensor_copy(SO[:, n, 0:3], lxp[:, T:T + 3]), reads=[t_lxp], writes=[t_SO])
            S.op("pool", lambda e, n=n: e.tensor_copy(SO[:, n, 6:22], lxs[:, 3, :]), reads=[t_lxs], writes=[t_SO])
            yield
            S.op("act", lambda e: e.copy(out=xcb, in_=xc), reads=[t_xc], writes=[t_xcb])
            yield
            for gi, (dst, tdst, bcol) in enumerate([(thr, t_thr, 12 + n), (thi, t_thi, 18 + n)]):
                lhs = [wab[:, gi, n, :]]
                xcb3 = xcb.unsqueeze(1)
                for (lo, hi) in HALVES:
                    pp, tk = unit_half(lhs, xcb3, lo, [t_wab, t_xcb])
                    S.op("act", lambda e, pp=pp, dst=dst, lo=lo, hi=hi, bcol=bcol: e.activation(
                        out=dst[:, lo:hi], in_=pp[:], func=AF.Tanh, scale=0.5, bias=dv[:, bcol:bcol + 1]),
                        reads=[tk, t_dv], writes=[tdst])
                sp_, tk = unit_samp(lhs, xcb3, [t_wab, t_xcb])
                S.op("act", lambda e, sp_=sp_, dst=dst, bcol=bcol: e.activation(
                    out=dst[:, T:TT], in_=sp_, func=AF.Tanh, scale=0.5, bias=dv[:, bcol:bcol + 1]),
                    reads=[tk, t_dv], writes=[tdst])
                yield
            S.op("act", lambda e, n=n: e.activation(out=a2b, in_=thr, func=AF.Exp, scale=dv[:, 6 + n:7 + n], bias=dv[:, 6 + n:7 + n]),
                 reads=[t_thr, t_dv], writes=[t_a2])
            S.op("act", lambda e, n=n: e.activation(out=thr, in_=thr, func=AF.Exp, scale=dv[:, n:n + 1], bias=dv[:, n:n + 1]),
                 reads=[t_thr, t_dv], writes=[t_thr])
            S.op("dve", lambda e: e.tensor_scalar(out=a2b, in0=a2b, scalar1=1.0, scalar2=-1.0, op0=ALU.min, op1=ALU.mult),
                 reads=[t_a2], writes=[t_a2])
            yield
            S.op("act", lambda e: e.activation(out=a2b, in_=a2b, func=AF.Sqrt, bias=1.0, scale=1.0), reads=[t_a2], writes=[t_a2])
            S.op("dve", lambda e: e.scalar_tensor_tensor(out=a2b, in0=a2b, scalar=0.5, in1=xc, op0=ALU.mult, op1=ALU.mult),
                 reads=[t_a2, t_xc], writes=[t_a2])
            S.op("dve", lambda e: e.scalar_tensor_tensor(out=thi, in0=thi, scalar=1.0, in1=a2b, op0=ALU.add, op1=ALU.mult),
                 reads=[t_thi, t_a2], writes=[t_thi])
            yield
            S.op("dve", lambda e: e.tensor_tensor_scan(out=a2b[:, 0:T], data0=thr[:, 0:T], data1=thi[:, 0:T], initial=0.0,
                                                       op0=ALU.mult, op1=ALU.add),
                 reads=[t_thr, t_thi], writes=[t_a2])
            S.op("dve", lambda e, n=n: e.tensor_tensor(out=a2b[:, T:TT], in0=thr[:, T:TT], in1=SIT[:, n * 6 + 3, :], op=ALU.mult),
                 reads=[t_thr, t_SIT], writes=[t_a2])
            S.op("dve", lambda e: e.tensor_tensor(out=a2b[:, T:TT], in0=a2b[:, T:TT], in1=thi[:, T:TT], op=ALU.add),
                 reads=[t_a2, t_thi], writes=[t_a2])
            yield
            S.op("pool", lambda e, n=n: e.tensor_copy(SO[:, n, 3:4], a2b[:, T - 1:T]), reads=[t_a2], writes=[t_SO])
            S.op("pool", lambda e, n=n: e.tensor_copy(SO[:, n, 22:38], a2b[:, T:TT]), reads=[t_a2], writes=[t_SO])
            S.op("dve", lambda e, n=n: e.tensor_tensor(out=gA[:, n, :], in0=a2b, in1=gA[:, n, :], op=ALU.mult),
                 reads=[t_a2, t_gA[n]], writes=[t_gA[n]])
            yield

        def gen_B(n):
            vws, wtk = w_get(WT[f"B{n}a"])
            wsg, wsb = vws
            lhs_sg = [wsg[:, k, :] for k in range(KC)]
            lhs_sb = [wsb[:, k, :] for k in range(KC)]
            for (lo, hi) in HALVES:
                pp, tk = unit_half(lhs_sg, uT, lo, [wtk])
                S.op("act", lambda e, pp=pp, n=n, lo=lo, hi=hi: e.activation(out=gB[:, n, lo:hi], in_=pp[:], func=AF.Silu),
                     reads=[tk], writes=[t_gB[n]])
            sp_, tk = unit_samp(lhs_sg, uT, [wtk])
            S.op("act", lambda e, sp_=sp_, n=n: e.activation(out=gB[:, n, T:TT], in_=sp_, func=AF.Silu),
                 reads=[tk], writes=[t_gB[n]])
            yield
            for (lo, hi) in HALVES:
                pp, tk = unit_half(lhs_sb, uT, lo, [wtk])
                S.op("dve", lambda e, pp=pp, n=n, lo=lo, hi=hi: e.tensor_tensor(out=gB[:, n, lo:hi], in0=pp[:], in1=gB[:, n, lo:hi],
                                                                                op=ALU.mult),
                     reads=[tk, t_gB[n]], writes=[t_gB[n]])
            sp_, tk = unit_samp(lhs_sb, uT, [wtk])
            S.op("dve", lambda e, sp_=sp_, n=n: e.tensor_tensor(out=gB[:, n, T:TT], in0=sp_, in1=gB[:, n, T:TT], op=ALU.mult),
                 reads=[tk, t_gB[n]], writes=[t_gB[n]])
            yield
            vws, wtk = w_get(WT[f"B{n}b"])
            wscg, wsh = vws
            lhs_scg = [wscg[:, k, :] for k in range(KC)]
            lhs_sh = [wsh[:, k, :] for k in range(KC)]
            for (lo, hi) in HALVES:
                pp, tk = unit_half(lhs_scg, uT, lo, [wtk])
                S.op("act", lambda e, pp=pp, lo=lo, hi=hi: e.copy(out=scg_sb[:, lo:hi], in_=pp[:]), reads=[tk], writes=[t_scg])
            sp_, tk = unit_samp(lhs_scg, uT, [wtk])
            S.op("act", lambda e, sp_=sp_: e.copy(out=scg_sb[:, T:TT], in_=sp_), reads=[tk], writes=[t_scg])
            yield
            for (lo, hi) in HALVES:
                pp, tk = unit_half(lhs_sh, uT, lo, [wtk])
                S.op("dve", lambda e, pp=pp, lo=lo, hi=hi: e.tensor_tensor(out=cinp[:, 2 + lo:2 + hi], in0=pp[:],
                                                                           in1=scg_sb[:, lo:hi], op=ALU.mult),
                     reads=[tk, t_scg], writes=[t_cinp])
            sp_, tk = unit_samp(lhs_sh, uT, [wtk])
            S.op("dve", lambda e, n=n: e.tensor_copy(cins[:, 0:2, :], SIT[:, n * 6 + 4:n * 6 + 6, :]), reads=[t_SIT], writes=[t_cins])
            S.op("dve", lambda e, sp_=sp_: e.tensor_tensor(out=cins[:, 2, :], in0=sp_, in1=scg_sb[:, T:TT], op=ALU.mult),
                 reads=[tk, t_scg], writes=[t_cins])
            yield
            sw = lambda k, n=n: pT[:, 64 + k * 6 + n: 65 + k * 6 + n]
            S.op("act", lambda e, sw=sw: e.activation(out=cy[:, 0:T], in_=cinp[:, 0:T], func=AF.Identity, scale=sw(0)),
                 reads=[t_cinp, t_pT], writes=[t_cy])
            S.op("act", lambda e, sw=sw: e.activation(out=cy[:, T:TT], in_=cins[:, 0, :], func=AF.Identity, scale=sw(0)),
                 reads=[t_cins, t_pT], writes=[t_cy])
            for k in range(1, 3):
                S.op("dve", lambda e, k=k, sw=sw: e.scalar_tensor_tensor(out=cy[:, 0:T], in0=cinp[:, k:k + T], scalar=sw(k),
                                                                         in1=cy[:, 0:T], op0=ALU.mult, op1=ALU.add),
                     reads=[t_cinp, t_cy], writes=[t_cy])
                S.op("dve", lambda e, k=k, sw=sw: e.scalar_tensor_tensor(out=cy[:, T:TT], in0=cins[:, k, :], scalar=sw(k),
                                                                         in1=cy[:, T:TT], op0=ALU.mult, op1=ALU.add),
                     reads=[t_cins, t_cy], writes=[t_cy])
            S.op("pool", lambda e, n=n: e.tensor_copy(SO[:, n, 4:6], cinp[:, T:T + 2]), reads=[t_cinp], writes=[t_SO])
            S.op("pool", lambda e, n=n: e.tensor_copy(SO[:, n, 38:54], cins[:, 2, :]), reads=[t_cins], writes=[t_SO])
            S.op("dve", lambda e, n=n: e.tensor_tensor(out=gB[:, n, :], in0=cy, in1=gB[:, n, :], op=ALU.mult),
                 reads=[t_cy, t_gB[n]], writes=[t_gB[n]])
            yield

        def interleave(*gens):
            gens = list(gens)
            while gens:
                for g in list(gens):
                    try:
                        next(g)
                    except StopIteration:
                        gens.remove(g)

        gens = {}

        def adv(kind, n):
            g = gens.get((kind, n))
            if g is None:
                return
            try:
                next(g)
            except StopIteration:
                pass

        for n in range(NCH + 1):
            if n < NCH:
                gens[("A", n)] = gen_A(n)
            if n >= 1:
                gens[("B", n - 1)] = gen_B(n - 1)
            adv("A", n)
            adv("B", n - 1)
            adv("A", n - 1)
            adv("A", n)
            adv("B", n - 1)
            adv("A", n)
            adv("A", n - 1)
            adv("B", n - 1)
            adv("B", n - 1)
            adv("A", n - 1)
            adv("A", n)
            adv("A", n)
            adv("A", n)
            adv("B", n - 1)
            adv("A", n)
        for g in gens.values():
            for _ in g:
                pass
        if STOP_AFTER == "B":
            dump([gB[:, 0, 0:512], gB[:, 5, T - 512:T], gB[:, 0, T:TT], gB[:, 5, T:TT], gB[:, 2, 1024:1536]], off_b=0)
        t_sot = Tok("sot")
        for g0 in range(0, NCH, 3):
            for n in range(g0, g0 + 3):
                S.op("pe", lambda e, n=n, g0=g0: e.transpose(PX[0:54, (n - g0) * 128:(n - g0 + 1) * 128], SO[:, n, :], ident[:, :]),
                     reads=[t_SO, t_ident], writes=[tPX], inc=(n == g0 + 2))
            S.op("act", lambda e, g0=g0: e.copy(out=so_tok[0:54, g0 * 128:(g0 + 3) * 128], in_=PX[0:54, 0:384]),
                 reads=[tPX], writes=[t_sot])
        S.dma("sp", "out", o_plc[:, :], so_tok[0:3, :], reads=[t_sot])
        S.dma("sp", "out", o_ph[:, :], so_tok[3:4, :], reads=[t_sot])
        S.dma("sp", "out", o_psc[:, :], so_tok[4:6, :], reads=[t_sot])
        S.dma("sp", "out", o_slc3[:, 2, :], so_tok[6:22, :], reads=[t_sot])
        S.dma("sp", "out", o_sh[:, :], so_tok[22:38, :], reads=[t_sot])
        S.dma("sp", "out", o_ssc3[:, 1, :], so_tok[38:54, :], reads=[t_sot])
        if STOP_AFTER == "B":
            dump([gB[:, 0, 0:512], gB[:, 5, T - 512:T], gB[:, 0, T:TT], gB[:, 5, T:TT], gB[:, 2, 1024:1536]], off_b=56320)
        S.barrier(no_wait=("pe",))
        if STOP_AFTER == "B":
            S.finish(block)
            return nc

        NTH = 3
        thb = [carve(i * 4096, 1024, F32) for i in range(NTH)]
        tacc = [carve(12288 + i * 4096, 1024, F32) for i in range(2)]
        mT_t = carve(20480, KC * TT, BF16)
        mT = mT_t.rearrange("p (k t) -> p k t", k=KC)
        wo_t = carve(53504, KC * D, BF16)
        wo = wo_t.rearrange("p (k c) -> p k c", k=KC)
        t_wo = Tok("wo")
        if STOP_AFTER == "M0":
            S.barrier()
            S.finish(block)
            return nc
        t_th = [Tok(f"th{i}") for i in range(NTH)]
        t_acc = [Tok("acc0"), Tok("acc1")]
        t_mT = Tok("mT")
        thc = {"i": 0, "a": 0}
        gsrc = [(gA, NCH), (gB, NCH), (og, 4)]
        gtok = [t_gA, t_gB, t_og]
        ths_f = carve(69888, 3 * TS, F32)
        tzs_f = carve(70080, 3 * TS, F32)
        accs_f = carve(70272, TS, F32)
        t_accs2 = Tok("accs2")
        t_ths = Tok("ths")
        t_accs = Tok("accs")
        for j in range(8):
            vg, wtkg = w_get(WT[f"Mg{j}"])
            vo, wtko = w_get(WT[f"Mo{j}"])
            S.dma("pool", "cs4", wo[:, j, :], w_out[j * 128:(j + 1) * 128, :], writes=[t_wo], nowait=True)
            tMs_g, tMs_o = tPSb, tPX
            for x in range(3):
                lhs_g = [vg[x][:, k, :] for k in range(KC)]
                mm_group(PS[:, x * TS:(x + 1) * TS], [(lhs_g[k], uT[:, k, T:TT]) for k in range(KC)], [wtkg], tMs_g)
            S.op("act", lambda e: e.activation(out=ths_f, in_=PS[:, 0:3 * TS], func=AF.Tanh, scale=0.5),
                 reads=[tMs_g], writes=[t_ths])
            for x in range(3):
                garr, nk = gsrc[x]
                lhs_o = [vo[x][:, k, :] for k in range(nk)]
                mm_group(PX[:, x * TS:(x + 1) * TS], [(lhs_o[k], garr[:, k, T:TT]) for k in range(nk)], [wtko] + gtok[x], tMs_o)
            S.op("dve", lambda e: e.scalar_tensor_tensor(out=tzs_f, in0=ths_f, scalar=1.0, in1=PX[:, 0:3 * TS],
                                                         op0=ALU.add, op1=ALU.mult),
                 reads=[t_ths, tMs_o], writes=[t_accs])
            S.op("dve", lambda e: e.tensor_reduce(out=accs_f, in_=tzs_f.rearrange("p (x b) -> p b x", x=3),
                                                  axis=AX.X, op=ALU.add),
                 reads=[t_accs], writes=[t_accs2])
            S.op("dve", lambda e, j=j: e.tensor_copy(mT[:, j, T:TT], accs_f), reads=[t_accs2], writes=[t_mT])
            for (lo, hi) in HALVES:
                acc, tacc_ = tacc[thc["a"] % 2], t_acc[thc["a"] % 2]
                thc["a"] += 1
                for x in range(3):
                    garr, nk = gsrc[x]
                    lhs_g = [vg[x][:, k, :] for k in range(KC)]
                    lhs_o = [vo[x][:, k, :] for k in range(nk)]
                    th_, tth = thb[thc["i"] % NTH], t_th[thc["i"] % NTH]
                    thc["i"] += 1
                    pp, tk = unit_half(lhs_g, uT, lo, [wtkg])
                    S.op("act", lambda e, pp=pp, th_=th_: e.activation(out=th_, in_=pp[:], func=AF.Tanh, scale=0.5),
                         reads=[tk], writes=[tth])
                    zp_, tkz = unit_half(lhs_o, garr, lo, [wtko] + gtok[x])
                    zp = zp_[:]
                    if x == 0:
                        S.op("dve", lambda e, acc=acc, th_=th_, zp=zp: e.scalar_tensor_tensor(out=acc, in0=th_, scalar=1.0, in1=zp,
                                                                                           op0=ALU.add, op1=ALU.mult),
                             reads=[tth, tkz], writes=[tacc_])
                    else:
                        S.op("dve", lambda e, th_=th_, zp=zp: e.scalar_tensor_tensor(out=th_, in0=th_, scalar=1.0, in1=zp,
                                                                                    op0=ALU.add, op1=ALU.mult),
                             reads=[tth, tkz], writes=[tth])
                        if x == 1:
                            S.op("pool", lambda e, acc=acc, th_=th_: e.tensor_tensor(out=acc, in0=acc, in1=th_, op=ALU.add),
                                 reads=[tth, tacc_], writes=[tacc_])
                        else:
                            S.op("pool", lambda e, acc=acc, th_=th_, j=j, lo=lo, hi=hi: e.tensor_tensor(
                                out=mT[:, j, lo:hi], in0=acc, in1=th_, op=ALU.add),
                                reads=[tth, tacc_], writes=[t_mT])
            if STOP_AFTER == "M5":
                S.barrier()
                S.finish(block)
                return nc
        S.barrier(no_wait=("pe",))
        if STOP_AFTER == "M":
            S.finish(block)
            return nc

        xr = [carve(i * 4096, 1024, F32) for i in range(2)]
        yr = [carve(8192 + i * 4096, 1024, F32) for i in range(2)] + [carve(69888, 1024, F32)]
        fgb = carve(16384, 1024, F32)
        t_fgb = Tok("fgb")
        S.dma("sp", "cs5", fgb, final_norm_g.partition_broadcast(128), writes=[t_fgb])
        t_xr = [Tok("xr0"), Tok("xr1")]
        t_yr = [Tok("yr0"), Tok("yr1"), Tok("yr2")]
        ftiles = [(xp[i * 128:(i + 1) * 128, :], y_p[i * 128:(i + 1) * 128, :], 128, i * 128) for i in range(16)]
        ftiles.append((xs[:, :], y_s[:, :], TS, T))
        xsem2 = ["fxa", "fxb"]
        ysem = ["ya", "yb", "xc"]
        t_stF = [Tok(f"stF{i}") for i in range(len(ftiles))]

        def f_stageA(ti):
            src, dst, nr, c0 = ftiles[ti]
            xr_, txr = xr[ti % 2], t_xr[ti % 2]
            yr_, tyr = yr[ti % 3], t_yr[ti % 3]
            S.dma("pool", xsem2[ti % 2], xr_[0:nr, :], src, writes=[txr])
            pp, tk = next_pp()
            for b in range(2):
                pairs = [(mT[:, k, c0:c0 + nr], wo[:, k, b * 512:(b + 1) * 512]) for k in range(KC)]
                mm_group(pp[0:nr, b * 512:(b + 1) * 512], pairs, [t_mT, t_wo], tk, last_inc=(b == 1))
            S.op("dve", lambda e, pp=pp, yr_=yr_, xr_=xr_, nr=nr: e.scalar_tensor_tensor(
                out=yr_[0:nr, :], in0=pp[0:nr, :], scalar=0.5, in1=xr_[0:nr, :], op0=ALU.mult, op1=ALU.add),
                reads=[tk, txr], writes=[tyr])

        def f_stageA2(ti):
            src, dst, nr, c0 = ftiles[ti]
            xr_, txr = xr[ti % 2], t_xr[ti % 2]
            yr_, tyr = yr[ti % 3], t_yr[ti % 3]
            col = 32 + ti
            S.op("act", lambda e, xr_=xr_, yr_=yr_, nr=nr, col=col: e.activation(out=xr_[0:nr, :], in_=yr_[0:nr, :], func=AF.Square,
                                                                                accum_out=stat2[0:nr, col:col + 1]),
                 reads=[tyr, t_sm], writes=[txr, t_stF[ti]])

        def f_stageB1(ti):
            src, dst, nr, c0 = ftiles[ti]
            col = 32 + ti
            S.op("act", lambda e, nr=nr, col=col: e.activation(out=stat2[0:nr, col:col + 1], in_=stat2[0:nr, col:col + 1], func=AF.Sqrt,
                                                               scale=1.0 / D, bias=epst[0:nr, 0:1]),
                 reads=[t_stF[ti], t_eps], writes=[t_stF[ti]])

        def f_stageB(ti):
            src, dst, nr, c0 = ftiles[ti]
            yr_, tyr = yr[ti % 3], t_yr[ti % 3]
            col = 32 + ti
            S.op("dve", lambda e, nr=nr, col=col: e.reciprocal(out=stat2[0:nr, col:col + 1], in_=stat2[0:nr, col:col + 1]),
                 reads=[t_stF[ti]], writes=[t_stF[ti]])
            S.op("dve", lambda e, yr_=yr_, nr=nr, col=col: e.scalar_tensor_tensor(
                out=yr_[0:nr, :], in0=yr_[0:nr, :], scalar=stat2[0:nr, col:col + 1], in1=fgb[0:nr, :], op0=ALU.mult, op1=ALU.mult),
                reads=[tyr, t_stF[ti], t_fgb], writes=[tyr])
            S.dma("sp", ysem[ti % 3], dst, yr_[0:nr, :], reads=[tyr])

        f_stageA(0)
        f_stageA2(0)
        for ti in range(len(ftiles)):
            if ti + 1 < len(ftiles):
                f_stageA(ti + 1)
            f_stageB1(ti)
            if ti + 1 < len(ftiles):
                f_stageA2(ti + 1)
            f_stageB(ti)
        S.barrier()
        S.finish(block)
    return nc


_CACHE = {}


def _get_program():
    if "nc" not in _CACHE:
        _CACHE["nc"] = build_program()
    return _CACHE["nc"]


def kernel(x_prompt, x_sample, cache_mem_k, cache_mem_v, state_lru_h, state_lru_conv, state_sconv, mem_prompt,
           norm_g, mem_norm_g, w_in, lru_conv_w, lru_conv_b, lru_wa, lru_ba, lru_wx, lru_bx, lru_lambda, lru_wo,
           sconv_w, sconv_wo, xa_wk, xa_wv, xa_wo, w_out, final_norm_g):
    f = lambda a: np.ascontiguousarray(np.asarray(a, dtype=np.float32))
    shared = {
        "norm_g": f(norm_g[0]), "mem_norm_g": f(mem_norm_g[0]), "w_in": f(w_in[0]),
        "lru_conv_w": f(lru_conv_w[0]), "lru_conv_b": f(lru_conv_b[0]), "lru_wa": f(lru_wa[0]),
        "lru_ba": f(lru_ba[0]), "lru_wx": f(lru_wx[0]), "lru_bx": f(lru_bx[0]), "lru_lambda": f(lru_lambda[0]),
        "lru_wo": f(lru_wo[0]), "sconv_w": f(sconv_w[0]), "sconv_wo": f(sconv_wo[0]), "xa_wk": f(xa_wk[0]),
        "xa_wv": f(xa_wv[0]), "xa_wo": f(xa_wo[0]), "w_out": f(w_out[0]), "final_norm_g": f(final_norm_g),
    }
    in_maps = []
    for c in range(NCORES):
        sl = slice(c * TS, (c + 1) * TS)
        m = dict(shared)
        m["xp"] = f(x_prompt[c])
        m["xs"] = f(np.asarray(x_sample)[sl, 0, :])
        m["memp"] = f(mem_prompt[c])
        m["ck"] = f(np.asarray(cache_mem_k)[0, sl].reshape(TS, NM, XW))
        m["cv"] = f(np.asarray(cache_mem_v)[0, sl].reshape(TS, NM, XW))
        m["st_h"] = f(np.asarray(state_lru_h)[0, sl])
        m["st_lc"] = f(np.asarray(state_lru_conv)[0, sl].reshape(TS, 3 * W))
        m["st_sc"] = f(np.asarray(state_sconv)[0, sl].reshape(TS, 2 * W))
        in_maps.append(m)
    nc = _get_program()
    res = run_bass_kernel_spmd(nc, in_maps, core_ids=list(range(NCORES)))
    rs = res.results
    cat = lambda k: np.concatenate([np.asarray(r[k]) for r in rs], axis=0)
    y_prompt = np.stack([np.asarray(r["y_p"]) for r in rs], axis=0).astype(np.float32)
    y_sample = cat("y_s").reshape(NCORES * TS, 1, D).astype(np.float32)
    p_mk = np.stack([np.asarray(r["o_pk"]) for r in rs], axis=0).reshape(1, NCORES, NM, 4, 128).astype(np.float32)
    p_mv = np.stack([np.asarray(r["o_pv"]) for r in rs], axis=0).reshape(1, NCORES, NM, 4, 128).astype(np.float32)
    p_h = cat("o_ph").reshape(1, NCORES, W).astype(np.float32)
    p_lc = np.stack([np.asarray(r["o_plc"]) for r in rs], axis=0).reshape(1, NCORES, 3, W).astype(np.float32)
    p_sc = np.stack([np.asarray(r["o_psc"]) for r in rs], axis=0).reshape(1, NCORES, 2, W).astype(np.float32)
    s_h = cat("o_sh").reshape(1, NCORES * TS, W).astype(np.float32)
    s_lc = cat("o_slc").reshape(1, NCORES * TS, 3, W).astype(np.float32)
    s_sc = cat("o_ssc").reshape(1, NCORES * TS, 2, W).astype(np.float32)
    return (y_prompt, y_sample, p_mk, p_mv, p_h, p_lc, p_sc, s_h, s_lc, s_sc)
```

```python
import math
from contextlib import ExitStack

import numpy as np
import concourse.bass as bass
import concourse.mybir as mybir
from concourse.bass_utils import run_bass_kernel_spmd

F32 = mybir.dt.float32
BF16 = mybir.dt.bfloat16
AF = mybir.ActivationFunctionType
ALU = mybir.AluOpType
AX = mybir.AxisListType

NCORES = 8
T = 2048
TS = 16
TT = T + TS
D = 1024
KC = 8
W = 768
NCH = 6
NM = 256
XW = 512
IN_COLS = 8704
EPS = 1e-6
SCALE = 1.0 / math.sqrt(128.0)
STOP_AFTER = None

C_LX, C_LG, C_SB, C_SCG, C_SH, C_SG, C_Q, C_QG, C_MG = 0, 768, 1536, 2304, 3072, 3840, 4608, 5120, 5632


class Tok:
    __slots__ = ("name", "w", "r")

    def __init__(self, name=""):
        self.name = name
        self.w = None
        self.r = {}


class Stream:
    def __init__(self, key, sem):
        self.key = key
        self.sem = sem
        self.cnt = 0
        self.seen = {}
        self.ops = []
        self.pending = False


class Sched:
    def __init__(self, nc):
        self.nc = nc
        self.sems = {}
        self.streams = {}
        self.dma_cnt = {}

    def add_stream(self, key, sem):
        self.sems[key] = sem
        self.streams[key] = Stream(key, sem)

    def add_dma_sem(self, key, sem):
        self.sems[key] = sem
        self.dma_cnt[key] = 0

    def _needs(self, st, reads, writes):
        needs = {}

        def need(ev):
            if ev is None:
                return
            k, v = ev
            if needs.get(k, 0) < v:
                needs[k] = v
        for t in reads:
            need(t.w)
        for t in writes:
            need(t.w)
            for k, v in t.r.items():
                if k == st.key:
                    continue
                need((k, v))
        out = []
        for k, v in needs.items():
            if k == st.key and k == "pe":
                continue
            if st.seen.get(k, 0) < v:
                st.seen[k] = v
                out.append((k, v))
        return out

    def op(self, key, fn, reads=(), writes=(), inc=True):
        st = self.streams[key]
        waits = self._needs(st, reads, writes)
        if inc:
            st.cnt += 1
            st.pending = False
            ev = (key, st.cnt)
        else:
            st.pending = True
            ev = (key, st.cnt + 1)
        sems = self.sems
        sem = st.sem

        def run(eng, waits=waits, fn=fn, inc=inc):
            for k, v in waits:
                eng.wait_ge(sems[k], v)
            ins = fn(eng)
            if inc:
                ins.then_inc(sem, 1)
        st.ops.append(run)
        for t in writes:
            t.w = ev
            t.r = {}
        for t in reads:
            if t.r.get(key, 0) < ev[1]:
                t.r[key] = ev[1]

    def dma(self, key, semkey, out, in_, reads=(), writes=(), nowait=False, **kw):
        st = self.streams[key]
        waits = [] if nowait else self._needs(st, reads, writes)
        self.dma_cnt[semkey] += 16
        ev = (semkey, self.dma_cnt[semkey])
        sems = self.sems

        def run(eng, waits=waits):
            for k, v in waits:
                eng.wait_ge(sems[k], v)
            eng.dma_start(out=out, in_=in_, **kw).then_inc(sems[semkey], 16)
        st.ops.append(run)
        for t in writes:
            t.w = ev
            t.r = {}
        for t in reads:
            if t.r.get(semkey, 0) < ev[1]:
                t.r[semkey] = ev[1]

    def wait_dma(self, keys, semkey):
        v = self.dma_cnt[semkey]
        sems = self.sems
        for k in keys:
            st = self.streams[k]
            if v and st.seen.get(semkey, 0) < v:
                st.seen[semkey] = v
                st.ops.append(lambda eng, v=v: eng.wait_ge(sems[semkey], v))

    def barrier(self, skip=(), no_wait=()):
        targets = {}
        for k, st in self.streams.items():
            assert not st.pending
            if st.cnt:
                targets[k] = st.cnt
        for k, v in self.dma_cnt.items():
            if v and k not in skip:
                targets[k] = v
        sems = self.sems
        for k, st in self.streams.items():
            if k in no_wait:
                continue
            waits = []
            for tk, tv in targets.items():
                if st.seen.get(tk, 0) < tv:
                    st.seen[tk] = tv
                    waits.append((tk, tv))

            def run(eng, waits=waits):
                for kk, v in waits:
                    eng.wait_ge(sems[kk], v)
            st.ops.append(run)

    def finish(self, block):
        for key, st in self.streams.items():
            assert not st.pending, key
        ss = self.streams

        def mk(key):
            def body(eng):
                for o in ss[key].ops:
                    o(eng)
            return body
        block.gpsimd(mk("pool"))
        block.tensor(mk("pe"))
        block.scalar(mk("act"))
        block.vector(mk("dve"))
        block.sync(mk("sp"))


def build_program():
    nc = bass.Bass("TRN2", target_bir_lowering=False)

    def din(name, shape):
        return nc.dram_tensor(name, shape, F32, kind="ExternalInput").ap()

    def dout(name, shape):
        return nc.dram_tensor(name, shape, F32, kind="ExternalOutput").ap()

    xp = din("xp", [T, D])
    xs = din("xs", [TS, D])
    memp = din("memp", [NM, D])
    ck = din("ck", [TS, NM, XW])
    cv = din("cv", [TS, NM, XW])
    st_h = din("st_h", [TS, W])
    st_lc = din("st_lc", [TS, 3 * W])
    st_sc = din("st_sc", [TS, 2 * W])
    norm_g = din("norm_g", [D])
    mem_norm_g = din("mem_norm_g", [D])
    w_in = din("w_in", [D, IN_COLS])
    lru_conv_w = din("lru_conv_w", [4, W])
    lru_conv_b = din("lru_conv_b", [W])
    lru_wa = din("lru_wa", [NCH, 128, 128])
    lru_ba = din("lru_ba", [W])
    lru_wx = din("lru_wx", [NCH, 128, 128])
    lru_bx = din("lru_bx", [W])
    lru_lambda = din("lru_lambda", [W])
    lru_wo = din("lru_wo", [W, D])
    sconv_w = din("sconv_w", [3, W])
    sconv_wo = din("sconv_wo", [W, D])
    xa_wk = din("xa_wk", [D, XW])
    xa_wv = din("xa_wv", [D, XW])
    xa_wo = din("xa_wo", [XW, D])
    w_out = din("w_out", [D, D])
    final_norm_g = din("final_norm_g", [D])

    y_p = dout("y_p", [T, D])
    y_s = dout("y_s", [TS, D])
    o_pk = dout("o_pk", [NM, XW])
    o_pv = dout("o_pv", [NM, XW])
    o_ph = dout("o_ph", [1, W])
    o_plc = dout("o_plc", [3, W])
    o_psc = dout("o_psc", [2, W])
    o_sh = dout("o_sh", [TS, W])
    o_slc = dout("o_slc", [TS, 3 * W])
    o_ssc = dout("o_ssc", [TS, 2 * W])

    dbg = dout("dbg", [128, 4096]) if STOP_AFTER else None
    es = ExitStack()
    with es:
        def sb(name, shape, dt):
            return es.enter_context(nc.sbuf_tensor(name, shape, dt))

        uT_t = sb("uT", [128, KC * TT], BF16)
        uT = uT_t[:].rearrange("p (k t) -> p k t", k=KC)
        gA_t = sb("gA", [128, NCH * TT], BF16)
        gA = gA_t[:].rearrange("p (k t) -> p k t", k=NCH)
        gB_t = sb("gB", [128, NCH * TT], BF16)
        gB = gB_t[:].rearrange("p (k t) -> p k t", k=NCH)
        og_t = sb("og", [128, 4 * TT], BF16)
        og = og_t[:].rearrange("p (k t) -> p k t", k=4)
        NSLOT = 5
        SLOT_E = 3072
        slots = [sb(f"wslot{i}", [128, SLOT_E], BF16) for i in range(NSLOT)]
        ident = sb("ident", [128, 128], F32)
        identb = sb("identb", [128, 128], BF16)
        onesb = sb("onesb", [128, 128], BF16)
        pT = sb("pT", [128, 82], F32)
        dv = sb("dv", [128, 24], F32)
        SIT_t = sb("SIT", [128, 36 * TS], F32)
        SIT = SIT_t[:].rearrange("p (a b) -> p a b", a=36)
        SO_t = sb("SO", [128, NCH * 54], F32)
        SO = SO_t[:].rearrange("p (a b) -> p a b", a=NCH)
        stat = sb("stat", [128, 64], F32)
        stat2 = sb("stat2", [128, 64], F32)
        epst = sb("epst", [128, 1], F32)
        q25 = sb("q25", [128, 1], F32)
        RW = 18816
        R = sb("R", [128, RW], F32)

        def carve(off_b, nelem, dt):
            assert off_b % 4 == 0
            if dt == F32:
                assert off_b // 4 + nelem <= RW, (off_b, nelem)
                return R[:, off_b // 4: off_b // 4 + nelem]
            assert nelem % 2 == 0 and off_b // 4 + nelem // 2 <= RW, (off_b, nelem)
            return R[:, off_b // 4: off_b // 4 + nelem // 2].bitcast(BF16)

        PP = [es.enter_context(nc.psum_tensor(f"PP{i}", [128, 1024], F32)) for i in range(3)]
        PS = es.enter_context(nc.psum_tensor("PS", [128, 512], F32))
        PX = es.enter_context(nc.psum_tensor("PX", [128, 512], F32))
        tPP = [Tok(f"PP{i}") for i in range(3)]
        tPS = [Tok(f"PS{i}") for i in range(8)]
        tPX = Tok("PX")
        PSb = PS[:].bitcast(BF16)
        PXb = PX[:].bitcast(BF16)

        S = Sched(nc)
        for k in ["pe", "act", "dve", "pool", "sp"]:
            S.add_stream(k, es.enter_context(nc.semaphore("s_" + k)))
        dsem_names = (["cst", "cs2", "cs3", "cs4", "cs5", "sva", "svb", "svc", "svd", "sve", "svf", "fxa", "fxb", "out", "xa", "xb", "xc", "ya", "yb", "ka", "kb", "va", "vb"]
                      + [f"ws{i}" for i in range(NSLOT)])
        for k in dsem_names:
            S.add_dma_sem(k, es.enter_context(nc.semaphore("d_" + k)))
        block = es.enter_context(nc.Block())

        state = {"pp": 0, "ps": 0}

        def next_pp():
            i = state["pp"] % 3
            state["pp"] += 1
            return PP[i], tPP[i]

        def next_ps():
            i = state["ps"] % 8
            state["ps"] += 1
            return i, tPS[i]

        wtiles = []
        wstate = {"issued": 0}
        tslot = [Tok(f"slot{i}") for i in range(NSLOT)]

        def w_in_cols(c0, ncols):
            return w_in.rearrange("(k p) c -> p k c", p=128)[:, :, c0:c0 + ncols]

        def add_wtile(pieces):
            off = 0
            lst = []
            for ap in pieces:
                kc, ncols = ap.shape[1], ap.shape[2]
                lst.append((ap, off, kc, ncols))
                off += kc * ncols
            assert off <= SLOT_E, off
            wtiles.append(lst)
            return len(wtiles) - 1

        def w_issue_upto(i):
            while wstate["issued"] <= min(i, len(wtiles) - 1):
                j = wstate["issued"]
                sl = j % NSLOT
                for pi, (ap, off, kc, ncols) in enumerate(wtiles[j]):
                    dst = slots[sl][:, off:off + kc * ncols].rearrange("p (k c) -> p k c", k=kc)
                    S.dma("pool", f"ws{sl}", dst, ap, writes=[tslot[sl]], nowait=(pi > 0))
                wstate["issued"] += 1

        def w_get(i):
            w_issue_upto(i + NSLOT - 2)
            sl = i % NSLOT
            views = []
            for (ap, off, kc, ncols) in wtiles[i]:
                views.append(slots[sl][:, off:off + kc * ncols].rearrange("p (k c) -> p k c", k=kc))
            return views, tslot[sl]

        WT = {}
        WT["wk0"] = add_wtile([xa_wk.rearrange("(k p) c -> p k c", p=128)[:, :, 0:256]])
        WT["wk1"] = add_wtile([xa_wk.rearrange("(k p) c -> p k c", p=128)[:, :, 256:512]])
        WT["wv0"] = add_wtile([xa_wv.rearrange("(k p) c -> p k c", p=128)[:, :, 0:256]])
        WT["wv1"] = add_wtile([xa_wv.rearrange("(k p) c -> p k c", p=128)[:, :, 256:512]])
        for hh in range(2):
            WT[f"q{hh}"] = add_wtile([w_in_cols(C_Q + hh * 256, 256)])
        for hh in range(2):
            WT[f"qg{hh}"] = add_wtile([w_in_cols(C_QG + hh * 256, 256)])
        for n in range(NCH + 1):
            if n < NCH:
                WT[f"A{n}"] = add_wtile([w_in_cols(C_LG + n * 128, 128), w_in_cols(C_LX + n * 128, 128)])
            if n >= 1:
                m_ = n - 1
                WT[f"B{m_}a"] = add_wtile([w_in_cols(C_SG + m_ * 128, 128), w_in_cols(C_SB + m_ * 128, 128)])
                WT[f"B{m_}b"] = add_wtile([w_in_cols(C_SCG + m_ * 128, 128), w_in_cols(C_SH + m_ * 128, 128)])
        lwo = lru_wo.rearrange("(k p) c -> p k c", p=128)
        swo = sconv_wo.rearrange("(k p) c -> p k c", p=128)
        xwo = xa_wo.rearrange("(k p) c -> p k c", p=128)
        for j in range(8):
            WT[f"Mg{j}"] = add_wtile([w_in_cols(C_MG + x * 1024 + j * 128, 128) for x in range(3)])
            WT[f"Mo{j}"] = add_wtile([lwo[:, :, j * 128:(j + 1) * 128], swo[:, :, j * 128:(j + 1) * 128],
                                      xwo[:, :, j * 128:(j + 1) * 128]])

        def mm_group(out_ap, pairs, reads, wtok, last_inc=True):
            n = len(pairs)
            for i, (l, r) in enumerate(pairs):
                S.op("pe", lambda e, l=l, r=r, i=i: e.matmul(out_ap, l, r, start=(i == 0), stop=(i == n - 1)),
                     reads=reads, writes=[wtok], inc=(last_inc and i == n - 1))

        def unit_half(lhs_list, rhs_arr, lo, reads):
            pp, tk = next_pp()
            nk = len(lhs_list)
            for b in range(2):
                pairs = [(lhs_list[k], rhs_arr[:, k, lo + b * 512: lo + (b + 1) * 512]) for k in range(nk)]
                mm_group(pp[:, b * 512:(b + 1) * 512], pairs, reads, tk, last_inc=(b == 1))
            return pp, tk

        tPSb = Tok("PSbank")
        tPS2 = [tPSb, tPX]

        def unit_samp(lhs_list, rhs_arr, reads):
            i = state.setdefault("ps2", 0) % 2
            state["ps2"] += 1
            tk = tPS2[i]
            nk = len(lhs_list)
            bank = PS if i == 0 else PX
            out_ap = bank[:, 0:TS]
            pairs = [(lhs_list[k], rhs_arr[:, k, T:TT]) for k in range(nk)]
            mm_group(out_ap, pairs, reads, tk)
            return out_ap, tk

        HALVES = [(0, 1024), (1024, 2048)]

        def dump(items, off_b=56320):
            dstage = carve(off_b, 4096, F32)
            S.barrier()
            S.op("dve", lambda e: e.memset(dstage[:], 0.0))
            S.barrier()
            off = 0
            for ap in items:
                P_, n_ = ap.shape[0], ap.shape[1]
                S.op("dve", lambda e, ap=ap, off=off, P_=P_, n_=n_: e.tensor_copy(dstage[0:P_, off:off + n_], ap))
                off += n_
            S.barrier()
            S.dma("sp", "out", dbg[:, :], dstage[:])
            S.barrier()

        t_ident = Tok("ident")
        S.op("pool", lambda e: e.memset(ident[:], 0.0), writes=[t_ident])
        S.op("pool", lambda e: e.affine_select(ident[:], ident[:], pattern=[[-1, 128]], compare_op=ALU.not_equal,
                                               fill=1.0, base=0, channel_multiplier=1),
             reads=[t_ident], writes=[t_ident])
        t_identb = Tok("identb")
        S.op("dve", lambda e: e.tensor_copy(identb[:], ident[:]), reads=[t_ident], writes=[t_identb])
        t_stat = Tok("stat")
        t_sm = Tok("sm")
        t_st3 = t_sm
        S.op("pool", lambda e: e.memset(stat[:], 0.0), writes=[t_stat])
        S.op("pool", lambda e: e.memset(stat2[:], 0.0), writes=[t_sm])
        t_ones = Tok("ones")
        S.op("pool", lambda e: e.memset(onesb[:], 1.0), writes=[t_ones])
        t_eps = Tok("eps")
        S.op("pool", lambda e: e.memset(epst[:], EPS), writes=[t_eps])
        S.op("pool", lambda e: e.memset(q25[:], 0.25), writes=[t_eps])

        prt = carve(0, 128, F32)[0:82, :]
        sin = carve(512, 4608, F32)
        NXB = 8
        xbuf = [carve(18944 + i * 4096, 1024, F32) for i in range(3)] + [carve(53760 + i * 4096, 1024, F32) for i in range(5)]
        xnb = [carve(31232 + i * 2048, 1024, BF16) for i in range(2)]
        junk = carve(35328, 1024, BF16)
        mnT_t = carve(37376, KC * NM, BF16)
        mnT = mnT_t.rearrange("p (k t) -> p k t", k=KC)
        kT_t = carve(41472, 4 * NM, BF16)
        kT = kT_t.rearrange("p (k t) -> p k t", k=4)
        vb_t = carve(43520, 2 * XW, BF16)
        vb = vb_t.rearrange("p (k t) -> p k t", k=2)
        kvout = [carve(45568 + i * 4096, 2 * XW, F32).rearrange("p (k t) -> p k t", k=2) for i in range(2)]

        t_prt = Tok("prt")

        def rows(v, r):
            return v.rearrange("(r c) -> r c", c=128)

        plist = [(norm_g, 0, 8, None), (mem_norm_g, 8, 8, None),
                 (lru_conv_w, 16, 24, "k (n c) -> (k n) c"), (lru_conv_b, 40, 6, None), (lru_ba, 46, 6, None),
                 (lru_bx, 52, 6, None), (lru_lambda, 58, 6, None), (sconv_w, 64, 18, "k (n c) -> (k n) c")]
        for (v, r0, nr, pat) in plist:
            src = v.rearrange(pat, c=128) if pat else v.rearrange("(r c) -> r c", c=128)
            S.dma("sp", "cst", prt[r0:r0 + nr, :], src, writes=[t_prt], nowait=True)
        if STOP_AFTER == "P0dma":
            S.barrier()
            S.finish(block)
            return nc
        t_pT = Tok("pT")
        S.op("pe", lambda e: e.transpose(PX[:, 0:82], prt, ident[0:82, 0:82]), reads=[t_prt, t_ident], writes=[tPX])
        S.op("act", lambda e: e.copy(out=pT[:], in_=PX[:, 0:82]), reads=[tPX], writes=[t_pT])
        if STOP_AFTER == "P0b":
            S.barrier()
            S.finish(block)
            return nc
        t_x = [Tok(f"x{i}") for i in range(NXB)]
        t_xn = [Tok(f"xn{i}") for i in range(2)]
        t_junk = Tok("junk")
        t_uT = Tok("uT")
        t_mnT = Tok("mnT")
        tiles = [(memp[i * 128:(i + 1) * 128, :], 128, "m", i * 128) for i in range(2)]
        tiles += [(xp[i * 128:(i + 1) * 128, :], 128, "u", i * 128) for i in range(16)]
        tiles.append((xs[:, :], TS, "u", T))
        xsem = ["xa", "xb", "xc", "ya", "yb", "ka", "kb", "va"]
        tPXh = [Tok("PXh0"), Tok("PXh1")]
        t_st0 = [Tok(f"st0_{i}") for i in range(len(tiles))]

        def p0_stageA(ti):
            src, nr, kind, c0 = tiles[ti]
            xb_, tx = xbuf[ti % NXB], t_x[ti % NXB]
            S.dma("sp", xsem[ti % NXB], xb_[0:nr, :], src, writes=[tx])
            S.op("act", lambda e, xb_=xb_, nr=nr, ti=ti: e.activation(out=junk[0:nr, :], in_=xb_[0:nr, :], func=AF.Square,
                                                                      accum_out=stat[0:nr, ti:ti + 1]),
                 reads=[tx, t_stat], writes=[t_st0[ti], t_junk])

        def p0_stageB(ti):
            src, nr, kind, c0 = tiles[ti]
            xb_, tx = xbuf[ti % NXB], t_x[ti % NXB]
            xn_, txn = xnb[ti % 2], t_xn[ti % 2]
            S.op("act", lambda e, nr=nr, ti=ti: e.activation(out=stat[0:nr, ti:ti + 1], in_=stat[0:nr, ti:ti + 1], func=AF.Sqrt,
                                                             scale=1.0 / D, bias=epst[0:nr, 0:1]),
                 reads=[t_st0[ti], t_eps], writes=[t_st0[ti]])
            S.op("dve", lambda e, nr=nr, ti=ti: e.reciprocal(out=stat[0:nr, ti:ti + 1], in_=stat[0:nr, ti:ti + 1]),
                 reads=[t_st0[ti]], writes=[t_st0[ti]])
            S.op("dve", lambda e, xb_=xb_, xn_=xn_, nr=nr, ti=ti: e.tensor_scalar(out=xn_[0:nr, :], in0=xb_[0:nr, :],
                                                                                scalar1=stat[0:nr, ti:ti + 1], scalar2=None,
                                                                                op0=ALU.mult),
                 reads=[tx, t_st0[ti]], writes=[txn])

        def p0_stageC(ti):
            src, nr, kind, c0 = tiles[ti]
            xn_, txn = xnb[ti % 2], t_xn[ti % 2]
            bankb, tbank = (PXb, tPX) if ti % 2 == 0 else (PSb, tPSb)
            for kc in range(KC):
                S.op("pe", lambda e, xn_=xn_, nr=nr, kc=kc, bankb=bankb: e.transpose(bankb[:, kc * 128: kc * 128 + nr],
                                                                        xn_[0:nr, kc * 128:(kc + 1) * 128],
                                                                        identb[0:nr, 0:nr]),
                     reads=[txn, t_identb], writes=[tbank], inc=(kc == KC - 1))
            pview = bankb.rearrange("p (k t) -> p k t", k=KC)[:, :, 0:nr]
            if kind == "u":
                dst = uT[:, :, c0:c0 + nr]
                gcols = pT[:, 0:8]
                tdst = t_uTi[ti]
            else:
                dst = mnT[:, :, c0:c0 + nr]
                gcols = pT[:, 8:16]
                tdst = t_mnT
            S.op("dve", lambda e, dst=dst, pview=pview, gcols=gcols, nr=nr: e.tensor_tensor(
                out=dst, in0=pview, in1=gcols.unsqueeze(2).to_broadcast([128, KC, nr]), op=ALU.mult),
                reads=[tbank, t_pT], writes=[tdst])

        t_kT = Tok("kT")
        t_vb = Tok("vb")
        t_kvout = [Tok("kvo0"), Tok("kvo1")]
        kvst = {}

        def kv_mm():
            for which in range(2):
                wv_, wt_ = [], []
                for hh in range(2):
                    vws, tk = w_get(WT[("wk" if which == 0 else "wv") + str(hh)])
                    wv_.append(vws[0])
                    wt_.append(tk)
                pp, tk = next_pp()
                for mc in range(2):
                    for hh in range(2):
                        pairs = [(mnT[:, k, mc * 128:(mc + 1) * 128], wv_[hh][:, k, :]) for k in range(KC)]
                        mm_group(pp[:, mc * 512 + hh * 256: mc * 512 + (hh + 1) * 256], pairs, [t_mnT, wt_[hh]], tk,
                                 last_inc=(mc == 1 and hh == 1))
                kvst[which] = (pp, tk)
                if which == 0:
                    pp2, tk2 = next_pp()
                    for dc in range(4):
                        hh, off = dc // 2, (dc % 2) * 128
                        pairs = [(wv_[hh][:, k, off:off + 128], mnT[:, k, :]) for k in range(KC)]
                        mm_group(pp2[:, dc * 256:(dc + 1) * 256], pairs, [t_mnT, wt_[hh]], tk2, last_inc=(dc == 3))
                    kvst["kT"] = (pp2, tk2)

        def kv_evac():
            for which in range(2):
                pp, tk = kvst[which]
                ko = kvout[which]
                S.op("act", lambda e, ko=ko, pp=pp: e.copy(out=ko, in_=pp[:].rearrange("p (k t) -> p k t", k=2)),
                     reads=[tk], writes=[t_kvout[which]])
                dsto = (o_pk if which == 0 else o_pv).rearrange("(k p) c -> p k c", p=128)
                S.dma("sp", "out", dsto, ko, reads=[t_kvout[which]])
            pp, tk = kvst[1]
            S.op("act", lambda e, pp=pp: e.copy(out=vb, in_=pp[:].rearrange("p (k t) -> p k t", k=2)),
                 reads=[tk], writes=[t_vb])
            pp2, tk2 = kvst["kT"]
            S.op("dve", lambda e, pp2=pp2: e.tensor_copy(kT, pp2[:].rearrange("p (k t) -> p k t", k=4)),
                 reads=[tk2], writes=[t_kT])

        t_uTi = [Tok(f"uT{i}") for i in range(len(tiles))]
        PD = 5
        for ti in range(PD):
            p0_stageA(ti)
        t_sin = Tok("sin")
        S.dma("sp", "cs2", sin[0:TS, 0:2304], st_lc[:, :], writes=[t_sin], nowait=True)
        S.dma("sp", "cs2", sin[0:TS, 2304:3072], st_h[:, :], writes=[t_sin], nowait=True)
        S.dma("sp", "cs2", sin[0:TS, 3072:4608], st_sc[:, :], writes=[t_sin], nowait=True)
        t_out = Tok("out")
        o_slc3 = o_slc.rearrange("b (k c) -> b k c", k=3)
        st_lc3 = st_lc.rearrange("b (k c) -> b k c", k=3)
        S.dma("sp", "out", o_slc3[:, 0:2, :], st_lc3[:, 1:3, :])
        o_ssc3 = o_ssc.rearrange("b (k c) -> b k c", k=2)
        st_sc3 = st_sc.rearrange("b (k c) -> b k c", k=2)
        S.dma("sp", "out", o_ssc3[:, 0:1, :], st_sc3[:, 1:2, :])

        p0_stageB(0)
        for ti in range(len(tiles)):
            if ti + PD < len(tiles):
                p0_stageA(ti + PD)
            if ti + 1 < len(tiles):
                p0_stageB(ti + 1)
            p0_stageC(ti)
            if ti == 1:
                kv_mm()
            if ti == 5:
                kv_evac()

        t_dv = Tok("dv")
        S.op("act", lambda e: e.activation(out=dv[:, 0:6], in_=pT[:, 58:64], func=AF.Exp, scale=-1.0),
             reads=[t_pT], writes=[t_dv])
        S.op("act", lambda e: e.activation(out=dv[:, 6:12], in_=dv[:, 0:6], func=AF.Ln, bias=1.0, scale=1.0),
             reads=[t_dv], writes=[t_dv])
        S.op("dve", lambda e: e.tensor_scalar(out=dv[:, 0:6], in0=dv[:, 6:12], scalar1=-4.0, scalar2=None, op0=ALU.mult),
             reads=[t_dv], writes=[t_dv])
        S.op("dve", lambda e: e.tensor_scalar(out=dv[:, 6:12], in0=dv[:, 6:12], scalar1=-8.0, scalar2=None, op0=ALU.mult),
             reads=[t_dv], writes=[t_dv])
        S.op("dve", lambda e: e.tensor_scalar(out=dv[:, 12:24], in0=pT[:, 46:58], scalar1=0.5, scalar2=None, op0=ALU.mult),
             reads=[t_pT, t_dv], writes=[t_dv])
        if STOP_AFTER == "P0a":
            S.barrier()
            S.finish(block)
            return nc
        t_SIT = Tok("SIT")
        blocks_ = []
        for n in range(NCH):
            for k in range(3):
                blocks_.append((n * 6 + k, k * W + n * 128))
            blocks_.append((n * 6 + 3, 2304 + n * 128))
            for k in range(2):
                blocks_.append((n * 6 + 4 + k, 3072 + k * W + n * 128))
        for g0 in range(0, 36, 18):
            grp = blocks_[g0:g0 + 18]
            for gi, (slot_i, c0) in enumerate(grp):
                S.op("pe", lambda e, gi=gi, c0=c0: e.transpose(PX[:, gi * TS:(gi + 1) * TS], sin[0:TS, c0:c0 + 128],
                                                              ident[0:TS, 0:TS]),
                     reads=[t_sin, t_ident], writes=[tPX], inc=(gi == len(grp) - 1))
            for gi, (slot_i, c0) in enumerate(grp):
                S.op("act", lambda e, gi=gi, slot_i=slot_i: e.copy(out=SIT[:, slot_i, :], in_=PX[:, gi * TS:(gi + 1) * TS]),
                     reads=[tPX], writes=[t_SIT])

        if STOP_AFTER == "P0c":
            dump([pT[:], stat[:, 0:19], uT[:, 0, 0:256], uT[:, 7, T - 128:TT], mnT[:, 0, :], mnT[:, 7, :]])
            S.barrier()
            S.finish(block)
            return nc
        S.barrier(skip=("out",))
        if STOP_AFTER == "KV":
            S.finish(block)
            return nc

        qT_t = carve(0, 4 * T, BF16)
        qT = qT_t.rearrange("p (k t) -> p k t", k=4)
        qs_tok = carve(16384, XW, BF16)
        sqg_tok = carve(17408, XW, F32)
        pTe = [carve(19456 + i * 2048, 1024, BF16).rearrange("p (k t) -> p k t", k=2) for i in range(2)]
        rden = [carve(23552 + i * 2048, 512, F32) for i in range(2)]
        o1b = [carve(27648 + i * 2048, 512, F32) for i in range(2)]
        selb_t = carve(31744, TS * 128, BF16)
        selb = selb_t.rearrange("p (b c) -> p b c", b=TS)
        eye16_t = carve(35840, 256, F32)
        eye16 = eye16_t.rearrange("p (a b) -> p a b", a=16)
        Sall_t = carve(45568, 128, F32)
        Sall = Sall_t.rearrange("p (c b h) -> p c b h", c=2, b=TS)
        Esm = carve(46080, 256, F32)
        Psm = carve(47104, 256, F32)
        Mk_t = carve(48128, 2 * 4 * 16 * 16, BF16)
        Mk = Mk_t.rearrange("p (c h b q) -> p c h b q", c=2, h=4, b=16)
        qb_sb = [carve(56320 + i * 2048, 512, F32) for i in range(2)]
        Kb = [carve(60416 + i * 4096, 1024, F32).rearrange("p (c f) -> p c f", c=2) for i in range(2)]
        prod = carve(68608, 1024, F32).rearrange("p (c f) -> p c f", c=2)

        t_qT = [Tok(f"qT{h}") for h in range(4)]
        t_og = [Tok(f"og{h}") for h in range(4)]
        t_qs = Tok("qs")
        t_sqg = Tok("sqg")
        def next_pp_c():
            while True:
                i = state["pp"] % 3
                state["pp"] += 1
                if i != state.get("pp_excl", -1):
                    return PP[i], tPP[i]

        def unit_half_c(lhs_list, rhs_arr, lo, reads):
            pp, tk = next_pp_c()
            nk = len(lhs_list)
            for b in range(2):
                pairs = [(lhs_list[k], rhs_arr[:, k, lo + b * 512: lo + (b + 1) * 512]) for k in range(nk)]
                mm_group(pp[:, b * 512:(b + 1) * 512], pairs, reads, tk, last_inc=(b == 1))
            return pp, tk

        t_pTe = [Tok("pTe0"), Tok("pTe1")]
        t_rden = [Tok("rden0"), Tok("rden1")]
        t_o1 = [Tok("o10"), Tok("o11")]

        def gen_C_prompt():
            for hh in range(2):
                vws, wtk = w_get(WT[f"q{hh}"])
                wq = vws[0]
                for hl in range(2):
                    h = hh * 2 + hl
                    lhs = [wq[:, k, hl * 128:(hl + 1) * 128] for k in range(KC)]
                    for (lo, hi) in HALVES:
                        pp, tk = unit_half_c(lhs, uT, lo, [wtk])
                        S.op("act", lambda e, pp=pp, h=h, lo=lo, hi=hi: e.copy(out=qT[:, h, lo:hi], in_=pp[:]),
                             reads=[tk], writes=[t_qT[h]])
                        yield
                pairs = [(uT[:, k, T:TT], wq[:, k, :]) for k in range(KC)]
                mm_group(PS[0:TS, 0:256], pairs, [wtk], tPSb)
                S.op("act", lambda e, hh=hh: e.copy(out=qs_tok[0:TS, hh * 256:(hh + 1) * 256], in_=PS[0:TS, 0:256]),
                     reads=[tPSb], writes=[t_qs])
                yield
            for hh in range(2):
                vws, wtk = w_get(WT[f"qg{hh}"])
                wq = vws[0]
                for hl in range(2):
                    h = hh * 2 + hl
                    lhs = [wq[:, k, hl * 128:(hl + 1) * 128] for k in range(KC)]
                    for (lo, hi) in HALVES:
                        pp, tk = unit_half_c(lhs, uT, lo, [wtk])
                        S.op("act", lambda e, pp=pp, h=h, lo=lo, hi=hi: e.activation(out=og[:, h, lo:hi], in_=pp[:], func=AF.Silu),
                             reads=[tk], writes=[t_og[h]])
                        yield
                pairs = [(uT[:, k, T:TT], wq[:, k, :]) for k in range(KC)]
                mm_group(PS[0:TS, 0:256], pairs, [wtk], tPSb)
                S.op("act", lambda e, hh=hh: e.activation(out=sqg_tok[0:TS, hh * 256:(hh + 1) * 256],
                                                         in_=PS[0:TS, 0:256], func=AF.Silu),
                     reads=[tPSb], writes=[t_sqg])
                yield
            iters = [(tb, h) for tb in range(4) for h in range(4)]
            stage1 = {}

            def att_s1(i):
                tb, h = iters[i]
                c0 = tb * 512
                pe_, tpe = pTe[i % 2], t_pTe[i % 2]
                pp, tk = next_pp_c()
                for mc in range(2):
                    mm_group(pp[:, mc * 512:(mc + 1) * 512], [(kT[:, h, mc * 128:(mc + 1) * 128], qT[:, h, c0:c0 + 512])],
                             [t_kT, t_qT[h]], tk, last_inc=(mc == 1))
                S.op("act", lambda e, pp=pp, pe_=pe_: e.activation(out=pe_, in_=pp[:].rearrange("p (k t) -> p k t", k=2),
                                                                  func=AF.Exp, scale=SCALE),
                     reads=[tk], writes=[tpe])

            def att_s2(i):
                tb, h = iters[i]
                c0 = tb * 512
                pe_, tpe = pTe[i % 2], t_pTe[i % 2]
                rd, trd = rden[i % 2], t_rden[i % 2]
                o1, to1 = o1b[i % 2], t_o1[i % 2]
                pp2, tk2 = next_pp_c()
                mm_group(pp2[:, 0:512], [(vb[:, mc, h * 128:(h + 1) * 128], pe_[:, mc, :]) for mc in range(2)],
                         [t_vb, tpe], tk2, last_inc=False)
                mm_group(pp2[:, 512:1024], [(onesb[:], pe_[:, mc, :]) for mc in range(2)], [t_ones, tpe], tk2)
                S.op("act", lambda e, pp2=pp2, rd=rd: e.activation(out=rd, in_=pp2[:, 512:1024], func=AF.Ln), reads=[tk2], writes=[trd])
                S.op("act", lambda e, rd=rd: e.activation(out=rd, in_=rd, func=AF.Exp, scale=-1.0), reads=[trd], writes=[trd])
                S.op("dve", lambda e, pp2=pp2, rd=rd, o1=o1: e.tensor_tensor(out=o1, in0=pp2[:, 0:512], in1=rd, op=ALU.mult),
                     reads=[tk2, trd], writes=[to1])
                S.op("dve", lambda e, o1=o1, h=h, c0=c0: e.tensor_tensor(out=og[:, h, c0:c0 + 512], in0=o1,
                                                                        in1=og[:, h, c0:c0 + 512], op=ALU.mult),
                     reads=[to1, t_og[h]], writes=[t_og[h]])

            att_s1(0)
            for i in range(len(iters)):
                if i + 1 < len(iters):
                    att_s1(i + 1)
                att_s2(i)
                yield

        t_selb = Tok("selb")
        t_eye = Tok("eye")
        t_qb = [Tok("qb0"), Tok("qb1")]
        t_Kb = [Tok("Kb0"), Tok("Kb1")]
        t_prod = Tok("prod")
        t_Sall = Tok("Sall")
        t_E = Tok("E")
        t_P = Tok("P")
        t_Mk = Tok("Mk")
        ksem = ["ka", "kb"]
        vsem = ["sva", "svb", "svc", "svd", "sve", "svf"]

        def gen_C_sample():
            S.wait_dma(["dve", "act", "pool", "pe"], "out")
            S.op("dve", lambda e: e.tensor_copy(selb[0:TS, :, :], identb[0:TS, 0:TS].unsqueeze(2).to_broadcast([TS, TS, 128])),
                 reads=[t_identb], writes=[t_selb])
            S.op("pool", lambda e: e.memset(eye16_t, 0.0), writes=[t_eye])
            S.op("pool", lambda e: e.affine_select(eye16, eye16, pattern=[[1, 16], [-1, 16]], compare_op=ALU.not_equal,
                                                   fill=1.0, base=0, channel_multiplier=0),
                 reads=[t_eye], writes=[t_eye])
            yield
            for b in range(TS):
                kb_, tkb = Kb[b % 2], t_Kb[b % 2]
                qb_, tqb = qb_sb[b % 2], t_qb[b % 2]
                S.dma("sp", ksem[b % 2], kb_, ck[b].rearrange("(c m) f -> m c f", c=2), writes=[tkb])
                S.op("pe", lambda e, b=b: e.matmul(PX[:, 0:512], selb[0:TS, b, :], qs_tok[0:TS, :], start=True, stop=True),
                     reads=[t_selb, t_qs], writes=[tPX])
                S.op("act", lambda e, qb_=qb_: e.copy(out=qb_, in_=PX[:, 0:512]), reads=[tPX], writes=[tqb])
                S.op("pool", lambda e, kb_=kb_, qb_=qb_: e.tensor_tensor(out=prod, in0=kb_,
                                                                         in1=qb_.unsqueeze(1).to_broadcast([128, 2, 512]), op=ALU.mult),
                     reads=[tkb, tqb], writes=[t_prod])
                S.op("dve", lambda e, b=b: e.tensor_reduce(out=Sall[:, :, b, :],
                                                           in_=prod.rearrange("p c (h d) -> p c h d", h=4), axis=AX.X, op=ALU.add),
                     reads=[t_prod], writes=[t_Sall])
                yield
            for c in range(2):
                S.op("pe", lambda e, c=c: e.transpose(PX[0:64, c * 128:(c + 1) * 128],
                                                      Sall_t[:, c * 64:(c + 1) * 64], ident[:, :]),
                     reads=[t_Sall, t_ident], writes=[tPX], inc=(c == 1))
            S.op("dve", lambda e: e.tensor_reduce(out=stat2[0:64, 0:1], in_=PX[0:64, 0:256], axis=AX.X, op=ALU.max),
                 reads=[tPX], writes=[t_sm])
            S.op("dve", lambda e: e.tensor_scalar(out=stat2[0:64, 0:1], in0=stat2[0:64, 0:1], scalar1=-SCALE, scalar2=None, op0=ALU.mult),
                 reads=[t_sm], writes=[t_sm])
            S.op("act", lambda e: e.activation(out=Esm[0:64, :], in_=PX[0:64, 0:256], func=AF.Exp, scale=SCALE,
                                               bias=stat2[0:64, 0:1], accum_out=stat2[0:64, 1:2]),
                 reads=[tPX, t_sm], writes=[t_E, t_sm])
            S.op("dve", lambda e: e.reciprocal(out=stat2[0:64, 1:2], in_=stat2[0:64, 1:2]), reads=[t_sm], writes=[t_sm])
            S.op("dve", lambda e: e.tensor_scalar(out=Psm[0:64, :], in0=Esm[0:64, :], scalar1=stat2[0:64, 1:2], scalar2=None, op0=ALU.mult),
                 reads=[t_E, t_sm], writes=[t_P])
            for c in range(2):
                S.op("pe", lambda e, c=c: e.transpose(PX[:, c * 64:(c + 1) * 64], Psm[0:64, c * 128:(c + 1) * 128], ident[0:64, 0:64]),
                     reads=[t_P, t_ident], writes=[tPX], inc=(c == 1))
            for c in range(2):
                S.op("dve", lambda e, c=c: e.tensor_tensor(
                    out=Mk[:, c],
                    in0=PX[:, c * 64:(c + 1) * 64].rearrange("p (b h) -> p h b", h=4).unsqueeze(3).to_broadcast([128, 4, 16, 16]),
                    in1=eye16.unsqueeze(1).to_broadcast([128, 4, 16, 16]), op=ALU.mult),
                    reads=[tPX, t_eye], writes=[t_Mk])
            yield
            ppa, tka = next_pp_c()
            state["pp_excl"] = PP.index(ppa)
            hbank = [(PS, 0, tPSb), (PX, 0, tPX), (ppa, 0, tka), (ppa, 512, tka)]
            NVB = 6
            Vb = ([carve(68608 + i * 2048, 1024, BF16).rearrange("p (c f) -> p c f", c=2) for i in range(2)]
                  + [carve(60416 + i * 2048, 1024, BF16).rearrange("p (c f) -> p c f", c=2) for i in range(4)])
            t_Vb = [Tok(f"Vb{i}") for i in range(NVB)]
            valias = [[t_prod], [t_prod], [t_Kb[0]], [t_Kb[0]], [t_Kb[1]], [t_Kb[1]]]

            def v_load(b):
                S.dma("pool", vsem[b % NVB], Vb[b % NVB], cv[b].rearrange("(c m) f -> m c f", c=2),
                      writes=[t_Vb[b % NVB]] + (valias[b] if b < NVB else []))

            for b in range(NVB - 1):
                v_load(b)
            for b in range(TS):
                vb_, tvb = Vb[b % NVB], t_Vb[b % NVB]
                if b + NVB - 1 < TS:
                    v_load(b + NVB - 1)
                for h in range(4):
                    pph, coff, tkh = hbank[h]
                    for c in range(2):
                        first = (b == 0 and c == 0)
                        last = (b == TS - 1 and c == 1)
                        S.op("pe", lambda e, vb_=vb_, b=b, h=h, c=c, first=first, last=last, pph=pph, coff=coff: e.matmul(
                            pph[0:TS, coff:coff + 128], Mk[:, c, h, b, :], vb_[:, c, h * 128:(h + 1) * 128],
                            start=first, stop=last, skip_group_check=True),
                            reads=[t_Mk, tvb], writes=[tkh], inc=(h == 3 and c == 1))
                yield
            ogs_tok = qs_tok
            for h in range(4):
                pph, coff, tkh = hbank[h]
                S.op("dve", lambda e, h=h, pph=pph, coff=coff: e.tensor_tensor(
                    out=ogs_tok[0:TS, h * 128:(h + 1) * 128], in0=pph[0:TS, coff:coff + 128],
                    in1=sqg_tok[0:TS, h * 128:(h + 1) * 128], op=ALU.mult),
                    reads=[tkh, t_sqg, t_qs], writes=[t_qs])
            state["pp_excl"] = -1
            for h in range(4):
                S.op("pe", lambda e, h=h: e.transpose(PXb[:, h * TS:(h + 1) * TS], ogs_tok[0:TS, h * 128:(h + 1) * 128],
                                                      identb[0:TS, 0:TS]),
                     reads=[t_qs, t_identb], writes=[tPX], inc=(h == 3))
            S.op("act", lambda e: e.copy(out=og[:, :, T:TT], in_=PXb[:, 0:4 * TS].rearrange("p (h b) -> p h b", h=4)),
                 reads=[tPX], writes=t_og)
            yield

        gp, gs = gen_C_prompt(), gen_C_sample()
        for _ in range(10):
            next(gp)
        alive = [gp, gs]
        while alive:
            for g in list(alive):
                try:
                    next(g)
                except StopIteration:
                    alive.remove(g)
        if STOP_AFTER == "C":
            dump([og[:, 0, 0:512], og[:, 3, T - 512:T], og[:, 0, T:TT], og[:, 1, T:TT], og[:, 2, T:TT], og[:, 3, T:TT], qT[:, 0, 0:256]], off_b=0)
        S.barrier(no_wait=("pe",))
        if STOP_AFTER == "C":
            S.finish(block)
            return nc

        wab_t = carve(0, 2 * NCH * 128, BF16)
        wab = wab_t.rearrange("p (a n d) -> p a n d", a=2, n=NCH)
        lxp = carve(3072, 3 + T, F32)
        lxs_t = carve(11280, 4 * TS, F32)
        lxs = lxs_t.rearrange("p (k b) -> p k b", k=4)
        xc = carve(11536, TT, F32)
        xcb = carve(19792, TT, BF16)
        thr = carve(23920, TT, F32)
        a2b = carve(32176, TT, F32)
        thi = carve(40432, TT, F32)
        scg_sb = carve(48752, TT, F32)
        cy = scg_sb
        cinp = carve(57008, 2 + T, F32)
        cins_t = carve(65208, 3 * TS, F32)
        cins = cins_t.rearrange("p (k b) -> p k b", k=3)
        so_tok = carve(65400, W, F32)
        t_wab = Tok("wab")
        S.dma("pool", "cs3", wab[:, 0], lru_wa.rearrange("n c d -> c n d"), writes=[t_wab])
        S.dma("pool", "cs3", wab[:, 1], lru_wx.rearrange("n c d -> c n d"), writes=[t_wab], nowait=True)
        t_lxp, t_lxs, t_xc, t_xcb, t_thr, t_a2, t_thi = (Tok("lxp"), Tok("lxs"), Tok("xc"), Tok("xcb"), Tok("thr"),
                                                        Tok("a2"), Tok("thi"))
        t_gA = [Tok(f"gA{n}") for n in range(NCH)]
        t_SO = Tok("SO")
        t_scg, t_cinp, t_cins = Tok("scg"), Tok("cinp"), Tok("cins")
        t_cy = t_scg
        t_gB = [Tok(f"gB{n}") for n in range(NCH)]
        S.op("pool", lambda e: e.memset(lxp[:, 0:3], 0.0), writes=[t_lxp])
        S.op("pool", lambda e: e.memset(cinp[:, 0:2], 0.0), writes=[t_cinp])

        def gen_A(n):
            vws, wtk = w_get(WT[f"A{n}"])
            wlg, wlx = vws
            lhs_lg = [wlg[:, k, :] for k in range(KC)]
            lhs_lx = [wlx[:, k, :] for k in range(KC)]
            for (lo, hi) in HALVES:
                pp, tk = unit_half(lhs_lg, uT, lo, [wtk])
                S.op("act", lambda e, pp=pp, n=n, lo=lo, hi=hi: e.activation(out=gA[:, n, lo:hi], in_=pp[:], func=AF.Silu),
                     reads=[tk], writes=[t_gA[n]])
            sp_, tk = unit_samp(lhs_lg, uT, [wtk])
            S.op("act", lambda e, sp_=sp_, n=n: e.activation(out=gA[:, n, T:TT], in_=sp_, func=AF.Silu),
                 reads=[tk], writes=[t_gA[n]])
            yield
            for (lo, hi) in HALVES:
                pp, tk = unit_half(lhs_lx, uT, lo, [wtk])
                S.op("dve", lambda e, pp=pp, lo=lo, hi=hi: e.tensor_copy(lxp[:, 3 + lo:3 + hi], pp[:]),
                     reads=[tk], writes=[t_lxp])
            sp_, tk = unit_samp(lhs_lx, uT, [wtk])
            S.op("dve", lambda e, n=n: e.tensor_copy(lxs[:, 0:3, :], SIT[:, n * 6:n * 6 + 3, :]), reads=[t_SIT], writes=[t_lxs])
            S.op("dve", lambda e, sp_=sp_: e.tensor_copy(lxs[:, 3, :], sp_), reads=[tk], writes=[t_lxs])
            yield
            cw = lambda k, n=n: pT[:, 16 + k * 6 + n: 17 + k * 6 + n]
            cbias = pT[:, 40 + n:41 + n]
            S.op("act", lambda e, cw=cw, cbias=cbias: e.activation(out=xc[:, 0:T], in_=lxp[:, 0:T], func=AF.Identity,
                                                                   scale=cw(0), bias=cbias),
                 reads=[t_lxp, t_pT], writes=[t_xc])
            S.op("act", lambda e, cw=cw, cbias=cbias: e.activation(out=xc[:, T:TT], in_=lxs[:, 0, :], func=AF.Identity,
                                                                   scale=cw(0), bias=cbias),
                 reads=[t_lxs, t_pT], writes=[t_xc])
            for k in range(1, 4):
                S.op("dve", lambda e, k=k, cw=cw: e.scalar_tensor_tensor(out=xc[:, 0:T], in0=lxp[:, k:k + T], scalar=cw(k),
                                                                         in1=xc[:, 0:T], op0=ALU.mult, op1=ALU.add),
                     reads=[t_lxp, t_xc], writes=[t_xc])
                S.op("dve", lambda e, k=k, cw=cw: e.scalar_tensor_tensor(out=xc[:, T:TT], in0=lxs[:, k, :], scalar=cw(k),
                                                                         in1=xc[:, T:TT], op0=ALU.mult, op1=ALU.add),
                     reads=[t_lxs, t_xc], writes=[t_xc])
            S.op("pool", lambda e, n=n: e.tensor_copy(SO[:, n, 0:3], lxp[:, T:T + 3]), reads=[t_lxp], writes=[t_SO])
            S.op("pool", lambda e, n=n: e.tensor_copy(SO[:, n, 6:22], lxs[:, 3, :]), reads=[t_lxs], writes=[t_SO])
            yield
            S.op("act", lambda e: e.copy(out=xcb, in_=xc), reads=[t_xc], writes=[t_xcb])
            yield
            for gi, (dst, tdst, bcol) in enumerate([(thr, t_thr, 12 + n), (thi, t_thi, 18 + n)]):
                lhs = [wab[:, gi, n, :]]
                xcb3 = xcb.unsqueeze(1)
                for (lo, hi) in HALVES:
                    pp, tk = unit_half(lhs, xcb3, lo, [t_wab, t_xcb])
                    S.op("act", lambda e, pp=pp, dst=dst, lo=lo, hi=hi, bcol=bcol: e.activation(
                        out=dst[:, lo:hi], in_=pp[:], func=AF.Tanh, scale=0.5, bias=dv[:, bcol:bcol + 1]),
                        reads=[tk, t_dv], writes=[tdst])
                sp_, tk = unit_samp(lhs, xcb3, [t_wab, t_xcb])
                S.op("act", lambda e, sp_=sp_, dst=dst, bcol=bcol: e.activation(
                    out=dst[:, T:TT], in_=sp_, func=AF.Tanh, scale=0.5, bias=dv[:, bcol:bcol + 1]),
                    reads=[tk, t_dv], writes=[tdst])
                yield
            S.op("act", lambda e, n=n: e.activation(out=a2b, in_=thr, func=AF.Exp, scale=dv[:, 6 + n:7 + n], bias=dv[:, 6 + n:7 + n]),
                 reads=[t_thr, t_dv], writes=[t_a2])
            S.op("act", lambda e, n=n: e.activation(out=thr, in_=thr, func=AF.Exp, scale=dv[:, n:n + 1], bias=dv[:, n:n + 1]),
                 reads=[t_thr, t_dv], writes=[t_thr])
            S.op("dve", lambda e: e.tensor_scalar(out=a2b, in0=a2b, scalar1=1.0, scalar2=-1.0, op0=ALU.min, op1=ALU.mult),
                 reads=[t_a2], writes=[t_a2])
            yield
            S.op("act", lambda e: e.activation(out=a2b, in_=a2b, func=AF.Sqrt, bias=1.0, scale=1.0), reads=[t_a2], writes=[t_a2])
            S.op("dve", lambda e: e.scalar_tensor_tensor(out=a2b, in0=a2b, scalar=0.5, in1=xc, op0=ALU.mult, op1=ALU.mult),
                 reads=[t_a2, t_xc], writes=[t_a2])
            S.op("dve", lambda e: e.scalar_tensor_tensor(out=thi, in0=thi, scalar=1.0, in1=a2b, op0=ALU.add, op1=ALU.mult),
                 reads=[t_thi, t_a2], writes=[t_thi])
            yield
            S.op("dve", lambda e: e.tensor_tensor_scan(out=a2b[:, 0:T], data0=thr[:, 0:T], data1=thi[:, 0:T], initial=0.0,
                                                       op0=ALU.mult, op1=ALU.add),
                 reads=[t_thr, t_thi], writes=[t_a2])
            S.op("dve", lambda e, n=n: e.tensor_tensor(out=a2b[:, T:TT], in0=thr[:, T:TT], in1=SIT[:, n * 6 + 3, :], op=ALU.mult),
                 reads=[t_thr, t_SIT], writes=[t_a2])
            S.op("dve", lambda e: e.tensor_tensor(out=a2b[:, T:TT], in0=a2b[:, T:TT], in1=thi[:, T:TT], op=ALU.add),
                 reads=[t_a2, t_thi], writes=[t_a2])
            yield
            S.op("pool", lambda e, n=n: e.tensor_copy(SO[:, n, 3:4], a2b[:, T - 1:T]), reads=[t_a2], writes=[t_SO])
            S.op("pool", lambda e, n=n: e.tensor_copy(SO[:, n, 22:38], a2b[:, T:TT]), reads=[t_a2], writes=[t_SO])
            S.op("dve", lambda e, n=n: e.tensor_tensor(out=gA[:, n, :], in0=a2b, in1=gA[:, n, :], op=ALU.mult),
                 reads=[t_a2, t_gA[n]], writes=[t_gA[n]])
            yield

        def gen_B(n):
            vws, wtk = w_get(WT[f"B{n}a"])
            wsg, wsb = vws
            lhs_sg = [wsg[:, k, :] for k in range(KC)]
            lhs_sb = [wsb[:, k, :] for k in range(KC)]
            for (lo, hi) in HALVES:
                pp, tk = unit_half(lhs_sg, uT, lo, [wtk])
                S.op("act", lambda e, pp=pp, n=n, lo=lo, hi=hi: e.activation(out=gB[:, n, lo:hi], in_=pp[:], func=AF.Silu),
                     reads=[tk], writes=[t_gB[n]])
            sp_, tk = unit_samp(lhs_sg, uT, [wtk])
            S.op("act", lambda e, sp_=sp_, n=n: e.activation(out=gB[:, n, T:TT], in_=sp_, func=AF.Silu),
                 reads=[tk], writes=[t_gB[n]])
            yield
            for (lo, hi) in HALVES:
                pp, tk = unit_half(lhs_sb, uT, lo, [wtk])
                S.op("dve", lambda e, pp=pp, n=n, lo=lo, hi=hi: e.tensor_tensor(out=gB[:, n, lo:hi], in0=pp[:], in1=gB[:, n, lo:hi],
                                                                                op=ALU.mult),
                     reads=[tk, t_gB[n]], writes=[t_gB[n]])
            sp_, tk = unit_samp(lhs_sb, uT, [wtk])
            S.op("dve", lambda e, sp_=sp_, n=n: e.tensor_tensor(out=gB[:, n, T:TT], in0=sp_, in1=gB[:, n, T:TT], op=ALU.mult),
                 reads=[tk, t_gB[n]], writes=[t_gB[n]])
            yield
            vws, wtk = w_get(WT[f"B{n}b"])
            wscg, wsh = vws
            lhs_scg = [wscg[:, k, :] for k in range(KC)]
            lhs_sh = [wsh[:, k, :] for k in range(KC)]
            for (lo, hi) in HALVES:
                pp, tk = unit_half(lhs_scg, uT, lo, [wtk])
                S.op("act", lambda e, pp=pp, lo=lo, hi=hi: e.copy(out=scg_sb[:, lo:hi], in_=pp[:]), reads=[tk], writes=[t_scg])
            sp_, tk = unit_samp(lhs_scg, uT, [wtk])
            S.op("act", lambda e, sp_=sp_: e.copy(out=scg_sb[:, T:TT], in_=sp_), reads=[tk], writes=[t_scg])
            yield
            for (lo, hi) in HALVES:
                pp, tk = unit_half(lhs_sh, uT, lo, [wtk])
                S.op("dve", lambda e, pp=pp, lo=lo, hi=hi: e.tensor_tensor(out=cinp[:, 2 + lo:2 + hi], in0=pp[:],
                                                                           in1=scg_sb[:, lo:hi], op=ALU.mult),
                     reads=[tk, t_scg], writes=[t_cinp])
            sp_, tk = unit_samp(lhs_sh, uT, [wtk])
            S.op("dve", lambda e, n=n: e.tensor_copy(cins[:, 0:2, :], SIT[:, n * 6 + 4:n * 6 + 6, :]), reads=[t_SIT], writes=[t_cins])
            S.op("dve", lambda e, sp_=sp_: e.tensor_tensor(out=cins[:, 2, :], in0=sp_, in1=scg_sb[:, T:TT], op=ALU.mult),
                 reads=[tk, t_scg], writes=[t_cins])
            yield
            sw = lambda k, n=n: pT[:, 64 + k * 6 + n: 65 + k * 6 + n]
            S.op("act", lambda e, sw=sw: e.activation(out=cy[:, 0:T], in_=cinp[:, 0:T], func=AF.Identity, scale=sw(0)),
                 reads=[t_cinp, t_pT], writes=[t_cy])
            S.op("act", lambda e, sw=sw: e.activation(out=cy[:, T:TT], in_=cins[:, 0, :], func=AF.Identity, scale=sw(0)),
                 reads=[t_cins, t_pT], writes=[t_cy])
            for k in range(1, 3):
                S.op("dve", lambda e, k=k, sw=sw: e.scalar_tensor_tensor(out=cy[:, 0:T], in0=cinp[:, k:k + T], scalar=sw(k),
                                                                         in1=cy[:, 0:T], op0=ALU.mult, op1=ALU.add),
                     reads=[t_cinp, t_cy], writes=[t_cy])
                S.op("dve", lambda e, k=k, sw=sw: e.scalar_tensor_tensor(out=cy[:, T:TT], in0=cins[:, k, :], scalar=sw(k),
                                                                         in1=cy[:, T:TT], op0=ALU.mult, op1=ALU.add),
                     reads=[t_cins, t_cy], writes=[t_cy])
            S.op("pool", lambda e, n=n: e.tensor_copy(SO[:, n, 4:6], cinp[:, T:T + 2]), reads=[t_cinp], writes=[t_SO])
            S.op("pool", lambda e, n=n: e.tensor_copy(SO[:, n, 38:54], cins[:, 2, :]), reads=[t_cins], writes=[t_SO])
            S.op("dve", lambda e, n=n: e.tensor_tensor(out=gB[:, n, :], in0=cy, in1=gB[:, n, :], op=ALU.mult),
                 reads=[t_cy, t_gB[n]], writes=[t_gB[n]])
            yield

        def interleave(*gens):
            gens = list(gens)
            while gens:
                for g in list(gens):
                    try:
                        next(g)
                    except StopIteration:
                        gens.remove(g)

        gens = {}

        def adv(kind, n):
            g = gens.get((kind, n))
            if g is None:
                return
            try:
                next(g)
            except StopIteration:
                pass

        for n in range(NCH + 1):
            if n < NCH:
                gens[("A", n)] = gen_A(n)
            if n >= 1:
                gens[("B", n - 1)] = gen_B(n - 1)
            adv("A", n)
            adv("B", n - 1)
            adv("A", n - 1)
            adv("A", n)
            adv("B", n - 1)
            adv("A", n)
            adv("A", n - 1)
            adv("B", n - 1)
            adv("B", n - 1)
            adv("A", n - 1)
            adv("A", n)
            adv("A", n)
            adv("A", n)
            adv("B", n - 1)
            adv("A", n)
        for g in gens.values():
            for _ in g:
                pass
        if STOP_AFTER == "B":
            dump([gB[:, 0, 0:512], gB[:, 5, T - 512:T], gB[:, 0, T:TT], gB[:, 5, T:TT], gB[:, 2, 1024:1536]], off_b=0)
        t_sot = Tok("sot")
        for g0 in range(0, NCH, 3):
            for n in range(g0, g0 + 3):
                S.op("pe", lambda e, n=n, g0=g0: e.transpose(PX[0:54, (n - g0) * 128:(n - g0 + 1) * 128], SO[:, n, :], ident[:, :]),
                     reads=[t_SO, t_ident], writes=[tPX], inc=(n == g0 + 2))
            S.op("act", lambda e, g0=g0: e.copy(out=so_tok[0:54, g0 * 128:(g0 + 3) * 128], in_=PX[0:54, 0:384]),
                 reads=[tPX], writes=[t_sot])
        S.dma("sp", "out", o_plc[:, :], so_tok[0:3, :], reads=[t_sot])
        S.dma("sp", "out", o_ph[:, :], so_tok[3:4, :], reads=[t_sot])
        S.dma("sp", "out", o_psc[:, :], so_tok[4:6, :], reads=[t_sot])
        S.dma("sp", "out", o_slc3[:, 2, :], so_tok[6:22, :], reads=[t_sot])
        S.dma("sp", "out", o_sh[:, :], so_tok[22:38, :], reads=[t_sot])
        S.dma("sp", "out", o_ssc3[:, 1, :], so_tok[38:54, :], reads=[t_sot])
        if STOP_AFTER == "B":
            dump([gB[:, 0, 0:512], gB[:, 5, T - 512:T], gB[:, 0, T:TT], gB[:, 5, T:TT], gB[:, 2, 1024:1536]], off_b=56320)
        S.barrier(no_wait=("pe",))
        if STOP_AFTER == "B":
            S.finish(block)
            return nc

        NTH = 3
        thb = [carve(i * 4096, 1024, F32) for i in range(NTH)]
        tacc = [carve(12288 + i * 4096, 1024, F32) for i in range(2)]
        mT_t = carve(20480, KC * TT, BF16)
        mT = mT_t.rearrange("p (k t) -> p k t", k=KC)
        wo_t = carve(53504, KC * D, BF16)
        wo = wo_t.rearrange("p (k c) -> p k c", k=KC)
        t_wo = Tok("wo")
        if STOP_AFTER == "M0":
            S.barrier()
            S.finish(block)
            return nc
        t_th = [Tok(f"th{i}") for i in range(NTH)]
        t_acc = [Tok("acc0"), Tok("acc1")]
        t_mT = Tok("mT")
        thc = {"i": 0, "a": 0}
        gsrc = [(gA, NCH), (gB, NCH), (og, 4)]
        gtok = [t_gA, t_gB, t_og]
        ths_f = carve(69888, 3 * TS, F32)
        tzs_f = carve(70080, 3 * TS, F32)
        accs_f = carve(70272, TS, F32)
        t_accs2 = Tok("accs2")
        t_ths = Tok("ths")
        t_accs = Tok("accs")
        for j in range(8):
            vg, wtkg = w_get(WT[f"Mg{j}"])
            vo, wtko = w_get(WT[f"Mo{j}"])
            S.dma("pool", "cs4", wo[:, j, :], w_out[j * 128:(j + 1) * 128, :], writes=[t_wo], nowait=True)
            tMs_g, tMs_o = tPSb, tPX
            for x in range(3):
                lhs_g = [vg[x][:, k, :] for k in range(KC)]
                mm_group(PS[:, x * TS:(x + 1) * TS], [(lhs_g[k], uT[:, k, T:TT]) for k in range(KC)], [wtkg], tMs_g)
            S.op("act", lambda e: e.activation(out=ths_f, in_=PS[:, 0:3 * TS], func=AF.Tanh, scale=0.5),
                 reads=[tMs_g], writes=[t_ths])
            for x in range(3):
                garr, nk = gsrc[x]
                lhs_o = [vo[x][:, k, :] for k in range(nk)]
                mm_group(PX[:, x * TS:(x + 1) * TS], [(lhs_o[k], garr[:, k, T:TT]) for k in range(nk)], [wtko] + gtok[x], tMs_o)
            S.op("dve", lambda e: e.scalar_tensor_tensor(out=tzs_f, in0=ths_f, scalar=1.0, in1=PX[:, 0:3 * TS],
                                                         op0=ALU.add, op1=ALU.mult),
                 reads=[t_ths, tMs_o], writes=[t_accs])
            S.op("dve", lambda e: e.tensor_reduce(out=accs_f, in_=tzs_f.rearrange("p (x b) -> p b x", x=3),
                                                  axis=AX.X, op=ALU.add),
                 reads=[t_accs], writes=[t_accs2])
            S.op("dve", lambda e, j=j: e.tensor_copy(mT[:, j, T:TT], accs_f), reads=[t_accs2], writes=[t_mT])
            for (lo, hi) in HALVES:
                acc, tacc_ = tacc[thc["a"] % 2], t_acc[thc["a"] % 2]
                thc["a"] += 1
                for x in range(3):
                    garr, nk = gsrc[x]
                    lhs_g = [vg[x][:, k, :] for k in range(KC)]
                    lhs_o = [vo[x][:, k, :] for k in range(nk)]
                    th_, tth = thb[thc["i"] % NTH], t_th[thc["i"] % NTH]
                    thc["i"] += 1
                    pp, tk = unit_half(lhs_g, uT, lo, [wtkg])
                    S.op("act", lambda e, pp=pp, th_=th_: e.activation(out=th_, in_=pp[:], func=AF.Tanh, scale=0.5),
                         reads=[tk], writes=[tth])
                    zp_, tkz = unit_half(lhs_o, garr, lo, [wtko] + gtok[x])
                    zp = zp_[:]
                    if x == 0:
                        S.op("dve", lambda e, acc=acc, th_=th_, zp=zp: e.scalar_tensor_tensor(out=acc, in0=th_, scalar=1.0, in1=zp,
                                                                                           op0=ALU.add, op1=ALU.mult),
                             reads=[tth, tkz], writes=[tacc_])
                    else:
                        S.op("dve", lambda e, th_=th_, zp=zp: e.scalar_tensor_tensor(out=th_, in0=th_, scalar=1.0, in1=zp,
                                                                                    op0=ALU.add, op1=ALU.mult),
                             reads=[tth, tkz], writes=[tth])
                        if x == 1:
                            S.op("pool", lambda e, acc=acc, th_=th_: e.tensor_tensor(out=acc, in0=acc, in1=th_, op=ALU.add),
                                 reads=[tth, tacc_], writes=[tacc_])
                        else:
                            S.op("pool", lambda e, acc=acc, th_=th_, j=j, lo=lo, hi=hi: e.tensor_tensor(
                                out=mT[:, j, lo:hi], in0=acc, in1=th_, op=ALU.add),
                                reads=[tth, tacc_], writes=[t_mT])
            if STOP_AFTER == "M5":
                S.barrier()
                S.finish(block)
                return nc
        S.barrier(no_wait=("pe",))
        if STOP_AFTER == "M":
            S.finish(block)
            return nc

        xr = [carve(i * 4096, 1024, F32) for i in range(2)]
        yr = [carve(8192 + i * 4096, 1024, F32) for i in range(2)] + [carve(69888, 1024, F32)]
        fgb = carve(16384, 1024, F32)
        t_fgb = Tok("fgb")
        S.dma("sp", "cs5", fgb, final_norm_g.partition_broadcast(128), writes=[t_fgb])
        t_xr = [Tok("xr0"), Tok("xr1")]
        t_yr = [Tok("yr0"), Tok("yr1"), Tok("yr2")]
        ftiles = [(xp[i * 128:(i + 1) * 128, :], y_p[i * 128:(i + 1) * 128, :], 128, i * 128) for i in range(16)]
        ftiles.append((xs[:, :], y_s[:, :], TS, T))
        xsem2 = ["fxa", "fxb"]
        ysem = ["ya", "yb", "xc"]
        t_stF = [Tok(f"stF{i}") for i in range(len(ftiles))]

        def f_stageA(ti):
            src, dst, nr, c0 = ftiles[ti]
            xr_, txr = xr[ti % 2], t_xr[ti % 2]
            yr_, tyr = yr[ti % 3], t_yr[ti % 3]
            S.dma("pool", xsem2[ti % 2], xr_[0:nr, :], src, writes=[txr])
            pp, tk = next_pp()
            for b in range(2):
                pairs = [(mT[:, k, c0:c0 + nr], wo[:, k, b * 512:(b + 1) * 512]) for k in range(KC)]
                mm_group(pp[0:nr, b * 512:(b + 1) * 512], pairs, [t_mT, t_wo], tk, last_inc=(b == 1))
            S.op("dve", lambda e, pp=pp, yr_=yr_, xr_=xr_, nr=nr: e.scalar_tensor_tensor(
                out=yr_[0:nr, :], in0=pp[0:nr, :], scalar=0.5, in1=xr_[0:nr, :], op0=ALU.mult, op1=ALU.add),
                reads=[tk, txr], writes=[tyr])

        def f_stageA2(ti):
            src, dst, nr, c0 = ftiles[ti]
            xr_, txr = xr[ti % 2], t_xr[ti % 2]
            yr_, tyr = yr[ti % 3], t_yr[ti % 3]
            col = 32 + ti
            S.op("act", lambda e, xr_=xr_, yr_=yr_, nr=nr, col=col: e.activation(out=xr_[0:nr, :], in_=yr_[0:nr, :], func=AF.Square,
                                                                                accum_out=stat2[0:nr, col:col + 1]),
                 reads=[tyr, t_sm], writes=[txr, t_stF[ti]])

        def f_stageB1(ti):
            src, dst, nr, c0 = ftiles[ti]
            col = 32 + ti
            S.op("act", lambda e, nr=nr, col=col: e.activation(out=stat2[0:nr, col:col + 1], in_=stat2[0:nr, col:col + 1], func=AF.Sqrt,
                                                               scale=1.0 / D, bias=epst[0:nr, 0:1]),
                 reads=[t_stF[ti], t_eps], writes=[t_stF[ti]])

        def f_stageB(ti):
            src, dst, nr, c0 = ftiles[ti]
            yr_, tyr = yr[ti % 3], t_yr[ti % 3]
            col = 32 + ti
            S.op("dve", lambda e, nr=nr, col=col: e.reciprocal(out=stat2[0:nr, col:col + 1], in_=stat2[0:nr, col:col + 1]),
                 reads=[t_stF[ti]], writes=[t_stF[ti]])
            S.op("dve", lambda e, yr_=yr_, nr=nr, col=col: e.scalar_tensor_tensor(
                out=yr_[0:nr, :], in0=yr_[0:nr, :], scalar=stat2[0:nr, col:col + 1], in1=fgb[0:nr, :], op0=ALU.mult, op1=ALU.mult),
                reads=[tyr, t_stF[ti], t_fgb], writes=[tyr])
            S.dma("sp", ysem[ti % 3], dst, yr_[0:nr, :], reads=[tyr])

        f_stageA(0)
        f_stageA2(0)
        for ti in range(len(ftiles)):
            if ti + 1 < len(ftiles):
                f_stageA(ti + 1)
            f_stageB1(ti)
            if ti + 1 < len(ftiles):
                f_stageA2(ti + 1)
            f_stageB(ti)
        S.barrier()
        S.finish(block)
    return nc


_CACHE = {}


def _get_program():
    if "nc" not in _CACHE:
        _CACHE["nc"] = build_program()
    return _CACHE["nc"]


def kernel(x_prompt, x_sample, cache_mem_k, cache_mem_v, state_lru_h, state_lru_conv, state_sconv, mem_prompt,
           norm_g, mem_norm_g, w_in, lru_conv_w, lru_conv_b, lru_wa, lru_ba, lru_wx, lru_bx, lru_lambda, lru_wo,
           sconv_w, sconv_wo, xa_wk, xa_wv, xa_wo, w_out, final_norm_g):
    f = lambda a: np.ascontiguousarray(np.asarray(a, dtype=np.float32))
    shared = {
        "norm_g": f(norm_g[0]), "mem_norm_g": f(mem_norm_g[0]), "w_in": f(w_in[0]),
        "lru_conv_w": f(lru_conv_w[0]), "lru_conv_b": f(lru_conv_b[0]), "lru_wa": f(lru_wa[0]),
        "lru_ba": f(lru_ba[0]), "lru_wx": f(lru_wx[0]), "lru_bx": f(lru_bx[0]), "lru_lambda": f(lru_lambda[0]),
        "lru_wo": f(lru_wo[0]), "sconv_w": f(sconv_w[0]), "sconv_wo": f(sconv_wo[0]), "xa_wk": f(xa_wk[0]),
        "xa_wv": f(xa_wv[0]), "xa_wo": f(xa_wo[0]), "w_out": f(w_out[0]), "final_norm_g": f(final_norm_g),
    }
    in_maps = []
    for c in range(NCORES):
        sl = slice(c * TS, (c + 1) * TS)
        m = dict(shared)
        m["xp"] = f(x_prompt[c])
        m["xs"] = f(np.asarray(x_sample)[sl, 0, :])
        m["memp"] = f(mem_prompt[c])
        m["ck"] = f(np.asarray(cache_mem_k)[0, sl].reshape(TS, NM, XW))
        m["cv"] = f(np.asarray(cache_mem_v)[0, sl].reshape(TS, NM, XW))
        m["st_h"] = f(np.asarray(state_lru_h)[0, sl])
        m["st_lc"] = f(np.asarray(state_lru_conv)[0, sl].reshape(TS, 3 * W))
        m["st_sc"] = f(np.asarray(state_sconv)[0, sl].reshape(TS, 2 * W))
        in_maps.append(m)
    nc = _get_program()
    res = run_bass_kernel_spmd(nc, in_maps, core_ids=list(range(NCORES)))
    rs = res.results
    cat = lambda k: np.concatenate([np.asarray(r[k]) for r in rs], axis=0)
    y_prompt = np.stack([np.asarray(r["y_p"]) for r in rs], axis=0).astype(np.float32)
    y_sample = cat("y_s").reshape(NCORES * TS, 1, D).astype(np.float32)
    p_mk = np.stack([np.asarray(r["o_pk"]) for r in rs], axis=0).reshape(1, NCORES, NM, 4, 128).astype(np.float32)
    p_mv = np.stack([np.asarray(r["o_pv"]) for r in rs], axis=0).reshape(1, NCORES, NM, 4, 128).astype(np.float32)
    p_h = cat("o_ph").reshape(1, NCORES, W).astype(np.float32)
    p_lc = np.stack([np.asarray(r["o_plc"]) for r in rs], axis=0).reshape(1, NCORES, 3, W).astype(np.float32)
    p_sc = np.stack([np.asarray(r["o_psc"]) for r in rs], axis=0).reshape(1, NCORES, 2, W).astype(np.float32)
    s_h = cat("o_sh").reshape(1, NCORES * TS, W).astype(np.float32)
    s_lc = cat("o_slc").reshape(1, NCORES * TS, 3, W).astype(np.float32)
    s_sc = cat("o_ssc").reshape(1, NCORES * TS, 2, W).astype(np.float32)
    return (y_prompt, y_sample, p_mk, p_mv, p_h, p_lc, p_sc, s_h, s_lc, s_sc)
```

```python
import math
from contextlib import ExitStack

import numpy as np
import concourse.bass as bass
import concourse.mybir as mybir
from concourse.bass_utils import run_bass_kernel_spmd

F32 = mybir.dt.float32
BF16 = mybir.dt.bfloat16
AF = mybir.ActivationFunctionType
ALU = mybir.AluOpType
AX = mybir.AxisListType

NCORES = 8
T = 2048
TS = 16
TT = T + TS
D = 1024
KC = 8
W = 768
NCH = 6
NM = 256
XW = 512
IN_COLS = 8704
EPS = 1e-6
SCALE = 1.0 / math.sqrt(128.0)
STOP_AFTER = None

C_LX, C_LG, C_SB, C_SCG, C_SH, C_SG, C_Q, C_QG, C_MG = 0, 768, 1536, 2304, 3072, 3840, 4608, 5120, 5632


class Tok:
    __slots__ = ("name", "w", "r")

    def __init__(self, name=""):
        self.name = name
        self.w = None
        self.r = {}


class Stream:
    def __init__(self, key, sem):
        self.key = key
        self.sem = sem
        self.cnt = 0
        self.seen = {}
        self.ops = []
        self.pending = False


class Sched:
    def __init__(self, nc):
        self.nc = nc
        self.sems = {}
        self.streams = {}
        self.dma_cnt = {}

    def add_stream(self, key, sem):
        self.sems[key] = sem
        self.streams[key] = Stream(key, sem)

    def add_dma_sem(self, key, sem):
        self.sems[key] = sem
        self.dma_cnt[key] = 0

    def _needs(self, st, reads, writes):
        needs = {}

        def need(ev):
            if ev is None:
                return
            k, v = ev
            if needs.get(k, 0) < v:
                needs[k] = v
        for t in reads:
            need(t.w)
        for t in writes:
            need(t.w)
            for k, v in t.r.items():
                if k == st.key:
                    continue
                need((k, v))
        out = []
        for k, v in needs.items():
            if k == st.key and k == "pe":
                continue
            if st.seen.get(k, 0) < v:
                st.seen[k] = v
                out.append((k, v))
        return out

    def op(self, key, fn, reads=(), writes=(), inc=True):
        st = self.streams[key]
        waits = self._needs(st, reads, writes)
        if inc:
            st.cnt += 1
            st.pending = False
            ev = (key, st.cnt)
        else:
            st.pending = True
            ev = (key, st.cnt + 1)
        sems = self.sems
        sem = st.sem

        def run(eng, waits=waits, fn=fn, inc=inc):
            for k, v in waits:
                eng.wait_ge(sems[k], v)
            ins = fn(eng)
            if inc:
                ins.then_inc(sem, 1)
        st.ops.append(run)
        for t in writes:
            t.w = ev
            t.r = {}
        for t in reads:
            if t.r.get(key, 0) < ev[1]:
                t.r[key] = ev[1]

    def dma(self, key, semkey, out, in_, reads=(), writes=(), nowait=False, **kw):
        st = self.streams[key]
        waits = [] if nowait else self._needs(st, reads, writes)
        self.dma_cnt[semkey] += 16
        ev = (semkey, self.dma_cnt[semkey])
        sems = self.sems

        def run(eng, waits=waits):
            for k, v in waits:
                eng.wait_ge(sems[k], v)
            eng.dma_start(out=out, in_=in_, **kw).then_inc(sems[semkey], 16)
        st.ops.append(run)
        for t in writes:
            t.w = ev
            t.r = {}
        for t in reads:
            if t.r.get(semkey, 0) < ev[1]:
                t.r[semkey] = ev[1]

    def wait_dma(self, keys, semkey):
        v = self.dma_cnt[semkey]
        sems = self.sems
        for k in keys:
            st = self.streams[k]
            if v and st.seen.get(semkey, 0) < v:
                st.seen[semkey] = v
                st.ops.append(lambda eng, v=v: eng.wait_ge(sems[semkey], v))

    def barrier(self, skip=(), no_wait=()):
        targets = {}
        for k, st in self.streams.items():
            assert not st.pending
            if st.cnt:
                targets[k] = st.cnt
        for k, v in self.dma_cnt.items():
            if v and k not in skip:
                targets[k] = v
        sems = self.sems
        for k, st in self.streams.items():
            if k in no_wait:
                continue
            waits = []
            for tk, tv in targets.items():
                if st.seen.get(tk, 0) < tv:
                    st.seen[tk] = tv
                    waits.append((tk, tv))

            def run(eng, waits=waits):
                for kk, v in waits:
                    eng.wait_ge(sems[kk], v)
            st.ops.append(run)

    def finish(self, block):
        for key, st in self.streams.items():
            assert not st.pending, key
        ss = self.streams

        def mk(key):
            def body(eng):
                for o in ss[key].ops:
                    o(eng)
            return body
        block.gpsimd(mk("pool"))
        block.tensor(mk("pe"))
        block.scalar(mk("act"))
        block.vector(mk("dve"))
        block.sync(mk("sp"))


def build_program():
    nc = bass.Bass("TRN2", target_bir_lowering=False)

    def din(name, shape):
        return nc.dram_tensor(name, shape, F32, kind="ExternalInput").ap()

    def dout(name, shape):
        return nc.dram_tensor(name, shape, F32, kind="ExternalOutput").ap()

    xp = din("xp", [T, D])
    xs = din("xs", [TS, D])
    memp = din("memp", [NM, D])
    ck = din("ck", [TS, NM, XW])
    cv = din("cv", [TS, NM, XW])
    st_h = din("st_h", [TS, W])
    st_lc = din("st_lc", [TS, 3 * W])
    st_sc = din("st_sc", [TS, 2 * W])
    norm_g = din("norm_g", [D])
    mem_norm_g = din("mem_norm_g", [D])
    w_in = din("w_in", [D, IN_COLS])
    lru_conv_w = din("lru_conv_w", [4, W])
    lru_conv_b = din("lru_conv_b", [W])
    lru_wa = din("lru_wa", [NCH, 128, 128])
    lru_ba = din("lru_ba", [W])
    lru_wx = din("lru_wx", [NCH, 128, 128])
    lru_bx = din("lru_bx", [W])
    lru_lambda = din("lru_lambda", [W])
    lru_wo = din("lru_wo", [W, D])
    sconv_w = din("sconv_w", [3, W])
    sconv_wo = din("sconv_wo", [W, D])
    xa_wk = din("xa_wk", [D, XW])
    xa_wv = din("xa_wv", [D, XW])
    xa_wo = din("xa_wo", [XW, D])
    w_out = din("w_out", [D, D])
    final_norm_g = din("final_norm_g", [D])

    y_p = dout("y_p", [T, D])
    y_s = dout("y_s", [TS, D])
    o_pk = dout("o_pk", [NM, XW])
    o_pv = dout("o_pv", [NM, XW])
    o_ph = dout("o_ph", [1, W])
    o_plc = dout("o_plc", [3, W])
    o_psc = dout("o_psc", [2, W])
    o_sh = dout("o_sh", [TS, W])
    o_slc = dout("o_slc", [TS, 3 * W])
    o_ssc = dout("o_ssc", [TS, 2 * W])

    dbg = dout("dbg", [128, 4096]) if STOP_AFTER else None
    es = ExitStack()
    with es:
        def sb(name, shape, dt):
            return es.enter_context(nc.sbuf_tensor(name, shape, dt))

        uT_t = sb("uT", [128, KC * TT], BF16)
        uT = uT_t[:].rearrange("p (k t) -> p k t", k=KC)
        gA_t = sb("gA", [128, NCH * TT], BF16)
        gA = gA_t[:].rearrange("p (k t) -> p k t", k=NCH)
        gB_t = sb("gB", [128, NCH * TT], BF16)
        gB = gB_t[:].rearrange("p (k t) -> p k t", k=NCH)
        og_t = sb("og", [128, 4 * TT], BF16)
        og = og_t[:].rearrange("p (k t) -> p k t", k=4)
        NSLOT = 5
        SLOT_E = 3072
        slots = [sb(f"wslot{i}", [128, SLOT_E], BF16) for i in range(NSLOT)]
        ident = sb("ident", [128, 128], F32)
        identb = sb("identb", [128, 128], BF16)
        onesb = sb("onesb", [128, 128], BF16)
        pT = sb("pT", [128, 82], F32)
        dv = sb("dv", [128, 24], F32)
        SIT_t = sb("SIT", [128, 36 * TS], F32)
        SIT = SIT_t[:].rearrange("p (a b) -> p a b", a=36)
        SO_t = sb("SO", [128, NCH * 54], F32)
        SO = SO_t[:].rearrange("p (a b) -> p a b", a=NCH)
        stat = sb("stat", [128, 64], F32)
        stat2 = sb("stat2", [128, 64], F32)
        epst = sb("epst", [128, 1], F32)
        q25 = sb("q25", [128, 1], F32)
        RW = 18816
        R = sb("R", [128, RW], F32)

        def carve(off_b, nelem, dt):
            assert off_b % 4 == 0
            if dt == F32:
                assert off_b // 4 + nelem <= RW, (off_b, nelem)
                return R[:, off_b // 4: off_b // 4 + nelem]
            assert nelem % 2 == 0 and off_b // 4 + nelem // 2 <= RW, (off_b, nelem)
            return R[:, off_b // 4: off_b // 4 + nelem // 2].bitcast(BF16)

        PP = [es.enter_context(nc.psum_tensor(f"PP{i}", [128, 1024], F32)) for i in range(3)]
        PS = es.enter_context(nc.psum_tensor("PS", [128, 512], F32))
        PX = es.enter_context(nc.psum_tensor("PX", [128, 512], F32))
        tPP = [Tok(f"PP{i}") for i in range(3)]
        tPS = [Tok(f"PS{i}") for i in range(8)]
        tPX = Tok("PX")
        PSb = PS[:].bitcast(BF16)
        PXb = PX[:].bitcast(BF16)

        S = Sched(nc)
        for k in ["pe", "act", "dve", "pool", "sp"]:
            S.add_stream(k, es.enter_context(nc.semaphore("s_" + k)))
        dsem_names = (["cst", "cs2", "cs3", "cs4", "cs5", "sva", "svb", "svc", "svd", "sve", "svf", "fxa", "fxb", "out", "xa", "xb", "xc", "ya", "yb", "ka", "kb", "va", "vb"]
                      + [f"ws{i}" for i in range(NSLOT)])
        for k in dsem_names:
            S.add_dma_sem(k, es.enter_context(nc.semaphore("d_" + k)))
        block = es.enter_context(nc.Block())

        state = {"pp": 0, "ps": 0}

        def next_pp():
            i = state["pp"] % 3
            state["pp"] += 1
            return PP[i], tPP[i]

        def next_ps():
            i = state["ps"] % 8
            state["ps"] += 1
            return i, tPS[i]

        wtiles = []
        wstate = {"issued": 0}
        tslot = [Tok(f"slot{i}") for i in range(NSLOT)]

        def w_in_cols(c0, ncols):
            return w_in.rearrange("(k p) c -> p k c", p=128)[:, :, c0:c0 + ncols]

        def add_wtile(pieces):
            off = 0
            lst = []
            for ap in pieces:
                kc, ncols = ap.shape[1], ap.shape[2]
                lst.append((ap, off, kc, ncols))
                off += kc * ncols
            assert off <= SLOT_E, off
            wtiles.append(lst)
            return len(wtiles) - 1

        def w_issue_upto(i):
            while wstate["issued"] <= min(i, len(wtiles) - 1):
                j = wstate["issued"]
                sl = j % NSLOT
                for pi, (ap, off, kc, ncols) in enumerate(wtiles[j]):
                    dst = slots[sl][:, off:off + kc * ncols].rearrange("p (k c) -> p k c", k=kc)
                    S.dma("pool", f"ws{sl}", dst, ap, writes=[tslot[sl]], nowait=(pi > 0))
                wstate["issued"] += 1

        def w_get(i):
            w_issue_upto(i + NSLOT - 2)
            sl = i % NSLOT
            views = []
            for (ap, off, kc, ncols) in wtiles[i]:
                views.append(slots[sl][:, off:off + kc * ncols].rearrange("p (k c) -> p k c", k=kc))
            return views, tslot[sl]

        WT = {}
        WT["wk0"] = add_wtile([xa_wk.rearrange("(k p) c -> p k c", p=128)[:, :, 0:256]])
        WT["wk1"] = add_wtile([xa_wk.rearrange("(k p) c -> p k c", p=128)[:, :, 256:512]])
        WT["wv0"] = add_wtile([xa_wv.rearrange("(k p) c -> p k c", p=128)[:, :, 0:256]])
        WT["wv1"] = add_wtile([xa_wv.rearrange("(k p) c -> p k c", p=128)[:, :, 256:512]])
        for hh in range(2):
            WT[f"q{hh}"] = add_wtile([w_in_cols(C_Q + hh * 256, 256)])
        for hh in range(2):
            WT[f"qg{hh}"] = add_wtile([w_in_cols(C_QG + hh * 256, 256)])
        for n in range(NCH + 1):
            if n < NCH:
                WT[f"A{n}"] = add_wtile([w_in_cols(C_LG + n * 128, 128), w_in_cols(C_LX + n * 128, 128)])
            if n >= 1:
                m_ = n - 1
                WT[f"B{m_}a"] = add_wtile([w_in_cols(C_SG + m_ * 128, 128), w_in_cols(C_SB + m_ * 128, 128)])
                WT[f"B{m_}b"] = add_wtile([w_in_cols(C_SCG + m_ * 128, 128), w_in_cols(C_SH + m_ * 128, 128)])
        lwo = lru_wo.rearrange("(k p) c -> p k c", p=128)
        swo = sconv_wo.rearrange("(k p) c -> p k c", p=128)
        xwo = xa_wo.rearrange("(k p) c -> p k c", p=128)
        for j in range(8):
            WT[f"Mg{j}"] = add_wtile([w_in_cols(C_MG + x * 1024 + j * 128, 128) for x in range(3)])
            WT[f"Mo{j}"] = add_wtile([lwo[:, :, j * 128:(j + 1) * 128], swo[:, :, j * 128:(j + 1) * 128],
                                      xwo[:, :, j * 128:(j + 1) * 128]])

        def mm_group(out_ap, pairs, reads, wtok, last_inc=True):
            n = len(pairs)
            for i, (l, r) in enumerate(pairs):
                S.op("pe", lambda e, l=l, r=r, i=i: e.matmul(out_ap, l, r, start=(i == 0), stop=(i == n - 1)),
                     reads=reads, writes=[wtok], inc=(last_inc and i == n - 1))

        def unit_half(lhs_list, rhs_arr, lo, reads):
            pp, tk = next_pp()
            nk = len(lhs_list)
            for b in range(2):
                pairs = [(lhs_list[k], rhs_arr[:, k, lo + b * 512: lo + (b + 1) * 512]) for k in range(nk)]
                mm_group(pp[:, b * 512:(b + 1) * 512], pairs, reads, tk, last_inc=(b == 1))
            return pp, tk

        tPSb = Tok("PSbank")
        tPS2 = [tPSb, tPX]

        def unit_samp(lhs_list, rhs_arr, reads):
            i = state.setdefault("ps2", 0) % 2
            state["ps2"] += 1
            tk = tPS2[i]
            nk = len(lhs_list)
            bank = PS if i == 0 else PX
            out_ap = bank[:, 0:TS]
            pairs = [(lhs_list[k], rhs_arr[:, k, T:TT]) for k in range(nk)]
            mm_group(out_ap, pairs, reads, tk)
            return out_ap, tk

        HALVES = [(0, 1024), (1024, 2048)]

        def dump(items, off_b=56320):
            dstage = carve(off_b, 4096, F32)
            S.barrier()
            S.op("dve", lambda e: e.memset(dstage[:], 0.0))
            S.barrier()
            off = 0
            for ap in items:
                P_, n_ = ap.shape[0], ap.shape[1]
                S.op("dve", lambda e, ap=ap, off=off, P_=P_, n_=n_: e.tensor_copy(dstage[0:P_, off:off + n_], ap))
                off += n_
            S.barrier()
            S.dma("sp", "out", dbg[:, :], dstage[:])
            S.barrier()

        t_ident = Tok("ident")
        S.op("pool", lambda e: e.memset(ident[:], 0.0), writes=[t_ident])
        S.op("pool", lambda e: e.affine_select(ident[:], ident[:], pattern=[[-1, 128]], compare_op=ALU.not_equal,
                                               fill=1.0, base=0, channel_multiplier=1),
             reads=[t_ident], writes=[t_ident])
        t_identb = Tok("identb")
        S.op("dve", lambda e: e.tensor_copy(identb[:], ident[:]), reads=[t_ident], writes=[t_identb])
        t_stat = Tok("stat")
        t_sm = Tok("sm")
        t_st3 = t_sm
        S.op("pool", lambda e: e.memset(stat[:], 0.0), writes=[t_stat])
        S.op("pool", lambda e: e.memset(stat2[:], 0.0), writes=[t_sm])
        t_ones = Tok("ones")
        S.op("pool", lambda e: e.memset(onesb[:], 1.0), writes=[t_ones])
        t_eps = Tok("eps")
        S.op("pool", lambda e: e.memset(epst[:], EPS), writes=[t_eps])
        S.op("pool", lambda e: e.memset(q25[:], 0.25), writes=[t_eps])

        prt = carve(0, 128, F32)[0:82, :]
        sin = carve(512, 4608, F32)
        NXB = 8
        xbuf = [carve(18944 + i * 4096, 1024, F32) for i in range(3)] + [carve(53760 + i * 4096, 1024, F32) for i in range(5)]
        xnb = [carve(31232 + i * 2048, 1024, BF16) for i in range(2)]
        junk = carve(35328, 1024, BF16)
        mnT_t = carve(37376, KC * NM, BF16)
        mnT = mnT_t.rearrange("p (k t) -> p k t", k=KC)
        kT_t = carve(41472, 4 * NM, BF16)
        kT = kT_t.rearrange("p (k t) -> p k t", k=4)
        vb_t = carve(43520, 2 * XW, BF16)
        vb = vb_t.rearrange("p (k t) -> p k t", k=2)
        kvout = [carve(45568 + i * 4096, 2 * XW, F32).rearrange("p (k t) -> p k t", k=2) for i in range(2)]

        t_prt = Tok("prt")

        def rows(v, r):
            return v.rearrange("(r c) -> r c", c=128)

        plist = [(norm_g, 0, 8, None), (mem_norm_g, 8, 8, None),
                 (lru_conv_w, 16, 24, "k (n c) -> (k n) c"), (lru_conv_b, 40, 6, None), (lru_ba, 46, 6, None),
                 (lru_bx, 52, 6, None), (lru_lambda, 58, 6, None), (sconv_w, 64, 18, "k (n c) -> (k n) c")]
        for (v, r0, nr, pat) in plist:
            src = v.rearrange(pat, c=128) if pat else v.rearrange("(r c) -> r c", c=128)
            S.dma("sp", "cst", prt[r0:r0 + nr, :], src, writes=[t_prt], nowait=True)
        if STOP_AFTER == "P0dma":
            S.barrier()
            S.finish(block)
            return nc
        t_pT = Tok("pT")
        S.op("pe", lambda e: e.transpose(PX[:, 0:82], prt, ident[0:82, 0:82]), reads=[t_prt, t_ident], writes=[tPX])
        S.op("act", lambda e: e.copy(out=pT[:], in_=PX[:, 0:82]), reads=[tPX], writes=[t_pT])
        if STOP_AFTER == "P0b":
            S.barrier()
            S.finish(block)
            return nc
        t_x = [Tok(f"x{i}") for i in range(NXB)]
        t_xn = [Tok(f"xn{i}") for i in range(2)]
        t_junk = Tok("junk")
        t_uT = Tok("uT")
        t_mnT = Tok("mnT")
        tiles = [(memp[i * 128:(i + 1) * 128, :], 128, "m", i * 128) for i in range(2)]
        tiles += [(xp[i * 128:(i + 1) * 128, :], 128, "u", i * 128) for i in range(16)]
        tiles.append((xs[:, :], TS, "u", T))
        xsem = ["xa", "xb", "xc", "ya", "yb", "ka", "kb", "va"]
        tPXh = [Tok("PXh0"), Tok("PXh1")]
        t_st0 = [Tok(f"st0_{i}") for i in range(len(tiles))]

        def p0_stageA(ti):
            src, nr, kind, c0 = tiles[ti]
            xb_, tx = xbuf[ti % NXB], t_x[ti % NXB]
            S.dma("sp", xsem[ti % NXB], xb_[0:nr, :], src, writes=[tx])
            S.op("act", lambda e, xb_=xb_, nr=nr, ti=ti: e.activation(out=junk[0:nr, :], in_=xb_[0:nr, :], func=AF.Square,
                                                                      accum_out=stat[0:nr, ti:ti + 1]),
                 reads=[tx, t_stat], writes=[t_st0[ti], t_junk])

        def p0_stageB(ti):
            src, nr, kind, c0 = tiles[ti]
            xb_, tx = xbuf[ti % NXB], t_x[ti % NXB]
            xn_, txn = xnb[ti % 2], t_xn[ti % 2]
            S.op("act", lambda e, nr=nr, ti=ti: e.activation(out=stat[0:nr, ti:ti + 1], in_=stat[0:nr, ti:ti + 1], func=AF.Sqrt,
                                                             scale=1.0 / D, bias=epst[0:nr, 0:1]),
                 reads=[t_st0[ti], t_eps], writes=[t_st0[ti]])
            S.op("dve", lambda e, nr=nr, ti=ti: e.reciprocal(out=stat[0:nr, ti:ti + 1], in_=stat[0:nr, ti:ti + 1]),
                 reads=[t_st0[ti]], writes=[t_st0[ti]])
            S.op("dve", lambda e, xb_=xb_, xn_=xn_, nr=nr, ti=ti: e.tensor_scalar(out=xn_[0:nr, :], in0=xb_[0:nr, :],
                                                                                scalar1=stat[0:nr, ti:ti + 1], scalar2=None,
                                                                                op0=ALU.mult),
                 reads=[tx, t_st0[ti]], writes=[txn])

        def p0_stageC(ti):
            src, nr, kind, c0 = tiles[ti]
            xn_, txn = xnb[ti % 2], t_xn[ti % 2]
            bankb, tbank = (PXb, tPX) if ti % 2 == 0 else (PSb, tPSb)
            for kc in range(KC):
                S.op("pe", lambda e, xn_=xn_, nr=nr, kc=kc, bankb=bankb: e.transpose(bankb[:, kc * 128: kc * 128 + nr],
                                                                        xn_[0:nr, kc * 128:(kc + 1) * 128],
                                                                        identb[0:nr, 0:nr]),
                     reads=[txn, t_identb], writes=[tbank], inc=(kc == KC - 1))
            pview = bankb.rearrange("p (k t) -> p k t", k=KC)[:, :, 0:nr]
            if kind == "u":
                dst = uT[:, :, c0:c0 + nr]
                gcols = pT[:, 0:8]
                tdst = t_uTi[ti]
            else:
                dst = mnT[:, :, c0:c0 + nr]
                gcols = pT[:, 8:16]
                tdst = t_mnT
            S.op("dve", lambda e, dst=dst, pview=pview, gcols=gcols, nr=nr: e.tensor_tensor(
                out=dst, in0=pview, in1=gcols.unsqueeze(2).to_broadcast([128, KC, nr]), op=ALU.mult),
                reads=[tbank, t_pT], writes=[tdst])

        t_kT = Tok("kT")
        t_vb = Tok("vb")
        t_kvout = [Tok("kvo0"), Tok("kvo1")]
        kvst = {}

        def kv_mm():
            for which in range(2):
                wv_, wt_ = [], []
                for hh in range(2):
                    vws, tk = w_get(WT[("wk" if which == 0 else "wv") + str(hh)])
                    wv_.append(vws[0])
                    wt_.append(tk)
                pp, tk = next_pp()
                for mc in range(2):
                    for hh in range(2):
                        pairs = [(mnT[:, k, mc * 128:(mc + 1) * 128], wv_[hh][:, k, :]) for k in range(KC)]
                        mm_group(pp[:, mc * 512 + hh * 256: mc * 512 + (hh + 1) * 256], pairs, [t_mnT, wt_[hh]], tk,
                                 last_inc=(mc == 1 and hh == 1))
                kvst[which] = (pp, tk)
                if which == 0:
                    pp2, tk2 = next_pp()
                    for dc in range(4):
                        hh, off = dc // 2, (dc % 2) * 128
                        pairs = [(wv_[hh][:, k, off:off + 128], mnT[:, k, :]) for k in range(KC)]
                        mm_group(pp2[:, dc * 256:(dc + 1) * 256], pairs, [t_mnT, wt_[hh]], tk2, last_inc=(dc == 3))
                    kvst["kT"] = (pp2, tk2)

        def kv_evac():
            for which in range(2):
                pp, tk = kvst[which]
                ko = kvout[which]
                S.op("act", lambda e, ko=ko, pp=pp: e.copy(out=ko, in_=pp[:].rearrange("p (k t) -> p k t", k=2)),
                     reads=[tk], writes=[t_kvout[which]])
                dsto = (o_pk if which == 0 else o_pv).rearrange("(k p) c -> p k c", p=128)
                S.dma("sp", "out", dsto, ko, reads=[t_kvout[which]])
            pp, tk = kvst[1]
            S.op("act", lambda e, pp=pp: e.copy(out=vb, in_=pp[:].rearrange("p (k t) -> p k t", k=2)),
                 reads=[tk], writes=[t_vb])
            pp2, tk2 = kvst["kT"]
            S.op("dve", lambda e, pp2=pp2: e.tensor_copy(kT, pp2[:].rearrange("p (k t) -> p k t", k=4)),
                 reads=[tk2], writes=[t_kT])

        t_uTi = [Tok(f"uT{i}") for i in range(len(tiles))]
        PD = 5
        p0_stageA(0)
        p0_stageB(0)
        for ti in range(1, PD):
            p0_stageA(ti)
        t_sin = Tok("sin")
        S.dma("sp", "cs2", sin[0:TS, 0:2304], st_lc[:, :], writes=[t_sin], nowait=True)
        S.dma("sp", "cs2", sin[0:TS, 2304:3072], st_h[:, :], writes=[t_sin], nowait=True)
        S.dma("sp", "cs2", sin[0:TS, 3072:4608], st_sc[:, :], writes=[t_sin], nowait=True)
        t_out = Tok("out")
        o_slc3 = o_slc.rearrange("b (k c) -> b k c", k=3)
        st_lc3 = st_lc.rearrange("b (k c) -> b k c", k=3)
        S.dma("sp", "out", o_slc3[:, 0:2, :], st_lc3[:, 1:3, :])
        o_ssc3 = o_ssc.rearrange("b (k c) -> b k c", k=2)
        st_sc3 = st_sc.rearrange("b (k c) -> b k c", k=2)
        S.dma("sp", "out", o_ssc3[:, 0:1, :], st_sc3[:, 1:2, :])

        t_dv = Tok("dv")
        t_SIT = Tok("SIT")
        qT_t = carve(0, 4 * T, BF16)
        qT = qT_t.rearrange("p (k t) -> p k t", k=4)
        t_qT = [Tok(f"qT{h}") for h in range(4)]

        def sit():
            blocks_ = []
            for n in range(NCH):
                for k in range(3):
                    blocks_.append((n * 6 + k, k * W + n * 128))
                blocks_.append((n * 6 + 3, 2304 + n * 128))
                for k in range(2):
                    blocks_.append((n * 6 + 4 + k, 3072 + k * W + n * 128))
            for g0 in range(0, 36, 18):
                grp = blocks_[g0:g0 + 18]
                for gi, (slot_i, c0) in enumerate(grp):
                    S.op("pe", lambda e, gi=gi, c0=c0: e.transpose(PX[:, gi * TS:(gi + 1) * TS], sin[0:TS, c0:c0 + 128],
                                                                  ident[0:TS, 0:TS]),
                         reads=[t_sin, t_ident], writes=[tPX], inc=(gi == len(grp) - 1))
                assert [b_[0] for b_ in grp] == list(range(g0, g0 + 18))
                S.op("act", lambda e, g0=g0: e.copy(out=SIT[:, g0:g0 + 18, :],
                                                    in_=PX[:, 0:18 * TS].rearrange("p (a b) -> p a b", a=18)),
                     reads=[tPX], writes=[t_SIT])


        def dv_derive():
            S.op("act", lambda e: e.activation(out=dv[:, 0:6], in_=pT[:, 58:64], func=AF.Exp, scale=-1.0),
                 reads=[t_pT], writes=[t_dv])
            S.op("act", lambda e: e.activation(out=dv[:, 6:12], in_=dv[:, 0:6], func=AF.Ln, bias=1.0, scale=1.0),
                 reads=[t_dv], writes=[t_dv])
            S.op("dve", lambda e: e.tensor_scalar(out=dv[:, 0:6], in0=dv[:, 6:12], scalar1=-4.0, scalar2=None, op0=ALU.mult),
                 reads=[t_dv], writes=[t_dv])
            S.op("dve", lambda e: e.tensor_scalar(out=dv[:, 6:12], in0=dv[:, 6:12], scalar1=-8.0, scalar2=None, op0=ALU.mult),
                 reads=[t_dv], writes=[t_dv])
            S.op("dve", lambda e: e.tensor_scalar(out=dv[:, 12:24], in0=pT[:, 46:58], scalar1=0.5, scalar2=None, op0=ALU.mult),
                 reads=[t_pT, t_dv], writes=[t_dv])

        eqst = {}

        def eq_bank(h, b):
            rd_u = [t_uTi[i] for i in range(2, 10)]
            hh, hl = h // 2, h % 2
            vws, wtk = w_get(WT[f"q{hh}"])
            wq = vws[0]
            if b == 0:
                eqst[h] = next_pp()
            pp, tk = eqst[h]
            pairs = [(wq[:, k, hl * 128:(hl + 1) * 128], uT[:, k, b * 512:(b + 1) * 512]) for k in range(KC)]
            mm_group(pp[:, b * 512:(b + 1) * 512], pairs, [wtk] + rd_u, tk, last_inc=True)

        def eq_evac(heads):
            for h in heads:
                pp, tk = eqst[h]
                S.op("act", lambda e, pp=pp, h=h: e.copy(out=qT[:, h, 0:1024], in_=pp[:]),
                     reads=[tk], writes=[t_qT[h], t_sin, t_prt])

        for ti in range(len(tiles)):
            if ti + PD < len(tiles):
                p0_stageA(ti + PD)
            if ti + 1 < len(tiles):
                p0_stageB(ti + 1)
            p0_stageC(ti)
            if ti == 1:
                kv_mm()
            if ti == 5:
                kv_evac()
            if ti == 6:
                sit()
            if 9 <= ti <= 16:
                eq_bank((ti - 9) // 2, (ti - 9) % 2)
            if ti in (12, 14, 16):
                eq_evac([(ti - 12) // 2])
            if ti == 18:
                eq_evac([3])

        dv_derive()
        if STOP_AFTER == "P0c":
            dump([pT[:], stat[:, 0:19], uT[:, 0, 0:256], uT[:, 7, T - 128:TT], mnT[:, 0, :], mnT[:, 7, :]])
            S.barrier()
            S.finish(block)
            return nc
        S.barrier(skip=("out",))
        if STOP_AFTER == "KV":
            S.finish(block)
            return nc

        qT_t = carve(0, 4 * T, BF16)
        qT = qT_t.rearrange("p (k t) -> p k t", k=4)
        qs_tok = carve(16384, XW, BF16)
        sqg_tok = carve(17408, XW, F32)
        pTe = [carve(19456 + i * 2048, 1024, BF16).rearrange("p (k t) -> p k t", k=2) for i in range(2)]
        rden = [carve(23552 + i * 2048, 512, F32) for i in range(2)]
        o1b = [carve(27648 + i * 2048, 512, F32) for i in range(2)]
        selb_t = carve(31744, TS * 128, BF16)
        selb = selb_t.rearrange("p (b c) -> p b c", b=TS)
        eye16_t = carve(35840, 256, F32)
        eye16 = eye16_t.rearrange("p (a b) -> p a b", a=16)
        Sall_t = carve(45568, 128, F32)
        Sall = Sall_t.rearrange("p (c b h) -> p c b h", c=2, b=TS)
        Esm = carve(46080, 256, F32)
        Psm = carve(47104, 256, F32)
        Mk_t = carve(48128, 2 * 4 * 16 * 16, BF16)
        Mk = Mk_t.rearrange("p (c h b q) -> p c h b q", c=2, h=4, b=16)
        qb_sb = [carve(56320 + i * 2048, 512, F32) for i in range(2)]
        Kb = [carve(60416 + i * 4096, 1024, F32).rearrange("p (c f) -> p c f", c=2) for i in range(2)]
        prod = carve(68608, 1024, F32).rearrange("p (c f) -> p c f", c=2)

        t_og = [Tok(f"og{h}") for h in range(4)]
        t_qs = Tok("qs")
        t_sqg = Tok("sqg")
        def next_pp_c():
            while True:
                i = state["pp"] % 3
                state["pp"] += 1
                if i != state.get("pp_excl", -1):
                    return PP[i], tPP[i]

        def unit_half_c(lhs_list, rhs_arr, lo, reads):
            pp, tk = next_pp_c()
            nk = len(lhs_list)
            for b in range(2):
                pairs = [(lhs_list[k], rhs_arr[:, k, lo + b * 512: lo + (b + 1) * 512]) for k in range(nk)]
                mm_group(pp[:, b * 512:(b + 1) * 512], pairs, reads, tk, last_inc=(b == 1))
            return pp, tk

        t_pTe = [Tok("pTe0"), Tok("pTe1")]
        t_rden = [Tok("rden0"), Tok("rden1")]
        t_o1 = [Tok("o10"), Tok("o11")]

        def gen_C_prompt():
            for hh in range(2):
                vws, wtk = w_get(WT[f"q{hh}"])
                wq = vws[0]
                for hl in range(2):
                    h = hh * 2 + hl
                    lhs = [wq[:, k, hl * 128:(hl + 1) * 128] for k in range(KC)]
                    for (lo, hi) in HALVES[1:]:
                        pp, tk = unit_half_c(lhs, uT, lo, [wtk])
                        S.op("act", lambda e, pp=pp, h=h, lo=lo, hi=hi: e.copy(out=qT[:, h, lo:hi], in_=pp[:]),
                             reads=[tk], writes=[t_qT[h]])
                        yield
                pairs = [(uT[:, k, T:TT], wq[:, k, :]) for k in range(KC)]
                mm_group(PS[0:TS, 0:256], pairs, [wtk], tPSb)
                S.op("act", lambda e, hh=hh: e.copy(out=qs_tok[0:TS, hh * 256:(hh + 1) * 256], in_=PS[0:TS, 0:256]),
                     reads=[tPSb], writes=[t_qs])
                yield
            for hh in range(2):
                vws, wtk = w_get(WT[f"qg{hh}"])
                wq = vws[0]
                for hl in range(2):
                    h = hh * 2 + hl
                    lhs = [wq[:, k, hl * 128:(hl + 1) * 128] for k in range(KC)]
                    for (lo, hi) in HALVES:
                        pp, tk = unit_half_c(lhs, uT, lo, [wtk])
                        S.op("act", lambda e, pp=pp, h=h, lo=lo, hi=hi: e.activation(out=og[:, h, lo:hi], in_=pp[:], func=AF.Silu),
                             reads=[tk], writes=[t_og[h]])
                        yield
                pairs = [(uT[:, k, T:TT], wq[:, k, :]) for k in range(KC)]
                mm_group(PS[0:TS, 0:256], pairs, [wtk], tPSb)
                S.op("act", lambda e, hh=hh: e.activation(out=sqg_tok[0:TS, hh * 256:(hh + 1) * 256],
                                                         in_=PS[0:TS, 0:256], func=AF.Silu),
                     reads=[tPSb], writes=[t_sqg])
                yield
            iters = [(tb, h) for tb in range(4) for h in range(4)]
            stage1 = {}

            def att_s1(i):
                tb, h = iters[i]
                c0 = tb * 512
                pe_, tpe = pTe[i % 2], t_pTe[i % 2]
                pp, tk = next_pp_c()
                for mc in range(2):
                    mm_group(pp[:, mc * 512:(mc + 1) * 512], [(kT[:, h, mc * 128:(mc + 1) * 128], qT[:, h, c0:c0 + 512])],
                             [t_kT, t_qT[h]], tk, last_inc=(mc == 1))
                S.op("act", lambda e, pp=pp, pe_=pe_: e.activation(out=pe_, in_=pp[:].rearrange("p (k t) -> p k t", k=2),
                                                                  func=AF.Exp, scale=SCALE),
                     reads=[tk], writes=[tpe])

            def att_s2(i):
                tb, h = iters[i]
                c0 = tb * 512
                pe_, tpe = pTe[i % 2], t_pTe[i % 2]
                rd, trd = rden[i % 2], t_rden[i % 2]
                o1, to1 = o1b[i % 2], t_o1[i % 2]
                pp2, tk2 = next_pp_c()
                mm_group(pp2[:, 0:512], [(vb[:, mc, h * 128:(h + 1) * 128], pe_[:, mc, :]) for mc in range(2)],
                         [t_vb, tpe], tk2, last_inc=False)
                mm_group(pp2[:, 512:1024], [(onesb[:], pe_[:, mc, :]) for mc in range(2)], [t_ones, tpe], tk2)
                S.op("act", lambda e, pp2=pp2, rd=rd: e.activation(out=rd, in_=pp2[:, 512:1024], func=AF.Ln), reads=[tk2], writes=[trd])
                S.op("act", lambda e, rd=rd: e.activation(out=rd, in_=rd, func=AF.Exp, scale=-1.0), reads=[trd], writes=[trd])
                S.op("dve", lambda e, pp2=pp2, rd=rd, o1=o1: e.tensor_tensor(out=o1, in0=pp2[:, 0:512], in1=rd, op=ALU.mult),
                     reads=[tk2, trd], writes=[to1])
                S.op("dve", lambda e, o1=o1, h=h, c0=c0: e.tensor_tensor(out=og[:, h, c0:c0 + 512], in0=o1,
                                                                        in1=og[:, h, c0:c0 + 512], op=ALU.mult),
                     reads=[to1, t_og[h]], writes=[t_og[h]])

            att_s1(0)
            for i in range(len(iters)):
                if i + 1 < len(iters):
                    att_s1(i + 1)
                att_s2(i)
                yield

        t_selb = Tok("selb")
        t_eye = Tok("eye")
        t_qb = [Tok("qb0"), Tok("qb1")]
        t_Kb = [Tok("Kb0"), Tok("Kb1")]
        t_prod = Tok("prod")
        t_Sall = Tok("Sall")
        t_E = Tok("E")
        t_P = Tok("P")
        t_Mk = Tok("Mk")
        ksem = ["ka", "kb"]
        vsem = ["sva", "svb", "svc", "svd", "sve", "svf"]

        def gen_C_sample():
            S.wait_dma(["dve", "act", "pool", "pe"], "out")
            S.op("dve", lambda e: e.tensor_copy(selb[0:TS, :, :], identb[0:TS, 0:TS].unsqueeze(2).to_broadcast([TS, TS, 128])),
                 reads=[t_identb], writes=[t_selb])
            S.op("pool", lambda e: e.memset(eye16_t, 0.0), writes=[t_eye])
            S.op("pool", lambda e: e.affine_select(eye16, eye16, pattern=[[1, 16], [-1, 16]], compare_op=ALU.not_equal,
                                                   fill=1.0, base=0, channel_multiplier=0),
                 reads=[t_eye], writes=[t_eye])
            yield
            for b in range(TS):
                kb_, tkb = Kb[b % 2], t_Kb[b % 2]
                qb_, tqb = qb_sb[b % 2], t_qb[b % 2]
                S.dma("sp", ksem[b % 2], kb_, ck[b].rearrange("(c m) f -> m c f", c=2), writes=[tkb])
                S.op("pe", lambda e, b=b: e.matmul(PX[:, 0:512], selb[0:TS, b, :], qs_tok[0:TS, :], start=True, stop=True),
                     reads=[t_selb, t_qs], writes=[tPX])
                S.op("act", lambda e, qb_=qb_: e.copy(out=qb_, in_=PX[:, 0:512]), reads=[tPX], writes=[tqb])
                S.op("pool", lambda e, kb_=kb_, qb_=qb_: e.tensor_tensor(out=prod, in0=kb_,
                                                                         in1=qb_.unsqueeze(1).to_broadcast([128, 2, 512]), op=ALU.mult),
                     reads=[tkb, tqb], writes=[t_prod])
                S.op("dve", lambda e, b=b: e.tensor_reduce(out=Sall[:, :, b, :],
                                                           in_=prod.rearrange("p c (h d) -> p c h d", h=4), axis=AX.X, op=ALU.add),
                     reads=[t_prod], writes=[t_Sall])
                yield
            for c in range(2):
                S.op("pe", lambda e, c=c: e.transpose(PX[0:64, c * 128:(c + 1) * 128],
                                                      Sall_t[:, c * 64:(c + 1) * 64], ident[:, :]),
                     reads=[t_Sall, t_ident], writes=[tPX], inc=(c == 1))
            S.op("dve", lambda e: e.tensor_reduce(out=stat2[0:64, 0:1], in_=PX[0:64, 0:256], axis=AX.X, op=ALU.max),
                 reads=[tPX], writes=[t_sm])
            S.op("dve", lambda e: e.tensor_scalar(out=stat2[0:64, 0:1], in0=stat2[0:64, 0:1], scalar1=-SCALE, scalar2=None, op0=ALU.mult),
                 reads=[t_sm], writes=[t_sm])
            S.op("act", lambda e: e.activation(out=Esm[0:64, :], in_=PX[0:64, 0:256], func=AF.Exp, scale=SCALE,
                                               bias=stat2[0:64, 0:1], accum_out=stat2[0:64, 1:2]),
                 reads=[tPX, t_sm], writes=[t_E, t_sm])
            S.op("dve", lambda e: e.reciprocal(out=stat2[0:64, 1:2], in_=stat2[0:64, 1:2]), reads=[t_sm], writes=[t_sm])
            S.op("dve", lambda e: e.tensor_scalar(out=Psm[0:64, :], in0=Esm[0:64, :], scalar1=stat2[0:64, 1:2], scalar2=None, op0=ALU.mult),
                 reads=[t_E, t_sm], writes=[t_P])
            for c in range(2):
                S.op("pe", lambda e, c=c: e.transpose(PX[:, c * 64:(c + 1) * 64], Psm[0:64, c * 128:(c + 1) * 128], ident[0:64, 0:64]),
                     reads=[t_P, t_ident], writes=[tPX], inc=(c == 1))
            for c in range(2):
                S.op("dve", lambda e, c=c: e.tensor_tensor(
                    out=Mk[:, c],
                    in0=PX[:, c * 64:(c + 1) * 64].rearrange("p (b h) -> p h b", h=4).unsqueeze(3).to_broadcast([128, 4, 16, 16]),
                    in1=eye16.unsqueeze(1).to_broadcast([128, 4, 16, 16]), op=ALU.mult),
                    reads=[tPX, t_eye], writes=[t_Mk])
            yield
            ppa, tka = next_pp_c()
            state["pp_excl"] = PP.index(ppa)
            hbank = [(PS, 0, tPSb), (PX, 0, tPX), (ppa, 0, tka), (ppa, 512, tka)]
            NVB = 6
            Vb = ([carve(68608 + i * 2048, 1024, BF16).rearrange("p (c f) -> p c f", c=2) for i in range(2)]
                  + [carve(60416 + i * 2048, 1024, BF16).rearrange("p (c f) -> p c f", c=2) for i in range(4)])
            t_Vb = [Tok(f"Vb{i}") for i in range(NVB)]
            valias = [[t_prod], [t_prod], [t_Kb[0]], [t_Kb[0]], [t_Kb[1]], [t_Kb[1]]]

            def v_load(b):
                S.dma("pool", vsem[b % NVB], Vb[b % NVB], cv[b].rearrange("(c m) f -> m c f", c=2),
                      writes=[t_Vb[b % NVB]] + (valias[b] if b < NVB else []))

            for b in range(NVB - 1):
                v_load(b)
            for b in range(TS):
                vb_, tvb = Vb[b % NVB], t_Vb[b % NVB]
                if b + NVB - 1 < TS:
                    v_load(b + NVB - 1)
                for h in range(4):
                    pph, coff, tkh = hbank[h]
                    for c in range(2):
                        first = (b == 0 and c == 0)
                        last = (b == TS - 1 and c == 1)
                        S.op("pe", lambda e, vb_=vb_, b=b, h=h, c=c, first=first, last=last, pph=pph, coff=coff: e.matmul(
                            pph[0:TS, coff:coff + 128], Mk[:, c, h, b, :], vb_[:, c, h * 128:(h + 1) * 128],
                            start=first, stop=last, skip_group_check=True),
                            reads=[t_Mk, tvb], writes=[tkh], inc=(h == 3 and c == 1))
                yield
            ogs_tok = qs_tok
            for h in range(4):
                pph, coff, tkh = hbank[h]
                S.op("dve", lambda e, h=h, pph=pph, coff=coff: e.tensor_tensor(
                    out=ogs_tok[0:TS, h * 128:(h + 1) * 128], in0=pph[0:TS, coff:coff + 128],
                    in1=sqg_tok[0:TS, h * 128:(h + 1) * 128], op=ALU.mult),
                    reads=[tkh, t_sqg, t_qs], writes=[t_qs])
            state["pp_excl"] = -1
            for h in range(4):
                S.op("pe", lambda e, h=h: e.transpose(PXb[:, h * TS:(h + 1) * TS], ogs_tok[0:TS, h * 128:(h + 1) * 128],
                                                      identb[0:TS, 0:TS]),
                     reads=[t_qs, t_identb], writes=[tPX], inc=(h == 3))
            S.op("act", lambda e: e.copy(out=og[:, :, T:TT], in_=PXb[:, 0:4 * TS].rearrange("p (h b) -> p h b", h=4)),
                 reads=[tPX], writes=t_og)
            yield

        gp, gs = gen_C_prompt(), gen_C_sample()
        for _ in range(6):
            next(gp)
        alive = [gp, gs]
        while alive:
            for g in list(alive):
                try:
                    next(g)
                except StopIteration:
                    alive.remove(g)
        if STOP_AFTER == "C":
            dump([og[:, 0, 0:512], og[:, 3, T - 512:T], og[:, 0, T:TT], og[:, 1, T:TT], og[:, 2, T:TT], og[:, 3, T:TT], qT[:, 0, 0:256]], off_b=0)
        S.barrier(no_wait=("pe",))
        if STOP_AFTER == "C":
            S.finish(block)
            return nc

        wab_t = carve(0, 2 * NCH * 128, BF16)
        wab = wab_t.rearrange("p (a n d) -> p a n d", a=2, n=NCH)
        lxp = carve(3072, 3 + T, F32)
        lxs_t = carve(11280, 4 * TS, F32)
        lxs = lxs_t.rearrange("p (k b) -> p k b", k=4)
        xc = carve(11536, TT, F32)
        xcb = carve(19792, TT, BF16)
        thr = carve(23920, TT, F32)
        a2b = carve(32176, TT, F32)
        thi = carve(40432, TT, F32)
        scg_sb = carve(48752, TT, F32)
        cy = scg_sb
        cinp = carve(57008, 2 + T, F32)
        cins_t = carve(65208, 3 * TS, F32)
        cins = cins_t.rearrange("p (k b) -> p k b", k=3)
        so_tok = carve(65400, W, F32)
        t_wab = Tok("wab")
        S.dma("pool", "cs3", wab[:, 0], lru_wa.rearrange("n c d -> c n d"), writes=[t_wab])
        S.dma("pool", "cs3", wab[:, 1], lru_wx.rearrange("n c d -> c n d"), writes=[t_wab], nowait=True)
        t_lxp, t_lxs, t_xc, t_xcb, t_thr, t_a2, t_thi = (Tok("lxp"), Tok("lxs"), Tok("xc"), Tok("xcb"), Tok("thr"),
                                                        Tok("a2"), Tok("thi"))
        t_gA = [Tok(f"gA{n}") for n in range(NCH)]
        t_SO = Tok("SO")
        t_scg, t_cinp, t_cins = Tok("scg"), Tok("cinp"), Tok("cins")
        t_cy = t_scg
        t_gB = [Tok(f"gB{n}") for n in range(NCH)]
        S.op("pool", lambda e: e.memset(lxp[:, 0:3], 0.0), writes=[t_lxp])
        S.op("pool", lambda e: e.memset(cinp[:, 0:2], 0.0), writes=[t_cinp])

        def gen_A(n):
            vws, wtk = w_get(WT[f"A{n}"])
            wlg, wlx = vws
            lhs_lg = [wlg[:, k, :] for k in range(KC)]
            lhs_lx = [wlx[:, k, :] for k in range(KC)]
            for (lo, hi) in HALVES:
                pp, tk = unit_half(lhs_lg, uT, lo, [wtk])
                S.op("act", lambda e, pp=pp, n=n, lo=lo, hi=hi: e.activation(out=gA[:, n, lo:hi], in_=pp[:], func=AF.Silu),
                     reads=[tk], writes=[t_gA[n]])
            sp_, tk = unit_samp(lhs_lg, uT, [wtk])
            S.op("act", lambda e, sp_=sp_, n=n: e.activation(out=gA[:, n, T:TT], in_=sp_, func=AF.Silu),
                 reads=[tk], writes=[t_gA[n]])
            yield
            for (lo, hi) in HALVES:
                pp, tk = unit_half(lhs_lx, uT, lo, [wtk])
                S.op("dve", lambda e, pp=pp, lo=lo, hi=hi: e.tensor_copy(lxp[:, 3 + lo:3 + hi], pp[:]),
                     reads=[tk], writes=[t_lxp])
            sp_, tk = unit_samp(lhs_lx, uT, [wtk])
            S.op("dve", lambda e, n=n: e.tensor_copy(lxs[:, 0:3, :], SIT[:, n * 6:n * 6 + 3, :]), reads=[t_SIT], writes=[t_lxs])
            S.op("dve", lambda e, sp_=sp_: e.tensor_copy(lxs[:, 3, :], sp_), reads=[tk], writes=[t_lxs])
            yield
            cw = lambda k, n=n: pT[:, 16 + k * 6 + n: 17 + k * 6 + n]
            cbias = pT[:, 40 + n:41 + n]
            S.op("act", lambda e, cw=cw, cbias=cbias: e.activation(out=xc[:, 0:T], in_=lxp[:, 0:T], func=AF.Identity,
                                                                   scale=cw(0), bias=cbias),
                 reads=[t_lxp, t_pT], writes=[t_xc])
            S.op("act", lambda e, cw=cw, cbias=cbias: e.activation(out=xc[:, T:TT], in_=lxs[:, 0, :], func=AF.Identity,
                                                                   scale=cw(0), bias=cbias),
                 reads=[t_lxs, t_pT], writes=[t_xc])
            for k in range(1, 4):
                S.op("dve", lambda e, k=k, cw=cw: e.scalar_tensor_tensor(out=xc[:, 0:T], in0=lxp[:, k:k + T], scalar=cw(k),
                                                                         in1=xc[:, 0:T], op0=ALU.mult, op1=ALU.add),
                     reads=[t_lxp, t_xc], writes=[t_xc])
                S.op("dve", lambda e, k=k, cw=cw: e.scalar_tensor_tensor(out=xc[:, T:TT], in0=lxs[:, k, :], scalar=cw(k),
                                                                         in1=xc[:, T:TT], op0=ALU.mult, op1=ALU.add),
                     reads=[t_lxs, t_xc], writes=[t_xc])
            S.op("pool", lambda e, n=n: e.tensor_copy(SO[:, n, 0:3], lxp[:, T:T + 3]), reads=[t_lxp], writes=[t_SO])
            S.op("pool", lambda e, n=n: e.tensor_copy(SO[:, n, 6:22], lxs[:, 3, :]), reads=[t_lxs], writes=[t_SO])
            yield
            S.op("act", lambda e: e.copy(out=xcb, in_=xc), reads=[t_xc], writes=[t_xcb])
            yield
            for gi, (dst, tdst, bcol) in enumerate([(thr, t_thr, 12 + n), (thi, t_thi, 18 + n)]):
                lhs = [wab[:, gi, n, :]]
                xcb3 = xcb.unsqueeze(1)
                for (lo, hi) in HALVES:
                    pp, tk = unit_half(lhs, xcb3, lo, [t_wab, t_xcb])
                    S.op("act", lambda e, pp=pp, dst=dst, lo=lo, hi=hi, bcol=bcol: e.activation(
                        out=dst[:, lo:hi], in_=pp[:], func=AF.Tanh, scale=0.5, bias=dv[:, bcol:bcol + 1]),
                        reads=[tk, t_dv], writes=[tdst])
                sp_, tk = unit_samp(lhs, xcb3, [t_wab, t_xcb])
                S.op("act", lambda e, sp_=sp_, dst=dst, bcol=bcol: e.activation(
                    out=dst[:, T:TT], in_=sp_, func=AF.Tanh, scale=0.5, bias=dv[:, bcol:bcol + 1]),
                    reads=[tk, t_dv], writes=[tdst])
                yield
            S.op("act", lambda e, n=n: e.activation(out=a2b, in_=thr, func=AF.Exp, scale=dv[:, 6 + n:7 + n], bias=dv[:, 6 + n:7 + n]),
                 reads=[t_thr, t_dv], writes=[t_a2])
            S.op("act", lambda e, n=n: e.activation(out=thr, in_=thr, func=AF.Exp, scale=dv[:, n:n + 1], bias=dv[:, n:n + 1]),
                 reads=[t_thr, t_dv], writes=[t_thr])
            S.op("dve", lambda e: e.tensor_scalar(out=a2b, in0=a2b, scalar1=1.0, scalar2=-1.0, op0=ALU.min, op1=ALU.mult),
                 reads=[t_a2], writes=[t_a2])
            yield
            S.op("act", lambda e: e.activation(out=a2b, in_=a2b, func=AF.Sqrt, bias=1.0, scale=1.0), reads=[t_a2], writes=[t_a2])
            S.op("dve", lambda e: e.scalar_tensor_tensor(out=a2b, in0=a2b, scalar=0.5, in1=xc, op0=ALU.mult, op1=ALU.mult),
                 reads=[t_a2, t_xc], writes=[t_a2])
            S.op("dve", lambda e: e.scalar_tensor_tensor(out=thi, in0=thi, scalar=1.0, in1=a2b, op0=ALU.add, op1=ALU.mult),
                 reads=[t_thi, t_a2], writes=[t_thi])
            yield
            S.op("dve", lambda e: e.tensor_tensor_scan(out=a2b[:, 0:T], data0=thr[:, 0:T], data1=thi[:, 0:T], initial=0.0,
                                                       op0=ALU.mult, op1=ALU.add),
                 reads=[t_thr, t_thi], writes=[t_a2])
            S.op("dve", lambda e, n=n: e.tensor_tensor(out=a2b[:, T:TT], in0=thr[:, T:TT], in1=SIT[:, n * 6 + 3, :], op=ALU.mult),
                 reads=[t_thr, t_SIT], writes=[t_a2])
            S.op("dve", lambda e: e.tensor_tensor(out=a2b[:, T:TT], in0=a2b[:, T:TT], in1=thi[:, T:TT], op=ALU.add),
                 reads=[t_a2, t_thi], writes=[t_a2])
            yield
            S.op("pool", lambda e, n=n: e.tensor_copy(SO[:, n, 3:4], a2b[:, T - 1:T]), reads=[t_a2], writes=[t_SO])
            S.op("pool", lambda e, n=n: e.tensor_copy(SO[:, n, 22:38], a2b[:, T:TT]), reads=[t_a2], writes=[t_SO])
            S.op("dve", lambda e, n=n: e.tensor_tensor(out=gA[:, n, :], in0=a2b, in1=gA[:, n, :], op=ALU.mult),
                 reads=[t_a2, t_gA[n]], writes=[t_gA[n]])
            yield

        def gen_B(n):
            vws, wtk = w_get(WT[f"B{n}a"])
            wsg, wsb = vws
            lhs_sg = [wsg[:, k, :] for k in range(KC)]
            lhs_sb = [wsb[:, k, :] for k in range(KC)]
            for (lo, hi) in HALVES:
                pp, tk = unit_half(lhs_sg, uT, lo, [wtk])
                S.op("act", lambda e, pp=pp, n=n, lo=lo, hi=hi: e.activation(out=gB[:, n, lo:hi], in_=pp[:], func=AF.Silu),
                     reads=[tk], writes=[t_gB[n]])
            sp_, tk = unit_samp(lhs_sg, uT, [wtk])
            S.op("act", lambda e, sp_=sp_, n=n: e.activation(out=gB[:, n, T:TT], in_=sp_, func=AF.Silu),
                 reads=[tk], writes=[t_gB[n]])
            yield
            for (lo, hi) in HALVES:
                pp, tk = unit_half(lhs_sb, uT, lo, [wtk])
                S.op("dve", lambda e, pp=pp, n=n, lo=lo, hi=hi: e.tensor_tensor(out=gB[:, n, lo:hi], in0=pp[:], in1=gB[:, n, lo:hi],
                                                                                op=ALU.mult),
                     reads=[tk, t_gB[n]], writes=[t_gB[n]])
            sp_, tk = unit_samp(lhs_sb, uT, [wtk])
            S.op("dve", lambda e, sp_=sp_, n=n: e.tensor_tensor(out=gB[:, n, T:TT], in0=sp_, in1=gB[:, n, T:TT], op=ALU.mult),
                 reads=[tk, t_gB[n]], writes=[t_gB[n]])
            yield
            vws, wtk = w_get(WT[f"B{n}b"])
            wscg, wsh = vws
            lhs_scg = [wscg[:, k, :] for k in range(KC)]
            lhs_sh = [wsh[:, k, :] for k in range(KC)]
            for (lo, hi) in HALVES:
                pp, tk = unit_half(lhs_scg, uT, lo, [wtk])
                S.op("act", lambda e, pp=pp, lo=lo, hi=hi: e.copy(out=scg_sb[:, lo:hi], in_=pp[:]), reads=[tk], writes=[t_scg])
            sp_, tk = unit_samp(lhs_scg, uT, [wtk])
            S.op("act", lambda e, sp_=sp_: e.copy(out=scg_sb[:, T:TT], in_=sp_), reads=[tk], writes=[t_scg])
            yield
            for (lo, hi) in HALVES:
                pp, tk = unit_half(lhs_sh, uT, lo, [wtk])
                S.op("dve", lambda e, pp=pp, lo=lo, hi=hi: e.tensor_tensor(out=cinp[:, 2 + lo:2 + hi], in0=pp[:],
                                                                           in1=scg_sb[:, lo:hi], op=ALU.mult),
                     reads=[tk, t_scg], writes=[t_cinp])
            sp_, tk = unit_samp(lhs_sh, uT, [wtk])
            S.op("dve", lambda e, n=n: e.tensor_copy(cins[:, 0:2, :], SIT[:, n * 6 + 4:n * 6 + 6, :]), reads=[t_SIT], writes=[t_cins])
            S.op("dve", lambda e, sp_=sp_: e.tensor_tensor(out=cins[:, 2, :], in0=sp_, in1=scg_sb[:, T:TT], op=ALU.mult),
                 reads=[tk, t_scg], writes=[t_cins])
            yield
            sw = lambda k, n=n: pT[:, 64 + k * 6 + n: 65 + k * 6 + n]
            S.op("act", lambda e, sw=sw: e.activation(out=cy[:, 0:T], in_=cinp[:, 0:T], func=AF.Identity, scale=sw(0)),
                 reads=[t_cinp, t_pT], writes=[t_cy])
            S.op("act", lambda e, sw=sw: e.activation(out=cy[:, T:TT], in_=cins[:, 0, :], func=AF.Identity, scale=sw(0)),
                 reads=[t_cins, t_pT], writes=[t_cy])
            for k in range(1, 3):
                S.op("dve", lambda e, k=k, sw=sw: e.scalar_tensor_tensor(out=cy[:, 0:T], in0=cinp[:, k:k + T], scalar=sw(k),
                                                                         in1=cy[:, 0:T], op0=ALU.mult, op1=ALU.add),
                     reads=[t_cinp, t_cy], writes=[t_cy])
                S.op("dve", lambda e, k=k, sw=sw: e.scalar_tensor_tensor(out=cy[:, T:TT], in0=cins[:, k, :], scalar=sw(k),
                                                                         in1=cy[:, T:TT], op0=ALU.mult, op1=ALU.add),
                     reads=[t_cins, t_cy], writes=[t_cy])
            S.op("pool", lambda e, n=n: e.tensor_copy(SO[:, n, 4:6], cinp[:, T:T + 2]), reads=[t_cinp], writes=[t_SO])
            S.op("pool", lambda e, n=n: e.tensor_copy(SO[:, n, 38:54], cins[:, 2, :]), reads=[t_cins], writes=[t_SO])
            S.op("dve", lambda e, n=n: e.tensor_tensor(out=gB[:, n, :], in0=cy, in1=gB[:, n, :], op=ALU.mult),
                 reads=[t_cy, t_gB[n]], writes=[t_gB[n]])
            yield

        def interleave(*gens):
            gens = list(gens)
            while gens:
                for g in list(gens):
                    try:
                        next(g)
                    except StopIteration:
                        gens.remove(g)

        gens = {}

        def adv(kind, n):
            g = gens.get((kind, n))
            if g is None:
                return
            try:
                next(g)
            except StopIteration:
                pass

        for n in range(NCH + 1):
            if n < NCH:
                gens[("A", n)] = gen_A(n)
            if n >= 1:
                gens[("B", n - 1)] = gen_B(n - 1)
            adv("A", n)
            adv("B", n - 1)
            adv("A", n - 1)
            adv("A", n)
            adv("B", n - 1)
            adv("A", n)
            adv("A", n - 1)
            adv("B", n - 1)
            adv("B", n - 1)
            adv("A", n - 1)
            adv("A", n)
            adv("A", n)
            adv("A", n)
            adv("B", n - 1)
            adv("A", n)
        for g in gens.values():
            for _ in g:
                pass
        if STOP_AFTER == "B":
            dump([gB[:, 0, 0:512], gB[:, 5, T - 512:T], gB[:, 0, T:TT], gB[:, 5, T:TT], gB[:, 2, 1024:1536]], off_b=0)
        t_sot = Tok("sot")
        for g0 in range(0, NCH, 3):
            for n in range(g0, g0 + 3):
                S.op("pe", lambda e, n=n, g0=g0: e.transpose(PX[0:54, (n - g0) * 128:(n - g0 + 1) * 128], SO[:, n, :], ident[:, :]),
                     reads=[t_SO, t_ident], writes=[tPX], inc=(n == g0 + 2))
            S.op("act", lambda e, g0=g0: e.copy(out=so_tok[0:54, g0 * 128:(g0 + 3) * 128], in_=PX[0:54, 0:384]),
                 reads=[tPX], writes=[t_sot])
        S.dma("sp", "out", o_plc[:, :], so_tok[0:3, :], reads=[t_sot])
        S.dma("sp", "out", o_ph[:, :], so_tok[3:4, :], reads=[t_sot])
        S.dma("sp", "out", o_psc[:, :], so_tok[4:6, :], reads=[t_sot])
        S.dma("sp", "out", o_slc3[:, 2, :], so_tok[6:22, :], reads=[t_sot])
        S.dma("sp", "out", o_sh[:, :], so_tok[22:38, :], reads=[t_sot])
        S.dma("sp", "out", o_ssc3[:, 1, :], so_tok[38:54, :], reads=[t_sot])
        if STOP_AFTER == "B":
            dump([gB[:, 0, 0:512], gB[:, 5, T - 512:T], gB[:, 0, T:TT], gB[:, 5, T:TT], gB[:, 2, 1024:1536]], off_b=56320)
        S.barrier(no_wait=("pe",))
        if STOP_AFTER == "B":
            S.finish(block)
            return nc

        NTH = 3
        thb = [carve(i * 4096, 1024, F32) for i in range(NTH)]
        tacc = [carve(12288 + i * 4096, 1024, F32) for i in range(2)]
        mT_t = carve(20480, KC * TT, BF16)
        mT = mT_t.rearrange("p (k t) -> p k t", k=KC)
        wo_t = carve(53504, KC * D, BF16)
        wo = wo_t.rearrange("p (k c) -> p k c", k=KC)
        t_wo = Tok("wo")
        if STOP_AFTER == "M0":
            S.barrier()
            S.finish(block)
            return nc
        t_th = [Tok(f"th{i}") for i in range(NTH)]
        t_acc = [Tok("acc0"), Tok("acc1")]
        t_mT = Tok("mT")
        thc = {"i": 0, "a": 0}
        gsrc = [(gA, NCH), (gB, NCH), (og, 4)]
        gtok = [t_gA, t_gB, t_og]
        ths_f = carve(69888, 3 * TS, F32)
        tzs_f = carve(70080, 3 * TS, F32)
        accs_f = carve(70272, TS, F32)
        t_accs2 = Tok("accs2")
        t_ths = Tok("ths")
        t_accs = Tok("accs")
        for j in range(8):
            vg, wtkg = w_get(WT[f"Mg{j}"])
            vo, wtko = w_get(WT[f"Mo{j}"])
            S.dma("pool", "cs4", wo[:, j, :], w_out[j * 128:(j + 1) * 128, :], writes=[t_wo], nowait=True)
            tMs_g, tMs_o = tPSb, tPX
            for x in range(3):
                lhs_g = [vg[x][:, k, :] for k in range(KC)]
                mm_group(PS[:, x * TS:(x + 1) * TS], [(lhs_g[k], uT[:, k, T:TT]) for k in range(KC)], [wtkg], tMs_g)
            S.op("act", lambda e: e.activation(out=ths_f, in_=PS[:, 0:3 * TS], func=AF.Tanh, scale=0.5),
                 reads=[tMs_g], writes=[t_ths])
            for x in range(3):
                garr, nk = gsrc[x]
                lhs_o = [vo[x][:, k, :] for k in range(nk)]
                mm_group(PX[:, x * TS:(x + 1) * TS], [(lhs_o[k], garr[:, k, T:TT]) for k in range(nk)], [wtko] + gtok[x], tMs_o)
            S.op("dve", lambda e: e.scalar_tensor_tensor(out=tzs_f, in0=ths_f, scalar=1.0, in1=PX[:, 0:3 * TS],
                                                         op0=ALU.add, op1=ALU.mult),
                 reads=[t_ths, tMs_o], writes=[t_accs])
            S.op("dve", lambda e: e.tensor_reduce(out=accs_f, in_=tzs_f.rearrange("p (x b) -> p b x", x=3),
                                                  axis=AX.X, op=ALU.add),
                 reads=[t_accs], writes=[t_accs2])
            S.op("dve", lambda e, j=j: e.tensor_copy(mT[:, j, T:TT], accs_f), reads=[t_accs2], writes=[t_mT])
            for (lo, hi) in HALVES:
                acc, tacc_ = tacc[thc["a"] % 2], t_acc[thc["a"] % 2]
                thc["a"] += 1
                for x in range(3):
                    garr, nk = gsrc[x]
                    lhs_g = [vg[x][:, k, :] for k in range(KC)]
                    lhs_o = [vo[x][:, k, :] for k in range(nk)]
                    th_, tth = thb[thc["i"] % NTH], t_th[thc["i"] % NTH]
                    thc["i"] += 1
                    pp, tk = unit_half(lhs_g, uT, lo, [wtkg])
                    S.op("act", lambda e, pp=pp, th_=th_: e.activation(out=th_, in_=pp[:], func=AF.Tanh, scale=0.5),
                         reads=[tk], writes=[tth])
                    zp_, tkz = unit_half(lhs_o, garr, lo, [wtko] + gtok[x])
                    zp = zp_[:]
                    if x == 0:
                        S.op("dve", lambda e, acc=acc, th_=th_, zp=zp: e.scalar_tensor_tensor(out=acc, in0=th_, scalar=1.0, in1=zp,
                                                                                           op0=ALU.add, op1=ALU.mult),
                             reads=[tth, tkz], writes=[tacc_])
                    else:
                        S.op("dve", lambda e, th_=th_, zp=zp: e.scalar_tensor_tensor(out=th_, in0=th_, scalar=1.0, in1=zp,
                                                                                    op0=ALU.add, op1=ALU.mult),
                             reads=[tth, tkz], writes=[tth])
                        if x == 1:
                            S.op("pool", lambda e, acc=acc, th_=th_: e.tensor_tensor(out=acc, in0=acc, in1=th_, op=ALU.add),
                                 reads=[tth, tacc_], writes=[tacc_])
                        else:
                            S.op("pool", lambda e, acc=acc, th_=th_, j=j, lo=lo, hi=hi: e.tensor_tensor(
                                out=mT[:, j, lo:hi], in0=acc, in1=th_, op=ALU.add),
                                reads=[tth, tacc_], writes=[t_mT])
            if STOP_AFTER == "M5":
                S.barrier()
                S.finish(block)
                return nc
        S.barrier(no_wait=("pe",))
        if STOP_AFTER == "M":
            S.finish(block)
            return nc

        xr = [carve(i * 4096, 1024, F32) for i in range(2)]
        yr = [carve(8192 + i * 4096, 1024, F32) for i in range(2)] + [carve(69888, 1024, F32)]
        fgb = carve(16384, 1024, F32)
        t_fgb = Tok("fgb")
        S.dma("sp", "cs5", fgb, final_norm_g.partition_broadcast(128), writes=[t_fgb])
        t_xr = [Tok("xr0"), Tok("xr1")]
        t_yr = [Tok("yr0"), Tok("yr1"), Tok("yr2")]
        ftiles = [(xp[i * 128:(i + 1) * 128, :], y_p[i * 128:(i + 1) * 128, :], 128, i * 128) for i in range(16)]
        ftiles.append((xs[:, :], y_s[:, :], TS, T))
        xsem2 = ["fxa", "fxb"]
        ysem = ["ya", "yb", "xc"]
        t_stF = [Tok(f"stF{i}") for i in range(len(ftiles))]

        def f_stageA(ti):
            src, dst, nr, c0 = ftiles[ti]
            xr_, txr = xr[ti % 2], t_xr[ti % 2]
            yr_, tyr = yr[ti % 3], t_yr[ti % 3]
            S.dma("pool", xsem2[ti % 2], xr_[0:nr, :], src, writes=[txr])
            pp, tk = next_pp()
            for b in range(2):
                pairs = [(mT[:, k, c0:c0 + nr], wo[:, k, b * 512:(b + 1) * 512]) for k in range(KC)]
                mm_group(pp[0:nr, b * 512:(b + 1) * 512], pairs, [t_mT, t_wo], tk, last_inc=(b == 1))
            S.op("dve", lambda e, pp=pp, yr_=yr_, xr_=xr_, nr=nr: e.scalar_tensor_tensor(
                out=yr_[0:nr, :], in0=pp[0:nr, :], scalar=0.5, in1=xr_[0:nr, :], op0=ALU.mult, op1=ALU.add),
                reads=[tk, txr], writes=[tyr])

        def f_stageA2(ti):
            src, dst, nr, c0 = ftiles[ti]
            xr_, txr = xr[ti % 2], t_xr[ti % 2]
            yr_, tyr = yr[ti % 3], t_yr[ti % 3]
            col = 32 + ti
            S.op("act", lambda e, xr_=xr_, yr_=yr_, nr=nr, col=col: e.activation(out=xr_[0:nr, :], in_=yr_[0:nr, :], func=AF.Square,
                                                                                accum_out=stat2[0:nr, col:col + 1]),
                 reads=[tyr, t_sm], writes=[txr, t_stF[ti]])

        def f_stageB1(ti):
            src, dst, nr, c0 = ftiles[ti]
            col = 32 + ti
            S.op("act", lambda e, nr=nr, col=col: e.activation(out=stat2[0:nr, col:col + 1], in_=stat2[0:nr, col:col + 1], func=AF.Sqrt,
                                                               scale=1.0 / D, bias=epst[0:nr, 0:1]),
                 reads=[t_stF[ti], t_eps], writes=[t_stF[ti]])

        def f_stageB(ti):
            src, dst, nr, c0 = ftiles[ti]
            yr_, tyr = yr[ti % 3], t_yr[ti % 3]
            col = 32 + ti
            S.op("dve", lambda e, nr=nr, col=col: e.reciprocal(out=stat2[0:nr, col:col + 1], in_=stat2[0:nr, col:col + 1]),
                 reads=[t_stF[ti]], writes=[t_stF[ti]])
            S.op("dve", lambda e, yr_=yr_, nr=nr, col=col: e.scalar_tensor_tensor(
                out=yr_[0:nr, :], in0=yr_[0:nr, :], scalar=stat2[0:nr, col:col + 1], in1=fgb[0:nr, :], op0=ALU.mult, op1=ALU.mult),
                reads=[tyr, t_stF[ti], t_fgb], writes=[tyr])
            S.dma("sp", ysem[ti % 3], dst, yr_[0:nr, :], reads=[tyr])

        f_stageA(0)
        f_stageA2(0)
        for ti in range(len(ftiles)):
            if ti + 1 < len(ftiles):
                f_stageA(ti + 1)
            f_stageB1(ti)
            if ti + 1 < len(ftiles):
                f_stageA2(ti + 1)
            f_stageB(ti)
        S.barrier()
        S.finish(block)
    return nc


_CACHE = {}


def _get_program():
    if "nc" not in _CACHE:
        _CACHE["nc"] = build_program()
    return _CACHE["nc"]


def kernel(x_prompt, x_sample, cache_mem_k, cache_mem_v, state_lru_h, state_lru_conv, state_sconv, mem_prompt,
           norm_g, mem_norm_g, w_in, lru_conv_w, lru_conv_b, lru_wa, lru_ba, lru_wx, lru_bx, lru_lambda, lru_wo,
           sconv_w, sconv_wo, xa_wk, xa_wv, xa_wo, w_out, final_norm_g):
    f = lambda a: np.ascontiguousarray(np.asarray(a, dtype=np.float32))
    shared = {
        "norm_g": f(norm_g[0]), "mem_norm_g": f(mem_norm_g[0]), "w_in": f(w_in[0]),
        "lru_conv_w": f(lru_conv_w[0]), "lru_conv_b": f(lru_conv_b[0]), "lru_wa": f(lru_wa[0]),
        "lru_ba": f(lru_ba[0]), "lru_wx": f(lru_wx[0]), "lru_bx": f(lru_bx[0]), "lru_lambda": f(lru_lambda[0]),
        "lru_wo": f(lru_wo[0]), "sconv_w": f(sconv_w[0]), "sconv_wo": f(sconv_wo[0]), "xa_wk": f(xa_wk[0]),
        "xa_wv": f(xa_wv[0]), "xa_wo": f(xa_wo[0]), "w_out": f(w_out[0]), "final_norm_g": f(final_norm_g),
    }
    in_maps = []
    for c in range(NCORES):
        sl = slice(c * TS, (c + 1) * TS)
        m = dict(shared)
        m["xp"] = f(x_prompt[c])
        m["xs"] = f(np.asarray(x_sample)[sl, 0, :])
        m["memp"] = f(mem_prompt[c])
        m["ck"] = f(np.asarray(cache_mem_k)[0, sl].reshape(TS, NM, XW))
        m["cv"] = f(np.asarray(cache_mem_v)[0, sl].reshape(TS, NM, XW))
        m["st_h"] = f(np.asarray(state_lru_h)[0, sl])
        m["st_lc"] = f(np.asarray(state_lru_conv)[0, sl].reshape(TS, 3 * W))
        m["st_sc"] = f(np.asarray(state_sconv)[0, sl].reshape(TS, 2 * W))
        in_maps.append(m)
    nc = _get_program()
    res = run_bass_kernel_spmd(nc, in_maps, core_ids=list(range(NCORES)))
    rs = res.results
    cat = lambda k: np.concatenate([np.asarray(r[k]) for r in rs], axis=0)
    y_prompt = np.stack([np.asarray(r["y_p"]) for r in rs], axis=0).astype(np.float32)
    y_sample = cat("y_s").reshape(NCORES * TS, 1, D).astype(np.float32)
    p_mk = np.stack([np.asarray(r["o_pk"]) for r in rs], axis=0).reshape(1, NCORES, NM, 4, 128).astype(np.float32)
    p_mv = np.stack([np.asarray(r["o_pv"]) for r in rs], axis=0).reshape(1, NCORES, NM, 4, 128).astype(np.float32)
    p_h = cat("o_ph").reshape(1, NCORES, W).astype(np.float32)
    p_lc = np.stack([np.asarray(r["o_plc"]) for r in rs], axis=0).reshape(1, NCORES, 3, W).astype(np.float32)
    p_sc = np.stack([np.asarray(r["o_psc"]) for r in rs], axis=0).reshape(1, NCORES, 2, W).astype(np.float32)
    s_h = cat("o_sh").reshape(1, NCORES * TS, W).astype(np.float32)
    s_lc = cat("o_slc").reshape(1, NCORES * TS, 3, W).astype(np.float32)
    s_sc = cat("o_ssc").reshape(1, NCORES * TS, 2, W).astype(np.float32)
    return (y_prompt, y_sample, p_mk, p_mv, p_h, p_lc, p_sc, s_h, s_lc, s_sc)
```

```python
import math
from contextlib import ExitStack

import numpy as np
import concourse.bass as bass
import concourse.mybir as mybir
from concourse.bass_utils import run_bass_kernel_spmd

F32 = mybir.dt.float32
BF16 = mybir.dt.bfloat16
AF = mybir.ActivationFunctionType
ALU = mybir.AluOpType
AX = mybir.AxisListType

NCORES = 8
T = 2048
TS = 16
TT = T + TS
D = 1024
KC = 8
W = 768
NCH = 6
NM = 256
XW = 512
IN_COLS = 8704
EPS = 1e-6
SCALE = 1.0 / math.sqrt(128.0)
STOP_AFTER = None

C_LX, C_LG, C_SB, C_SCG, C_SH, C_SG, C_Q, C_QG, C_MG = 0, 768, 1536, 2304, 3072, 3840, 4608, 5120, 5632


class Tok:
    __slots__ = ("name", "w", "r")

    def __init__(self, name=""):
        self.name = name
        self.w = None
        self.r = {}


class Stream:
    def __init__(self, key, sem):
        self.key = key
        self.sem = sem
        self.cnt = 0
        self.seen = {}
        self.ops = []
        self.pending = False


class Sched:
    def __init__(self, nc):
        self.nc = nc
        self.sems = {}
        self.streams = {}
        self.dma_cnt = {}

    def add_stream(self, key, sem):
        self.sems[key] = sem
        self.streams[key] = Stream(key, sem)

    def add_dma_sem(self, key, sem):
        self.sems[key] = sem
        self.dma_cnt[key] = 0

    def _needs(self, st, reads, writes):
        needs = {}

        def need(ev):
            if ev is None:
                return
            k, v = ev
            if needs.get(k, 0) < v:
                needs[k] = v
        for t in reads:
            need(t.w)
        for t in writes:
            need(t.w)
            for k, v in t.r.items():
                if k == st.key:
                    continue
                need((k, v))
        out = []
        for k, v in needs.items():
            if k == st.key and k == "pe":
                continue
            if st.seen.get(k, 0) < v:
                st.seen[k] = v
                out.append((k, v))
        return out

    def op(self, key, fn, reads=(), writes=(), inc=True):
        st = self.streams[key]
        waits = self._needs(st, reads, writes)
        if inc:
            st.cnt += 1
            st.pending = False
            ev = (key, st.cnt)
        else:
            st.pending = True
            ev = (key, st.cnt + 1)
        sems = self.sems
        sem = st.sem

        def run(eng, waits=waits, fn=fn, inc=inc):
            for k, v in waits:
                eng.wait_ge(sems[k], v)
            ins = fn(eng)
            if inc:
                ins.then_inc(sem, 1)
        st.ops.append(run)
        for t in writes:
            t.w = ev
            t.r = {}
        for t in reads:
            if t.r.get(key, 0) < ev[1]:
                t.r[key] = ev[1]

    def dma(self, key, semkey, out, in_, reads=(), writes=(), nowait=False, **kw):
        st = self.streams[key]
        waits = [] if nowait else self._needs(st, reads, writes)
        self.dma_cnt[semkey] += 16
        ev = (semkey, self.dma_cnt[semkey])
        sems = self.sems

        def run(eng, waits=waits):
            for k, v in waits:
                eng.wait_ge(sems[k], v)
            eng.dma_start(out=out, in_=in_, **kw).then_inc(sems[semkey], 16)
        st.ops.append(run)
        for t in writes:
            t.w = ev
            t.r = {}
        for t in reads:
            if t.r.get(semkey, 0) < ev[1]:
                t.r[semkey] = ev[1]

    def wait_dma(self, keys, semkey):
        v = self.dma_cnt[semkey]
        sems = self.sems
        for k in keys:
            st = self.streams[k]
            if v and st.seen.get(semkey, 0) < v:
                st.seen[semkey] = v
                st.ops.append(lambda eng, v=v: eng.wait_ge(sems[semkey], v))

    def barrier(self, skip=(), no_wait=()):
        targets = {}
        for k, st in self.streams.items():
            assert not st.pending
            if st.cnt:
                targets[k] = st.cnt
        for k, v in self.dma_cnt.items():
            if v and k not in skip:
                targets[k] = v
        sems = self.sems
        for k, st in self.streams.items():
            if k in no_wait:
                continue
            waits = []
            for tk, tv in targets.items():
                if st.seen.get(tk, 0) < tv:
                    st.seen[tk] = tv
                    waits.append((tk, tv))

            def run(eng, waits=waits):
                for kk, v in waits:
                    eng.wait_ge(sems[kk], v)
            st.ops.append(run)

    def finish(self, block):
        for key, st in self.streams.items():
            assert not st.pending, key
        ss = self.streams

        def mk(key):
            def body(eng):
                for o in ss[key].ops:
                    o(eng)
            return body
        block.gpsimd(mk("pool"))
        block.tensor(mk("pe"))
        block.scalar(mk("act"))
        block.vector(mk("dve"))
        block.sync(mk("sp"))


def build_program():
    nc = bass.Bass("TRN2", target_bir_lowering=False)

    def din(name, shape):
        return nc.dram_tensor(name, shape, F32, kind="ExternalInput").ap()

    def dout(name, shape):
        return nc.dram_tensor(name, shape, F32, kind="ExternalOutput").ap()

    xp = din("xp", [T, D])
    xs = din("xs", [TS, D])
    memp = din("memp", [NM, D])
    ck = din("ck", [TS, NM, XW])
    cv = din("cv", [TS, NM, XW])
    st_h = din("st_h", [TS, W])
    st_lc = din("st_lc", [TS, 3 * W])
    st_sc = din("st_sc", [TS, 2 * W])
    norm_g = din("norm_g", [D])
    mem_norm_g = din("mem_norm_g", [D])
    w_in = din("w_in", [D, IN_COLS])
    lru_conv_w = din("lru_conv_w", [4, W])
    lru_conv_b = din("lru_conv_b", [W])
    lru_wa = din("lru_wa", [NCH, 128, 128])
    lru_ba = din("lru_ba", [W])
    lru_wx = din("lru_wx", [NCH, 128, 128])
    lru_bx = din("lru_bx", [W])
    lru_lambda = din("lru_lambda", [W])
    lru_wo = din("lru_wo", [W, D])
    sconv_w = din("sconv_w", [3, W])
    sconv_wo = din("sconv_wo", [W, D])
    xa_wk = din("xa_wk", [D, XW])
    xa_wv = din("xa_wv", [D, XW])
    xa_wo = din("xa_wo", [XW, D])
    w_out = din("w_out", [D, D])
    final_norm_g = din("final_norm_g", [D])

    y_p = dout("y_p", [T, D])
    y_s = dout("y_s", [TS, D])
    o_pk = dout("o_pk", [NM, XW])
    o_pv = dout("o_pv", [NM, XW])
    o_ph = dout("o_ph", [1, W])
    o_plc = dout("o_plc", [3, W])
    o_psc = dout("o_psc", [2, W])
    o_sh = dout("o_sh", [TS, W])
    o_slc = dout("o_slc", [TS, 3 * W])
    o_ssc = dout("o_ssc", [TS, 2 * W])

    dbg = dout("dbg", [128, 4096]) if STOP_AFTER else None
    es = ExitStack()
    with es:
        def sb(name, shape, dt):
            return es.enter_context(nc.sbuf_tensor(name, shape, dt))

        uT_t = sb("uT", [128, KC * TT], BF16)
        uT = uT_t[:].rearrange("p (k t) -> p k t", k=KC)
        gA_t = sb("gA", [128, NCH * TT], BF16)
        gA = gA_t[:].rearrange("p (k t) -> p k t", k=NCH)
        gB_t = sb("gB", [128, NCH * TT], BF16)
        gB = gB_t[:].rearrange("p (k t) -> p k t", k=NCH)
        og_t = sb("og", [128, 4 * TT], BF16)
        og = og_t[:].rearrange("p (k t) -> p k t", k=4)
        NSLOT = 5
        SLOT_E = 3072
        slots = [sb(f"wslot{i}", [128, SLOT_E], BF16) for i in range(NSLOT)]
        ident = sb("ident", [128, 128], F32)
        identb = sb("identb", [128, 128], BF16)
        onesb = sb("onesb", [128, 128], BF16)
        pT = sb("pT", [128, 82], F32)
        dv = sb("dv", [128, 24], F32)
        SIT_t = sb("SIT", [128, 36 * TS], F32)
        SIT = SIT_t[:].rearrange("p (a b) -> p a b", a=36)
        SO_t = sb("SO", [128, NCH * 54], F32)
        SO = SO_t[:].rearrange("p (a b) -> p a b", a=NCH)
        stat = sb("stat", [128, 64], F32)
        stat2 = sb("stat2", [128, 64], F32)
        epst = sb("epst", [128, 1], F32)
        q25 = sb("q25", [128, 1], F32)
        RW = 18816
        R = sb("R", [128, RW], F32)

        def carve(off_b, nelem, dt):
            assert off_b % 4 == 0
            if dt == F32:
                assert off_b // 4 + nelem <= RW, (off_b, nelem)
                return R[:, off_b // 4: off_b // 4 + nelem]
            assert nelem % 2 == 0 and off_b // 4 + nelem // 2 <= RW, (off_b, nelem)
            return R[:, off_b // 4: off_b // 4 + nelem // 2].bitcast(BF16)

        PP = [es.enter_context(nc.psum_tensor(f"PP{i}", [128, 1024], F32)) for i in range(3)]
        PS = es.enter_context(nc.psum_tensor("PS", [128, 512], F32))
        PX = es.enter_context(nc.psum_tensor("PX", [128, 512], F32))
        tPP = [Tok(f"PP{i}") for i in range(3)]
        tPS = [Tok(f"PS{i}") for i in range(8)]
        tPX = Tok("PX")
        PSb = PS[:].bitcast(BF16)
        PXb = PX[:].bitcast(BF16)

        S = Sched(nc)
        for k in ["pe", "act", "dve", "pool", "sp"]:
            S.add_stream(k, es.enter_context(nc.semaphore("s_" + k)))
        dsem_names = (["cst", "cs2", "cs3", "cs4", "cs5", "sva", "svb", "svc", "svd", "sve", "svf", "fxa", "fxb", "out", "xa", "xb", "xc", "ya", "yb", "ka", "kb", "va", "vb"]
                      + [f"ws{i}" for i in range(NSLOT)])
        for k in dsem_names:
            S.add_dma_sem(k, es.enter_context(nc.semaphore("d_" + k)))
        block = es.enter_context(nc.Block())

        state = {"pp": 0, "ps": 0}

        def next_pp():
            i = state["pp"] % 3
            state["pp"] += 1
            return PP[i], tPP[i]

        def next_ps():
            i = state["ps"] % 8
            state["ps"] += 1
            return i, tPS[i]

        wtiles = []
        wstate = {"issued": 0}
        tslot = [Tok(f"slot{i}") for i in range(NSLOT)]

        def w_in_cols(c0, ncols):
            return w_in.rearrange("(k p) c -> p k c", p=128)[:, :, c0:c0 + ncols]

        def add_wtile(pieces):
            off = 0
            lst = []
            for ap in pieces:
                kc, ncols = ap.shape[1], ap.shape[2]
                lst.append((ap, off, kc, ncols))
                off += kc * ncols
            assert off <= SLOT_E, off
            wtiles.append(lst)
            return len(wtiles) - 1

        def w_issue_upto(i):
            while wstate["issued"] <= min(i, len(wtiles) - 1):
                j = wstate["issued"]
                sl = j % NSLOT
                for pi, (ap, off, kc, ncols) in enumerate(wtiles[j]):
                    dst = slots[sl][:, off:off + kc * ncols].rearrange("p (k c) -> p k c", k=kc)
                    S.dma("pool", f"ws{sl}", dst, ap, writes=[tslot[sl]], nowait=(pi > 0))
                wstate["issued"] += 1

        def w_get(i):
            w_issue_upto(i + NSLOT - 2)
            sl = i % NSLOT
            views = []
            for (ap, off, kc, ncols) in wtiles[i]:
                views.append(slots[sl][:, off:off + kc * ncols].rearrange("p (k c) -> p k c", k=kc))
            return views, tslot[sl]

        WT = {}
        WT["wk0"] = add_wtile([xa_wk.rearrange("(k p) c -> p k c", p=128)[:, :, 0:256]])
        WT["wk1"] = add_wtile([xa_wk.rearrange("(k p) c -> p k c", p=128)[:, :, 256:512]])
        WT["wv0"] = add_wtile([xa_wv.rearrange("(k p) c -> p k c", p=128)[:, :, 0:256]])
        WT["wv1"] = add_wtile([xa_wv.rearrange("(k p) c -> p k c", p=128)[:, :, 256:512]])
        for hh in range(2):
            WT[f"q{hh}"] = add_wtile([w_in_cols(C_Q + hh * 256, 256)])
        for hh in range(2):
            WT[f"qg{hh}"] = add_wtile([w_in_cols(C_QG + hh * 256, 256)])
        for n in range(NCH + 1):
            if n < NCH:
                WT[f"A{n}"] = add_wtile([w_in_cols(C_LG + n * 128, 128), w_in_cols(C_LX + n * 128, 128)])
            if n >= 1:
                m_ = n - 1
                WT[f"B{m_}a"] = add_wtile([w_in_cols(C_SG + m_ * 128, 128), w_in_cols(C_SB + m_ * 128, 128)])
                WT[f"B{m_}b"] = add_wtile([w_in_cols(C_SCG + m_ * 128, 128), w_in_cols(C_SH + m_ * 128, 128)])
        lwo = lru_wo.rearrange("(k p) c -> p k c", p=128)
        swo = sconv_wo.rearrange("(k p) c -> p k c", p=128)
        xwo = xa_wo.rearrange("(k p) c -> p k c", p=128)
        for j in range(8):
            WT[f"Mg{j}"] = add_wtile([w_in_cols(C_MG + x * 1024 + j * 128, 128) for x in range(3)])
            WT[f"Mo{j}"] = add_wtile([lwo[:, :, j * 128:(j + 1) * 128], swo[:, :, j * 128:(j + 1) * 128],
                                      xwo[:, :, j * 128:(j + 1) * 128]])

        def mm_group(out_ap, pairs, reads, wtok, last_inc=True):
            n = len(pairs)
            for i, (l, r) in enumerate(pairs):
                S.op("pe", lambda e, l=l, r=r, i=i: e.matmul(out_ap, l, r, start=(i == 0), stop=(i == n - 1)),
                     reads=reads, writes=[wtok], inc=(last_inc and i == n - 1))

        def unit_half(lhs_list, rhs_arr, lo, reads):
            pp, tk = next_pp()
            nk = len(lhs_list)
            for b in range(2):
                pairs = [(lhs_list[k], rhs_arr[:, k, lo + b * 512: lo + (b + 1) * 512]) for k in range(nk)]
                mm_group(pp[:, b * 512:(b + 1) * 512], pairs, reads, tk, last_inc=(b == 1))
            return pp, tk

        tPSb = Tok("PSbank")
        tPS2 = [tPSb, tPX]

        def unit_samp(lhs_list, rhs_arr, reads):
            i = state.setdefault("ps2", 0) % 2
            state["ps2"] += 1
            tk = tPS2[i]
            nk = len(lhs_list)
            bank = PS if i == 0 else PX
            out_ap = bank[:, 0:TS]
            pairs = [(lhs_list[k], rhs_arr[:, k, T:TT]) for k in range(nk)]
            mm_group(out_ap, pairs, reads, tk)
            return out_ap, tk

        HALVES = [(0, 1024), (1024, 2048)]

        def dump(items, off_b=56320):
            dstage = carve(off_b, 4096, F32)
            S.barrier()
            S.op("dve", lambda e: e.memset(dstage[:], 0.0))
            S.barrier()
            off = 0
            for ap in items:
                P_, n_ = ap.shape[0], ap.shape[1]
                S.op("dve", lambda e, ap=ap, off=off, P_=P_, n_=n_: e.tensor_copy(dstage[0:P_, off:off + n_], ap))
                off += n_
            S.barrier()
            S.dma("sp", "out", dbg[:, :], dstage[:])
            S.barrier()

        t_ident = Tok("ident")
        S.op("pool", lambda e: e.memset(ident[:], 0.0), writes=[t_ident])
        S.op("pool", lambda e: e.affine_select(ident[:], ident[:], pattern=[[-1, 128]], compare_op=ALU.not_equal,
                                               fill=1.0, base=0, channel_multiplier=1),
             reads=[t_ident], writes=[t_ident])
        t_identb = Tok("identb")
        S.op("dve", lambda e: e.tensor_copy(identb[:], ident[:]), reads=[t_ident], writes=[t_identb])
        t_stat = Tok("stat")
        t_sm = Tok("sm")
        t_st3 = t_sm
        S.op("pool", lambda e: e.memset(stat[:], 0.0), writes=[t_stat])
        S.op("pool", lambda e: e.memset(stat2[:], 0.0), writes=[t_sm])
        t_ones = Tok("ones")
        S.op("pool", lambda e: e.memset(onesb[:], 1.0), writes=[t_ones])
        t_eps = Tok("eps")
        S.op("pool", lambda e: e.memset(epst[:], EPS), writes=[t_eps])
        S.op("pool", lambda e: e.memset(q25[:], 0.25), writes=[t_eps])

        prt = carve(0, 128, F32)[0:82, :]
        sin = carve(512, 4608, F32)
        NXB = 8
        xbuf = [carve(18944 + i * 4096, 1024, F32) for i in range(3)] + [carve(53760 + i * 4096, 1024, F32) for i in range(5)]
        xnb = [carve(31232 + i * 2048, 1024, BF16) for i in range(2)]
        junk = carve(35328, 1024, BF16)
        mnT_t = carve(37376, KC * NM, BF16)
        mnT = mnT_t.rearrange("p (k t) -> p k t", k=KC)
        kT_t = carve(41472, 4 * NM, BF16)
        kT = kT_t.rearrange("p (k t) -> p k t", k=4)
        vb_t = carve(43520, 2 * XW, BF16)
        vb = vb_t.rearrange("p (k t) -> p k t", k=2)
        kvout = [carve(45568 + i * 4096, 2 * XW, F32).rearrange("p (k t) -> p k t", k=2) for i in range(2)]

        t_prt = Tok("prt")

        def rows(v, r):
            return v.rearrange("(r c) -> r c", c=128)

        plist = [(norm_g, 0, 8, None), (mem_norm_g, 8, 8, None),
                 (lru_conv_w, 16, 24, "k (n c) -> (k n) c"), (lru_conv_b, 40, 6, None), (lru_ba, 46, 6, None),
                 (lru_bx, 52, 6, None), (lru_lambda, 58, 6, None), (sconv_w, 64, 18, "k (n c) -> (k n) c")]
        for (v, r0, nr, pat) in plist:
            src = v.rearrange(pat, c=128) if pat else v.rearrange("(r c) -> r c", c=128)
            S.dma("sp", "cst", prt[r0:r0 + nr, :], src, writes=[t_prt], nowait=True)
        if STOP_AFTER == "P0dma":
            S.barrier()
            S.finish(block)
            return nc
        t_pT = Tok("pT")
        S.op("pe", lambda e: e.transpose(PX[:, 0:82], prt, ident[0:82, 0:82]), reads=[t_prt, t_ident], writes=[tPX])
        S.op("act", lambda e: e.copy(out=pT[:], in_=PX[:, 0:82]), reads=[tPX], writes=[t_pT])
        if STOP_AFTER == "P0b":
            S.barrier()
            S.finish(block)
            return nc
        t_x = [Tok(f"x{i}") for i in range(NXB)]
        t_xn = [Tok(f"xn{i}") for i in range(2)]
        t_junk = Tok("junk")
        t_uT = Tok("uT")
        t_mnT = Tok("mnT")
        tiles = [(memp[i * 128:(i + 1) * 128, :], 128, "m", i * 128) for i in range(2)]
        tiles += [(xp[i * 128:(i + 1) * 128, :], 128, "u", i * 128) for i in range(16)]
        tiles.append((xs[:, :], TS, "u", T))
        xsem = ["xa", "xb", "xc", "ya", "yb", "ka", "kb", "va"]
        tPXh = [Tok("PXh0"), Tok("PXh1")]
        t_st0 = [Tok(f"st0_{i}") for i in range(len(tiles))]

        def p0_stageA(ti):
            src, nr, kind, c0 = tiles[ti]
            xb_, tx = xbuf[ti % NXB], t_x[ti % NXB]
            S.dma("sp", xsem[ti % NXB], xb_[0:nr, :], src, writes=[tx])
            S.op("act", lambda e, xb_=xb_, nr=nr, ti=ti: e.activation(out=junk[0:nr, :], in_=xb_[0:nr, :], func=AF.Square,
                                                                      accum_out=stat[0:nr, ti:ti + 1]),
                 reads=[tx, t_stat], writes=[t_st0[ti], t_junk])

        def p0_stageB(ti):
            src, nr, kind, c0 = tiles[ti]
            xb_, tx = xbuf[ti % NXB], t_x[ti % NXB]
            xn_, txn = xnb[ti % 2], t_xn[ti % 2]
            S.op("act", lambda e, nr=nr, ti=ti: e.activation(out=stat[0:nr, ti:ti + 1], in_=stat[0:nr, ti:ti + 1], func=AF.Sqrt,
                                                             scale=1.0 / D, bias=epst[0:nr, 0:1]),
                 reads=[t_st0[ti], t_eps], writes=[t_st0[ti]])
            S.op("dve", lambda e, nr=nr, ti=ti: e.reciprocal(out=stat[0:nr, ti:ti + 1], in_=stat[0:nr, ti:ti + 1]),
                 reads=[t_st0[ti]], writes=[t_st0[ti]])
            S.op("dve", lambda e, xb_=xb_, xn_=xn_, nr=nr, ti=ti: e.tensor_scalar(out=xn_[0:nr, :], in0=xb_[0:nr, :],
                                                                                scalar1=stat[0:nr, ti:ti + 1], scalar2=None,
                                                                                op0=ALU.mult),
                 reads=[tx, t_st0[ti]], writes=[txn])

        def p0_stageC(ti):
            src, nr, kind, c0 = tiles[ti]
            xn_, txn = xnb[ti % 2], t_xn[ti % 2]
            bankb, tbank = (PXb, tPX) if ti % 2 == 0 else (PSb, tPSb)
            for kc in range(KC):
                S.op("pe", lambda e, xn_=xn_, nr=nr, kc=kc, bankb=bankb: e.transpose(bankb[:, kc * 128: kc * 128 + nr],
                                                                        xn_[0:nr, kc * 128:(kc + 1) * 128],
                                                                        identb[0:nr, 0:nr]),
                     reads=[txn, t_identb], writes=[tbank], inc=(kc == KC - 1))
            pview = bankb.rearrange("p (k t) -> p k t", k=KC)[:, :, 0:nr]
            if kind == "u":
                dst = uT[:, :, c0:c0 + nr]
                gcols = pT[:, 0:8]
                tdst = t_uTi[ti]
            else:
                dst = mnT[:, :, c0:c0 + nr]
                gcols = pT[:, 8:16]
                tdst = t_mnT
            S.op("dve", lambda e, dst=dst, pview=pview, gcols=gcols, nr=nr: e.tensor_tensor(
                out=dst, in0=pview, in1=gcols.unsqueeze(2).to_broadcast([128, KC, nr]), op=ALU.mult),
                reads=[tbank, t_pT], writes=[tdst])

        t_kT = Tok("kT")
        t_vb = Tok("vb")
        t_kvout = [Tok("kvo0"), Tok("kvo1")]
        kvst = {}

        def kv_mm():
            for which in range(2):
                wv_, wt_ = [], []
                for hh in range(2):
                    vws, tk = w_get(WT[("wk" if which == 0 else "wv") + str(hh)])
                    wv_.append(vws[0])
                    wt_.append(tk)
                pp, tk = next_pp()
                for mc in range(2):
                    for hh in range(2):
                        pairs = [(mnT[:, k, mc * 128:(mc + 1) * 128], wv_[hh][:, k, :]) for k in range(KC)]
                        mm_group(pp[:, mc * 512 + hh * 256: mc * 512 + (hh + 1) * 256], pairs, [t_mnT, wt_[hh]], tk,
                                 last_inc=(mc == 1 and hh == 1))
                kvst[which] = (pp, tk)
                if which == 0:
                    pp2, tk2 = next_pp()
                    for dc in range(4):
                        hh, off = dc // 2, (dc % 2) * 128
                        pairs = [(wv_[hh][:, k, off:off + 128], mnT[:, k, :]) for k in range(KC)]
                        mm_group(pp2[:, dc * 256:(dc + 1) * 256], pairs, [t_mnT, wt_[hh]], tk2, last_inc=(dc == 3))
                    kvst["kT"] = (pp2, tk2)

        def kv_evac():
            for which in range(2):
                pp, tk = kvst[which]
                ko = kvout[which]
                S.op("act", lambda e, ko=ko, pp=pp: e.copy(out=ko, in_=pp[:].rearrange("p (k t) -> p k t", k=2)),
                     reads=[tk], writes=[t_kvout[which]])
                dsto = (o_pk if which == 0 else o_pv).rearrange("(k p) c -> p k c", p=128)
                S.dma("sp", "out", dsto, ko, reads=[t_kvout[which]])
            pp, tk = kvst[1]
            S.op("act", lambda e, pp=pp: e.copy(out=vb, in_=pp[:].rearrange("p (k t) -> p k t", k=2)),
                 reads=[tk], writes=[t_vb])
            pp2, tk2 = kvst["kT"]
            S.op("dve", lambda e, pp2=pp2: e.tensor_copy(kT, pp2[:].rearrange("p (k t) -> p k t", k=4)),
                 reads=[tk2], writes=[t_kT])

        t_uTi = [Tok(f"uT{i}") for i in range(len(tiles))]
        PD = 5
        p0_stageA(0)
        p0_stageB(0)
        for ti in range(1, PD):
            p0_stageA(ti)
        t_sin = Tok("sin")
        S.dma("sp", "cs2", sin[0:TS, 0:2304], st_lc[:, :], writes=[t_sin], nowait=True)
        S.dma("sp", "cs2", sin[0:TS, 2304:3072], st_h[:, :], writes=[t_sin], nowait=True)
        S.dma("sp", "cs2", sin[0:TS, 3072:4608], st_sc[:, :], writes=[t_sin], nowait=True)
        t_out = Tok("out")
        o_slc3 = o_slc.rearrange("b (k c) -> b k c", k=3)
        st_lc3 = st_lc.rearrange("b (k c) -> b k c", k=3)
        S.dma("sp", "out", o_slc3[:, 0:2, :], st_lc3[:, 1:3, :])
        o_ssc3 = o_ssc.rearrange("b (k c) -> b k c", k=2)
        st_sc3 = st_sc.rearrange("b (k c) -> b k c", k=2)
        S.dma("sp", "out", o_ssc3[:, 0:1, :], st_sc3[:, 1:2, :])

        t_dv = Tok("dv")
        t_SIT = Tok("SIT")
        qT_t = carve(0, 4 * T, BF16)
        qT = qT_t.rearrange("p (k t) -> p k t", k=4)
        t_qT = [Tok(f"qT{h}") for h in range(4)]

        def sit():
            blocks_ = []
            for n in range(NCH):
                for k in range(3):
                    blocks_.append((n * 6 + k, k * W + n * 128))
                blocks_.append((n * 6 + 3, 2304 + n * 128))
                for k in range(2):
                    blocks_.append((n * 6 + 4 + k, 3072 + k * W + n * 128))
            for g0 in range(0, 36, 18):
                grp = blocks_[g0:g0 + 18]
                for gi, (slot_i, c0) in enumerate(grp):
                    S.op("pe", lambda e, gi=gi, c0=c0: e.transpose(PX[:, gi * TS:(gi + 1) * TS], sin[0:TS, c0:c0 + 128],
                                                                  ident[0:TS, 0:TS]),
                         reads=[t_sin, t_ident], writes=[tPX], inc=(gi == len(grp) - 1))
                assert [b_[0] for b_ in grp] == list(range(g0, g0 + 18))
                S.op("act", lambda e, g0=g0: e.copy(out=SIT[:, g0:g0 + 18, :],
                                                    in_=PX[:, 0:18 * TS].rearrange("p (a b) -> p a b", a=18)),
                     reads=[tPX], writes=[t_SIT])


        def dv_derive():
            S.op("act", lambda e: e.activation(out=dv[:, 0:6], in_=pT[:, 58:64], func=AF.Exp, scale=-1.0),
                 reads=[t_pT], writes=[t_dv])
            S.op("act", lambda e: e.activation(out=dv[:, 6:12], in_=dv[:, 0:6], func=AF.Ln, bias=1.0, scale=1.0),
                 reads=[t_dv], writes=[t_dv])
            S.op("dve", lambda e: e.tensor_scalar(out=dv[:, 0:6], in0=dv[:, 6:12], scalar1=-4.0, scalar2=None, op0=ALU.mult),
                 reads=[t_dv], writes=[t_dv])
            S.op("dve", lambda e: e.tensor_scalar(out=dv[:, 6:12], in0=dv[:, 6:12], scalar1=-8.0, scalar2=None, op0=ALU.mult),
                 reads=[t_dv], writes=[t_dv])
            S.op("dve", lambda e: e.tensor_scalar(out=dv[:, 12:24], in0=pT[:, 46:58], scalar1=0.5, scalar2=None, op0=ALU.mult),
                 reads=[t_pT, t_dv], writes=[t_dv])

        eqst = {}

        def eq_bank(h, b):
            rd_u = [t_uTi[i] for i in range(2, 10)]
            hh, hl = h // 2, h % 2
            vws, wtk = w_get(WT[f"q{hh}"])
            wq = vws[0]
            if b == 0:
                eqst[h] = next_pp()
            pp, tk = eqst[h]
            pairs = [(wq[:, k, hl * 128:(hl + 1) * 128], uT[:, k, b * 512:(b + 1) * 512]) for k in range(KC)]
            mm_group(pp[:, b * 512:(b + 1) * 512], pairs, [wtk] + rd_u, tk, last_inc=True)

        def eq_evac(heads):
            for h in heads:
                pp, tk = eqst[h]
                S.op("act", lambda e, pp=pp, h=h: e.copy(out=qT[:, h, 0:1024], in_=pp[:]),
                     reads=[tk], writes=[t_qT[h], t_sin, t_prt])

        for ti in range(len(tiles)):
            if ti + PD < len(tiles):
                p0_stageA(ti + PD)
            if ti + 1 < len(tiles):
                p0_stageB(ti + 1)
            p0_stageC(ti)
            if ti == 1:
                kv_mm()
            if ti == 5:
                kv_evac()
            if ti == 6:
                sit()
            if 9 <= ti <= 16:
                eq_bank((ti - 9) // 2, (ti - 9) % 2)
            if ti in (12, 14, 16):
                eq_evac([(ti - 12) // 2])
            if ti == 18:
                eq_evac([3])

        dv_derive()
        if STOP_AFTER == "P0c":
            dump([pT[:], stat[:, 0:19], uT[:, 0, 0:256], uT[:, 7, T - 128:TT], mnT[:, 0, :], mnT[:, 7, :]])
            S.barrier()
            S.finish(block)
            return nc
        S.barrier(skip=("out",))
        if STOP_AFTER == "KV":
            S.finish(block)
            return nc

        qT_t = carve(0, 4 * T, BF16)
        qT = qT_t.rearrange("p (k t) -> p k t", k=4)
        qs_tok = carve(16384, XW, BF16)
        sqg_tok = carve(17408, XW, F32)
        pTe = [carve(19456 + i * 2048, 1024, BF16).rearrange("p (k t) -> p k t", k=2) for i in range(2)]
        rden = [carve(23552 + i * 2048, 512, F32) for i in range(2)]
        o1b = [carve(27648 + i * 2048, 512, F32) for i in range(2)]
        selb_t = carve(31744, TS * 128, BF16)
        selb = selb_t.rearrange("p (b c) -> p b c", b=TS)
        eye16_t = carve(35840, 256, F32)
        eye16 = eye16_t.rearrange("p (a b) -> p a b", a=16)
        Sall_t = carve(45568, 128, F32)
        Sall = Sall_t.rearrange("p (c b h) -> p c b h", c=2, b=TS)
        Esm = carve(46080, 256, F32)
        Psm = carve(47104, 256, F32)
        Mk_t = carve(48128, 2 * 4 * 16 * 16, BF16)
        Mk = Mk_t.rearrange("p (c h b q) -> p c h b q", c=2, h=4, b=16)
        qb_sb = [carve(56320 + i * 2048, 512, F32) for i in range(2)]
        Kb = [carve(60416 + i * 4096, 1024, F32).rearrange("p (c f) -> p c f", c=2) for i in range(2)]
        prod = carve(68608, 1024, F32).rearrange("p (c f) -> p c f", c=2)

        t_og = [Tok(f"og{h}") for h in range(4)]
        t_qs = Tok("qs")
        t_sqg = Tok("sqg")
        def next_pp_c():
            while True:
                i = state["pp"] % 3
                state["pp"] += 1
                if i != state.get("pp_excl", -1):
                    return PP[i], tPP[i]

        def unit_half_c(lhs_list, rhs_arr, lo, reads):
            pp, tk = next_pp_c()
            nk = len(lhs_list)
            for b in range(2):
                pairs = [(lhs_list[k], rhs_arr[:, k, lo + b * 512: lo + (b + 1) * 512]) for k in range(nk)]
                mm_group(pp[:, b * 512:(b + 1) * 512], pairs, reads, tk, last_inc=(b == 1))
            return pp, tk

        t_pTe = [Tok("pTe0"), Tok("pTe1")]
        t_rden = [Tok("rden0"), Tok("rden1")]
        t_o1 = [Tok("o10"), Tok("o11")]

        def gen_C_prompt():
            for hh in range(2):
                vws, wtk = w_get(WT[f"q{hh}"])
                wq = vws[0]
                pairs = [(uT[:, k, T:TT], wq[:, k, :]) for k in range(KC)]
                mm_group(PS[0:TS, 0:256], pairs, [wtk], tPSb)
                S.op("act", lambda e, hh=hh: e.copy(out=qs_tok[0:TS, hh * 256:(hh + 1) * 256], in_=PS[0:TS, 0:256]),
                     reads=[tPSb], writes=[t_qs])
                yield
            for hh in range(2):
                vws, wtk = w_get(WT[f"q{hh}"])
                wq = vws[0]
                for hl in range(2):
                    h = hh * 2 + hl
                    lhs = [wq[:, k, hl * 128:(hl + 1) * 128] for k in range(KC)]
                    for (lo, hi) in HALVES[1:]:
                        pp, tk = unit_half_c(lhs, uT, lo, [wtk])
                        S.op("act", lambda e, pp=pp, h=h, lo=lo, hi=hi: e.copy(out=qT[:, h, lo:hi], in_=pp[:]),
                             reads=[tk], writes=[t_qT[h]])
                        yield
            for hh in range(2):
                vws, wtk = w_get(WT[f"qg{hh}"])
                wq = vws[0]
                for hl in range(2):
                    h = hh * 2 + hl
                    lhs = [wq[:, k, hl * 128:(hl + 1) * 128] for k in range(KC)]
                    for (lo, hi) in HALVES:
                        pp, tk = unit_half_c(lhs, uT, lo, [wtk])
                        S.op("act", lambda e, pp=pp, h=h, lo=lo, hi=hi: e.activation(out=og[:, h, lo:hi], in_=pp[:], func=AF.Silu),
                             reads=[tk], writes=[t_og[h]])
                        yield
                pairs = [(uT[:, k, T:TT], wq[:, k, :]) for k in range(KC)]
                mm_group(PS[0:TS, 0:256], pairs, [wtk], tPSb)
                S.op("act", lambda e, hh=hh: e.activation(out=sqg_tok[0:TS, hh * 256:(hh + 1) * 256],
                                                         in_=PS[0:TS, 0:256], func=AF.Silu),
                     reads=[tPSb], writes=[t_sqg])
                yield
            iters = [(tb, h) for tb in range(4) for h in range(4)]
            stage1 = {}

            def att_s1(i):
                tb, h = iters[i]
                c0 = tb * 512
                pe_, tpe = pTe[i % 2], t_pTe[i % 2]
                pp, tk = next_pp_c()
                for mc in range(2):
                    mm_group(pp[:, mc * 512:(mc + 1) * 512], [(kT[:, h, mc * 128:(mc + 1) * 128], qT[:, h, c0:c0 + 512])],
                             [t_kT, t_qT[h]], tk, last_inc=(mc == 1))
                S.op("act", lambda e, pp=pp, pe_=pe_: e.activation(out=pe_, in_=pp[:].rearrange("p (k t) -> p k t", k=2),
                                                                  func=AF.Exp, scale=SCALE),
                     reads=[tk], writes=[tpe])

            def att_s2(i):
                tb, h = iters[i]
                c0 = tb * 512
                pe_, tpe = pTe[i % 2], t_pTe[i % 2]
                rd, trd = rden[i % 2], t_rden[i % 2]
                o1, to1 = o1b[i % 2], t_o1[i % 2]
                pp2, tk2 = next_pp_c()
                mm_group(pp2[:, 0:512], [(vb[:, mc, h * 128:(h + 1) * 128], pe_[:, mc, :]) for mc in range(2)],
                         [t_vb, tpe], tk2, last_inc=False)
                mm_group(pp2[:, 512:1024], [(onesb[:], pe_[:, mc, :]) for mc in range(2)], [t_ones, tpe], tk2)
                S.op("act", lambda e, pp2=pp2, rd=rd: e.activation(out=rd, in_=pp2[:, 512:1024], func=AF.Ln), reads=[tk2], writes=[trd])
                S.op("act", lambda e, rd=rd: e.activation(out=rd, in_=rd, func=AF.Exp, scale=-1.0), reads=[trd], writes=[trd])
                S.op("dve", lambda e, pp2=pp2, rd=rd, o1=o1: e.tensor_tensor(out=o1, in0=pp2[:, 0:512], in1=rd, op=ALU.mult),
                     reads=[tk2, trd], writes=[to1])
                S.op("dve", lambda e, o1=o1, h=h, c0=c0: e.tensor_tensor(out=og[:, h, c0:c0 + 512], in0=o1,
                                                                        in1=og[:, h, c0:c0 + 512], op=ALU.mult),
                     reads=[to1, t_og[h]], writes=[t_og[h]])

            att_s1(0)
            for i in range(len(iters)):
                if i + 1 < len(iters):
                    att_s1(i + 1)
                att_s2(i)
                yield

        t_selb = Tok("selb")
        t_eye = Tok("eye")
        t_qb = [Tok("qb0"), Tok("qb1")]
        t_Kb = [Tok("Kb0"), Tok("Kb1")]
        t_prod = Tok("prod")
        t_Sall = Tok("Sall")
        t_E = Tok("E")
        t_P = Tok("P")
        t_Mk = Tok("Mk")
        ksem = ["ka", "kb"]
        vsem = ["sva", "svb", "svc", "svd", "sve", "svf"]

        def gen_C_sample():
            S.wait_dma(["dve", "act", "pool", "pe"], "out")
            S.op("dve", lambda e: e.tensor_copy(selb[0:TS, :, :], identb[0:TS, 0:TS].unsqueeze(2).to_broadcast([TS, TS, 128])),
                 reads=[t_identb], writes=[t_selb])
            S.op("pool", lambda e: e.memset(eye16_t, 0.0), writes=[t_eye])
            S.op("pool", lambda e: e.affine_select(eye16, eye16, pattern=[[1, 16], [-1, 16]], compare_op=ALU.not_equal,
                                                   fill=1.0, base=0, channel_multiplier=0),
                 reads=[t_eye], writes=[t_eye])
            yield
            for b in range(TS):
                kb_, tkb = Kb[b % 2], t_Kb[b % 2]
                qb_, tqb = qb_sb[b % 2], t_qb[b % 2]
                S.dma("sp", ksem[b % 2], kb_, ck[b].rearrange("(c m) f -> m c f", c=2), writes=[tkb])
                S.op("pe", lambda e, b=b: e.matmul(PX[:, 0:512], selb[0:TS, b, :], qs_tok[0:TS, :], start=True, stop=True),
                     reads=[t_selb, t_qs], writes=[tPX])
                S.op("act", lambda e, qb_=qb_: e.copy(out=qb_, in_=PX[:, 0:512]), reads=[tPX], writes=[tqb])
                S.op("pool", lambda e, kb_=kb_, qb_=qb_: e.tensor_tensor(out=prod, in0=kb_,
                                                                         in1=qb_.unsqueeze(1).to_broadcast([128, 2, 512]), op=ALU.mult),
                     reads=[tkb, tqb], writes=[t_prod])
                S.op("dve", lambda e, b=b: e.tensor_reduce(out=Sall[:, :, b, :],
                                                           in_=prod.rearrange("p c (h d) -> p c h d", h=4), axis=AX.X, op=ALU.add),
                     reads=[t_prod], writes=[t_Sall])
                yield
            for c in range(2):
                S.op("pe", lambda e, c=c: e.transpose(PX[0:64, c * 128:(c + 1) * 128],
                                                      Sall_t[:, c * 64:(c + 1) * 64], ident[:, :]),
                     reads=[t_Sall, t_ident], writes=[tPX], inc=(c == 1))
            S.op("dve", lambda e: e.tensor_reduce(out=stat2[0:64, 0:1], in_=PX[0:64, 0:256], axis=AX.X, op=ALU.max),
                 reads=[tPX], writes=[t_sm])
            S.op("dve", lambda e: e.tensor_scalar(out=stat2[0:64, 0:1], in0=stat2[0:64, 0:1], scalar1=-SCALE, scalar2=None, op0=ALU.mult),
                 reads=[t_sm], writes=[t_sm])
            S.op("act", lambda e: e.activation(out=Esm[0:64, :], in_=PX[0:64, 0:256], func=AF.Exp, scale=SCALE,
                                               bias=stat2[0:64, 0:1], accum_out=stat2[0:64, 1:2]),
                 reads=[tPX, t_sm], writes=[t_E, t_sm])
            S.op("dve", lambda e: e.reciprocal(out=stat2[0:64, 1:2], in_=stat2[0:64, 1:2]), reads=[t_sm], writes=[t_sm])
            S.op("dve", lambda e: e.tensor_scalar(out=Psm[0:64, :], in0=Esm[0:64, :], scalar1=stat2[0:64, 1:2], scalar2=None, op0=ALU.mult),
                 reads=[t_E, t_sm], writes=[t_P])
            for c in range(2):
                S.op("pe", lambda e, c=c: e.transpose(PX[:, c * 64:(c + 1) * 64], Psm[0:64, c * 128:(c + 1) * 128], ident[0:64, 0:64]),
                     reads=[t_P, t_ident], writes=[tPX], inc=(c == 1))
            for c in range(2):
                S.op("dve", lambda e, c=c: e.tensor_tensor(
                    out=Mk[:, c],
                    in0=PX[:, c * 64:(c + 1) * 64].rearrange("p (b h) -> p h b", h=4).unsqueeze(3).to_broadcast([128, 4, 16, 16]),
                    in1=eye16.unsqueeze(1).to_broadcast([128, 4, 16, 16]), op=ALU.mult),
                    reads=[tPX, t_eye], writes=[t_Mk])
            yield
            ppa, tka = next_pp_c()
            state["pp_excl"] = PP.index(ppa)
            hbank = [(PS, 0, tPSb), (PX, 0, tPX), (ppa, 0, tka), (ppa, 512, tka)]
            NVB = 6
            Vb = ([carve(68608 + i * 2048, 1024, BF16).rearrange("p (c f) -> p c f", c=2) for i in range(2)]
                  + [carve(60416 + i * 2048, 1024, BF16).rearrange("p (c f) -> p c f", c=2) for i in range(4)])
            t_Vb = [Tok(f"Vb{i}") for i in range(NVB)]
            valias = [[t_prod], [t_prod], [t_Kb[0]], [t_Kb[0]], [t_Kb[1]], [t_Kb[1]]]

            def v_load(b):
                S.dma("pool", vsem[b % NVB], Vb[b % NVB], cv[b].rearrange("(c m) f -> m c f", c=2),
                      writes=[t_Vb[b % NVB]] + (valias[b] if b < NVB else []))

            for b in range(NVB - 1):
                v_load(b)
            for b in range(TS):
                vb_, tvb = Vb[b % NVB], t_Vb[b % NVB]
                if b + NVB - 1 < TS:
                    v_load(b + NVB - 1)
                for h in range(4):
                    pph, coff, tkh = hbank[h]
                    for c in range(2):
                        first = (b == 0 and c == 0)
                        last = (b == TS - 1 and c == 1)
                        S.op("pe", lambda e, vb_=vb_, b=b, h=h, c=c, first=first, last=last, pph=pph, coff=coff: e.matmul(
                            pph[0:TS, coff:coff + 128], Mk[:, c, h, b, :], vb_[:, c, h * 128:(h + 1) * 128],
                            start=first, stop=last, skip_group_check=True),
                            reads=[t_Mk, tvb], writes=[tkh], inc=(h == 3 and c == 1))
                yield
            ogs_tok = qs_tok
            for h in range(4):
                pph, coff, tkh = hbank[h]
                S.op("dve", lambda e, h=h, pph=pph, coff=coff: e.tensor_tensor(
                    out=ogs_tok[0:TS, h * 128:(h + 1) * 128], in0=pph[0:TS, coff:coff + 128],
                    in1=sqg_tok[0:TS, h * 128:(h + 1) * 128], op=ALU.mult),
                    reads=[tkh, t_sqg, t_qs], writes=[t_qs])
            state["pp_excl"] = -1
            for h in range(4):
                S.op("pe", lambda e, h=h: e.transpose(PXb[:, h * TS:(h + 1) * TS], ogs_tok[0:TS, h * 128:(h + 1) * 128],
                                                      identb[0:TS, 0:TS]),
                     reads=[t_qs, t_identb], writes=[tPX], inc=(h == 3))
            S.op("act", lambda e: e.copy(out=og[:, :, T:TT], in_=PXb[:, 0:4 * TS].rearrange("p (h b) -> p h b", h=4)),
                 reads=[tPX], writes=t_og)
            yield

        gp, gs = gen_C_prompt(), gen_C_sample()
        for _ in range(2):
            next(gp)
        alive = [gp, gs]
        while alive:
            for g in list(alive):
                try:
                    next(g)
                except StopIteration:
                    alive.remove(g)
        if STOP_AFTER == "C":
            dump([og[:, 0, 0:512], og[:, 3, T - 512:T], og[:, 0, T:TT], og[:, 1, T:TT], og[:, 2, T:TT], og[:, 3, T:TT], qT[:, 0, 0:256]], off_b=0)
        S.barrier(no_wait=("pe",))
        if STOP_AFTER == "C":
            S.finish(block)
            return nc

        wab_t = carve(0, 2 * NCH * 128, BF16)
        wab = wab_t.rearrange("p (a n d) -> p a n d", a=2, n=NCH)
        lxp = carve(3072, 3 + T, F32)
        lxs_t = carve(11280, 4 * TS, F32)
        lxs = lxs_t.rearrange("p (k b) -> p k b", k=4)
        xc = carve(11536, TT, F32)
        xcb = carve(19792, TT, BF16)
        thr = carve(23920, TT, F32)
        a2b = carve(32176, TT, F32)
        thi = carve(40432, TT, F32)
        scg_sb = carve(48752, TT, F32)
        cy = scg_sb
        cinp = carve(57008, 2 + T, F32)
        cins_t = carve(65208, 3 * TS, F32)
        cins = cins_t.rearrange("p (k b) -> p k b", k=3)
        so_tok = carve(65400, W, F32)
        t_wab = Tok("wab")
        S.dma("pool", "cs3", wab[:, 0], lru_wa.rearrange("n c d -> c n d"), writes=[t_wab])
        S.dma("pool", "cs3", wab[:, 1], lru_wx.rearrange("n c d -> c n d"), writes=[t_wab], nowait=True)
        t_lxp, t_lxs, t_xc, t_xcb, t_thr, t_a2, t_thi = (Tok("lxp"), Tok("lxs"), Tok("xc"), Tok("xcb"), Tok("thr"),
                                                        Tok("a2"), Tok("thi"))
        t_gA = [Tok(f"gA{n}") for n in range(NCH)]
        t_SO = Tok("SO")
        t_scg, t_cinp, t_cins = Tok("scg"), Tok("cinp"), Tok("cins")
        t_cy = t_scg
        t_gB = [Tok(f"gB{n}") for n in range(NCH)]
        S.op("pool", lambda e: e.memset(lxp[:, 0:3], 0.0), writes=[t_lxp])
        S.op("pool", lambda e: e.memset(cinp[:, 0:2], 0.0), writes=[t_cinp])

        def gen_A(n):
            vws, wtk = w_get(WT[f"A{n}"])
            wlg, wlx = vws
            lhs_lg = [wlg[:, k, :] for k in range(KC)]
            lhs_lx = [wlx[:, k, :] for k in range(KC)]
            for (lo, hi) in HALVES:
                pp, tk = unit_half(lhs_lg, uT, lo, [wtk])
                S.op("act", lambda e, pp=pp, n=n, lo=lo, hi=hi: e.activation(out=gA[:, n, lo:hi], in_=pp[:], func=AF.Silu),
                     reads=[tk], writes=[t_gA[n]])
            sp_, tk = unit_samp(lhs_lg, uT, [wtk])
            S.op("act", lambda e, sp_=sp_, n=n: e.activation(out=gA[:, n, T:TT], in_=sp_, func=AF.Silu),
                 reads=[tk], writes=[t_gA[n]])
            yield
            for (lo, hi) in HALVES:
                pp, tk = unit_half(lhs_lx, uT, lo, [wtk])
                S.op("dve", lambda e, pp=pp, lo=lo, hi=hi: e.tensor_copy(lxp[:, 3 + lo:3 + hi], pp[:]),
                     reads=[tk], writes=[t_lxp])
            sp_, tk = unit_samp(lhs_lx, uT, [wtk])
            S.op("dve", lambda e, n=n: e.tensor_copy(lxs[:, 0:3, :], SIT[:, n * 6:n * 6 + 3, :]), reads=[t_SIT], writes=[t_lxs])
            S.op("dve", lambda e, sp_=sp_: e.tensor_copy(lxs[:, 3, :], sp_), reads=[tk], writes=[t_lxs])
            yield
            cw = lambda k, n=n: pT[:, 16 + k * 6 + n: 17 + k * 6 + n]
            cbias = pT[:, 40 + n:41 + n]
            S.op("act", lambda e, cw=cw, cbias=cbias: e.activation(out=xc[:, 0:T], in_=lxp[:, 0:T], func=AF.Identity,
                                                                   scale=cw(0), bias=cbias),
                 reads=[t_lxp, t_pT], writes=[t_xc])
            S.op("act", lambda e, cw=cw, cbias=cbias: e.activation(out=xc[:, T:TT], in_=lxs[:, 0, :], func=AF.Identity,
                                                                   scale=cw(0), bias=cbias),
                 reads=[t_lxs, t_pT], writes=[t_xc])
            for k in range(1, 4):
                S.op("dve", lambda e, k=k, cw=cw: e.scalar_tensor_tensor(out=xc[:, 0:T], in0=lxp[:, k:k + T], scalar=cw(k),
                                                                         in1=xc[:, 0:T], op0=ALU.mult, op1=ALU.add),
                     reads=[t_lxp, t_xc], writes=[t_xc])
                S.op("dve", lambda e, k=k, cw=cw: e.scalar_tensor_tensor(out=xc[:, T:TT], in0=lxs[:, k, :], scalar=cw(k),
                                                                         in1=xc[:, T:TT], op0=ALU.mult, op1=ALU.add),
                     reads=[t_lxs, t_xc], writes=[t_xc])
            S.op("pool", lambda e, n=n: e.tensor_copy(SO[:, n, 0:3], lxp[:, T:T + 3]), reads=[t_lxp], writes=[t_SO])
            S.op("pool", lambda e, n=n: e.tensor_copy(SO[:, n, 6:22], lxs[:, 3, :]), reads=[t_lxs], writes=[t_SO])
            yield
            S.op("act", lambda e: e.copy(out=xcb, in_=xc), reads=[t_xc], writes=[t_xcb])
            yield
            for gi, (dst, tdst, bcol) in enumerate([(thr, t_thr, 12 + n), (thi, t_thi, 18 + n)]):
                lhs = [wab[:, gi, n, :]]
                xcb3 = xcb.unsqueeze(1)
                for (lo, hi) in HALVES:
                    pp, tk = unit_half(lhs, xcb3, lo, [t_wab, t_xcb])
                    S.op("act", lambda e, pp=pp, dst=dst, lo=lo, hi=hi, bcol=bcol: e.activation(
                        out=dst[:, lo:hi], in_=pp[:], func=AF.Tanh, scale=0.5, bias=dv[:, bcol:bcol + 1]),
                        reads=[tk, t_dv], writes=[tdst])
                sp_, tk = unit_samp(lhs, xcb3, [t_wab, t_xcb])
                S.op("act", lambda e, sp_=sp_, dst=dst, bcol=bcol: e.activation(
                    out=dst[:, T:TT], in_=sp_, func=AF.Tanh, scale=0.5, bias=dv[:, bcol:bcol + 1]),
                    reads=[tk, t_dv], writes=[tdst])
                yield
            S.op("act", lambda e, n=n: e.activation(out=a2b, in_=thr, func=AF.Exp, scale=dv[:, 6 + n:7 + n], bias=dv[:, 6 + n:7 + n]),
                 reads=[t_thr, t_dv], writes=[t_a2])
            S.op("act", lambda e, n=n: e.activation(out=thr, in_=thr, func=AF.Exp, scale=dv[:, n:n + 1], bias=dv[:, n:n + 1]),
                 reads=[t_thr, t_dv], writes=[t_thr])
            S.op("dve", lambda e: e.tensor_scalar(out=a2b, in0=a2b, scalar1=1.0, scalar2=-1.0, op0=ALU.min, op1=ALU.mult),
                 reads=[t_a2], writes=[t_a2])
            yield
            S.op("act", lambda e: e.activation(out=a2b, in_=a2b, func=AF.Sqrt, bias=1.0, scale=1.0), reads=[t_a2], writes=[t_a2])
            S.op("dve", lambda e: e.scalar_tensor_tensor(out=a2b, in0=a2b, scalar=0.5, in1=xc, op0=ALU.mult, op1=ALU.mult),
                 reads=[t_a2, t_xc], writes=[t_a2])
            S.op("dve", lambda e: e.scalar_tensor_tensor(out=thi, in0=thi, scalar=1.0, in1=a2b, op0=ALU.add, op1=ALU.mult),
                 reads=[t_thi, t_a2], writes=[t_thi])
            yield
            S.op("dve", lambda e: e.tensor_tensor_scan(out=a2b[:, 0:T], data0=thr[:, 0:T], data1=thi[:, 0:T], initial=0.0,
                                                       op0=ALU.mult, op1=ALU.add),
                 reads=[t_thr, t_thi], writes=[t_a2])
            S.op("dve", lambda e, n=n: e.tensor_tensor(out=a2b[:, T:TT], in0=thr[:, T:TT], in1=SIT[:, n * 6 + 3, :], op=ALU.mult),
                 reads=[t_thr, t_SIT], writes=[t_a2])
            S.op("dve", lambda e: e.tensor_tensor(out=a2b[:, T:TT], in0=a2b[:, T:TT], in1=thi[:, T:TT], op=ALU.add),
                 reads=[t_a2, t_thi], writes=[t_a2])
            yield
            S.op("pool", lambda e, n=n: e.tensor_copy(SO[:, n, 3:4], a2b[:, T - 1:T]), reads=[t_a2], writes=[t_SO])
            S.op("pool", lambda e, n=n: e.tensor_copy(SO[:, n, 22:38], a2b[:, T:TT]), reads=[t_a2], writes=[t_SO])
            S.op("dve", lambda e, n=n: e.tensor_tensor(out=gA[:, n, :], in0=a2b, in1=gA[:, n, :], op=ALU.mult),
                 reads=[t_a2, t_gA[n]], writes=[t_gA[n]])
            yield

        def gen_B(n):
            vws, wtk = w_get(WT[f"B{n}a"])
            wsg, wsb = vws
            lhs_sg = [wsg[:, k, :] for k in range(KC)]
            lhs_sb = [wsb[:, k, :] for k in range(KC)]
            for (lo, hi) in HALVES:
                pp, tk = unit_half(lhs_sg, uT, lo, [wtk])
                S.op("act", lambda e, pp=pp, n=n, lo=lo, hi=hi: e.activation(out=gB[:, n, lo:hi], in_=pp[:], func=AF.Silu),
                     reads=[tk], writes=[t_gB[n]])
            sp_, tk = unit_samp(lhs_sg, uT, [wtk])
            S.op("act", lambda e, sp_=sp_, n=n: e.activation(out=gB[:, n, T:TT], in_=sp_, func=AF.Silu),
                 reads=[tk], writes=[t_gB[n]])
            yield
            for (lo, hi) in HALVES:
                pp, tk = unit_half(lhs_sb, uT, lo, [wtk])
                S.op("dve", lambda e, pp=pp, n=n, lo=lo, hi=hi: e.tensor_tensor(out=gB[:, n, lo:hi], in0=pp[:], in1=gB[:, n, lo:hi],
                                                                                op=ALU.mult),
                     reads=[tk, t_gB[n]], writes=[t_gB[n]])
            sp_, tk = unit_samp(lhs_sb, uT, [wtk])
            S.op("dve", lambda e, sp_=sp_, n=n: e.tensor_tensor(out=gB[:, n, T:TT], in0=sp_, in1=gB[:, n, T:TT], op=ALU.mult),
                 reads=[tk, t_gB[n]], writes=[t_gB[n]])
            yield
            vws, wtk = w_get(WT[f"B{n}b"])
            wscg, wsh = vws
            lhs_scg = [wscg[:, k, :] for k in range(KC)]
            lhs_sh = [wsh[:, k, :] for k in range(KC)]
            for (lo, hi) in HALVES:
                pp, tk = unit_half(lhs_scg, uT, lo, [wtk])
                S.op("act", lambda e, pp=pp, lo=lo, hi=hi: e.copy(out=scg_sb[:, lo:hi], in_=pp[:]), reads=[tk], writes=[t_scg])
            sp_, tk = unit_samp(lhs_scg, uT, [wtk])
            S.op("act", lambda e, sp_=sp_: e.copy(out=scg_sb[:, T:TT], in_=sp_), reads=[tk], writes=[t_scg])
            yield
            for (lo, hi) in HALVES:
                pp, tk = unit_half(lhs_sh, uT, lo, [wtk])
                S.op("dve", lambda e, pp=pp, lo=lo, hi=hi: e.tensor_tensor(out=cinp[:, 2 + lo:2 + hi], in0=pp[:],
                                                                           in1=scg_sb[:, lo:hi], op=ALU.mult),
                     reads=[tk, t_scg], writes=[t_cinp])
            sp_, tk = unit_samp(lhs_sh, uT, [wtk])
            S.op("dve", lambda e, n=n: e.tensor_copy(cins[:, 0:2, :], SIT[:, n * 6 + 4:n * 6 + 6, :]), reads=[t_SIT], writes=[t_cins])
            S.op("dve", lambda e, sp_=sp_: e.tensor_tensor(out=cins[:, 2, :], in0=sp_, in1=scg_sb[:, T:TT], op=ALU.mult),
                 reads=[tk, t_scg], writes=[t_cins])
            yield
            sw = lambda k, n=n: pT[:, 64 + k * 6 + n: 65 + k * 6 + n]
            S.op("act", lambda e, sw=sw: e.activation(out=cy[:, 0:T], in_=cinp[:, 0:T], func=AF.Identity, scale=sw(0)),
                 reads=[t_cinp, t_pT], writes=[t_cy])
            S.op("act", lambda e, sw=sw: e.activation(out=cy[:, T:TT], in_=cins[:, 0, :], func=AF.Identity, scale=sw(0)),
                 reads=[t_cins, t_pT], writes=[t_cy])
            for k in range(1, 3):
                S.op("dve", lambda e, k=k, sw=sw: e.scalar_tensor_tensor(out=cy[:, 0:T], in0=cinp[:, k:k + T], scalar=sw(k),
                                                                         in1=cy[:, 0:T], op0=ALU.mult, op1=ALU.add),
                     reads=[t_cinp, t_cy], writes=[t_cy])
                S.op("dve", lambda e, k=k, sw=sw: e.scalar_tensor_tensor(out=cy[:, T:TT], in0=cins[:, k, :], scalar=sw(k),
                                                                         in1=cy[:, T:TT], op0=ALU.mult, op1=ALU.add),
                     reads=[t_cins, t_cy], writes=[t_cy])
            S.op("pool", lambda e, n=n: e.tensor_copy(SO[:, n, 4:6], cinp[:, T:T + 2]), reads=[t_cinp], writes=[t_SO])
            S.op("pool", lambda e, n=n: e.tensor_copy(SO[:, n, 38:54], cins[:, 2, :]), reads=[t_cins], writes=[t_SO])
            S.op("dve", lambda e, n=n: e.tensor_tensor(out=gB[:, n, :], in0=cy, in1=gB[:, n, :], op=ALU.mult),
                 reads=[t_cy, t_gB[n]], writes=[t_gB[n]])
            yield

        def interleave(*gens):
            gens = list(gens)
            while gens:
                for g in list(gens):
                    try:
                        next(g)
                    except StopIteration:
                        gens.remove(g)

        gens = {}

        def adv(kind, n):
            g = gens.get((kind, n))
            if g is None:
                return
            try:
                next(g)
            except StopIteration:
                pass

        for n in range(NCH + 1):
            if n < NCH:
                gens[("A", n)] = gen_A(n)
            if n >= 1:
                gens[("B", n - 1)] = gen_B(n - 1)
            adv("A", n)
            adv("B", n - 1)
            adv("A", n - 1)
            adv("A", n)
            adv("B", n - 1)
            adv("A", n)
            adv("A", n - 1)
            adv("B", n - 1)
            adv("B", n - 1)
            adv("A", n - 1)
            adv("A", n)
            adv("A", n)
            adv("A", n)
            adv("B", n - 1)
            adv("A", n)
        for g in gens.values():
            for _ in g:
                pass
        if STOP_AFTER == "B":
            dump([gB[:, 0, 0:512], gB[:, 5, T - 512:T], gB[:, 0, T:TT], gB[:, 5, T:TT], gB[:, 2, 1024:1536]], off_b=0)
        t_sot = Tok("sot")
        for g0 in range(0, NCH, 3):
            for n in range(g0, g0 + 3):
                S.op("pe", lambda e, n=n, g0=g0: e.transpose(PX[0:54, (n - g0) * 128:(n - g0 + 1) * 128], SO[:, n, :], ident[:, :]),
                     reads=[t_SO, t_ident], writes=[tPX], inc=(n == g0 + 2))
            S.op("act", lambda e, g0=g0: e.copy(out=so_tok[0:54, g0 * 128:(g0 + 3) * 128], in_=PX[0:54, 0:384]),
                 reads=[tPX], writes=[t_sot])
        S.dma("sp", "out", o_plc[:, :], so_tok[0:3, :], reads=[t_sot])
        S.dma("sp", "out", o_ph[:, :], so_tok[3:4, :], reads=[t_sot])
        S.dma("sp", "out", o_psc[:, :], so_tok[4:6, :], reads=[t_sot])
        S.dma("sp", "out", o_slc3[:, 2, :], so_tok[6:22, :], reads=[t_sot])
        S.dma("sp", "out", o_sh[:, :], so_tok[22:38, :], reads=[t_sot])
        S.dma("sp", "out", o_ssc3[:, 1, :], so_tok[38:54, :], reads=[t_sot])
        if STOP_AFTER == "B":
            dump([gB[:, 0, 0:512], gB[:, 5, T - 512:T], gB[:, 0, T:TT], gB[:, 5, T:TT], gB[:, 2, 1024:1536]], off_b=56320)
        S.barrier(no_wait=("pe",))
        if STOP_AFTER == "B":
            S.finish(block)
            return nc

        NTH = 3
        thb = [carve(i * 4096, 1024, F32) for i in range(NTH)]
        tacc = [carve(12288 + i * 4096, 1024, F32) for i in range(2)]
        mT_t = carve(20480, KC * TT, BF16)
        mT = mT_t.rearrange("p (k t) -> p k t", k=KC)
        wo_t = carve(53504, KC * D, BF16)
        wo = wo_t.rearrange("p (k c) -> p k c", k=KC)
        t_wo = Tok("wo")
        if STOP_AFTER == "M0":
            S.barrier()
            S.finish(block)
            return nc
        t_th = [Tok(f"th{i}") for i in range(NTH)]
        t_acc = [Tok("acc0"), Tok("acc1")]
        t_mT = Tok("mT")
        thc = {"i": 0, "a": 0}
        gsrc = [(gA, NCH), (gB, NCH), (og, 4)]
        gtok = [t_gA, t_gB, t_og]
        ths_f = carve(69888, 3 * TS, F32)
        tzs_f = carve(70080, 3 * TS, F32)
        accs_f = carve(70272, TS, F32)
        t_accs2 = Tok("accs2")
        t_ths = Tok("ths")
        t_accs = Tok("accs")
        for j in range(8):
            vg, wtkg = w_get(WT[f"Mg{j}"])
            vo, wtko = w_get(WT[f"Mo{j}"])
            S.dma("pool", "cs4", wo[:, j, :], w_out[j * 128:(j + 1) * 128, :], writes=[t_wo], nowait=True)
            tMs_g, tMs_o = tPSb, tPX
            for x in range(3):
                lhs_g = [vg[x][:, k, :] for k in range(KC)]
                mm_group(PS[:, x * TS:(x + 1) * TS], [(lhs_g[k], uT[:, k, T:TT]) for k in range(KC)], [wtkg], tMs_g)
            S.op("act", lambda e: e.activation(out=ths_f, in_=PS[:, 0:3 * TS], func=AF.Tanh, scale=0.5),
                 reads=[tMs_g], writes=[t_ths])
            for x in range(3):
                garr, nk = gsrc[x]
                lhs_o = [vo[x][:, k, :] for k in range(nk)]
                mm_group(PX[:, x * TS:(x + 1) * TS], [(lhs_o[k], garr[:, k, T:TT]) for k in range(nk)], [wtko] + gtok[x], tMs_o)
            S.op("dve", lambda e: e.scalar_tensor_tensor(out=tzs_f, in0=ths_f, scalar=1.0, in1=PX[:, 0:3 * TS],
                                                         op0=ALU.add, op1=ALU.mult),
                 reads=[t_ths, tMs_o], writes=[t_accs])
            S.op("dve", lambda e: e.tensor_reduce(out=accs_f, in_=tzs_f.rearrange("p (x b) -> p b x", x=3),
                                                  axis=AX.X, op=ALU.add),
                 reads=[t_accs], writes=[t_accs2])
            S.op("dve", lambda e, j=j: e.tensor_copy(mT[:, j, T:TT], accs_f), reads=[t_accs2], writes=[t_mT])
            for (lo, hi) in HALVES:
                acc, tacc_ = tacc[thc["a"] % 2], t_acc[thc["a"] % 2]
                thc["a"] += 1
                for x in range(3):
                    garr, nk = gsrc[x]
                    lhs_g = [vg[x][:, k, :] for k in range(KC)]
                    lhs_o = [vo[x][:, k, :] for k in range(nk)]
                    th_, tth = thb[thc["i"] % NTH], t_th[thc["i"] % NTH]
                    thc["i"] += 1
                    pp, tk = unit_half(lhs_g, uT, lo, [wtkg])
                    S.op("act", lambda e, pp=pp, th_=th_: e.activation(out=th_, in_=pp[:], func=AF.Tanh, scale=0.5),
                         reads=[tk], writes=[tth])
                    zp_, tkz = unit_half(lhs_o, garr, lo, [wtko] + gtok[x])
                    zp = zp_[:]
                    if x == 0:
                        S.op("dve", lambda e, acc=acc, th_=th_, zp=zp: e.scalar_tensor_tensor(out=acc, in0=th_, scalar=1.0, in1=zp,
                                                                                           op0=ALU.add, op1=ALU.mult),
                             reads=[tth, tkz], writes=[tacc_])
                    else:
                        S.op("dve", lambda e, th_=th_, zp=zp: e.scalar_tensor_tensor(out=th_, in0=th_, scalar=1.0, in1=zp,
                                                                                    op0=ALU.add, op1=ALU.mult),
                             reads=[tth, tkz], writes=[tth])
                        if x == 1:
                            S.op("pool", lambda e, acc=acc, th_=th_: e.tensor_tensor(out=acc, in0=acc, in1=th_, op=ALU.add),
                                 reads=[tth, tacc_], writes=[tacc_])
                        else:
                            S.op("pool", lambda e, acc=acc, th_=th_, j=j, lo=lo, hi=hi: e.tensor_tensor(
                                out=mT[:, j, lo:hi], in0=acc, in1=th_, op=ALU.add),
                                reads=[tth, tacc_], writes=[t_mT])
            if STOP_AFTER == "M5":
                S.barrier()
                S.finish(block)
                return nc
        S.barrier(no_wait=("pe",))
        if STOP_AFTER == "M":
            S.finish(block)
            return nc

        xr = [carve(i * 4096, 1024, F32) for i in range(2)]
        yr = [carve(8192 + i * 4096, 1024, F32) for i in range(2)] + [carve(69888, 1024, F32)]
        fgb = carve(16384, 1024, F32)
        t_fgb = Tok("fgb")
        S.dma("sp", "cs5", fgb, final_norm_g.partition_broadcast(128), writes=[t_fgb])
        t_xr = [Tok("xr0"), Tok("xr1")]
        t_yr = [Tok("yr0"), Tok("yr1"), Tok("yr2")]
        ftiles = [(xp[i * 128:(i + 1) * 128, :], y_p[i * 128:(i + 1) * 128, :], 128, i * 128) for i in range(16)]
        ftiles.append((xs[:, :], y_s[:, :], TS, T))
        xsem2 = ["fxa", "fxb"]
        ysem = ["ya", "yb", "xc"]
        t_stF = [Tok(f"stF{i}") for i in range(len(ftiles))]

        def f_stageA(ti):
            src, dst, nr, c0 = ftiles[ti]
            xr_, txr = xr[ti % 2], t_xr[ti % 2]
            yr_, tyr = yr[ti % 3], t_yr[ti % 3]
            S.dma("pool", xsem2[ti % 2], xr_[0:nr, :], src, writes=[txr])
            pp, tk = next_pp()
            for b in range(2):
                pairs = [(mT[:, k, c0:c0 + nr], wo[:, k, b * 512:(b + 1) * 512]) for k in range(KC)]
                mm_group(pp[0:nr, b * 512:(b + 1) * 512], pairs, [t_mT, t_wo], tk, last_inc=(b == 1))
            S.op("dve", lambda e, pp=pp, yr_=yr_, xr_=xr_, nr=nr: e.scalar_tensor_tensor(
                out=yr_[0:nr, :], in0=pp[0:nr, :], scalar=0.5, in1=xr_[0:nr, :], op0=ALU.mult, op1=ALU.add),
                reads=[tk, txr], writes=[tyr])

        def f_stageA2(ti):
            src, dst, nr, c0 = ftiles[ti]
            xr_, txr = xr[ti % 2], t_xr[ti % 2]
            yr_, tyr = yr[ti % 3], t_yr[ti % 3]
            col = 32 + ti
            S.op("act", lambda e, xr_=xr_, yr_=yr_, nr=nr, col=col: e.activation(out=xr_[0:nr, :], in_=yr_[0:nr, :], func=AF.Square,
                                                                                accum_out=stat2[0:nr, col:col + 1]),
                 reads=[tyr, t_sm], writes=[txr, t_stF[ti]])

        def f_stageB1(ti):
            src, dst, nr, c0 = ftiles[ti]
            col = 32 + ti
            S.op("act", lambda e, nr=nr, col=col: e.activation(out=stat2[0:nr, col:col + 1], in_=stat2[0:nr, col:col + 1], func=AF.Sqrt,
                                                               scale=1.0 / D, bias=epst[0:nr, 0:1]),
                 reads=[t_stF[ti], t_eps], writes=[t_stF[ti]])

        def f_stageB(ti):
            src, dst, nr, c0 = ftiles[ti]
            yr_, tyr = yr[ti % 3], t_yr[ti % 3]
            col = 32 + ti
            S.op("dve", lambda e, nr=nr, col=col: e.reciprocal(out=stat2[0:nr, col:col + 1], in_=stat2[0:nr, col:col + 1]),
                 reads=[t_stF[ti]], writes=[t_stF[ti]])
            S.op("dve", lambda e, yr_=yr_, nr=nr, col=col: e.scalar_tensor_tensor(
                out=yr_[0:nr, :], in0=yr_[0:nr, :], scalar=stat2[0:nr, col:col + 1], in1=fgb[0:nr, :], op0=ALU.mult, op1=ALU.mult),
                reads=[tyr, t_stF[ti], t_fgb], writes=[tyr])
            S.dma("sp", ysem[ti % 3], dst, yr_[0:nr, :], reads=[tyr])

        f_stageA(0)
        f_stageA2(0)
        for ti in range(len(ftiles)):
            if ti + 1 < len(ftiles):
                f_stageA(ti + 1)
            f_stageB1(ti)
            if ti + 1 < len(ftiles):
                f_stageA2(ti + 1)
            f_stageB(ti)
        S.barrier()
        S.finish(block)
    return nc


_CACHE = {}


def _get_program():
    if "nc" not in _CACHE:
        _CACHE["nc"] = build_program()
    return _CACHE["nc"]


def kernel(x_prompt, x_sample, cache_mem_k, cache_mem_v, state_lru_h, state_lru_conv, state_sconv, mem_prompt,
           norm_g, mem_norm_g, w_in, lru_conv_w, lru_conv_b, lru_wa, lru_ba, lru_wx, lru_bx, lru_lambda, lru_wo,
           sconv_w, sconv_wo, xa_wk, xa_wv, xa_wo, w_out, final_norm_g):
    f = lambda a: np.ascontiguousarray(np.asarray(a, dtype=np.float32))
    shared = {
        "norm_g": f(norm_g[0]), "mem_norm_g": f(mem_norm_g[0]), "w_in": f(w_in[0]),
        "lru_conv_w": f(lru_conv_w[0]), "lru_conv_b": f(lru_conv_b[0]), "lru_wa": f(lru_wa[0]),
        "lru_ba": f(lru_ba[0]), "lru_wx": f(lru_wx[0]), "lru_bx": f(lru_bx[0]), "lru_lambda": f(lru_lambda[0]),
        "lru_wo": f(lru_wo[0]), "sconv_w": f(sconv_w[0]), "sconv_wo": f(sconv_wo[0]), "xa_wk": f(xa_wk[0]),
        "xa_wv": f(xa_wv[0]), "xa_wo": f(xa_wo[0]), "w_out": f(w_out[0]), "final_norm_g": f(final_norm_g),
    }
    in_maps = []
    for c in range(NCORES):
        sl = slice(c * TS, (c + 1) * TS)
        m = dict(shared)
        m["xp"] = f(x_prompt[c])
        m["xs"] = f(np.asarray(x_sample)[sl, 0, :])
        m["memp"] = f(mem_prompt[c])
        m["ck"] = f(np.asarray(cache_mem_k)[0, sl].reshape(TS, NM, XW))
        m["cv"] = f(np.asarray(cache_mem_v)[0, sl].reshape(TS, NM, XW))
        m["st_h"] = f(np.asarray(state_lru_h)[0, sl])
        m["st_lc"] = f(np.asarray(state_lru_conv)[0, sl].reshape(TS, 3 * W))
        m["st_sc"] = f(np.asarray(state_sconv)[0, sl].reshape(TS, 2 * W))
        in_maps.append(m)
    nc = _get_program()
    res = run_bass_kernel_spmd(nc, in_maps, core_ids=list(range(NCORES)))
    rs = res.results
    cat = lambda k: np.concatenate([np.asarray(r[k]) for r in rs], axis=0)
    y_prompt = np.stack([np.asarray(r["y_p"]) for r in rs], axis=0).astype(np.float32)
    y_sample = cat("y_s").reshape(NCORES * TS, 1, D).astype(np.float32)
    p_mk = np.stack([np.asarray(r["o_pk"]) for r in rs], axis=0).reshape(1, NCORES, NM, 4, 128).astype(np.float32)
    p_mv = np.stack([np.asarray(r["o_pv"]) for r in rs], axis=0).reshape(1, NCORES, NM, 4, 128).astype(np.float32)
    p_h = cat("o_ph").reshape(1, NCORES, W).astype(np.float32)
    p_lc = np.stack([np.asarray(r["o_plc"]) for r in rs], axis=0).reshape(1, NCORES, 3, W).astype(np.float32)
    p_sc = np.stack([np.asarray(r["o_psc"]) for r in rs], axis=0).reshape(1, NCORES, 2, W).astype(np.float32)
    s_h = cat("o_sh").reshape(1, NCORES * TS, W).astype(np.float32)
    s_lc = cat("o_slc").reshape(1, NCORES * TS, 3, W).astype(np.float32)
    s_sc = cat("o_ssc").reshape(1, NCORES * TS, 2, W).astype(np.float32)
    return (y_prompt, y_sample, p_mk, p_mv, p_h, p_lc, p_sc, s_h, s_lc, s_sc)
```

```python
import math
from contextlib import ExitStack

import numpy as np
import concourse.bass as bass
import concourse.mybir as mybir
from concourse.bass_utils import run_bass_kernel_spmd

F32 = mybir.dt.float32
BF16 = mybir.dt.bfloat16
AF = mybir.ActivationFunctionType
ALU = mybir.AluOpType
AX = mybir.AxisListType

NCORES = 8
T = 2048
TS = 16
TT = T + TS
D = 1024
KC = 8
W = 768
NCH = 6
NM = 256
XW = 512
IN_COLS = 8704
EPS = 1e-6
SCALE = 1.0 / math.sqrt(128.0)
STOP_AFTER = None

C_LX, C_LG, C_SB, C_SCG, C_SH, C_SG, C_Q, C_QG, C_MG = 0, 768, 1536, 2304, 3072, 3840, 4608, 5120, 5632


class Tok:
    __slots__ = ("name", "w", "r")

    def __init__(self, name=""):
        self.name = name
        self.w = None
        self.r = {}


class Stream:
    def __init__(self, key, sem):
        self.key = key
        self.sem = sem
        self.cnt = 0
        self.seen = {}
        self.ops = []
        self.pending = False


class Sched:
    def __init__(self, nc):
        self.nc = nc
        self.sems = {}
        self.streams = {}
        self.dma_cnt = {}

    def add_stream(self, key, sem):
        self.sems[key] = sem
        self.streams[key] = Stream(key, sem)

    def add_dma_sem(self, key, sem):
        self.sems[key] = sem
        self.dma_cnt[key] = 0

    def _needs(self, st, reads, writes):
        needs = {}

        def need(ev):
            if ev is None:
                return
            k, v = ev
            if needs.get(k, 0) < v:
                needs[k] = v
        for t in reads:
            need(t.w)
        for t in writes:
            need(t.w)
            for k, v in t.r.items():
                if k == st.key:
                    continue
                need((k, v))
        out = []
        for k, v in needs.items():
            if k == st.key and k == "pe":
                continue
            if st.seen.get(k, 0) < v:
                st.seen[k] = v
                out.append((k, v))
        return out

    def op(self, key, fn, reads=(), writes=(), inc=True):
        st = self.streams[key]
        waits = self._needs(st, reads, writes)
        if inc:
            st.cnt += 1
            st.pending = False
            ev = (key, st.cnt)
        else:
            st.pending = True
            ev = (key, st.cnt + 1)
        sems = self.sems
        sem = st.sem

        def run(eng, waits=waits, fn=fn, inc=inc):
            for k, v in waits:
                eng.wait_ge(sems[k], v)
            ins = fn(eng)
            if inc:
                ins.then_inc(sem, 1)
        st.ops.append(run)
        for t in writes:
            t.w = ev
            t.r = {}
        for t in reads:
            if t.r.get(key, 0) < ev[1]:
                t.r[key] = ev[1]

    def dma(self, key, semkey, out, in_, reads=(), writes=(), nowait=False, **kw):
        st = self.streams[key]
        waits = [] if nowait else self._needs(st, reads, writes)
        self.dma_cnt[semkey] += 16
        ev = (semkey, self.dma_cnt[semkey])
        sems = self.sems

        def run(eng, waits=waits):
            for k, v in waits:
                eng.wait_ge(sems[k], v)
            eng.dma_start(out=out, in_=in_, **kw).then_inc(sems[semkey], 16)
        st.ops.append(run)
        for t in writes:
            t.w = ev
            t.r = {}
        for t in reads:
            if t.r.get(semkey, 0) < ev[1]:
                t.r[semkey] = ev[1]

    def wait_dma(self, keys, semkey):
        v = self.dma_cnt[semkey]
        sems = self.sems
        for k in keys:
            st = self.streams[k]
            if v and st.seen.get(semkey, 0) < v:
                st.seen[semkey] = v
                st.ops.append(lambda eng, v=v: eng.wait_ge(sems[semkey], v))

    def barrier(self, skip=(), no_wait=()):
        targets = {}
        for k, st in self.streams.items():
            assert not st.pending
            if st.cnt:
                targets[k] = st.cnt
        for k, v in self.dma_cnt.items():
            if v and k not in skip:
                targets[k] = v
        sems = self.sems
        for k, st in self.streams.items():
            if k in no_wait:
                continue
            waits = []
            for tk, tv in targets.items():
                if st.seen.get(tk, 0) < tv:
                    st.seen[tk] = tv
                    waits.append((tk, tv))

            def run(eng, waits=waits):
                for kk, v in waits:
                    eng.wait_ge(sems[kk], v)
            st.ops.append(run)

    def finish(self, block):
        for key, st in self.streams.items():
            assert not st.pending, key
        ss = self.streams

        def mk(key):
            def body(eng):
                for o in ss[key].ops:
                    o(eng)
            return body
        block.gpsimd(mk("pool"))
        block.tensor(mk("pe"))
        block.scalar(mk("act"))
        block.vector(mk("dve"))
        block.sync(mk("sp"))


def build_program():
    nc = bass.Bass("TRN2", target_bir_lowering=False)

    def din(name, shape):
        return nc.dram_tensor(name, shape, F32, kind="ExternalInput").ap()

    def dout(name, shape):
        return nc.dram_tensor(name, shape, F32, kind="ExternalOutput").ap()

    xp = din("xp", [T, D])
    xs = din("xs", [TS, D])
    memp = din("memp", [NM, D])
    ck = din("ck", [TS, NM, XW])
    cv = din("cv", [TS, NM, XW])
    st_h = din("st_h", [TS, W])
    st_lc = din("st_lc", [TS, 3 * W])
    st_sc = din("st_sc", [TS, 2 * W])
    norm_g = din("norm_g", [D])
    mem_norm_g = din("mem_norm_g", [D])
    w_in = din("w_in", [D, IN_COLS])
    lru_conv_w = din("lru_conv_w", [4, W])
    lru_conv_b = din("lru_conv_b", [W])
    lru_wa = din("lru_wa", [NCH, 128, 128])
    lru_ba = din("lru_ba", [W])
    lru_wx = din("lru_wx", [NCH, 128, 128])
    lru_bx = din("lru_bx", [W])
    lru_lambda = din("lru_lambda", [W])
    lru_wo = din("lru_wo", [W, D])
    sconv_w = din("sconv_w", [3, W])
    sconv_wo = din("sconv_wo", [W, D])
    xa_wk = din("xa_wk", [D, XW])
    xa_wv = din("xa_wv", [D, XW])
    xa_wo = din("xa_wo", [XW, D])
    w_out = din("w_out", [D, D])
    final_norm_g = din("final_norm_g", [D])

    y_p = dout("y_p", [T, D])
    y_s = dout("y_s", [TS, D])
    o_pk = dout("o_pk", [NM, XW])
    o_pv = dout("o_pv", [NM, XW])
    o_ph = dout("o_ph", [1, W])
    o_plc = dout("o_plc", [3, W])
    o_psc = dout("o_psc", [2, W])
    o_sh = dout("o_sh", [TS, W])
    o_slc = dout("o_slc", [TS, 3 * W])
    o_ssc = dout("o_ssc", [TS, 2 * W])

    dbg = dout("dbg", [128, 4096]) if STOP_AFTER else None
    es = ExitStack()
    with es:
        def sb(name, shape, dt):
            return es.enter_context(nc.sbuf_tensor(name, shape, dt))

        uT_t = sb("uT", [128, KC * TT], BF16)
        uT = uT_t[:].rearrange("p (k t) -> p k t", k=KC)
        gA_t = sb("gA", [128, NCH * TT], BF16)
        gA = gA_t[:].rearrange("p (k t) -> p k t", k=NCH)
        gB_t = sb("gB", [128, NCH * TT], BF16)
        gB = gB_t[:].rearrange("p (k t) -> p k t", k=NCH)
        og_t = sb("og", [128, 4 * TT], BF16)
        og = og_t[:].rearrange("p (k t) -> p k t", k=4)
        NSLOT = 5
        SLOT_E = 3072
        slots = [sb(f"wslot{i}", [128, SLOT_E], BF16) for i in range(NSLOT)]
        ident = sb("ident", [128, 128], F32)
        identb = sb("identb", [128, 128], BF16)
        onesb = sb("onesb", [128, 128], BF16)
        pT = sb("pT", [128, 82], F32)
        dv = sb("dv", [128, 24], F32)
        SIT_t = sb("SIT", [128, 36 * TS], F32)
        SIT = SIT_t[:].rearrange("p (a b) -> p a b", a=36)
        SO_t = sb("SO", [128, NCH * 54], F32)
        SO = SO_t[:].rearrange("p (a b) -> p a b", a=NCH)
        stat = sb("stat", [128, 64], F32)
        stat2 = sb("stat2", [128, 64], F32)
        epst = sb("epst", [128, 1], F32)
        q25 = sb("q25", [128, 1], F32)
        RW = 18816
        R = sb("R", [128, RW], F32)

        def carve(off_b, nelem, dt):
            assert off_b % 4 == 0
            if dt == F32:
                assert off_b // 4 + nelem <= RW, (off_b, nelem)
                return R[:, off_b // 4: off_b // 4 + nelem]
            assert nelem % 2 == 0 and off_b // 4 + nelem // 2 <= RW, (off_b, nelem)
            return R[:, off_b // 4: off_b // 4 + nelem // 2].bitcast(BF16)

        PP = [es.enter_context(nc.psum_tensor(f"PP{i}", [128, 1024], F32)) for i in range(3)]
        PS = es.enter_context(nc.psum_tensor("PS", [128, 512], F32))
        PX = es.enter_context(nc.psum_tensor("PX", [128, 512], F32))
        tPP = [Tok(f"PP{i}") for i in range(3)]
        tPS = [Tok(f"PS{i}") for i in range(8)]
        tPX = Tok("PX")
        PSb = PS[:].bitcast(BF16)
        PXb = PX[:].bitcast(BF16)

        S = Sched(nc)
        for k in ["pe", "act", "dve", "pool", "sp"]:
            S.add_stream(k, es.enter_context(nc.semaphore("s_" + k)))
        dsem_names = (["cst", "cs2", "cs3", "cs4", "cs5", "sva", "svb", "svc", "svd", "sve", "svf", "fxa", "fxb", "out", "xa", "xb", "xc", "ya", "yb", "ka", "kb", "va", "vb"]
                      + [f"ws{i}" for i in range(NSLOT)])
        for k in dsem_names:
            S.add_dma_sem(k, es.enter_context(nc.semaphore("d_" + k)))
        block = es.enter_context(nc.Block())

        state = {"pp": 0, "ps": 0}

        def next_pp():
            i = state["pp"] % 3
            state["pp"] += 1
            return PP[i], tPP[i]

        def next_ps():
            i = state["ps"] % 8
            state["ps"] += 1
            return i, tPS[i]

        wtiles = []
        wstate = {"issued": 0}
        tslot = [Tok(f"slot{i}") for i in range(NSLOT)]

        def w_in_cols(c0, ncols):
            return w_in.rearrange("(k p) c -> p k c", p=128)[:, :, c0:c0 + ncols]

        def add_wtile(pieces):
            off = 0
            lst = []
            for ap in pieces:
                kc, ncols = ap.shape[1], ap.shape[2]
                lst.append((ap, off, kc, ncols))
                off += kc * ncols
            assert off <= SLOT_E, off
            wtiles.append(lst)
            return len(wtiles) - 1

        def w_issue_upto(i):
            while wstate["issued"] <= min(i, len(wtiles) - 1):
                j = wstate["issued"]
                sl = j % NSLOT
                for pi, (ap, off, kc, ncols) in enumerate(wtiles[j]):
                    dst = slots[sl][:, off:off + kc * ncols].rearrange("p (k c) -> p k c", k=kc)
                    S.dma("pool", f"ws{sl}", dst, ap, writes=[tslot[sl]], nowait=(pi > 0))
                wstate["issued"] += 1

        def w_get(i):
            w_issue_upto(i + NSLOT - 2)
            sl = i % NSLOT
            views = []
            for (ap, off, kc, ncols) in wtiles[i]:
                views.append(slots[sl][:, off:off + kc * ncols].rearrange("p (k c) -> p k c", k=kc))
            return views, tslot[sl]

        WT = {}
        WT["wk0"] = add_wtile([xa_wk.rearrange("(k p) c -> p k c", p=128)[:, :, 0:256]])
        WT["wk1"] = add_wtile([xa_wk.rearrange("(k p) c -> p k c", p=128)[:, :, 256:512]])
        WT["wv0"] = add_wtile([xa_wv.rearrange("(k p) c -> p k c", p=128)[:, :, 0:256]])
        WT["wv1"] = add_wtile([xa_wv.rearrange("(k p) c -> p k c", p=128)[:, :, 256:512]])
        for hh in range(2):
            WT[f"q{hh}"] = add_wtile([w_in_cols(C_Q + hh * 256, 256)])
        for hh in range(2):
            WT[f"qg{hh}"] = add_wtile([w_in_cols(C_QG + hh * 256, 256)])
        for n in range(NCH + 1):
            if n < NCH:
                WT[f"A{n}"] = add_wtile([w_in_cols(C_LG + n * 128, 128), w_in_cols(C_LX + n * 128, 128)])
            if n >= 1:
                m_ = n - 1
                WT[f"B{m_}a"] = add_wtile([w_in_cols(C_SG + m_ * 128, 128), w_in_cols(C_SB + m_ * 128, 128)])
                WT[f"B{m_}b"] = add_wtile([w_in_cols(C_SCG + m_ * 128, 128), w_in_cols(C_SH + m_ * 128, 128)])
        lwo = lru_wo.rearrange("(k p) c -> p k c", p=128)
        swo = sconv_wo.rearrange("(k p) c -> p k c", p=128)
        xwo = xa_wo.rearrange("(k p) c -> p k c", p=128)
        for j in range(8):
            WT[f"Mg{j}"] = add_wtile([w_in_cols(C_MG + x * 1024 + j * 128, 128) for x in range(3)])
            WT[f"Mo{j}"] = add_wtile([lwo[:, :, j * 128:(j + 1) * 128], swo[:, :, j * 128:(j + 1) * 128],
                                      xwo[:, :, j * 128:(j + 1) * 128]])

        def mm_group(out_ap, pairs, reads, wtok, last_inc=True):
            n = len(pairs)
            for i, (l, r) in enumerate(pairs):
                S.op("pe", lambda e, l=l, r=r, i=i: e.matmul(out_ap, l, r, start=(i == 0), stop=(i == n - 1)),
                     reads=reads, writes=[wtok], inc=(last_inc and i == n - 1))

        def unit_half(lhs_list, rhs_arr, lo, reads):
            pp, tk = next_pp()
            nk = len(lhs_list)
            for b in range(2):
                pairs = [(lhs_list[k], rhs_arr[:, k, lo + b * 512: lo + (b + 1) * 512]) for k in range(nk)]
                mm_group(pp[:, b * 512:(b + 1) * 512], pairs, reads, tk, last_inc=(b == 1))
            return pp, tk

        tPSb = Tok("PSbank")
        tPS2 = [tPSb, tPX]

        def unit_samp(lhs_list, rhs_arr, reads):
            i = state.setdefault("ps2", 0) % 2
            state["ps2"] += 1
            tk = tPS2[i]
            nk = len(lhs_list)
            bank = PS if i == 0 else PX
            out_ap = bank[:, 0:TS]
            pairs = [(lhs_list[k], rhs_arr[:, k, T:TT]) for k in range(nk)]
            mm_group(out_ap, pairs, reads, tk)
            return out_ap, tk

        HALVES = [(0, 1024), (1024, 2048)]

        def dump(items, off_b=56320):
            dstage = carve(off_b, 4096, F32)
            S.barrier()
            S.op("dve", lambda e: e.memset(dstage[:], 0.0))
            S.barrier()
            off = 0
            for ap in items:
                P_, n_ = ap.shape[0], ap.shape[1]
                S.op("dve", lambda e, ap=ap, off=off, P_=P_, n_=n_: e.tensor_copy(dstage[0:P_, off:off + n_], ap))
                off += n_
            S.barrier()
            S.dma("sp", "out", dbg[:, :], dstage[:])
            S.barrier()

        t_ident = Tok("ident")
        S.op("pool", lambda e: e.memset(ident[:], 0.0), writes=[t_ident])
        S.op("pool", lambda e: e.affine_select(ident[:], ident[:], pattern=[[-1, 128]], compare_op=ALU.not_equal,
                                               fill=1.0, base=0, channel_multiplier=1),
             reads=[t_ident], writes=[t_ident])
        t_identb = Tok("identb")
        S.op("dve", lambda e: e.tensor_copy(identb[:], ident[:]), reads=[t_ident], writes=[t_identb])
        t_stat = Tok("stat")
        t_sm = Tok("sm")
        t_st3 = t_sm
        S.op("pool", lambda e: e.memset(stat[:], 0.0), writes=[t_stat])
        S.op("pool", lambda e: e.memset(stat2[:], 0.0), writes=[t_sm])
        t_ones = Tok("ones")
        S.op("pool", lambda e: e.memset(onesb[:], 1.0), writes=[t_ones])
        t_eps = Tok("eps")
        S.op("pool", lambda e: e.memset(epst[:], EPS), writes=[t_eps])
        S.op("pool", lambda e: e.memset(q25[:], 0.25), writes=[t_eps])

        prt = carve(0, 128, F32)[0:82, :]
        sin = carve(512, 4608, F32)
        NXB = 8
        xbuf = [carve(18944 + i * 4096, 1024, F32) for i in range(3)] + [carve(53760 + i * 4096, 1024, F32) for i in range(5)]
        xnb = [carve(31232 + i * 2048, 1024, BF16) for i in range(2)]
        junk = carve(35328, 1024, BF16)
        mnT_t = carve(37376, KC * NM, BF16)
        mnT = mnT_t.rearrange("p (k t) -> p k t", k=KC)
        kT_t = carve(41472, 4 * NM, BF16)
        kT = kT_t.rearrange("p (k t) -> p k t", k=4)
        vb_t = carve(43520, 2 * XW, BF16)
        vb = vb_t.rearrange("p (k t) -> p k t", k=2)
        kvout = [carve(45568 + i * 4096, 2 * XW, F32).rearrange("p (k t) -> p k t", k=2) for i in range(2)]

        t_prt = Tok("prt")

        def rows(v, r):
            return v.rearrange("(r c) -> r c", c=128)

        plist = [(norm_g, 0, 8, None), (mem_norm_g, 8, 8, None),
                 (lru_conv_w, 16, 24, "k (n c) -> (k n) c"), (lru_conv_b, 40, 6, None), (lru_ba, 46, 6, None),
                 (lru_bx, 52, 6, None), (lru_lambda, 58, 6, None), (sconv_w, 64, 18, "k (n c) -> (k n) c")]
        for (v, r0, nr, pat) in plist:
            src = v.rearrange(pat, c=128) if pat else v.rearrange("(r c) -> r c", c=128)
            S.dma("sp", "cst", prt[r0:r0 + nr, :], src, writes=[t_prt], nowait=True)
        if STOP_AFTER == "P0dma":
            S.barrier()
            S.finish(block)
            return nc
        t_pT = Tok("pT")
        S.op("pe", lambda e: e.transpose(PX[:, 0:82], prt, ident[0:82, 0:82]), reads=[t_prt, t_ident], writes=[tPX])
        S.op("act", lambda e: e.copy(out=pT[:], in_=PX[:, 0:82]), reads=[tPX], writes=[t_pT])
        if STOP_AFTER == "P0b":
            S.barrier()
            S.finish(block)
            return nc
        t_x = [Tok(f"x{i}") for i in range(NXB)]
        t_xn = [Tok(f"xn{i}") for i in range(2)]
        t_junk = Tok("junk")
        t_uT = Tok("uT")
        t_mnT = Tok("mnT")
        tiles = [(memp[i * 128:(i + 1) * 128, :], 128, "m", i * 128) for i in range(2)]
        tiles += [(xp[i * 128:(i + 1) * 128, :], 128, "u", i * 128) for i in range(16)]
        tiles.append((xs[:, :], TS, "u", T))
        xsem = ["xa", "xb", "xc", "ya", "yb", "ka", "kb", "va"]
        tPXh = [Tok("PXh0"), Tok("PXh1")]
        t_st0 = [Tok(f"st0_{i}") for i in range(len(tiles))]

        def p0_stageA(ti):
            src, nr, kind, c0 = tiles[ti]
            xb_, tx = xbuf[ti % NXB], t_x[ti % NXB]
            S.dma("sp", xsem[ti % NXB], xb_[0:nr, :], src, writes=[tx])
            S.op("act", lambda e, xb_=xb_, nr=nr, ti=ti: e.activation(out=junk[0:nr, :], in_=xb_[0:nr, :], func=AF.Square,
                                                                      accum_out=stat[0:nr, ti:ti + 1]),
                 reads=[tx, t_stat], writes=[t_st0[ti], t_junk])

        def p0_stageB(ti):
            src, nr, kind, c0 = tiles[ti]
            xb_, tx = xbuf[ti % NXB], t_x[ti % NXB]
            xn_, txn = xnb[ti % 2], t_xn[ti % 2]
            S.op("act", lambda e, nr=nr, ti=ti: e.activation(out=stat[0:nr, ti:ti + 1], in_=stat[0:nr, ti:ti + 1], func=AF.Sqrt,
                                                             scale=1.0 / D, bias=epst[0:nr, 0:1]),
                 reads=[t_st0[ti], t_eps], writes=[t_st0[ti]])
            S.op("dve", lambda e, nr=nr, ti=ti: e.reciprocal(out=stat[0:nr, ti:ti + 1], in_=stat[0:nr, ti:ti + 1]),
                 reads=[t_st0[ti]], writes=[t_st0[ti]])
            S.op("dve", lambda e, xb_=xb_, xn_=xn_, nr=nr, ti=ti: e.tensor_scalar(out=xn_[0:nr, :], in0=xb_[0:nr, :],
                                                                                scalar1=stat[0:nr, ti:ti + 1], scalar2=None,
                                                                                op0=ALU.mult),
                 reads=[tx, t_st0[ti]], writes=[txn])

        def p0_stageC(ti):
            src, nr, kind, c0 = tiles[ti]
            xn_, txn = xnb[ti % 2], t_xn[ti % 2]
            bankb, tbank = (PXb, tPX) if ti % 2 == 0 else (PSb, tPSb)
            for kc in range(KC):
                S.op("pe", lambda e, xn_=xn_, nr=nr, kc=kc, bankb=bankb: e.transpose(bankb[:, kc * 128: kc * 128 + nr],
                                                                        xn_[0:nr, kc * 128:(kc + 1) * 128],
                                                                        identb[0:nr, 0:nr]),
                     reads=[txn, t_identb], writes=[tbank], inc=(kc == KC - 1))
            pview = bankb.rearrange("p (k t) -> p k t", k=KC)[:, :, 0:nr]
            if kind == "u":
                dst = uT[:, :, c0:c0 + nr]
                gcols = pT[:, 0:8]
                tdst = t_uTi[ti]
            else:
                dst = mnT[:, :, c0:c0 + nr]
                gcols = pT[:, 8:16]
                tdst = t_mnT
            S.op("dve", lambda e, dst=dst, pview=pview, gcols=gcols, nr=nr: e.tensor_tensor(
                out=dst, in0=pview, in1=gcols.unsqueeze(2).to_broadcast([128, KC, nr]), op=ALU.mult),
                reads=[tbank, t_pT], writes=[tdst])

        t_kT = Tok("kT")
        t_vb = Tok("vb")
        t_kvout = [Tok("kvo0"), Tok("kvo1")]
        kvst = {}

        def kv_mm():
            for which in range(2):
                wv_, wt_ = [], []
                for hh in range(2):
                    vws, tk = w_get(WT[("wk" if which == 0 else "wv") + str(hh)])
                    wv_.append(vws[0])
                    wt_.append(tk)
                pp, tk = next_pp()
                for mc in range(2):
                    for hh in range(2):
                        pairs = [(mnT[:, k, mc * 128:(mc + 1) * 128], wv_[hh][:, k, :]) for k in range(KC)]
                        mm_group(pp[:, mc * 512 + hh * 256: mc * 512 + (hh + 1) * 256], pairs, [t_mnT, wt_[hh]], tk,
                                 last_inc=(mc == 1 and hh == 1))
                kvst[which] = (pp, tk)
                if which == 0:
                    pp2, tk2 = next_pp()
                    for dc in range(4):
                        hh, off = dc // 2, (dc % 2) * 128
                        pairs = [(wv_[hh][:, k, off:off + 128], mnT[:, k, :]) for k in range(KC)]
                        mm_group(pp2[:, dc * 256:(dc + 1) * 256], pairs, [t_mnT, wt_[hh]], tk2, last_inc=(dc == 3))
                    kvst["kT"] = (pp2, tk2)

        def kv_evac():
            for which in range(2):
                pp, tk = kvst[which]
                ko = kvout[which]
                S.op("act", lambda e, ko=ko, pp=pp: e.copy(out=ko, in_=pp[:].rearrange("p (k t) -> p k t", k=2)),
                     reads=[tk], writes=[t_kvout[which]])
                dsto = (o_pk if which == 0 else o_pv).rearrange("(k p) c -> p k c", p=128)
                S.dma("sp", "out", dsto, ko, reads=[t_kvout[which]])
            pp, tk = kvst[1]
            S.op("act", lambda e, pp=pp: e.copy(out=vb, in_=pp[:].rearrange("p (k t) -> p k t", k=2)),
                 reads=[tk], writes=[t_vb])
            pp2, tk2 = kvst["kT"]
            S.op("dve", lambda e, pp2=pp2: e.tensor_copy(kT, pp2[:].rearrange("p (k t) -> p k t", k=4)),
                 reads=[tk2], writes=[t_kT])

        t_uTi = [Tok(f"uT{i}") for i in range(len(tiles))]
        PD = 5
        p0_stageA(0)
        p0_stageB(0)
        for ti in range(1, PD):
            p0_stageA(ti)
        t_sin = Tok("sin")
        S.dma("sp", "cs2", sin[0:TS, 0:2304], st_lc[:, :], writes=[t_sin], nowait=True)
        S.dma("sp", "cs2", sin[0:TS, 2304:3072], st_h[:, :], writes=[t_sin], nowait=True)
        S.dma("sp", "cs2", sin[0:TS, 3072:4608], st_sc[:, :], writes=[t_sin], nowait=True)
        t_out = Tok("out")
        o_slc3 = o_slc.rearrange("b (k c) -> b k c", k=3)
        st_lc3 = st_lc.rearrange("b (k c) -> b k c", k=3)
        S.dma("sp", "out", o_slc3[:, 0:2, :], st_lc3[:, 1:3, :])
        o_ssc3 = o_ssc.rearrange("b (k c) -> b k c", k=2)
        st_sc3 = st_sc.rearrange("b (k c) -> b k c", k=2)
        S.dma("sp", "out", o_ssc3[:, 0:1, :], st_sc3[:, 1:2, :])

        t_dv = Tok("dv")
        t_SIT = Tok("SIT")
        qT_t = carve(0, 4 * T, BF16)
        qT = qT_t.rearrange("p (k t) -> p k t", k=4)
        t_qT = [Tok(f"qT{h}") for h in range(4)]

        def sit():
            blocks_ = []
            for n in range(NCH):
                for k in range(3):
                    blocks_.append((n * 6 + k, k * W + n * 128))
                blocks_.append((n * 6 + 3, 2304 + n * 128))
                for k in range(2):
                    blocks_.append((n * 6 + 4 + k, 3072 + k * W + n * 128))
            for g0 in range(0, 36, 18):
                grp = blocks_[g0:g0 + 18]
                for gi, (slot_i, c0) in enumerate(grp):
                    S.op("pe", lambda e, gi=gi, c0=c0: e.transpose(PX[:, gi * TS:(gi + 1) * TS], sin[0:TS, c0:c0 + 128],
                                                                  ident[0:TS, 0:TS]),
                         reads=[t_sin, t_ident], writes=[tPX], inc=(gi == len(grp) - 1))
                assert [b_[0] for b_ in grp] == list(range(g0, g0 + 18))
                S.op("act", lambda e, g0=g0: e.copy(out=SIT[:, g0:g0 + 18, :],
                                                    in_=PX[:, 0:18 * TS].rearrange("p (a b) -> p a b", a=18)),
                     reads=[tPX], writes=[t_SIT])


        def dv_derive():
            S.op("act", lambda e: e.activation(out=dv[:, 0:6], in_=pT[:, 58:64], func=AF.Exp, scale=-1.0),
                 reads=[t_pT], writes=[t_dv])
            S.op("act", lambda e: e.activation(out=dv[:, 6:12], in_=dv[:, 0:6], func=AF.Ln, bias=1.0, scale=1.0),
                 reads=[t_dv], writes=[t_dv])
            S.op("dve", lambda e: e.tensor_scalar(out=dv[:, 0:6], in0=dv[:, 6:12], scalar1=-4.0, scalar2=None, op0=ALU.mult),
                 reads=[t_dv], writes=[t_dv])
            S.op("dve", lambda e: e.tensor_scalar(out=dv[:, 6:12], in0=dv[:, 6:12], scalar1=-8.0, scalar2=None, op0=ALU.mult),
                 reads=[t_dv], writes=[t_dv])
            S.op("dve", lambda e: e.tensor_scalar(out=dv[:, 12:24], in0=pT[:, 46:58], scalar1=0.5, scalar2=None, op0=ALU.mult),
                 reads=[t_pT, t_dv], writes=[t_dv])

        eqst = {}

        def eq_bank(h, b):
            rd_u = [t_uTi[i] for i in range(2, 10)]
            hh, hl = h // 2, h % 2
            vws, wtk = w_get(WT[f"q{hh}"])
            wq = vws[0]
            if b == 0:
                eqst[h] = next_pp()
            pp, tk = eqst[h]
            pairs = [(wq[:, k, hl * 128:(hl + 1) * 128], uT[:, k, b * 512:(b + 1) * 512]) for k in range(KC)]
            mm_group(pp[:, b * 512:(b + 1) * 512], pairs, [wtk] + rd_u, tk, last_inc=True)

        def eq_evac(heads):
            for h in heads:
                pp, tk = eqst[h]
                S.op("act", lambda e, pp=pp, h=h: e.copy(out=qT[:, h, 0:1024], in_=pp[:]),
                     reads=[tk], writes=[t_qT[h], t_sin, t_prt])

        for ti in range(len(tiles)):
            if ti + PD < len(tiles):
                p0_stageA(ti + PD)
            if ti + 1 < len(tiles):
                p0_stageB(ti + 1)
            p0_stageC(ti)
            if ti == 1:
                kv_mm()
            if ti == 5:
                kv_evac()
            if ti == 6:
                sit()
            if 9 <= ti <= 16:
                eq_bank((ti - 9) // 2, (ti - 9) % 2)
            if ti in (12, 14, 16):
                eq_evac([(ti - 12) // 2])
            if ti == 18:
                eq_evac([3])

        dv_derive()
        if STOP_AFTER == "P0c":
            dump([pT[:], stat[:, 0:19], uT[:, 0, 0:256], uT[:, 7, T - 128:TT], mnT[:, 0, :], mnT[:, 7, :]])
            S.barrier()
            S.finish(block)
            return nc
        S.barrier(skip=("out",))
        if STOP_AFTER == "KV":
            S.finish(block)
            return nc

        qT_t = carve(0, 4 * T, BF16)
        qT = qT_t.rearrange("p (k t) -> p k t", k=4)
        qs_tok = carve(16384, XW, BF16)
        sqg_tok = carve(17408, XW, F32)
        pTe = [carve(19456 + i * 2048, 1024, BF16).rearrange("p (k t) -> p k t", k=2) for i in range(2)]
        rden = [carve(23552 + i * 2048, 512, F32) for i in range(2)]
        o1b = [carve(27648 + i * 2048, 512, F32) for i in range(2)]
        selb_t = carve(31744, TS * 128, BF16)
        selb = selb_t.rearrange("p (b c) -> p b c", b=TS)
        eye16_t = carve(35840, 256, F32)
        eye16 = eye16_t.rearrange("p (a b) -> p a b", a=16)
        Sall_t = carve(45568, 128, F32)
        Sall = Sall_t.rearrange("p (c b h) -> p c b h", c=2, b=TS)
        Esm = carve(46080, 256, F32)
        Psm = carve(47104, 256, F32)
        Mk_t = carve(48128, 2 * 4 * 16 * 16, BF16)
        Mk = Mk_t.rearrange("p (c h b q) -> p c h b q", c=2, h=4, b=16)
        qb_sb = [carve(56320 + i * 2048, 512, F32) for i in range(2)]
        Kb = [carve(60416 + i * 4096, 1024, F32).rearrange("p (c f) -> p c f", c=2) for i in range(2)]
        prod = carve(68608, 1024, F32).rearrange("p (c f) -> p c f", c=2)

        t_og = [Tok(f"og{h}") for h in range(4)]
        t_qs = Tok("qs")
        t_sqg = Tok("sqg")
        def next_pp_c():
            while True:
                i = state["pp"] % 3
                state["pp"] += 1
                if i != state.get("pp_excl", -1):
                    return PP[i], tPP[i]

        def unit_half_c(lhs_list, rhs_arr, lo, reads):
            pp, tk = next_pp_c()
            nk = len(lhs_list)
            for b in range(2):
                pairs = [(lhs_list[k], rhs_arr[:, k, lo + b * 512: lo + (b + 1) * 512]) for k in range(nk)]
                mm_group(pp[:, b * 512:(b + 1) * 512], pairs, reads, tk, last_inc=(b == 1))
            return pp, tk

        t_pTe = [Tok("pTe0"), Tok("pTe1")]
        t_rden = [Tok("rden0"), Tok("rden1")]
        t_o1 = [Tok("o10"), Tok("o11")]

        def gen_C_prompt():
            for hh in range(2):
                vws, wtk = w_get(WT[f"q{hh}"])
                wq = vws[0]
                pairs = [(uT[:, k, T:TT], wq[:, k, :]) for k in range(KC)]
                mm_group(PS[0:TS, 0:256], pairs, [wtk], tPSb)
                S.op("act", lambda e, hh=hh: e.copy(out=qs_tok[0:TS, hh * 256:(hh + 1) * 256], in_=PS[0:TS, 0:256]),
                     reads=[tPSb], writes=[t_qs])
                yield
            for hh in range(2):
                vws, wtk = w_get(WT[f"q{hh}"])
                wq = vws[0]
                for hl in range(2):
                    h = hh * 2 + hl
                    lhs = [wq[:, k, hl * 128:(hl + 1) * 128] for k in range(KC)]
                    for (lo, hi) in HALVES[1:]:
                        pp, tk = unit_half_c(lhs, uT, lo, [wtk])
                        S.op("act", lambda e, pp=pp, h=h, lo=lo, hi=hi: e.copy(out=qT[:, h, lo:hi], in_=pp[:]),
                             reads=[tk], writes=[t_qT[h]])
                        yield
            for hh in range(2):
                vws, wtk = w_get(WT[f"qg{hh}"])
                wq = vws[0]
                for hl in range(2):
                    h = hh * 2 + hl
                    lhs = [wq[:, k, hl * 128:(hl + 1) * 128] for k in range(KC)]
                    for (lo, hi) in HALVES:
                        pp, tk = unit_half_c(lhs, uT, lo, [wtk])
                        S.op("act", lambda e, pp=pp, h=h, lo=lo, hi=hi: e.activation(out=og[:, h, lo:hi], in_=pp[:], func=AF.Silu),
                             reads=[tk], writes=[t_og[h]])
                        yield
                pairs = [(uT[:, k, T:TT], wq[:, k, :]) for k in range(KC)]
                mm_group(PS[0:TS, 0:256], pairs, [wtk], tPSb)
                S.op("act", lambda e, hh=hh: e.activation(out=sqg_tok[0:TS, hh * 256:(hh + 1) * 256],
                                                         in_=PS[0:TS, 0:256], func=AF.Silu),
                     reads=[tPSb], writes=[t_sqg])
                yield
            iters = [(tb, h) for tb in range(4) for h in range(4)]
            stage1 = {}

            def att_s1(i):
                tb, h = iters[i]
                c0 = tb * 512
                pe_, tpe = pTe[i % 2], t_pTe[i % 2]
                pp, tk = next_pp_c()
                for mc in range(2):
                    mm_group(pp[:, mc * 512:(mc + 1) * 512], [(kT[:, h, mc * 128:(mc + 1) * 128], qT[:, h, c0:c0 + 512])],
                             [t_kT, t_qT[h]], tk, last_inc=(mc == 1))
                S.op("act", lambda e, pp=pp, pe_=pe_: e.activation(out=pe_, in_=pp[:].rearrange("p (k t) -> p k t", k=2),
                                                                  func=AF.Exp, scale=SCALE),
                     reads=[tk], writes=[tpe])

            def att_s2(i):
                tb, h = iters[i]
                c0 = tb * 512
                pe_, tpe = pTe[i % 2], t_pTe[i % 2]
                rd, trd = rden[i % 2], t_rden[i % 2]
                o1, to1 = o1b[i % 2], t_o1[i % 2]
                pp2, tk2 = next_pp_c()
                mm_group(pp2[:, 0:512], [(vb[:, mc, h * 128:(h + 1) * 128], pe_[:, mc, :]) for mc in range(2)],
                         [t_vb, tpe], tk2, last_inc=False)
                mm_group(pp2[:, 512:1024], [(onesb[:], pe_[:, mc, :]) for mc in range(2)], [t_ones, tpe], tk2)
                S.op("act", lambda e, pp2=pp2, rd=rd: e.activation(out=rd, in_=pp2[:, 512:1024], func=AF.Ln), reads=[tk2], writes=[trd])
                S.op("act", lambda e, rd=rd: e.activation(out=rd, in_=rd, func=AF.Exp, scale=-1.0), reads=[trd], writes=[trd])
                S.op("dve", lambda e, pp2=pp2, rd=rd, o1=o1: e.tensor_tensor(out=o1, in0=pp2[:, 0:512], in1=rd, op=ALU.mult),
                     reads=[tk2, trd], writes=[to1])
                S.op("dve", lambda e, o1=o1, h=h, c0=c0: e.tensor_tensor(out=og[:, h, c0:c0 + 512], in0=o1,
                                                                        in1=og[:, h, c0:c0 + 512], op=ALU.mult),
                     reads=[to1, t_og[h]], writes=[t_og[h]])

            att_s1(0)
            for i in range(len(iters)):
                if i + 1 < len(iters):
                    att_s1(i + 1)
                att_s2(i)
                yield

        t_selb = Tok("selb")
        t_eye = Tok("eye")
        t_qb = [Tok("qb0"), Tok("qb1")]
        t_Kb = [Tok("Kb0"), Tok("Kb1")]
        t_prod = Tok("prod")
        t_Sall = Tok("Sall")
        t_E = Tok("E")
        t_P = Tok("P")
        t_Mk = Tok("Mk")
        ksem = ["ka", "kb"]
        vsem = ["sva", "svb", "svc", "svd", "sve", "svf"]

        def gen_C_sample():
            S.wait_dma(["dve", "act", "pool", "pe"], "out")
            S.op("dve", lambda e: e.tensor_copy(selb[0:TS, :, :], identb[0:TS, 0:TS].unsqueeze(2).to_broadcast([TS, TS, 128])),
                 reads=[t_identb], writes=[t_selb])
            S.op("pool", lambda e: e.memset(eye16_t, 0.0), writes=[t_eye])
            S.op("pool", lambda e: e.affine_select(eye16, eye16, pattern=[[1, 16], [-1, 16]], compare_op=ALU.not_equal,
                                                   fill=1.0, base=0, channel_multiplier=0),
                 reads=[t_eye], writes=[t_eye])
            yield
            for b in range(TS):
                kb_, tkb = Kb[b % 2], t_Kb[b % 2]
                qb_, tqb = qb_sb[b % 2], t_qb[b % 2]
                S.dma("sp", ksem[b % 2], kb_, ck[b].rearrange("(c m) f -> m c f", c=2), writes=[tkb])
                S.op("pe", lambda e, b=b: e.matmul(PX[:, 0:512], selb[0:TS, b, :], qs_tok[0:TS, :], start=True, stop=True),
                     reads=[t_selb, t_qs], writes=[tPX])
                S.op("act", lambda e, qb_=qb_: e.copy(out=qb_, in_=PX[:, 0:512]), reads=[tPX], writes=[tqb])
                S.op("pool", lambda e, kb_=kb_, qb_=qb_: e.tensor_tensor(out=prod, in0=kb_,
                                                                         in1=qb_.unsqueeze(1).to_broadcast([128, 2, 512]), op=ALU.mult),
                     reads=[tkb, tqb], writes=[t_prod])
                S.op("dve", lambda e, b=b: e.tensor_reduce(out=Sall[:, :, b, :],
                                                           in_=prod.rearrange("p c (h d) -> p c h d", h=4), axis=AX.X, op=ALU.add),
                     reads=[t_prod], writes=[t_Sall])
                yield
            for c in range(2):
                S.op("pe", lambda e, c=c: e.transpose(PX[0:64, c * 128:(c + 1) * 128],
                                                      Sall_t[:, c * 64:(c + 1) * 64], ident[:, :]),
                     reads=[t_Sall, t_ident], writes=[tPX], inc=(c == 1))
            S.op("dve", lambda e: e.tensor_reduce(out=stat2[0:64, 0:1], in_=PX[0:64, 0:256], axis=AX.X, op=ALU.max),
                 reads=[tPX], writes=[t_sm])
            S.op("dve", lambda e: e.tensor_scalar(out=stat2[0:64, 0:1], in0=stat2[0:64, 0:1], scalar1=-SCALE, scalar2=None, op0=ALU.mult),
                 reads=[t_sm], writes=[t_sm])
            S.op("act", lambda e: e.activation(out=Esm[0:64, :], in_=PX[0:64, 0:256], func=AF.Exp, scale=SCALE,
                                               bias=stat2[0:64, 0:1], accum_out=stat2[0:64, 1:2]),
                 reads=[tPX, t_sm], writes=[t_E, t_sm])
            S.op("dve", lambda e: e.reciprocal(out=stat2[0:64, 1:2], in_=stat2[0:64, 1:2]), reads=[t_sm], writes=[t_sm])
            S.op("dve", lambda e: e.tensor_scalar(out=Psm[0:64, :], in0=Esm[0:64, :], scalar1=stat2[0:64, 1:2], scalar2=None, op0=ALU.mult),
                 reads=[t_E, t_sm], writes=[t_P])
            for c in range(2):
                S.op("pe", lambda e, c=c: e.transpose(PX[:, c * 64:(c + 1) * 64], Psm[0:64, c * 128:(c + 1) * 128], ident[0:64, 0:64]),
                     reads=[t_P, t_ident], writes=[tPX], inc=(c == 1))
            for c in range(2):
                S.op("dve", lambda e, c=c: e.tensor_tensor(
                    out=Mk[:, c],
                    in0=PX[:, c * 64:(c + 1) * 64].rearrange("p (b h) -> p h b", h=4).unsqueeze(3).to_broadcast([128, 4, 16, 16]),
                    in1=eye16.unsqueeze(1).to_broadcast([128, 4, 16, 16]), op=ALU.mult),
                    reads=[tPX, t_eye], writes=[t_Mk])
            yield
            ppa, tka = next_pp_c()
            state["pp_excl"] = PP.index(ppa)
            hbank = [(PS, 0, tPSb), (PX, 0, tPX), (ppa, 0, tka), (ppa, 512, tka)]
            NVB = 6
            Vb = ([carve(68608 + i * 2048, 1024, BF16).rearrange("p (c f) -> p c f", c=2) for i in range(2)]
                  + [carve(60416 + i * 2048, 1024, BF16).rearrange("p (c f) -> p c f", c=2) for i in range(4)])
            t_Vb = [Tok(f"Vb{i}") for i in range(NVB)]
            valias = [[t_prod], [t_prod], [t_Kb[0]], [t_Kb[0]], [t_Kb[1]], [t_Kb[1]]]

            def v_load(b):
                S.dma("pool", vsem[b % NVB], Vb[b % NVB], cv[b].rearrange("(c m) f -> m c f", c=2),
                      writes=[t_Vb[b % NVB]] + (valias[b] if b < NVB else []))

            for b in range(NVB - 1):
                v_load(b)
            for b in range(TS):
                vb_, tvb = Vb[b % NVB], t_Vb[b % NVB]
                if b + NVB - 1 < TS:
                    v_load(b + NVB - 1)
                for h in range(4):
                    pph, coff, tkh = hbank[h]
                    for c in range(2):
                        first = (b == 0 and c == 0)
                        last = (b == TS - 1 and c == 1)
                        S.op("pe", lambda e, vb_=vb_, b=b, h=h, c=c, first=first, last=last, pph=pph, coff=coff: e.matmul(
                            pph[0:TS, coff:coff + 128], Mk[:, c, h, b, :], vb_[:, c, h * 128:(h + 1) * 128],
                            start=first, stop=last, skip_group_check=True),
                            reads=[t_Mk, tvb], writes=[tkh], inc=(h == 3 and c == 1))
                yield
            ogs_tok = qs_tok
            for h in range(4):
                pph, coff, tkh = hbank[h]
                S.op("dve", lambda e, h=h, pph=pph, coff=coff: e.tensor_tensor(
                    out=ogs_tok[0:TS, h * 128:(h + 1) * 128], in0=pph[0:TS, coff:coff + 128],
                    in1=sqg_tok[0:TS, h * 128:(h + 1) * 128], op=ALU.mult),
                    reads=[tkh, t_sqg, t_qs], writes=[t_qs])
            state["pp_excl"] = -1
            for h in range(4):
                S.op("pe", lambda e, h=h: e.transpose(PXb[:, h * TS:(h + 1) * TS], ogs_tok[0:TS, h * 128:(h + 1) * 128],
                                                      identb[0:TS, 0:TS]),
                     reads=[t_qs, t_identb], writes=[tPX], inc=(h == 3))
            S.op("act", lambda e: e.copy(out=og[:, :, T:TT], in_=PXb[:, 0:4 * TS].rearrange("p (h b) -> p h b", h=4)),
                 reads=[tPX], writes=t_og)
            yield

        gp, gs = gen_C_prompt(), gen_C_sample()
        for _ in range(2):
            next(gp)
        alive = [gp, gs]
        while alive:
            for g in list(alive):
                try:
                    next(g)
                except StopIteration:
                    alive.remove(g)
        if STOP_AFTER == "C":
            dump([og[:, 0, 0:512], og[:, 3, T - 512:T], og[:, 0, T:TT], og[:, 1, T:TT], og[:, 2, T:TT], og[:, 3, T:TT], qT[:, 0, 0:256]], off_b=0)
        S.barrier(no_wait=("pe",))
        if STOP_AFTER == "C":
            S.finish(block)
            return nc

        wab_t = carve(0, 2 * NCH * 128, BF16)
        wab = wab_t.rearrange("p (a n d) -> p a n d", a=2, n=NCH)
        lxp = carve(3072, 3 + T, F32)
        lxs_t = carve(11280, 4 * TS, F32)
        lxs = lxs_t.rearrange("p (k b) -> p k b", k=4)
        xc = carve(11536, TT, F32)
        xcb = carve(19792, TT, BF16)
        thr = carve(23920, TT, F32)
        a2b = carve(32176, TT, F32)
        thi = carve(40432, TT, F32)
        scg_sb = carve(48752, TT, F32)
        cy = scg_sb
        cinp = carve(57008, 2 + T, F32)
        cins_t = carve(65208, 3 * TS, F32)
        cins = cins_t.rearrange("p (k b) -> p k b", k=3)
        so_tok = carve(65400, W, F32)
        t_wab = Tok("wab")
        S.dma("pool", "cs3", wab[:, 0], lru_wa.rearrange("n c d -> c n d"), writes=[t_wab])
        S.dma("pool", "cs3", wab[:, 1], lru_wx.rearrange("n c d -> c n d"), writes=[t_wab], nowait=True)
        t_lxp, t_lxs, t_xc, t_xcb, t_thr, t_a2, t_thi = (Tok("lxp"), Tok("lxs"), Tok("xc"), Tok("xcb"), Tok("thr"),
                                                        Tok("a2"), Tok("thi"))
        t_gA = [Tok(f"gA{n}") for n in range(NCH)]
        t_SO = Tok("SO")
        t_scg, t_cinp, t_cins = Tok("scg"), Tok("cinp"), Tok("cins")
        t_cy = t_scg
        t_gB = [Tok(f"gB{n}") for n in range(NCH)]
        S.op("pool", lambda e: e.memset(lxp[:, 0:3], 0.0), writes=[t_lxp])
        S.op("pool", lambda e: e.memset(cinp[:, 0:2], 0.0), writes=[t_cinp])

        def gen_A(n):
            vws, wtk = w_get(WT[f"A{n}"])
            wlg, wlx = vws
            lhs_lg = [wlg[:, k, :] for k in range(KC)]
            lhs_lx = [wlx[:, k, :] for k in range(KC)]
            for (lo, hi) in HALVES:
                pp, tk = unit_half(lhs_lg, uT, lo, [wtk])
                S.op("act", lambda e, pp=pp, n=n, lo=lo, hi=hi: e.activation(out=gA[:, n, lo:hi], in_=pp[:], func=AF.Silu),
                     reads=[tk], writes=[t_gA[n]])
            sp_, tk = unit_samp(lhs_lg, uT, [wtk])
            S.op("act", lambda e, sp_=sp_, n=n: e.activation(out=gA[:, n, T:TT], in_=sp_, func=AF.Silu),
                 reads=[tk], writes=[t_gA[n]])
            yield
            for (lo, hi) in HALVES:
                pp, tk = unit_half(lhs_lx, uT, lo, [wtk])
                S.op("dve", lambda e, pp=pp, lo=lo, hi=hi: e.tensor_copy(lxp[:, 3 + lo:3 + hi], pp[:]),
                     reads=[tk], writes=[t_lxp])
            sp_, tk = unit_samp(lhs_lx, uT, [wtk])
            S.op("dve", lambda e, n=n: e.tensor_copy(lxs[:, 0:3, :], SIT[:, n * 6:n * 6 + 3, :]), reads=[t_SIT], writes=[t_lxs])
            S.op("dve", lambda e, sp_=sp_: e.tensor_copy(lxs[:, 3, :], sp_), reads=[tk], writes=[t_lxs])
            yield
            cw = lambda k, n=n: pT[:, 16 + k * 6 + n: 17 + k * 6 + n]
            cbias = pT[:, 40 + n:41 + n]
            S.op("act", lambda e, cw=cw, cbias=cbias: e.activation(out=xc[:, 0:T], in_=lxp[:, 0:T], func=AF.Identity,
                                                                   scale=cw(0), bias=cbias),
                 reads=[t_lxp, t_pT], writes=[t_xc])
            S.op("act", lambda e, cw=cw, cbias=cbias: e.activation(out=xc[:, T:TT], in_=lxs[:, 0, :], func=AF.Identity,
                                                                   scale=cw(0), bias=cbias),
                 reads=[t_lxs, t_pT], writes=[t_xc])
            for k in range(1, 4):
                S.op("dve", lambda e, k=k, cw=cw: e.scalar_tensor_tensor(out=xc[:, 0:T], in0=lxp[:, k:k + T], scalar=cw(k),
                                                                         in1=xc[:, 0:T], op0=ALU.mult, op1=ALU.add),
                     reads=[t_lxp, t_xc], writes=[t_xc])
                S.op("dve", lambda e, k=k, cw=cw: e.scalar_tensor_tensor(out=xc[:, T:TT], in0=lxs[:, k, :], scalar=cw(k),
                                                                         in1=xc[:, T:TT], op0=ALU.mult, op1=ALU.add),
                     reads=[t_lxs, t_xc], writes=[t_xc])
            S.op("pool", lambda e, n=n: e.tensor_copy(SO[:, n, 0:3], lxp[:, T:T + 3]), reads=[t_lxp], writes=[t_SO])
            S.op("pool", lambda e, n=n: e.tensor_copy(SO[:, n, 6:22], lxs[:, 3, :]), reads=[t_lxs], writes=[t_SO])
            yield
            S.op("act", lambda e: e.copy(out=xcb, in_=xc), reads=[t_xc], writes=[t_xcb])
            yield
            for gi, (dst, tdst, bcol) in enumerate([(thr, t_thr, 12 + n), (thi, t_thi, 18 + n)]):
                lhs = [wab[:, gi, n, :]]
                xcb3 = xcb.unsqueeze(1)
                for (lo, hi) in HALVES:
                    pp, tk = unit_half(lhs, xcb3, lo, [t_wab, t_xcb])
                    S.op("act", lambda e, pp=pp, dst=dst, lo=lo, hi=hi, bcol=bcol: e.activation(
                        out=dst[:, lo:hi], in_=pp[:], func=AF.Tanh, scale=0.5, bias=dv[:, bcol:bcol + 1]),
                        reads=[tk, t_dv], writes=[tdst])
                sp_, tk = unit_samp(lhs, xcb3, [t_wab, t_xcb])
                S.op("act", lambda e, sp_=sp_, dst=dst, bcol=bcol: e.activation(
                    out=dst[:, T:TT], in_=sp_, func=AF.Tanh, scale=0.5, bias=dv[:, bcol:bcol + 1]),
                    reads=[tk, t_dv], writes=[tdst])
                yield
            S.op("act", lambda e, n=n: e.activation(out=a2b, in_=thr, func=AF.Exp, scale=dv[:, 6 + n:7 + n], bias=dv[:, 6 + n:7 + n]),
                 reads=[t_thr, t_dv], writes=[t_a2])
            S.op("act", lambda e, n=n: e.activation(out=thr, in_=thr, func=AF.Exp, scale=dv[:, n:n + 1], bias=dv[:, n:n + 1]),
                 reads=[t_thr, t_dv], writes=[t_thr])
            S.op("dve", lambda e: e.tensor_scalar(out=a2b, in0=a2b, scalar1=1.0, scalar2=-1.0, op0=ALU.min, op1=ALU.mult),
                 reads=[t_a2], writes=[t_a2])
            yield
            S.op("act", lambda e: e.activation(out=a2b, in_=a2b, func=AF.Sqrt, bias=1.0, scale=1.0), reads=[t_a2], writes=[t_a2])
            S.op("dve", lambda e: e.scalar_tensor_tensor(out=a2b, in0=a2b, scalar=0.5, in1=xc, op0=ALU.mult, op1=ALU.mult),
                 reads=[t_a2, t_xc], writes=[t_a2])
            S.op("dve", lambda e: e.scalar_tensor_tensor(out=thi, in0=thi, scalar=1.0, in1=a2b, op0=ALU.add, op1=ALU.mult),
                 reads=[t_thi, t_a2], writes=[t_thi])
            yield
            S.op("dve", lambda e: e.tensor_tensor_scan(out=a2b[:, 0:T], data0=thr[:, 0:T], data1=thi[:, 0:T], initial=0.0,
                                                       op0=ALU.mult, op1=ALU.add),
                 reads=[t_thr, t_thi], writes=[t_a2])
            S.op("dve", lambda e, n=n: e.tensor_tensor(out=a2b[:, T:TT], in0=thr[:, T:TT], in1=SIT[:, n * 6 + 3, :], op=ALU.mult),
                 reads=[t_thr, t_SIT], writes=[t_a2])
            S.op("dve", lambda e: e.tensor_tensor(out=a2b[:, T:TT], in0=a2b[:, T:TT], in1=thi[:, T:TT], op=ALU.add),
                 reads=[t_a2, t_thi], writes=[t_a2])
            yield
            S.op("pool", lambda e, n=n: e.tensor_copy(SO[:, n, 3:4], a2b[:, T - 1:T]), reads=[t_a2], writes=[t_SO])
            S.op("pool", lambda e, n=n: e.tensor_copy(SO[:, n, 22:38], a2b[:, T:TT]), reads=[t_a2], writes=[t_SO])
            S.op("dve", lambda e, n=n: e.tensor_tensor(out=gA[:, n, :], in0=a2b, in1=gA[:, n, :], op=ALU.mult),
                 reads=[t_a2, t_gA[n]], writes=[t_gA[n]])
            yield

        def gen_B(n):
            vws, wtk = w_get(WT[f"B{n}a"])
            wsg, wsb = vws
            lhs_sg = [wsg[:, k, :] for k in range(KC)]
            lhs_sb = [wsb[:, k, :] for k in range(KC)]
            for (lo, hi) in HALVES:
                pp, tk = unit_half(lhs_sg, uT, lo, [wtk])
                S.op("act", lambda e, pp=pp, n=n, lo=lo, hi=hi: e.activation(out=gB[:, n, lo:hi], in_=pp[:], func=AF.Silu),
                     reads=[tk], writes=[t_gB[n]])
            sp_, tk = unit_samp(lhs_sg, uT, [wtk])
            S.op("act", lambda e, sp_=sp_, n=n: e.activation(out=gB[:, n, T:TT], in_=sp_, func=AF.Silu),
                 reads=[tk], writes=[t_gB[n]])
            yield
            for (lo, hi) in HALVES:
                pp, tk = unit_half(lhs_sb, uT, lo, [wtk])
                S.op("dve", lambda e, pp=pp, n=n, lo=lo, hi=hi: e.tensor_tensor(out=gB[:, n, lo:hi], in0=pp[:], in1=gB[:, n, lo:hi],
                                                                                op=ALU.mult),
                     reads=[tk, t_gB[n]], writes=[t_gB[n]])
            sp_, tk = unit_samp(lhs_sb, uT, [wtk])
            S.op("dve", lambda e, sp_=sp_, n=n: e.tensor_tensor(out=gB[:, n, T:TT], in0=sp_, in1=gB[:, n, T:TT], op=ALU.mult),
                 reads=[tk, t_gB[n]], writes=[t_gB[n]])
            yield
            vws, wtk = w_get(WT[f"B{n}b"])
            wscg, wsh = vws
            lhs_scg = [wscg[:, k, :] for k in range(KC)]
            lhs_sh = [wsh[:, k, :] for k in range(KC)]
            for (lo, hi) in HALVES:
                pp, tk = unit_half(lhs_scg, uT, lo, [wtk])
                S.op("act", lambda e, pp=pp, lo=lo, hi=hi: e.copy(out=scg_sb[:, lo:hi], in_=pp[:]), reads=[tk], writes=[t_scg])
            sp_, tk = unit_samp(lhs_scg, uT, [wtk])
            S.op("act", lambda e, sp_=sp_: e.copy(out=scg_sb[:, T:TT], in_=sp_), reads=[tk], writes=[t_scg])
            yield
            for (lo, hi) in HALVES:
                pp, tk = unit_half(lhs_sh, uT, lo, [wtk])
                S.op("dve", lambda e, pp=pp, lo=lo, hi=hi: e.tensor_tensor(out=cinp[:, 2 + lo:2 + hi], in0=pp[:],
                                                                           in1=scg_sb[:, lo:hi], op=ALU.mult),
                     reads=[tk, t_scg], writes=[t_cinp])
            sp_, tk = unit_samp(lhs_sh, uT, [wtk])
            S.op("dve", lambda e, n=n: e.tensor_copy(cins[:, 0:2, :], SIT[:, n * 6 + 4:n * 6 + 6, :]), reads=[t_SIT], writes=[t_cins])
            S.op("dve", lambda e, sp_=sp_: e.tensor_tensor(out=cins[:, 2, :], in0=sp_, in1=scg_sb[:, T:TT], op=ALU.mult),
                 reads=[tk, t_scg], writes=[t_cins])
            yield
            sw = lambda k, n=n: pT[:, 64 + k * 6 + n: 65 + k * 6 + n]
            S.op("act", lambda e, sw=sw: e.activation(out=cy[:, 0:T], in_=cinp[:, 0:T], func=AF.Identity, scale=sw(0)),
                 reads=[t_cinp, t_pT], writes=[t_cy])
            S.op("act", lambda e, sw=sw: e.activation(out=cy[:, T:TT], in_=cins[:, 0, :], func=AF.Identity, scale=sw(0)),
                 reads=[t_cins, t_pT], writes=[t_cy])
            for k in range(1, 3):
                S.op("dve", lambda e, k=k, sw=sw: e.scalar_tensor_tensor(out=cy[:, 0:T], in0=cinp[:, k:k + T], scalar=sw(k),
                                                                         in1=cy[:, 0:T], op0=ALU.mult, op1=ALU.add),
                     reads=[t_cinp, t_cy], writes=[t_cy])
                S.op("dve", lambda e, k=k, sw=sw: e.scalar_tensor_tensor(out=cy[:, T:TT], in0=cins[:, k, :], scalar=sw(k),
                                                                         in1=cy[:, T:TT], op0=ALU.mult, op1=ALU.add),
                     reads=[t_cins, t_cy], writes=[t_cy])
            S.op("pool", lambda e, n=n: e.tensor_copy(SO[:, n, 4:6], cinp[:, T:T + 2]), reads=[t_cinp], writes=[t_SO])
            S.op("pool", lambda e, n=n: e.tensor_copy(SO[:, n, 38:54], cins[:, 2, :]), reads=[t_cins], writes=[t_SO])
            S.op("dve", lambda e, n=n: e.tensor_tensor(out=gB[:, n, :], in0=cy, in1=gB[:, n, :], op=ALU.mult),
                 reads=[t_cy, t_gB[n]], writes=[t_gB[n]])
            yield

        def interleave(*gens):
            gens = list(gens)
            while gens:
                for g in list(gens):
                    try:
                        next(g)
                    except StopIteration:
                        gens.remove(g)

        gens = {}

        def adv(kind, n):
            g = gens.get((kind, n))
            if g is None:
                return
            try:
                next(g)
            except StopIteration:
                pass

        for n in range(NCH + 1):
            if n < NCH:
                gens[("A", n)] = gen_A(n)
            if n >= 1:
                gens[("B", n - 1)] = gen_B(n - 1)
            adv("A", n)
            adv("B", n - 1)
            adv("A", n - 1)
            adv("A", n)
            adv("B", n - 1)
            adv("A", n)
            adv("A", n - 1)
            adv("B", n - 1)
            adv("B", n - 1)
            adv("A", n - 1)
            adv("A", n)
            adv("A", n)
            adv("A", n)
            adv("B", n - 1)
            adv("A", n)
        for g in gens.values():
            for _ in g:
                pass
        if STOP_AFTER == "B":
            dump([gB[:, 0, 0:512], gB[:, 5, T - 512:T], gB[:, 0, T:TT], gB[:, 5, T:TT], gB[:, 2, 1024:1536]], off_b=0)
        t_sot = Tok("sot")
        for g0 in range(0, NCH, 3):
            for n in range(g0, g0 + 3):
                S.op("pe", lambda e, n=n, g0=g0: e.transpose(PX[0:54, (n - g0) * 128:(n - g0 + 1) * 128], SO[:, n, :], ident[:, :]),
                     reads=[t_SO, t_ident], writes=[tPX], inc=(n == g0 + 2))
            S.op("act", lambda e, g0=g0: e.copy(out=so_tok[0:54, g0 * 128:(g0 + 3) * 128], in_=PX[0:54, 0:384]),
                 reads=[tPX], writes=[t_sot])
        S.dma("sp", "out", o_plc[:, :], so_tok[0:3, :], reads=[t_sot])
        S.dma("sp", "out", o_ph[:, :], so_tok[3:4, :], reads=[t_sot])
        S.dma("sp", "out", o_psc[:, :], so_tok[4:6, :], reads=[t_sot])
        S.dma("sp", "out", o_slc3[:, 2, :], so_tok[6:22, :], reads=[t_sot])
        S.dma("sp", "out", o_sh[:, :], so_tok[22:38, :], reads=[t_sot])
        S.dma("sp", "out", o_ssc3[:, 1, :], so_tok[38:54, :], reads=[t_sot])
        if STOP_AFTER == "B":
            dump([gB[:, 0, 0:512], gB[:, 5, T - 512:T], gB[:, 0, T:TT], gB[:, 5, T:TT], gB[:, 2, 1024:1536]], off_b=56320)
        S.barrier(no_wait=("pe",))
        if STOP_AFTER == "B":
            S.finish(block)
            return nc

        NTH = 3
        thb = [carve(i * 4096, 1024, F32) for i in range(NTH)]
        tacc = [carve(12288 + i * 4096, 1024, F32) for i in range(2)]
        mT_t = carve(20480, KC * TT, BF16)
        mT = mT_t.rearrange("p (k t) -> p k t", k=KC)
        wo_t = carve(53504, KC * D, BF16)
        wo = wo_t.rearrange("p (k c) -> p k c", k=KC)
        t_wo = Tok("wo")
        if STOP_AFTER == "M0":
            S.barrier()
            S.finish(block)
            return nc
        t_th = [Tok(f"th{i}") for i in range(NTH)]
        t_acc = [Tok("acc0"), Tok("acc1")]
        t_mT = Tok("mT")
        thc = {"i": 0, "a": 0}
        gsrc = [(gA, NCH), (gB, NCH), (og, 4)]
        xorder = [2, 1, 0]
        gtok = [t_gA, t_gB, t_og]
        ths_f = carve(69888, 3 * TS, F32)
        tzs_f = carve(70080, 3 * TS, F32)
        accs_f = carve(70272, TS, F32)
        t_accs2 = Tok("accs2")
        t_ths = Tok("ths")
        t_accs = Tok("accs")
        for j in range(8):
            vg, wtkg = w_get(WT[f"Mg{j}"])
            vo, wtko = w_get(WT[f"Mo{j}"])
            S.dma("pool", "cs4", wo[:, j, :], w_out[j * 128:(j + 1) * 128, :], writes=[t_wo], nowait=True)
            for (lo, hi) in HALVES:
                acc, tacc_ = tacc[thc["a"] % 2], t_acc[thc["a"] % 2]
                thc["a"] += 1
                for xi, x in enumerate(xorder):
                    garr, nk = gsrc[x]
                    lhs_g = [vg[x][:, k, :] for k in range(KC)]
                    lhs_o = [vo[x][:, k, :] for k in range(nk)]
                    th_, tth = thb[thc["i"] % NTH], t_th[thc["i"] % NTH]
                    thc["i"] += 1
                    pp, tk = unit_half(lhs_g, uT, lo, [wtkg])
                    S.op("act", lambda e, pp=pp, th_=th_: e.activation(out=th_, in_=pp[:], func=AF.Tanh, scale=0.5),
                         reads=[tk], writes=[tth])
                    zp_, tkz = unit_half(lhs_o, garr, lo, [wtko] + gtok[x])
                    zp = zp_[:]
                    if xi == 0:
                        S.op("dve", lambda e, acc=acc, th_=th_, zp=zp: e.scalar_tensor_tensor(out=acc, in0=th_, scalar=1.0, in1=zp,
                                                                                           op0=ALU.add, op1=ALU.mult),
                             reads=[tth, tkz], writes=[tacc_])
                    else:
                        S.op("dve", lambda e, th_=th_, zp=zp: e.scalar_tensor_tensor(out=th_, in0=th_, scalar=1.0, in1=zp,
                                                                                    op0=ALU.add, op1=ALU.mult),
                             reads=[tth, tkz], writes=[tth])
                        if xi == 1:
                            S.op("pool", lambda e, acc=acc, th_=th_: e.tensor_tensor(out=acc, in0=acc, in1=th_, op=ALU.add),
                                 reads=[tth, tacc_], writes=[tacc_])
                        else:
                            S.op("pool", lambda e, acc=acc, th_=th_, j=j, lo=lo, hi=hi: e.tensor_tensor(
                                out=mT[:, j, lo:hi], in0=acc, in1=th_, op=ALU.add),
                                reads=[tth, tacc_], writes=[t_mT])
            tMs_g, tMs_o = tPSb, tPX
            for x in xorder:
                lhs_g = [vg[x][:, k, :] for k in range(KC)]
                mm_group(PS[:, x * TS:(x + 1) * TS], [(lhs_g[k], uT[:, k, T:TT]) for k in range(KC)], [wtkg], tMs_g)
            S.op("act", lambda e: e.activation(out=ths_f, in_=PS[:, 0:3 * TS], func=AF.Tanh, scale=0.5),
                 reads=[tMs_g], writes=[t_ths])
            for x in xorder:
                garr, nk = gsrc[x]
                lhs_o = [vo[x][:, k, :] for k in range(nk)]
                mm_group(PX[:, x * TS:(x + 1) * TS], [(lhs_o[k], garr[:, k, T:TT]) for k in range(nk)], [wtko] + gtok[x], tMs_o)
            S.op("dve", lambda e: e.scalar_tensor_tensor(out=tzs_f, in0=ths_f, scalar=1.0, in1=PX[:, 0:3 * TS],
                                                         op0=ALU.add, op1=ALU.mult),
                 reads=[t_ths, tMs_o], writes=[t_accs])
            S.op("dve", lambda e: e.tensor_reduce(out=accs_f, in_=tzs_f.rearrange("p (x b) -> p b x", x=3),
                                                  axis=AX.X, op=ALU.add),
                 reads=[t_accs], writes=[t_accs2])
            S.op("dve", lambda e, j=j: e.tensor_copy(mT[:, j, T:TT], accs_f), reads=[t_accs2], writes=[t_mT])
            if STOP_AFTER == "M5":
                S.barrier()
                S.finish(block)
                return nc
        S.barrier(no_wait=("pe",))
        if STOP_AFTER == "M":
            S.finish(block)
            return nc

        xr = [carve(i * 4096, 1024, F32) for i in range(2)]
        yr = [carve(8192 + i * 4096, 1024, F32) for i in range(2)] + [carve(69888, 1024, F32)]
        fgb = carve(16384, 1024, F32)
        t_fgb = Tok("fgb")
        S.dma("sp", "cs5", fgb, final_norm_g.partition_broadcast(128), writes=[t_fgb])
        t_xr = [Tok("xr0"), Tok("xr1")]
        t_yr = [Tok("yr0"), Tok("yr1"), Tok("yr2")]
        ftiles = [(xp[i * 128:(i + 1) * 128, :], y_p[i * 128:(i + 1) * 128, :], 128, i * 128) for i in range(16)]
        ftiles.append((xs[:, :], y_s[:, :], TS, T))
        xsem2 = ["fxa", "fxb"]
        ysem = ["ya", "yb", "xc"]
        t_stF = [Tok(f"stF{i}") for i in range(len(ftiles))]

        def f_stageA(ti):
            src, dst, nr, c0 = ftiles[ti]
            xr_, txr = xr[ti % 2], t_xr[ti % 2]
            yr_, tyr = yr[ti % 3], t_yr[ti % 3]
            S.dma("pool", xsem2[ti % 2], xr_[0:nr, :], src, writes=[txr])
            pp, tk = next_pp()
            for b in range(2):
                pairs = [(mT[:, k, c0:c0 + nr], wo[:, k, b * 512:(b + 1) * 512]) for k in range(KC)]
                mm_group(pp[0:nr, b * 512:(b + 1) * 512], pairs, [t_mT, t_wo], tk, last_inc=(b == 1))
            S.op("dve", lambda e, pp=pp, yr_=yr_, xr_=xr_, nr=nr: e.scalar_tensor_tensor(
                out=yr_[0:nr, :], in0=pp[0:nr, :], scalar=0.5, in1=xr_[0:nr, :], op0=ALU.mult, op1=ALU.add),
                reads=[tk, txr], writes=[tyr])

        def f_stageA2(ti):
            src, dst, nr, c0 = ftiles[ti]
            xr_, txr = xr[ti % 2], t_xr[ti % 2]
            yr_, tyr = yr[ti % 3], t_yr[ti % 3]
            col = 32 + ti
            S.op("act", lambda e, xr_=xr_, yr_=yr_, nr=nr, col=col: e.activation(out=xr_[0:nr, :], in_=yr_[0:nr, :], func=AF.Square,
                                                                                accum_out=stat2[0:nr, col:col + 1]),
                 reads=[tyr, t_sm], writes=[txr, t_stF[ti]])

        def f_stageB1(ti):
            src, dst, nr, c0 = ftiles[ti]
            col = 32 + ti
            S.op("act", lambda e, nr=nr, col=col: e.activation(out=stat2[0:nr, col:col + 1], in_=stat2[0:nr, col:col + 1], func=AF.Sqrt,
                                                               scale=1.0 / D, bias=epst[0:nr, 0:1]),
                 reads=[t_stF[ti], t_eps], writes=[t_stF[ti]])

        def f_stageB(ti):
            src, dst, nr, c0 = ftiles[ti]
            yr_, tyr = yr[ti % 3], t_yr[ti % 3]
            col = 32 + ti
            S.op("dve", lambda e, nr=nr, col=col: e.reciprocal(out=stat2[0:nr, col:col + 1], in_=stat2[0:nr, col:col + 1]),
                 reads=[t_stF[ti]], writes=[t_stF[ti]])
            S.op("dve", lambda e, yr_=yr_, nr=nr, col=col: e.scalar_tensor_tensor(
                out=yr_[0:nr, :], in0=yr_[0:nr, :], scalar=stat2[0:nr, col:col + 1], in1=fgb[0:nr, :], op0=ALU.mult, op1=ALU.mult),
                reads=[tyr, t_stF[ti], t_fgb], writes=[tyr])
            S.dma("sp", ysem[ti % 3], dst, yr_[0:nr, :], reads=[tyr])

        f_stageA(0)
        f_stageA2(0)
        for ti in range(len(ftiles)):
            if ti + 1 < len(ftiles):
                f_stageA(ti + 1)
            f_stageB1(ti)
            if ti + 1 < len(ftiles):
                f_stageA2(ti + 1)
            f_stageB(ti)
        S.barrier()
        S.finish(block)
    return nc


_CACHE = {}


def _get_program():
    if "nc" not in _CACHE:
        _CACHE["nc"] = build_program()
    return _CACHE["nc"]


def kernel(x_prompt, x_sample, cache_mem_k, cache_mem_v, state_lru_h, state_lru_conv, state_sconv, mem_prompt,
           norm_g, mem_norm_g, w_in, lru_conv_w, lru_conv_b, lru_wa, lru_ba, lru_wx, lru_bx, lru_lambda, lru_wo,
           sconv_w, sconv_wo, xa_wk, xa_wv, xa_wo, w_out, final_norm_g):
    f = lambda a: np.ascontiguousarray(np.asarray(a, dtype=np.float32))
    shared = {
        "norm_g": f(norm_g[0]), "mem_norm_g": f(mem_norm_g[0]), "w_in": f(w_in[0]),
        "lru_conv_w": f(lru_conv_w[0]), "lru_conv_b": f(lru_conv_b[0]), "lru_wa": f(lru_wa[0]),
        "lru_ba": f(lru_ba[0]), "lru_wx": f(lru_wx[0]), "lru_bx": f(lru_bx[0]), "lru_lambda": f(lru_lambda[0]),
        "lru_wo": f(lru_wo[0]), "sconv_w": f(sconv_w[0]), "sconv_wo": f(sconv_wo[0]), "xa_wk": f(xa_wk[0]),
        "xa_wv": f(xa_wv[0]), "xa_wo": f(xa_wo[0]), "w_out": f(w_out[0]), "final_norm_g": f(final_norm_g),
    }
    in_maps = []
    for c in range(NCORES):
        sl = slice(c * TS, (c + 1) * TS)
        m = dict(shared)
        m["xp"] = f(x_prompt[c])
        m["xs"] = f(np.asarray(x_sample)[sl, 0, :])
        m["memp"] = f(mem_prompt[c])
        m["ck"] = f(np.asarray(cache_mem_k)[0, sl].reshape(TS, NM, XW))
        m["cv"] = f(np.asarray(cache_mem_v)[0, sl].reshape(TS, NM, XW))
        m["st_h"] = f(np.asarray(state_lru_h)[0, sl])
        m["st_lc"] = f(np.asarray(state_lru_conv)[0, sl].reshape(TS, 3 * W))
        m["st_sc"] = f(np.asarray(state_sconv)[0, sl].reshape(TS, 2 * W))
        in_maps.append(m)
    nc = _get_program()
    res = run_bass_kernel_spmd(nc, in_maps, core_ids=list(range(NCORES)))
    rs = res.results
    cat = lambda k: np.concatenate([np.asarray(r[k]) for r in rs], axis=0)
    y_prompt = np.stack([np.asarray(r["y_p"]) for r in rs], axis=0).astype(np.float32)
    y_sample = cat("y_s").reshape(NCORES * TS, 1, D).astype(np.float32)
    p_mk = np.stack([np.asarray(r["o_pk"]) for r in rs], axis=0).reshape(1, NCORES, NM, 4, 128).astype(np.float32)
    p_mv = np.stack([np.asarray(r["o_pv"]) for r in rs], axis=0).reshape(1, NCORES, NM, 4, 128).astype(np.float32)
    p_h = cat("o_ph").reshape(1, NCORES, W).astype(np.float32)
    p_lc = np.stack([np.asarray(r["o_plc"]) for r in rs], axis=0).reshape(1, NCORES, 3, W).astype(np.float32)
    p_sc = np.stack([np.asarray(r["o_psc"]) for r in rs], axis=0).reshape(1, NCORES, 2, W).astype(np.float32)
    s_h = cat("o_sh").reshape(1, NCORES * TS, W).astype(np.float32)
    s_lc = cat("o_slc").reshape(1, NCORES * TS, 3, W).astype(np.float32)
    s_sc = cat("o_ssc").reshape(1, NCORES * TS, 2, W).astype(np.float32)
    return (y_prompt, y_sample, p_mk, p_mv, p_h, p_lc, p_sc, s_h, s_lc, s_sc)
```

```python
import math
from contextlib import ExitStack

import numpy as np
import concourse.bass as bass
import concourse.mybir as mybir
from concourse.bass_utils import run_bass_kernel_spmd

F32 = mybir.dt.float32
BF16 = mybir.dt.bfloat16
AF = mybir.ActivationFunctionType
ALU = mybir.AluOpType
AX = mybir.AxisListType

NCORES = 8
T = 2048
TS = 16
TT = T + TS
D = 1024
KC = 8
W = 768
NCH = 6
NM = 256
XW = 512
IN_COLS = 8704
EPS = 1e-6
SCALE = 1.0 / math.sqrt(128.0)
STOP_AFTER = None

C_LX, C_LG, C_SB, C_SCG, C_SH, C_SG, C_Q, C_QG, C_MG = 0, 768, 1536, 2304, 3072, 3840, 4608, 5120, 5632


class Tok:
    __slots__ = ("name", "w", "r")

    def __init__(self, name=""):
        self.name = name
        self.w = None
        self.r = {}


class Stream:
    def __init__(self, key, sem):
        self.key = key
        self.sem = sem
        self.cnt = 0
        self.seen = {}
        self.ops = []
        self.pending = False


class Sched:
    def __init__(self, nc):
        self.nc = nc
        self.sems = {}
        self.streams = {}
        self.dma_cnt = {}

    def add_stream(self, key, sem):
        self.sems[key] = sem
        self.streams[key] = Stream(key, sem)

    def add_dma_sem(self, key, sem):
        self.sems[key] = sem
        self.dma_cnt[key] = 0

    def _needs(self, st, reads, writes):
        needs = {}

        def need(ev):
            if ev is None:
                return
            k, v = ev
            if needs.get(k, 0) < v:
                needs[k] = v
        for t in reads:
            need(t.w)
        for t in writes:
            need(t.w)
            for k, v in t.r.items():
                if k == st.key:
                    continue
                need((k, v))
        out = []
        for k, v in needs.items():
            if k == st.key and k == "pe":
                continue
            if st.seen.get(k, 0) < v:
                st.seen[k] = v
                out.append((k, v))
        return out

    def op(self, key, fn, reads=(), writes=(), inc=True):
        st = self.streams[key]
        waits = self._needs(st, reads, writes)
        if inc:
            st.cnt += 1
            st.pending = False
            ev = (key, st.cnt)
        else:
            st.pending = True
            ev = (key, st.cnt + 1)
        sems = self.sems
        sem = st.sem

        def run(eng, waits=waits, fn=fn, inc=inc):
            for k, v in waits:
                eng.wait_ge(sems[k], v)
            ins = fn(eng)
            if inc:
                ins.then_inc(sem, 1)
        st.ops.append(run)
        for t in writes:
            t.w = ev
            t.r = {}
        for t in reads:
            if t.r.get(key, 0) < ev[1]:
                t.r[key] = ev[1]

    def dma(self, key, semkey, out, in_, reads=(), writes=(), nowait=False, **kw):
        st = self.streams[key]
        waits = [] if nowait else self._needs(st, reads, writes)
        self.dma_cnt[semkey] += 16
        ev = (semkey, self.dma_cnt[semkey])
        sems = self.sems

        def run(eng, waits=waits):
            for k, v in waits:
                eng.wait_ge(sems[k], v)
            eng.dma_start(out=out, in_=in_, **kw).then_inc(sems[semkey], 16)
        st.ops.append(run)
        for t in writes:
            t.w = ev
            t.r = {}
        for t in reads:
            if t.r.get(semkey, 0) < ev[1]:
                t.r[semkey] = ev[1]

    def wait_dma(self, keys, semkey):
        v = self.dma_cnt[semkey]
        sems = self.sems
        for k in keys:
            st = self.streams[k]
            if v and st.seen.get(semkey, 0) < v:
                st.seen[semkey] = v
                st.ops.append(lambda eng, v=v: eng.wait_ge(sems[semkey], v))

    def barrier(self, skip=(), no_wait=()):
        targets = {}
        for k, st in self.streams.items():
            assert not st.pending
            if st.cnt:
                targets[k] = st.cnt
        for k, v in self.dma_cnt.items():
            if v and k not in skip:
                targets[k] = v
        sems = self.sems
        for k, st in self.streams.items():
            if k in no_wait:
                continue
            waits = []
            for tk, tv in targets.items():
                if st.seen.get(tk, 0) < tv:
                    st.seen[tk] = tv
                    waits.append((tk, tv))

            def run(eng, waits=waits):
                for kk, v in waits:
                    eng.wait_ge(sems[kk], v)
            st.ops.append(run)

    def finish(self, block):
        for key, st in self.streams.items():
            assert not st.pending, key
        ss = self.streams

        def mk(key):
            def body(eng):
                for o in ss[key].ops:
                    o(eng)
            return body
        block.gpsimd(mk("pool"))
        block.tensor(mk("pe"))
        block.scalar(mk("act"))
        block.vector(mk("dve"))
        block.sync(mk("sp"))


def build_program():
    nc = bass.Bass("TRN2", target_bir_lowering=False)

    def din(name, shape):
        return nc.dram_tensor(name, shape, F32, kind="ExternalInput").ap()

    def dout(name, shape):
        return nc.dram_tensor(name, shape, F32, kind="ExternalOutput").ap()

    xp = din("xp", [T, D])
    xs = din("xs", [TS, D])
    memp = din("memp", [NM, D])
    ck = din("ck", [TS, NM, XW])
    cv = din("cv", [TS, NM, XW])
    st_h = din("st_h", [TS, W])
    st_lc = din("st_lc", [TS, 3 * W])
    st_sc = din("st_sc", [TS, 2 * W])
    norm_g = din("norm_g", [D])
    mem_norm_g = din("mem_norm_g", [D])
    w_in = din("w_in", [D, IN_COLS])
    lru_conv_w = din("lru_conv_w", [4, W])
    lru_conv_b = din("lru_conv_b", [W])
    lru_wa = din("lru_wa", [NCH, 128, 128])
    lru_ba = din("lru_ba", [W])
    lru_wx = din("lru_wx", [NCH, 128, 128])
    lru_bx = din("lru_bx", [W])
    lru_lambda = din("lru_lambda", [W])
    lru_wo = din("lru_wo", [W, D])
    sconv_w = din("sconv_w", [3, W])
    sconv_wo = din("sconv_wo", [W, D])
    xa_wk = din("xa_wk", [D, XW])
    xa_wv = din("xa_wv", [D, XW])
    xa_wo = din("xa_wo", [XW, D])
    w_out = din("w_out", [D, D])
    final_norm_g = din("final_norm_g", [D])

    y_p = dout("y_p", [T, D])
    y_s = dout("y_s", [TS, D])
    o_pk = dout("o_pk", [NM, XW])
    o_pv = dout("o_pv", [NM, XW])
    o_ph = dout("o_ph", [1, W])
    o_plc = dout("o_plc", [3, W])
    o_psc = dout("o_psc", [2, W])
    o_sh = dout("o_sh", [TS, W])
    o_slc = dout("o_slc", [TS, 3 * W])
    o_ssc = dout("o_ssc", [TS, 2 * W])

    dbg = dout("dbg", [128, 4096]) if STOP_AFTER else None
    es = ExitStack()
    with es:
        def sb(name, shape, dt):
            return es.enter_context(nc.sbuf_tensor(name, shape, dt))

        uT_t = sb("uT", [128, KC * TT], BF16)
        uT = uT_t[:].rearrange("p (k t) -> p k t", k=KC)
        gA_t = sb("gA", [128, NCH * TT], BF16)
        gA = gA_t[:].rearrange("p (k t) -> p k t", k=NCH)
        gB_t = sb("gB", [128, NCH * TT], BF16)
        gB = gB_t[:].rearrange("p (k t) -> p k t", k=NCH)
        og_t = sb("og", [128, 4 * TT], BF16)
        og = og_t[:].rearrange("p (k t) -> p k t", k=4)
        NSLOT = 5
        SLOT_E = 3072
        slots = [sb(f"wslot{i}", [128, SLOT_E], BF16) for i in range(NSLOT)]
        ident = sb("ident", [128, 128], F32)
        identb = sb("identb", [128, 128], BF16)
        onesb = sb("onesb", [128, 128], BF16)
        pT = sb("pT", [128, 82], F32)
        dv = sb("dv", [128, 24], F32)
        SIT_t = sb("SIT", [128, 36 * TS], F32)
        SIT = SIT_t[:].rearrange("p (a b) -> p a b", a=36)
        SO_t = sb("SO", [128, NCH * 54], F32)
        SO = SO_t[:].rearrange("p (a b) -> p a b", a=NCH)
        stat = sb("stat", [128, 64], F32)
        stat2 = sb("stat2", [128, 64], F32)
        epst = sb("epst", [128, 1], F32)
        q25 = sb("q25", [128, 1], F32)
        RW = 18816
        R = sb("R", [128, RW], F32)

        def carve(off_b, nelem, dt):
            assert off_b % 4 == 0
            if dt == F32:
                assert off_b // 4 + nelem <= RW, (off_b, nelem)
                return R[:, off_b // 4: off_b // 4 + nelem]
            assert nelem % 2 == 0 and off_b // 4 + nelem // 2 <= RW, (off_b, nelem)
            return R[:, off_b // 4: off_b // 4 + nelem // 2].bitcast(BF16)

        PP = [es.enter_context(nc.psum_tensor(f"PP{i}", [128, 1024], F32)) for i in range(3)]
        PS = es.enter_context(nc.psum_tensor("PS", [128, 512], F32))
        PX = es.enter_context(nc.psum_tensor("PX", [128, 512], F32))
        tPP = [Tok(f"PP{i}") for i in range(3)]
        tPS = [Tok(f"PS{i}") for i in range(8)]
        tPX = Tok("PX")
        PSb = PS[:].bitcast(BF16)
        PXb = PX[:].bitcast(BF16)

        S = Sched(nc)
        for k in ["pe", "act", "dve", "pool", "sp"]:
            S.add_stream(k, es.enter_context(nc.semaphore("s_" + k)))
        dsem_names = (["cst", "cs2", "cs3", "cs4", "cs5", "sva", "svb", "svc", "svd", "sve", "svf", "fxa", "fxb", "out", "xa", "xb", "xc", "ya", "yb", "ka", "kb", "va", "vb"]
                      + [f"ws{i}" for i in range(NSLOT)])
        for k in dsem_names:
            S.add_dma_sem(k, es.enter_context(nc.semaphore("d_" + k)))
        block = es.enter_context(nc.Block())

        state = {"pp": 0, "ps": 0}

        def next_pp():
            i = state["pp"] % 3
            state["pp"] += 1
            return PP[i], tPP[i]

        def next_ps():
            i = state["ps"] % 8
            state["ps"] += 1
            return i, tPS[i]

        wtiles = []
        wstate = {"issued": 0}
        tslot = [Tok(f"slot{i}") for i in range(NSLOT)]

        def w_in_cols(c0, ncols):
            return w_in.rearrange("(k p) c -> p k c", p=128)[:, :, c0:c0 + ncols]

        def add_wtile(pieces):
            off = 0
            lst = []
            for ap in pieces:
                kc, ncols = ap.shape[1], ap.shape[2]
                lst.append((ap, off, kc, ncols))
                off += kc * ncols
            assert off <= SLOT_E, off
            wtiles.append(lst)
            return len(wtiles) - 1

        def w_issue_upto(i):
            while wstate["issued"] <= min(i, len(wtiles) - 1):
                j = wstate["issued"]
                sl = j % NSLOT
                for pi, (ap, off, kc, ncols) in enumerate(wtiles[j]):
                    dst = slots[sl][:, off:off + kc * ncols].rearrange("p (k c) -> p k c", k=kc)
                    S.dma("pool", f"ws{sl}", dst, ap, writes=[tslot[sl]], nowait=(pi > 0))
                wstate["issued"] += 1

        def w_get(i):
            w_issue_upto(i + NSLOT - 2)
            sl = i % NSLOT
            views = []
            for (ap, off, kc, ncols) in wtiles[i]:
                views.append(slots[sl][:, off:off + kc * ncols].rearrange("p (k c) -> p k c", k=kc))
            return views, tslot[sl]

        WT = {}
        WT["wk0"] = add_wtile([xa_wk.rearrange("(k p) c -> p k c", p=128)[:, :, 0:256]])
        WT["wk1"] = add_wtile([xa_wk.rearrange("(k p) c -> p k c", p=128)[:, :, 256:512]])
        WT["wv0"] = add_wtile([xa_wv.rearrange("(k p) c -> p k c", p=128)[:, :, 0:256]])
        WT["wv1"] = add_wtile([xa_wv.rearrange("(k p) c -> p k c", p=128)[:, :, 256:512]])
        for hh in range(2):
            WT[f"q{hh}"] = add_wtile([w_in_cols(C_Q + hh * 256, 256)])
        for hh in range(2):
            WT[f"qg{hh}"] = add_wtile([w_in_cols(C_QG + hh * 256, 256)])
        for n in range(NCH + 1):
            if n < NCH:
                WT[f"A{n}"] = add_wtile([w_in_cols(C_LG + n * 128, 128), w_in_cols(C_LX + n * 128, 128)])
            if n >= 1:
                m_ = n - 1
                WT[f"B{m_}a"] = add_wtile([w_in_cols(C_SG + m_ * 128, 128), w_in_cols(C_SB + m_ * 128, 128)])
                WT[f"B{m_}b"] = add_wtile([w_in_cols(C_SCG + m_ * 128, 128), w_in_cols(C_SH + m_ * 128, 128)])
        lwo = lru_wo.rearrange("(k p) c -> p k c", p=128)
        swo = sconv_wo.rearrange("(k p) c -> p k c", p=128)
        xwo = xa_wo.rearrange("(k p) c -> p k c", p=128)
        for j in range(8):
            WT[f"Mg{j}"] = add_wtile([w_in_cols(C_MG + x * 1024 + j * 128, 128) for x in range(3)])
            WT[f"Mo{j}"] = add_wtile([lwo[:, :, j * 128:(j + 1) * 128], swo[:, :, j * 128:(j + 1) * 128],
                                      xwo[:, :, j * 128:(j + 1) * 128]])

        def mm_group(out_ap, pairs, reads, wtok, last_inc=True):
            n = len(pairs)
            for i, (l, r) in enumerate(pairs):
                S.op("pe", lambda e, l=l, r=r, i=i: e.matmul(out_ap, l, r, start=(i == 0), stop=(i == n - 1)),
                     reads=reads, writes=[wtok], inc=(last_inc and i == n - 1))

        def unit_half(lhs_list, rhs_arr, lo, reads):
            pp, tk = next_pp()
            nk = len(lhs_list)
            for b in range(2):
                pairs = [(lhs_list[k], rhs_arr[:, k, lo + b * 512: lo + (b + 1) * 512]) for k in range(nk)]
                mm_group(pp[:, b * 512:(b + 1) * 512], pairs, reads, tk, last_inc=(b == 1))
            return pp, tk

        tPSb = Tok("PSbank")
        tPS2 = [tPSb, tPX]

        def unit_samp(lhs_list, rhs_arr, reads):
            i = state.setdefault("ps2", 0) % 2
            state["ps2"] += 1
            tk = tPS2[i]
            nk = len(lhs_list)
            bank = PS if i == 0 else PX
            out_ap = bank[:, 0:TS]
            pairs = [(lhs_list[k], rhs_arr[:, k, T:TT]) for k in range(nk)]
            mm_group(out_ap, pairs, reads, tk)
            return out_ap, tk

        HALVES = [(0, 1024), (1024, 2048)]

        def dump(items, off_b=56320):
            dstage = carve(off_b, 4096, F32)
            S.barrier()
            S.op("dve", lambda e: e.memset(dstage[:], 0.0))
            S.barrier()
            off = 0
            for ap in items:
                P_, n_ = ap.shape[0], ap.shape[1]
                S.op("dve", lambda e, ap=ap, off=off, P_=P_, n_=n_: e.tensor_copy(dstage[0:P_, off:off + n_], ap))
                off += n_
            S.barrier()
            S.dma("sp", "out", dbg[:, :], dstage[:])
            S.barrier()

        t_ident = Tok("ident")
        S.op("pool", lambda e: e.memset(ident[:], 0.0), writes=[t_ident])
        S.op("pool", lambda e: e.affine_select(ident[:], ident[:], pattern=[[-1, 128]], compare_op=ALU.not_equal,
                                               fill=1.0, base=0, channel_multiplier=1),
             reads=[t_ident], writes=[t_ident])
        t_identb = Tok("identb")
        S.op("dve", lambda e: e.tensor_copy(identb[:], ident[:]), reads=[t_ident], writes=[t_identb])
        t_stat = Tok("stat")
        t_sm = Tok("sm")
        t_st3 = t_sm
        S.op("pool", lambda e: e.memset(stat[:], 0.0), writes=[t_stat])
        S.op("pool", lambda e: e.memset(stat2[:], 0.0), writes=[t_sm])
        t_ones = Tok("ones")
        S.op("pool", lambda e: e.memset(onesb[:], 1.0), writes=[t_ones])
        t_eps = Tok("eps")
        S.op("pool", lambda e: e.memset(epst[:], EPS), writes=[t_eps])
        S.op("pool", lambda e: e.memset(q25[:], 0.25), writes=[t_eps])

        prt = carve(0, 128, F32)[0:82, :]
        sin = carve(512, 4608, F32)
        NXB = 8
        xbuf = [carve(18944 + i * 4096, 1024, F32) for i in range(3)] + [carve(53760 + i * 4096, 1024, F32) for i in range(5)]
        xnb = [carve(31232 + i * 2048, 1024, BF16) for i in range(2)]
        junk = carve(35328, 1024, BF16)
        mnT_t = carve(37376, KC * NM, BF16)
        mnT = mnT_t.rearrange("p (k t) -> p k t", k=KC)
        kT_t = carve(41472, 4 * NM, BF16)
        kT = kT_t.rearrange("p (k t) -> p k t", k=4)
        vb_t = carve(43520, 2 * XW, BF16)
        vb = vb_t.rearrange("p (k t) -> p k t", k=2)
        kvout = [carve(45568 + i * 4096, 2 * XW, F32).rearrange("p (k t) -> p k t", k=2) for i in range(2)]

        t_prt = Tok("prt")

        def rows(v, r):
            return v.rearrange("(r c) -> r c", c=128)

        plist = [(norm_g, 0, 8, None), (mem_norm_g, 8, 8, None),
                 (lru_conv_w, 16, 24, "k (n c) -> (k n) c"), (lru_conv_b, 40, 6, None), (lru_ba, 46, 6, None),
                 (lru_bx, 52, 6, None), (lru_lambda, 58, 6, None), (sconv_w, 64, 18, "k (n c) -> (k n) c")]
        for (v, r0, nr, pat) in plist:
            src = v.rearrange(pat, c=128) if pat else v.rearrange("(r c) -> r c", c=128)
            S.dma("sp", "cst", prt[r0:r0 + nr, :], src, writes=[t_prt], nowait=True)
        if STOP_AFTER == "P0dma":
            S.barrier()
            S.finish(block)
            return nc
        t_pT = Tok("pT")
        S.op("pe", lambda e: e.transpose(PX[:, 0:82], prt, ident[0:82, 0:82]), reads=[t_prt, t_ident], writes=[tPX])
        S.op("act", lambda e: e.copy(out=pT[:], in_=PX[:, 0:82]), reads=[tPX], writes=[t_pT])
        if STOP_AFTER == "P0b":
            S.barrier()
            S.finish(block)
            return nc
        t_x = [Tok(f"x{i}") for i in range(NXB)]
        t_xn = [Tok(f"xn{i}") for i in range(2)]
        t_junk = Tok("junk")
        t_uT = Tok("uT")
        t_mnT = Tok("mnT")
        tiles = [(memp[i * 128:(i + 1) * 128, :], 128, "m", i * 128) for i in range(2)]
        tiles += [(xp[i * 128:(i + 1) * 128, :], 128, "u", i * 128) for i in range(16)]
        tiles.append((xs[:, :], TS, "u", T))
        xsem = ["xa", "xb", "xc", "ya", "yb", "ka", "kb", "va"]
        tPXh = [Tok("PXh0"), Tok("PXh1")]
        t_st0 = [Tok(f"st0_{i}") for i in range(len(tiles))]

        def p0_stageA(ti):
            src, nr, kind, c0 = tiles[ti]
            xb_, tx = xbuf[ti % NXB], t_x[ti % NXB]
            S.dma("sp", xsem[ti % NXB], xb_[0:nr, :], src, writes=[tx])
            S.op("act", lambda e, xb_=xb_, nr=nr, ti=ti: e.activation(out=junk[0:nr, :], in_=xb_[0:nr, :], func=AF.Square,
                                                                      accum_out=stat[0:nr, ti:ti + 1]),
                 reads=[tx, t_stat], writes=[t_st0[ti], t_junk])

        def p0_stageB(ti):
            src, nr, kind, c0 = tiles[ti]
            xb_, tx = xbuf[ti % NXB], t_x[ti % NXB]
            xn_, txn = xnb[ti % 2], t_xn[ti % 2]
            S.op("act", lambda e, nr=nr, ti=ti: e.activation(out=stat[0:nr, ti:ti + 1], in_=stat[0:nr, ti:ti + 1], func=AF.Sqrt,
                                                             scale=1.0 / D, bias=epst[0:nr, 0:1]),
                 reads=[t_st0[ti], t_eps], writes=[t_st0[ti]])
            S.op("dve", lambda e, nr=nr, ti=ti: e.reciprocal(out=stat[0:nr, ti:ti + 1], in_=stat[0:nr, ti:ti + 1]),
                 reads=[t_st0[ti]], writes=[t_st0[ti]])
            S.op("dve", lambda e, xb_=xb_, xn_=xn_, nr=nr, ti=ti: e.tensor_scalar(out=xn_[0:nr, :], in0=xb_[0:nr, :],
                                                                                scalar1=stat[0:nr, ti:ti + 1], scalar2=None,
                                                                                op0=ALU.mult),
                 reads=[tx, t_st0[ti]], writes=[txn])

        def p0_stageC(ti):
            src, nr, kind, c0 = tiles[ti]
            xn_, txn = xnb[ti % 2], t_xn[ti % 2]
            bankb, tbank = (PXb, tPX) if ti % 2 == 0 else (PSb, tPSb)
            for kc in range(KC):
                S.op("pe", lambda e, xn_=xn_, nr=nr, kc=kc, bankb=bankb: e.transpose(bankb[:, kc * 128: kc * 128 + nr],
                                                                        xn_[0:nr, kc * 128:(kc + 1) * 128],
                                                                        identb[0:nr, 0:nr]),
                     reads=[txn, t_identb], writes=[tbank], inc=(kc == KC - 1))
            pview = bankb.rearrange("p (k t) -> p k t", k=KC)[:, :, 0:nr]
            if kind == "u":
                dst = uT[:, :, c0:c0 + nr]
                gcols = pT[:, 0:8]
                tdst = t_uTi[ti]
            else:
                dst = mnT[:, :, c0:c0 + nr]
                gcols = pT[:, 8:16]
                tdst = t_mnT
            S.op("dve", lambda e, dst=dst, pview=pview, gcols=gcols, nr=nr: e.tensor_tensor(
                out=dst, in0=pview, in1=gcols.unsqueeze(2).to_broadcast([128, KC, nr]), op=ALU.mult),
                reads=[tbank, t_pT], writes=[tdst])

        t_kT = Tok("kT")
        t_vb = Tok("vb")
        t_kvout = [Tok("kvo0"), Tok("kvo1")]
        kvst = {}

        def kv_mm():
            for which in range(2):
                wv_, wt_ = [], []
                for hh in range(2):
                    vws, tk = w_get(WT[("wk" if which == 0 else "wv") + str(hh)])
                    wv_.append(vws[0])
                    wt_.append(tk)
                pp, tk = next_pp()
                for mc in range(2):
                    for hh in range(2):
                        pairs = [(mnT[:, k, mc * 128:(mc + 1) * 128], wv_[hh][:, k, :]) for k in range(KC)]
                        mm_group(pp[:, mc * 512 + hh * 256: mc * 512 + (hh + 1) * 256], pairs, [t_mnT, wt_[hh]], tk,
                                 last_inc=(mc == 1 and hh == 1))
                kvst[which] = (pp, tk)
                if which == 0:
                    pp2, tk2 = next_pp()
                    for dc in range(4):
                        hh, off = dc // 2, (dc % 2) * 128
                        pairs = [(wv_[hh][:, k, off:off + 128], mnT[:, k, :]) for k in range(KC)]
                        mm_group(pp2[:, dc * 256:(dc + 1) * 256], pairs, [t_mnT, wt_[hh]], tk2, last_inc=(dc == 3))
                    kvst["kT"] = (pp2, tk2)

        def kv_evac():
            for which in range(2):
                pp, tk = kvst[which]
                ko = kvout[which]
                S.op("act", lambda e, ko=ko, pp=pp: e.copy(out=ko, in_=pp[:].rearrange("p (k t) -> p k t", k=2)),
                     reads=[tk], writes=[t_kvout[which]])
                dsto = (o_pk if which == 0 else o_pv).rearrange("(k p) c -> p k c", p=128)
                S.dma("sp", "out", dsto, ko, reads=[t_kvout[which]])
            pp, tk = kvst[1]
            S.op("act", lambda e, pp=pp: e.copy(out=vb, in_=pp[:].rearrange("p (k t) -> p k t", k=2)),
                 reads=[tk], writes=[t_vb])
            pp2, tk2 = kvst["kT"]
            S.op("dve", lambda e, pp2=pp2: e.tensor_copy(kT, pp2[:].rearrange("p (k t) -> p k t", k=4)),
                 reads=[tk2], writes=[t_kT])

        t_uTi = [Tok(f"uT{i}") for i in range(len(tiles))]
        PD = 5
        p0_stageA(0)
        p0_stageB(0)
        for ti in range(1, PD):
            p0_stageA(ti)
        t_sin = Tok("sin")
        S.dma("sp", "cs2", sin[0:TS, 0:2304], st_lc[:, :], writes=[t_sin], nowait=True)
        S.dma("sp", "cs2", sin[0:TS, 2304:3072], st_h[:, :], writes=[t_sin], nowait=True)
        S.dma("sp", "cs2", sin[0:TS, 3072:4608], st_sc[:, :], writes=[t_sin], nowait=True)
        t_out = Tok("out")
        o_slc3 = o_slc.rearrange("b (k c) -> b k c", k=3)
        st_lc3 = st_lc.rearrange("b (k c) -> b k c", k=3)
        S.dma("sp", "out", o_slc3[:, 0:2, :], st_lc3[:, 1:3, :])
        o_ssc3 = o_ssc.rearrange("b (k c) -> b k c", k=2)
        st_sc3 = st_sc.rearrange("b (k c) -> b k c", k=2)
        S.dma("sp", "out", o_ssc3[:, 0:1, :], st_sc3[:, 1:2, :])

        t_dv = Tok("dv")
        t_SIT = Tok("SIT")
        qT_t = carve(0, 4 * T, BF16)
        qT = qT_t.rearrange("p (k t) -> p k t", k=4)
        t_qT = [Tok(f"qT{h}") for h in range(4)]

        def sit():
            blocks_ = []
            for n in range(NCH):
                for k in range(3):
                    blocks_.append((n * 6 + k, k * W + n * 128))
                blocks_.append((n * 6 + 3, 2304 + n * 128))
                for k in range(2):
                    blocks_.append((n * 6 + 4 + k, 3072 + k * W + n * 128))
            for g0 in range(0, 36, 18):
                grp = blocks_[g0:g0 + 18]
                for gi, (slot_i, c0) in enumerate(grp):
                    S.op("pe", lambda e, gi=gi, c0=c0: e.transpose(PX[:, gi * TS:(gi + 1) * TS], sin[0:TS, c0:c0 + 128],
                                                                  ident[0:TS, 0:TS]),
                         reads=[t_sin, t_ident], writes=[tPX], inc=(gi == len(grp) - 1))
                assert [b_[0] for b_ in grp] == list(range(g0, g0 + 18))
                S.op("act", lambda e, g0=g0: e.copy(out=SIT[:, g0:g0 + 18, :],
                                                    in_=PX[:, 0:18 * TS].rearrange("p (a b) -> p a b", a=18)),
                     reads=[tPX], writes=[t_SIT])


        def dv_derive():
            S.op("act", lambda e: e.activation(out=dv[:, 0:6], in_=pT[:, 58:64], func=AF.Exp, scale=-1.0),
                 reads=[t_pT], writes=[t_dv])
            S.op("act", lambda e: e.activation(out=dv[:, 6:12], in_=dv[:, 0:6], func=AF.Ln, bias=1.0, scale=1.0),
                 reads=[t_dv], writes=[t_dv])
            S.op("dve", lambda e: e.tensor_scalar(out=dv[:, 0:6], in0=dv[:, 6:12], scalar1=-4.0, scalar2=None, op0=ALU.mult),
                 reads=[t_dv], writes=[t_dv])
            S.op("dve", lambda e: e.tensor_scalar(out=dv[:, 6:12], in0=dv[:, 6:12], scalar1=-8.0, scalar2=None, op0=ALU.mult),
                 reads=[t_dv], writes=[t_dv])
            S.op("dve", lambda e: e.tensor_scalar(out=dv[:, 12:24], in0=pT[:, 46:58], scalar1=0.5, scalar2=None, op0=ALU.mult),
                 reads=[t_pT, t_dv], writes=[t_dv])

        eqst = {}

        def eq_bank(h, b):
            rd_u = [t_uTi[i] for i in range(2, 10)]
            hh, hl = h // 2, h % 2
            vws, wtk = w_get(WT[f"q{hh}"])
            wq = vws[0]
            if b == 0:
                eqst[h] = next_pp()
            pp, tk = eqst[h]
            pairs = [(wq[:, k, hl * 128:(hl + 1) * 128], uT[:, k, b * 512:(b + 1) * 512]) for k in range(KC)]
            mm_group(pp[:, b * 512:(b + 1) * 512], pairs, [wtk] + rd_u, tk, last_inc=True)

        def eq_evac(heads):
            for h in heads:
                pp, tk = eqst[h]
                S.op("act", lambda e, pp=pp, h=h: e.copy(out=qT[:, h, 0:1024], in_=pp[:]),
                     reads=[tk], writes=[t_qT[h], t_sin, t_prt])

        for ti in range(len(tiles)):
            if ti + PD < len(tiles):
                p0_stageA(ti + PD)
            if ti + 1 < len(tiles):
                p0_stageB(ti + 1)
            p0_stageC(ti)
            if ti == 1:
                kv_mm()
            if ti == 5:
                kv_evac()
            if ti == 6:
                sit()
            if 9 <= ti <= 16:
                eq_bank((ti - 9) // 2, (ti - 9) % 2)
            if ti in (12, 14, 16):
                eq_evac([(ti - 12) // 2])
            if ti == 18:
                eq_evac([3])

        dv_derive()
        if STOP_AFTER == "P0c":
            dump([pT[:], stat[:, 0:19], uT[:, 0, 0:256], uT[:, 7, T - 128:TT], mnT[:, 0, :], mnT[:, 7, :]])
            S.barrier()
            S.finish(block)
            return nc
        S.barrier(skip=("out",))
        if STOP_AFTER == "KV":
            S.finish(block)
            return nc

        qT_t = carve(0, 4 * T, BF16)
        qT = qT_t.rearrange("p (k t) -> p k t", k=4)
        qs_tok = carve(16384, XW, BF16)
        sqg_tok = carve(17408, XW, F32)
        pTe = [carve(19456 + i * 2048, 1024, BF16).rearrange("p (k t) -> p k t", k=2) for i in range(2)]
        rden = [carve(23552 + i * 2048, 512, F32) for i in range(2)]
        o1b = [carve(27648 + i * 2048, 512, F32) for i in range(2)]
        selb_t = carve(31744, TS * 128, BF16)
        selb = selb_t.rearrange("p (b c) -> p b c", b=TS)
        eye16_t = carve(35840, 256, F32)
        eye16 = eye16_t.rearrange("p (a b) -> p a b", a=16)
        Sall_t = carve(45568, 128, F32)
        Sall = Sall_t.rearrange("p (c b h) -> p c b h", c=2, b=TS)
        Esm = carve(46080, 256, F32)
        Psm = carve(47104, 256, F32)
        Mk_t = carve(48128, 2 * 4 * 16 * 16, BF16)
        Mk = Mk_t.rearrange("p (c h b q) -> p c h b q", c=2, h=4, b=16)
        qb_sb = [carve(56320 + i * 2048, 512, F32) for i in range(2)]
        Kb = [carve(60416 + i * 4096, 1024, F32).rearrange("p (c f) -> p c f", c=2) for i in range(2)]
        prod = carve(68608, 1024, F32).rearrange("p (c f) -> p c f", c=2)

        t_og = [Tok(f"og{h}") for h in range(4)]
        t_qs = Tok("qs")
        t_sqg = Tok("sqg")
        def next_pp_c():
            while True:
                i = state["pp"] % 3
                state["pp"] += 1
                if i != state.get("pp_excl", -1):
                    return PP[i], tPP[i]

        def unit_half_c(lhs_list, rhs_arr, lo, reads):
            pp, tk = next_pp_c()
            nk = len(lhs_list)
            for b in range(2):
                pairs = [(lhs_list[k], rhs_arr[:, k, lo + b * 512: lo + (b + 1) * 512]) for k in range(nk)]
                mm_group(pp[:, b * 512:(b + 1) * 512], pairs, reads, tk, last_inc=(b == 1))
            return pp, tk

        t_pTe = [Tok("pTe0"), Tok("pTe1")]
        t_rden = [Tok("rden0"), Tok("rden1")]
        t_o1 = [Tok("o10"), Tok("o11")]

        def gen_C_prompt():
            for hh in range(2):
                vws, wtk = w_get(WT[f"q{hh}"])
                wq = vws[0]
                pairs = [(uT[:, k, T:TT], wq[:, k, :]) for k in range(KC)]
                mm_group(PS[0:TS, 0:256], pairs, [wtk], tPSb)
                S.op("act", lambda e, hh=hh: e.copy(out=qs_tok[0:TS, hh * 256:(hh + 1) * 256], in_=PS[0:TS, 0:256]),
                     reads=[tPSb], writes=[t_qs])
                yield
            for hh in range(2):
                vws, wtk = w_get(WT[f"q{hh}"])
                wq = vws[0]
                for hl in range(2):
                    h = hh * 2 + hl
                    lhs = [wq[:, k, hl * 128:(hl + 1) * 128] for k in range(KC)]
                    for (lo, hi) in HALVES[1:]:
                        pp, tk = unit_half_c(lhs, uT, lo, [wtk])
                        S.op("act", lambda e, pp=pp, h=h, lo=lo, hi=hi: e.copy(out=qT[:, h, lo:hi], in_=pp[:]),
                             reads=[tk], writes=[t_qT[h]])
                        yield
            for hh in range(2):
                vws, wtk = w_get(WT[f"qg{hh}"])
                wq = vws[0]
                for hl in range(2):
                    h = hh * 2 + hl
                    lhs = [wq[:, k, hl * 128:(hl + 1) * 128] for k in range(KC)]
                    for (lo, hi) in HALVES:
                        pp, tk = unit_half_c(lhs, uT, lo, [wtk])
                        S.op("act", lambda e, pp=pp, h=h, lo=lo, hi=hi: e.activation(out=og[:, h, lo:hi], in_=pp[:], func=AF.Silu),
                             reads=[tk], writes=[t_og[h]])
                        yield
                pairs = [(uT[:, k, T:TT], wq[:, k, :]) for k in range(KC)]
                mm_group(PS[0:TS, 0:256], pairs, [wtk], tPSb)
                S.op("act", lambda e, hh=hh: e.activation(out=sqg_tok[0:TS, hh * 256:(hh + 1) * 256],
                                                         in_=PS[0:TS, 0:256], func=AF.Silu),
                     reads=[tPSb], writes=[t_sqg])
                yield
            iters = [(tb, h) for tb in range(4) for h in range(4)]
            stage1 = {}

            def att_s1(i):
                tb, h = iters[i]
                c0 = tb * 512
                pe_, tpe = pTe[i % 2], t_pTe[i % 2]
                pp, tk = next_pp_c()
                for mc in range(2):
                    mm_group(pp[:, mc * 512:(mc + 1) * 512], [(kT[:, h, mc * 128:(mc + 1) * 128], qT[:, h, c0:c0 + 512])],
                             [t_kT, t_qT[h]], tk, last_inc=(mc == 1))
                S.op("act", lambda e, pp=pp, pe_=pe_: e.activation(out=pe_, in_=pp[:].rearrange("p (k t) -> p k t", k=2),
                                                                  func=AF.Exp, scale=SCALE),
                     reads=[tk], writes=[tpe])

            def att_s2(i):
                tb, h = iters[i]
                c0 = tb * 512
                pe_, tpe = pTe[i % 2], t_pTe[i % 2]
                rd, trd = rden[i % 2], t_rden[i % 2]
                o1, to1 = o1b[i % 2], t_o1[i % 2]
                pp2, tk2 = next_pp_c()
                mm_group(pp2[:, 0:512], [(vb[:, mc, h * 128:(h + 1) * 128], pe_[:, mc, :]) for mc in range(2)],
                         [t_vb, tpe], tk2, last_inc=False)
                mm_group(pp2[:, 512:1024], [(onesb[:], pe_[:, mc, :]) for mc in range(2)], [t_ones, tpe], tk2)
                S.op("act", lambda e, pp2=pp2, rd=rd: e.activation(out=rd, in_=pp2[:, 512:1024], func=AF.Ln), reads=[tk2], writes=[trd])
                S.op("act", lambda e, rd=rd: e.activation(out=rd, in_=rd, func=AF.Exp, scale=-1.0), reads=[trd], writes=[trd])
                S.op("dve", lambda e, pp2=pp2, rd=rd, o1=o1: e.tensor_tensor(out=o1, in0=pp2[:, 0:512], in1=rd, op=ALU.mult),
                     reads=[tk2, trd], writes=[to1])
                S.op("dve", lambda e, o1=o1, h=h, c0=c0: e.tensor_tensor(out=og[:, h, c0:c0 + 512], in0=o1,
                                                                        in1=og[:, h, c0:c0 + 512], op=ALU.mult),
                     reads=[to1, t_og[h]], writes=[t_og[h]])

            att_s1(0)
            for i in range(len(iters)):
                if i + 1 < len(iters):
                    att_s1(i + 1)
                att_s2(i)
                yield

        t_selb = Tok("selb")
        t_eye = Tok("eye")
        t_qb = [Tok("qb0"), Tok("qb1")]
        t_Kb = [Tok("Kb0"), Tok("Kb1")]
        t_prod = Tok("prod")
        t_Sall = Tok("Sall")
        t_E = Tok("E")
        t_P = Tok("P")
        t_Mk = Tok("Mk")
        ksem = ["ka", "kb"]
        vsem = ["sva", "svb", "svc", "svd", "sve", "svf"]

        def gen_C_sample():
            S.wait_dma(["dve", "act", "pool", "pe"], "out")
            S.op("dve", lambda e: e.tensor_copy(selb[0:TS, :, :], identb[0:TS, 0:TS].unsqueeze(2).to_broadcast([TS, TS, 128])),
                 reads=[t_identb], writes=[t_selb])
            S.op("pool", lambda e: e.memset(eye16_t, 0.0), writes=[t_eye])
            S.op("pool", lambda e: e.affine_select(eye16, eye16, pattern=[[1, 16], [-1, 16]], compare_op=ALU.not_equal,
                                                   fill=1.0, base=0, channel_multiplier=0),
                 reads=[t_eye], writes=[t_eye])
            yield
            for b in range(TS):
                kb_, tkb = Kb[b % 2], t_Kb[b % 2]
                qb_, tqb = qb_sb[b % 2], t_qb[b % 2]
                S.dma("sp", ksem[b % 2], kb_, ck[b].rearrange("(c m) f -> m c f", c=2), writes=[tkb])
                S.op("pe", lambda e, b=b: e.matmul(PX[:, 0:512], selb[0:TS, b, :], qs_tok[0:TS, :], start=True, stop=True),
                     reads=[t_selb, t_qs], writes=[tPX])
                S.op("act", lambda e, qb_=qb_: e.copy(out=qb_, in_=PX[:, 0:512]), reads=[tPX], writes=[tqb])
                S.op("pool", lambda e, kb_=kb_, qb_=qb_: e.tensor_tensor(out=prod, in0=kb_,
                                                                         in1=qb_.unsqueeze(1).to_broadcast([128, 2, 512]), op=ALU.mult),
                     reads=[tkb, tqb], writes=[t_prod])
                S.op("dve", lambda e, b=b: e.tensor_reduce(out=Sall[:, :, b, :],
                                                           in_=prod.rearrange("p c (h d) -> p c h d", h=4), axis=AX.X, op=ALU.add),
                     reads=[t_prod], writes=[t_Sall])
                yield
            for c in range(2):
                S.op("pe", lambda e, c=c: e.transpose(PX[0:64, c * 128:(c + 1) * 128],
                                                      Sall_t[:, c * 64:(c + 1) * 64], ident[:, :]),
                     reads=[t_Sall, t_ident], writes=[tPX], inc=(c == 1))
            S.op("dve", lambda e: e.tensor_reduce(out=stat2[0:64, 0:1], in_=PX[0:64, 0:256], axis=AX.X, op=ALU.max),
                 reads=[tPX], writes=[t_sm])
            S.op("dve", lambda e: e.tensor_scalar(out=stat2[0:64, 0:1], in0=stat2[0:64, 0:1], scalar1=-SCALE, scalar2=None, op0=ALU.mult),
                 reads=[t_sm], writes=[t_sm])
            S.op("act", lambda e: e.activation(out=Esm[0:64, :], in_=PX[0:64, 0:256], func=AF.Exp, scale=SCALE,
                                               bias=stat2[0:64, 0:1], accum_out=stat2[0:64, 1:2]),
                 reads=[tPX, t_sm], writes=[t_E, t_sm])
            S.op("dve", lambda e: e.reciprocal(out=stat2[0:64, 1:2], in_=stat2[0:64, 1:2]), reads=[t_sm], writes=[t_sm])
            S.op("dve", lambda e: e.tensor_scalar(out=Psm[0:64, :], in0=Esm[0:64, :], scalar1=stat2[0:64, 1:2], scalar2=None, op0=ALU.mult),
                 reads=[t_E, t_sm], writes=[t_P])
            for c in range(2):
                S.op("pe", lambda e, c=c: e.transpose(PX[:, c * 64:(c + 1) * 64], Psm[0:64, c * 128:(c + 1) * 128], ident[0:64, 0:64]),
                     reads=[t_P, t_ident], writes=[tPX], inc=(c == 1))
            for c in range(2):
                S.op("dve", lambda e, c=c: e.tensor_tensor(
                    out=Mk[:, c],
                    in0=PX[:, c * 64:(c + 1) * 64].rearrange("p (b h) -> p h b", h=4).unsqueeze(3).to_broadcast([128, 4, 16, 16]),
                    in1=eye16.unsqueeze(1).to_broadcast([128, 4, 16, 16]), op=ALU.mult),
                    reads=[tPX, t_eye], writes=[t_Mk])
            yield
            ppa, tka = next_pp_c()
            state["pp_excl"] = PP.index(ppa)
            hbank = [(PS, 0, tPSb), (PX, 0, tPX), (ppa, 0, tka), (ppa, 512, tka)]
            NVB = 6
            Vb = ([carve(68608 + i * 2048, 1024, BF16).rearrange("p (c f) -> p c f", c=2) for i in range(2)]
                  + [carve(60416 + i * 2048, 1024, BF16).rearrange("p (c f) -> p c f", c=2) for i in range(4)])
            t_Vb = [Tok(f"Vb{i}") for i in range(NVB)]
            valias = [[t_prod], [t_prod], [t_Kb[0]], [t_Kb[0]], [t_Kb[1]], [t_Kb[1]]]

            def v_load(b):
                S.dma("pool", vsem[b % NVB], Vb[b % NVB], cv[b].rearrange("(c m) f -> m c f", c=2),
                      writes=[t_Vb[b % NVB]] + (valias[b] if b < NVB else []))

            for b in range(NVB - 1):
                v_load(b)
            for b in range(TS):
                vb_, tvb = Vb[b % NVB], t_Vb[b % NVB]
                if b + NVB - 1 < TS:
                    v_load(b + NVB - 1)
                for h in range(4):
                    pph, coff, tkh = hbank[h]
                    for c in range(2):
                        first = (b == 0 and c == 0)
                        last = (b == TS - 1 and c == 1)
                        S.op("pe", lambda e, vb_=vb_, b=b, h=h, c=c, first=first, last=last, pph=pph, coff=coff: e.matmul(
                            pph[0:TS, coff:coff + 128], Mk[:, c, h, b, :], vb_[:, c, h * 128:(h + 1) * 128],
                            start=first, stop=last, skip_group_check=True),
                            reads=[t_Mk, tvb], writes=[tkh], inc=(h == 3 and c == 1))
                yield
            ogs_tok = qs_tok
            for h in range(4):
                pph, coff, tkh = hbank[h]
                S.op("dve", lambda e, h=h, pph=pph, coff=coff: e.tensor_tensor(
                    out=ogs_tok[0:TS, h * 128:(h + 1) * 128], in0=pph[0:TS, coff:coff + 128],
                    in1=sqg_tok[0:TS, h * 128:(h + 1) * 128], op=ALU.mult),
                    reads=[tkh, t_sqg, t_qs], writes=[t_qs])
            state["pp_excl"] = -1
            for h in range(4):
                S.op("pe", lambda e, h=h: e.transpose(PXb[:, h * TS:(h + 1) * TS], ogs_tok[0:TS, h * 128:(h + 1) * 128],
                                                      identb[0:TS, 0:TS]),
                     reads=[t_qs, t_identb], writes=[tPX], inc=(h == 3))
            S.op("act", lambda e: e.copy(out=og[:, :, T:TT], in_=PXb[:, 0:4 * TS].rearrange("p (h b) -> p h b", h=4)),
                 reads=[tPX], writes=t_og)
            yield

        gp, gs = gen_C_prompt(), gen_C_sample()
        for _ in range(2):
            next(gp)
        alive = [gp, gs]
        while alive:
            for g in list(alive):
                try:
                    next(g)
                except StopIteration:
                    alive.remove(g)
        if STOP_AFTER == "C":
            dump([og[:, 0, 0:512], og[:, 3, T - 512:T], og[:, 0, T:TT], og[:, 1, T:TT], og[:, 2, T:TT], og[:, 3, T:TT], qT[:, 0, 0:256]], off_b=0)
        S.barrier(no_wait=("pe",))
        if STOP_AFTER == "C":
            S.finish(block)
            return nc

        wab_t = carve(0, 2 * NCH * 128, BF16)
        wab = wab_t.rearrange("p (a n d) -> p a n d", a=2, n=NCH)
        lxp = carve(3072, 3 + T, F32)
        lxs_t = carve(11280, 4 * TS, F32)
        lxs = lxs_t.rearrange("p (k b) -> p k b", k=4)
        xc = carve(11536, TT, F32)
        xcb = carve(19792, TT, BF16)
        thr = carve(23920, TT, F32)
        a2b = carve(32176, TT, F32)
        thi = carve(40432, TT, F32)
        scg_sb = carve(48752, TT, F32)
        cy = scg_sb
        cinp = carve(57008, 2 + T, F32)
        cins_t = carve(65208, 3 * TS, F32)
        cins = cins_t.rearrange("p (k b) -> p k b", k=3)
        so_tok = carve(65400, W, F32)
        t_wab = Tok("wab")
        S.dma("pool", "cs3", wab[:, 0], lru_wa.rearrange("n c d -> c n d"), writes=[t_wab])
        S.dma("pool", "cs3", wab[:, 1], lru_wx.rearrange("n c d -> c n d"), writes=[t_wab], nowait=True)
        t_lxp, t_lxs, t_xc, t_xcb, t_thr, t_a2, t_thi = (Tok("lxp"), Tok("lxs"), Tok("xc"), Tok("xcb"), Tok("thr"),
                                                        Tok("a2"), Tok("thi"))
        t_gA = [Tok(f"gA{n}") for n in range(NCH)]
        t_SO = Tok("SO")
        t_scg, t_cinp, t_cins = Tok("scg"), Tok("cinp"), Tok("cins")
        t_cy = t_scg
        t_gB = [Tok(f"gB{n}") for n in range(NCH)]
        S.op("pool", lambda e: e.memset(lxp[:, 0:3], 0.0), writes=[t_lxp])
        S.op("pool", lambda e: e.memset(cinp[:, 0:2], 0.0), writes=[t_cinp])

        def gen_A(n):
            vws, wtk = w_get(WT[f"A{n}"])
            wlg, wlx = vws
            lhs_lg = [wlg[:, k, :] for k in range(KC)]
            lhs_lx = [wlx[:, k, :] for k in range(KC)]
            for (lo, hi) in HALVES:
                pp, tk = unit_half(lhs_lg, uT, lo, [wtk])
                S.op("act", lambda e, pp=pp, n=n, lo=lo, hi=hi: e.activation(out=gA[:, n, lo:hi], in_=pp[:], func=AF.Silu),
                     reads=[tk], writes=[t_gA[n]])
            sp_, tk = unit_samp(lhs_lg, uT, [wtk])
            S.op("act", lambda e, sp_=sp_, n=n: e.activation(out=gA[:, n, T:TT], in_=sp_, func=AF.Silu),
                 reads=[tk], writes=[t_gA[n]])
            yield
            for (lo, hi) in HALVES:
                pp, tk = unit_half(lhs_lx, uT, lo, [wtk])
                S.op("dve", lambda e, pp=pp, lo=lo, hi=hi: e.tensor_copy(lxp[:, 3 + lo:3 + hi], pp[:]),
                     reads=[tk], writes=[t_lxp])
            sp_, tk = unit_samp(lhs_lx, uT, [wtk])
            S.op("dve", lambda e, n=n: e.tensor_copy(lxs[:, 0:3, :], SIT[:, n * 6:n * 6 + 3, :]), reads=[t_SIT], writes=[t_lxs])
            S.op("dve", lambda e, sp_=sp_: e.tensor_copy(lxs[:, 3, :], sp_), reads=[tk], writes=[t_lxs])
            yield
            cw = lambda k, n=n: pT[:, 16 + k * 6 + n: 17 + k * 6 + n]
            cbias = pT[:, 40 + n:41 + n]
            S.op("act", lambda e, cw=cw, cbias=cbias: e.activation(out=xc[:, 0:T], in_=lxp[:, 0:T], func=AF.Identity,
                                                                   scale=cw(0), bias=cbias),
                 reads=[t_lxp, t_pT], writes=[t_xc])
            S.op("act", lambda e, cw=cw, cbias=cbias: e.activation(out=xc[:, T:TT], in_=lxs[:, 0, :], func=AF.Identity,
                                                                   scale=cw(0), bias=cbias),
                 reads=[t_lxs, t_pT], writes=[t_xc])
            for k in range(1, 4):
                S.op("dve", lambda e, k=k, cw=cw: e.scalar_tensor_tensor(out=xc[:, 0:T], in0=lxp[:, k:k + T], scalar=cw(k),
                                                                         in1=xc[:, 0:T], op0=ALU.mult, op1=ALU.add),
                     reads=[t_lxp, t_xc], writes=[t_xc])
                S.op("dve", lambda e, k=k, cw=cw: e.scalar_tensor_tensor(out=xc[:, T:TT], in0=lxs[:, k, :], scalar=cw(k),
                                                                         in1=xc[:, T:TT], op0=ALU.mult, op1=ALU.add),
                     reads=[t_lxs, t_xc], writes=[t_xc])
            S.op("pool", lambda e, n=n: e.tensor_copy(SO[:, n, 0:3], lxp[:, T:T + 3]), reads=[t_lxp], writes=[t_SO])
            S.op("pool", lambda e, n=n: e.tensor_copy(SO[:, n, 6:22], lxs[:, 3, :]), reads=[t_lxs], writes=[t_SO])
            yield
            S.op("act", lambda e: e.copy(out=xcb, in_=xc), reads=[t_xc], writes=[t_xcb])
            yield
            for gi, (dst, tdst, bcol) in enumerate([(thr, t_thr, 12 + n), (thi, t_thi, 18 + n)]):
                lhs = [wab[:, gi, n, :]]
                xcb3 = xcb.unsqueeze(1)
                for (lo, hi) in HALVES:
                    pp, tk = unit_half(lhs, xcb3, lo, [t_wab, t_xcb])
                    S.op("act", lambda e, pp=pp, dst=dst, lo=lo, hi=hi, bcol=bcol: e.activation(
                        out=dst[:, lo:hi], in_=pp[:], func=AF.Tanh, scale=0.5, bias=dv[:, bcol:bcol + 1]),
                        reads=[tk, t_dv], writes=[tdst])
                sp_, tk = unit_samp(lhs, xcb3, [t_wab, t_xcb])
                S.op("act", lambda e, sp_=sp_, dst=dst, bcol=bcol: e.activation(
                    out=dst[:, T:TT], in_=sp_, func=AF.Tanh, scale=0.5, bias=dv[:, bcol:bcol + 1]),
                    reads=[tk, t_dv], writes=[tdst])
                yield
            S.op("act", lambda e, n=n: e.activation(out=a2b, in_=thr, func=AF.Exp, scale=dv[:, 6 + n:7 + n], bias=dv[:, 6 + n:7 + n]),
                 reads=[t_thr, t_dv], writes=[t_a2])
            S.op("act", lambda e, n=n: e.activation(out=thr, in_=thr, func=AF.Exp, scale=dv[:, n:n + 1], bias=dv[:, n:n + 1]),
                 reads=[t_thr, t_dv], writes=[t_thr])
            S.op("dve", lambda e: e.tensor_scalar(out=a2b, in0=a2b, scalar1=1.0, scalar2=-1.0, op0=ALU.min, op1=ALU.mult),
                 reads=[t_a2], writes=[t_a2])
            yield
            S.op("act", lambda e: e.activation(out=a2b, in_=a2b, func=AF.Sqrt, bias=1.0, scale=1.0), reads=[t_a2], writes=[t_a2])
            S.op("dve", lambda e: e.scalar_tensor_tensor(out=a2b, in0=a2b, scalar=0.5, in1=xc, op0=ALU.mult, op1=ALU.mult),
                 reads=[t_a2, t_xc], writes=[t_a2])
            S.op("dve", lambda e: e.scalar_tensor_tensor(out=thi, in0=thi, scalar=1.0, in1=a2b, op0=ALU.add, op1=ALU.mult),
                 reads=[t_thi, t_a2], writes=[t_thi])
            yield
            S.op("dve", lambda e: e.tensor_tensor_scan(out=a2b[:, 0:T], data0=thr[:, 0:T], data1=thi[:, 0:T], initial=0.0,
                                                       op0=ALU.mult, op1=ALU.add),
                 reads=[t_thr, t_thi], writes=[t_a2])
            S.op("dve", lambda e, n=n: e.tensor_tensor(out=a2b[:, T:TT], in0=thr[:, T:TT], in1=SIT[:, n * 6 + 3, :], op=ALU.mult),
                 reads=[t_thr, t_SIT], writes=[t_a2])
            S.op("dve", lambda e: e.tensor_tensor(out=a2b[:, T:TT], in0=a2b[:, T:TT], in1=thi[:, T:TT], op=ALU.add),
                 reads=[t_a2, t_thi], writes=[t_a2])
            yield
            S.op("pool", lambda e, n=n: e.tensor_copy(SO[:, n, 3:4], a2b[:, T - 1:T]), reads=[t_a2], writes=[t_SO])
            S.op("pool", lambda e, n=n: e.tensor_copy(SO[:, n, 22:38], a2b[:, T:TT]), reads=[t_a2], writes=[t_SO])
            S.op("dve", lambda e, n=n: e.tensor_tensor(out=gA[:, n, :], in0=a2b, in1=gA[:, n, :], op=ALU.mult),
                 reads=[t_a2, t_gA[n]], writes=[t_gA[n]])
            yield

        def gen_B(n):
            vws, wtk = w_get(WT[f"B{n}a"])
            wsg, wsb = vws
            lhs_sg = [wsg[:, k, :] for k in range(KC)]
            lhs_sb = [wsb[:, k, :] for k in range(KC)]
            for (lo, hi) in HALVES:
                pp, tk = unit_half(lhs_sg, uT, lo, [wtk])
                S.op("act", lambda e, pp=pp, n=n, lo=lo, hi=hi: e.activation(out=gB[:, n, lo:hi], in_=pp[:], func=AF.Silu),
                     reads=[tk], writes=[t_gB[n]])
            sp_, tk = unit_samp(lhs_sg, uT, [wtk])
            S.op("act", lambda e, sp_=sp_, n=n: e.activation(out=gB[:, n, T:TT], in_=sp_, func=AF.Silu),
                 reads=[tk], writes=[t_gB[n]])
            yield
            for (lo, hi) in HALVES:
                pp, tk = unit_half(lhs_sb, uT, lo, [wtk])
                S.op("dve", lambda e, pp=pp, n=n, lo=lo, hi=hi: e.tensor_tensor(out=gB[:, n, lo:hi], in0=pp[:], in1=gB[:, n, lo:hi],
                                                                                op=ALU.mult),
                     reads=[tk, t_gB[n]], writes=[t_gB[n]])
            sp_, tk = unit_samp(lhs_sb, uT, [wtk])
            S.op("dve", lambda e, sp_=sp_, n=n: e.tensor_tensor(out=gB[:, n, T:TT], in0=sp_, in1=gB[:, n, T:TT], op=ALU.mult),
                 reads=[tk, t_gB[n]], writes=[t_gB[n]])
            yield
            vws, wtk = w_get(WT[f"B{n}b"])
            wscg, wsh = vws
            lhs_scg = [wscg[:, k, :] for k in range(KC)]
            lhs_sh = [wsh[:, k, :] for k in range(KC)]
            for (lo, hi) in HALVES:
                pp, tk = unit_half(lhs_scg, uT, lo, [wtk])
                S.op("act", lambda e, pp=pp, lo=lo, hi=hi: e.copy(out=scg_sb[:, lo:hi], in_=pp[:]), reads=[tk], writes=[t_scg])
            sp_, tk = unit_samp(lhs_scg, uT, [wtk])
            S.op("act", lambda e, sp_=sp_: e.copy(out=scg_sb[:, T:TT], in_=sp_), reads=[tk], writes=[t_scg])
            yield
            for (lo, hi) in HALVES:
                pp, tk = unit_half(lhs_sh, uT, lo, [wtk])
                S.op("dve", lambda e, pp=pp, lo=lo, hi=hi: e.tensor_tensor(out=cinp[:, 2 + lo:2 + hi], in0=pp[:],
                                                                           in1=scg_sb[:, lo:hi], op=ALU.mult),
                     reads=[tk, t_scg], writes=[t_cinp])
            sp_, tk = unit_samp(lhs_sh, uT, [wtk])
            S.op("dve", lambda e, n=n: e.tensor_copy(cins[:, 0:2, :], SIT[:, n * 6 + 4:n * 6 + 6, :]), reads=[t_SIT], writes=[t_cins])
            S.op("dve", lambda e, sp_=sp_: e.tensor_tensor(out=cins[:, 2, :], in0=sp_, in1=scg_sb[:, T:TT], op=ALU.mult),
                 reads=[tk, t_scg], writes=[t_cins])
            yield
            sw = lambda k, n=n: pT[:, 64 + k * 6 + n: 65 + k * 6 + n]
            S.op("act", lambda e, sw=sw: e.activation(out=cy[:, 0:T], in_=cinp[:, 0:T], func=AF.Identity, scale=sw(0)),
                 reads=[t_cinp, t_pT], writes=[t_cy])
            S.op("act", lambda e, sw=sw: e.activation(out=cy[:, T:TT], in_=cins[:, 0, :], func=AF.Identity, scale=sw(0)),
                 reads=[t_cins, t_pT], writes=[t_cy])
            for k in range(1, 3):
                S.op("dve", lambda e, k=k, sw=sw: e.scalar_tensor_tensor(out=cy[:, 0:T], in0=cinp[:, k:k + T], scalar=sw(k),
                                                                         in1=cy[:, 0:T], op0=ALU.mult, op1=ALU.add),
                     reads=[t_cinp, t_cy], writes=[t_cy])
                S.op("dve", lambda e, k=k, sw=sw: e.scalar_tensor_tensor(out=cy[:, T:TT], in0=cins[:, k, :], scalar=sw(k),
                                                                         in1=cy[:, T:TT], op0=ALU.mult, op1=ALU.add),
                     reads=[t_cins, t_cy], writes=[t_cy])
            S.op("pool", lambda e, n=n: e.tensor_copy(SO[:, n, 4:6], cinp[:, T:T + 2]), reads=[t_cinp], writes=[t_SO])
            S.op("pool", lambda e, n=n: e.tensor_copy(SO[:, n, 38:54], cins[:, 2, :]), reads=[t_cins], writes=[t_SO])
            S.op("dve", lambda e, n=n: e.tensor_tensor(out=gB[:, n, :], in0=cy, in1=gB[:, n, :], op=ALU.mult),
                 reads=[t_cy, t_gB[n]], writes=[t_gB[n]])
            yield

        def interleave(*gens):
            gens = list(gens)
            while gens:
                for g in list(gens):
                    try:
                        next(g)
                    except StopIteration:
                        gens.remove(g)

        gens = {}

        def adv(kind, n):
            g = gens.get((kind, n))
            if g is None:
                return
            try:
                next(g)
            except StopIteration:
                pass

        for n in range(NCH + 1):
            if n < NCH:
                gens[("A", n)] = gen_A(n)
            if n >= 1:
                gens[("B", n - 1)] = gen_B(n - 1)
            adv("A", n)
            adv("B", n - 1)
            adv("A", n - 1)
            adv("A", n)
            adv("B", n - 1)
            adv("A", n)
            adv("A", n - 1)
            adv("B", n - 1)
            adv("B", n - 1)
            adv("A", n - 1)
            adv("A", n)
            adv("A", n)
            adv("A", n)
            adv("B", n - 1)
            adv("A", n)
        for g in gens.values():
            for _ in g:
                pass
        if STOP_AFTER == "B":
            dump([gB[:, 0, 0:512], gB[:, 5, T - 512:T], gB[:, 0, T:TT], gB[:, 5, T:TT], gB[:, 2, 1024:1536]], off_b=0)
        t_sot = Tok("sot")
        for g0 in range(0, NCH, 3):
            for n in range(g0, g0 + 3):
                S.op("pe", lambda e, n=n, g0=g0: e.transpose(PX[0:54, (n - g0) * 128:(n - g0 + 1) * 128], SO[:, n, :], ident[:, :]),
                     reads=[t_SO, t_ident], writes=[tPX], inc=(n == g0 + 2))
            S.op("act", lambda e, g0=g0: e.copy(out=so_tok[0:54, g0 * 128:(g0 + 3) * 128], in_=PX[0:54, 0:384]),
                 reads=[tPX], writes=[t_sot])
        S.dma("sp", "out", o_plc[:, :], so_tok[0:3, :], reads=[t_sot])
        S.dma("sp", "out", o_ph[:, :], so_tok[3:4, :], reads=[t_sot])
        S.dma("sp", "out", o_psc[:, :], so_tok[4:6, :], reads=[t_sot])
        S.dma("sp", "out", o_slc3[:, 2, :], so_tok[6:22, :], reads=[t_sot])
        S.dma("sp", "out", o_sh[:, :], so_tok[22:38, :], reads=[t_sot])
        S.dma("sp", "out", o_ssc3[:, 1, :], so_tok[38:54, :], reads=[t_sot])
        if STOP_AFTER == "B":
            dump([gB[:, 0, 0:512], gB[:, 5, T - 512:T], gB[:, 0, T:TT], gB[:, 5, T:TT], gB[:, 2, 1024:1536]], off_b=56320)
        S.barrier(no_wait=("pe",))
        if STOP_AFTER == "B":
            S.finish(block)
            return nc

        NTH = 3
        thb = [carve(i * 4096, 1024, F32) for i in range(NTH)]
        tacc = [carve(12288 + i * 4096, 1024, F32) for i in range(2)]
        mT_t = carve(20480, KC * TT, BF16)
        mT = mT_t.rearrange("p (k t) -> p k t", k=KC)
        wo_t = carve(53504, KC * D, BF16)
        wo = wo_t.rearrange("p (k c) -> p k c", k=KC)
        t_wo = Tok("wo")
        if STOP_AFTER == "M0":
            S.barrier()
            S.finish(block)
            return nc
        t_th = [Tok(f"th{i}") for i in range(NTH)]
        t_acc = [Tok("acc0"), Tok("acc1")]
        t_mT = Tok("mT")
        t_mTr = {0: Tok("mT_h0"), 1024: Tok("mT_h1"), T: Tok("mT_s")}
        thc = {"i": 0, "a": 0}
        gsrc = [(gA, NCH), (gB, NCH), (og, 4)]
        xorder = [2, 1, 0]
        gtok = [t_gA, t_gB, t_og]
        ths_f = carve(69888, 3 * TS, F32)
        tzs_f = carve(70080, 3 * TS, F32)
        accs_f = carve(70272, TS, F32)
        t_accs2 = Tok("accs2")
        t_ths = Tok("ths")
        t_accs = Tok("accs")
        for j in range(8):
            vg, wtkg = w_get(WT[f"Mg{j}"])
            vo, wtko = w_get(WT[f"Mo{j}"])
            S.dma("pool", "cs4", wo[:, j, :], w_out[j * 128:(j + 1) * 128, :], writes=[t_wo], nowait=True)
            for (lo, hi) in HALVES:
                acc, tacc_ = tacc[thc["a"] % 2], t_acc[thc["a"] % 2]
                thc["a"] += 1
                for xi, x in enumerate(xorder):
                    garr, nk = gsrc[x]
                    lhs_g = [vg[x][:, k, :] for k in range(KC)]
                    lhs_o = [vo[x][:, k, :] for k in range(nk)]
                    th_, tth = thb[thc["i"] % NTH], t_th[thc["i"] % NTH]
                    thc["i"] += 1
                    pp, tk = unit_half(lhs_g, uT, lo, [wtkg])
                    S.op("act", lambda e, pp=pp, th_=th_: e.activation(out=th_, in_=pp[:], func=AF.Tanh, scale=0.5),
                         reads=[tk], writes=[tth])
                    zp_, tkz = unit_half(lhs_o, garr, lo, [wtko] + gtok[x])
                    zp = zp_[:]
                    if xi == 0:
                        S.op("dve", lambda e, acc=acc, th_=th_, zp=zp: e.scalar_tensor_tensor(out=acc, in0=th_, scalar=1.0, in1=zp,
                                                                                           op0=ALU.add, op1=ALU.mult),
                             reads=[tth, tkz], writes=[tacc_])
                    else:
                        S.op("dve", lambda e, th_=th_, zp=zp: e.scalar_tensor_tensor(out=th_, in0=th_, scalar=1.0, in1=zp,
                                                                                    op0=ALU.add, op1=ALU.mult),
                             reads=[tth, tkz], writes=[tth])
                        if xi == 1:
                            S.op("pool", lambda e, acc=acc, th_=th_: e.tensor_tensor(out=acc, in0=acc, in1=th_, op=ALU.add),
                                 reads=[tth, tacc_], writes=[tacc_])
                        else:
                            S.op("pool", lambda e, acc=acc, th_=th_, j=j, lo=lo, hi=hi: e.tensor_tensor(
                                out=mT[:, j, lo:hi], in0=acc, in1=th_, op=ALU.add),
                                reads=[tth, tacc_], writes=[t_mTr[lo]])
            tMs_g, tMs_o = tPSb, tPX
            for x in xorder:
                lhs_g = [vg[x][:, k, :] for k in range(KC)]
                mm_group(PS[:, x * TS:(x + 1) * TS], [(lhs_g[k], uT[:, k, T:TT]) for k in range(KC)], [wtkg], tMs_g)
            S.op("act", lambda e: e.activation(out=ths_f, in_=PS[:, 0:3 * TS], func=AF.Tanh, scale=0.5),
                 reads=[tMs_g], writes=[t_ths])
            for x in xorder:
                garr, nk = gsrc[x]
                lhs_o = [vo[x][:, k, :] for k in range(nk)]
                mm_group(PX[:, x * TS:(x + 1) * TS], [(lhs_o[k], garr[:, k, T:TT]) for k in range(nk)], [wtko] + gtok[x], tMs_o)
            S.op("dve", lambda e: e.scalar_tensor_tensor(out=tzs_f, in0=ths_f, scalar=1.0, in1=PX[:, 0:3 * TS],
                                                         op0=ALU.add, op1=ALU.mult),
                 reads=[t_ths, tMs_o], writes=[t_accs])
            S.op("dve", lambda e: e.tensor_reduce(out=accs_f, in_=tzs_f.rearrange("p (x b) -> p b x", x=3),
                                                  axis=AX.X, op=ALU.add),
                 reads=[t_accs], writes=[t_accs2])
            S.op("dve", lambda e, j=j: e.tensor_copy(mT[:, j, T:TT], accs_f), reads=[t_accs2], writes=[t_mTr[T]])
            if STOP_AFTER == "M5":
                S.barrier()
                S.finish(block)
                return nc
        S.barrier(no_wait=("pe",))
        if STOP_AFTER == "M":
            S.finish(block)
            return nc

        xr = [carve(i * 4096, 1024, F32) for i in range(2)]
        yr = [carve(8192 + i * 4096, 1024, F32) for i in range(2)] + [carve(69888, 1024, F32)]
        fgb = carve(16384, 1024, F32)
        t_fgb = Tok("fgb")
        S.dma("sp", "cs5", fgb, final_norm_g.partition_broadcast(128), writes=[t_fgb])
        t_xr = [Tok("xr0"), Tok("xr1")]
        t_yr = [Tok("yr0"), Tok("yr1"), Tok("yr2")]
        ftiles = [(xp[i * 128:(i + 1) * 128, :], y_p[i * 128:(i + 1) * 128, :], 128, i * 128) for i in range(16)]
        ftiles.append((xs[:, :], y_s[:, :], TS, T))
        xsem2 = ["fxa", "fxb"]
        ysem = ["ya", "yb", "xc"]
        t_stF = [Tok(f"stF{i}") for i in range(len(ftiles))]

        def f_stageA(ti):
            src, dst, nr, c0 = ftiles[ti]
            xr_, txr = xr[ti % 2], t_xr[ti % 2]
            yr_, tyr = yr[ti % 3], t_yr[ti % 3]
            S.dma("pool", xsem2[ti % 2], xr_[0:nr, :], src, writes=[txr])
            pp, tk = next_pp()
            for b in range(2):
                pairs = [(mT[:, k, c0:c0 + nr], wo[:, k, b * 512:(b + 1) * 512]) for k in range(KC)]
                mm_group(pp[0:nr, b * 512:(b + 1) * 512], pairs, [t_mTr[0 if c0 < 1024 else (1024 if c0 < T else T)], t_wo], tk,
                         last_inc=(b == 1))
            S.op("dve", lambda e, pp=pp, yr_=yr_, xr_=xr_, nr=nr: e.scalar_tensor_tensor(
                out=yr_[0:nr, :], in0=pp[0:nr, :], scalar=0.5, in1=xr_[0:nr, :], op0=ALU.mult, op1=ALU.add),
                reads=[tk, txr], writes=[tyr])

        def f_stageA2(ti):
            src, dst, nr, c0 = ftiles[ti]
            xr_, txr = xr[ti % 2], t_xr[ti % 2]
            yr_, tyr = yr[ti % 3], t_yr[ti % 3]
            col = 32 + ti
            S.op("act", lambda e, xr_=xr_, yr_=yr_, nr=nr, col=col: e.activation(out=xr_[0:nr, :], in_=yr_[0:nr, :], func=AF.Square,
                                                                                accum_out=stat2[0:nr, col:col + 1]),
                 reads=[tyr, t_sm], writes=[txr, t_stF[ti]])

        def f_stageB1(ti):
            src, dst, nr, c0 = ftiles[ti]
            col = 32 + ti
            S.op("act", lambda e, nr=nr, col=col: e.activation(out=stat2[0:nr, col:col + 1], in_=stat2[0:nr, col:col + 1], func=AF.Sqrt,
                                                               scale=1.0 / D, bias=epst[0:nr, 0:1]),
                 reads=[t_stF[ti], t_eps], writes=[t_stF[ti]])

        def f_stageB(ti):
            src, dst, nr, c0 = ftiles[ti]
            yr_, tyr = yr[ti % 3], t_yr[ti % 3]
            col = 32 + ti
            S.op("dve", lambda e, nr=nr, col=col: e.reciprocal(out=stat2[0:nr, col:col + 1], in_=stat2[0:nr, col:col + 1]),
                 reads=[t_stF[ti]], writes=[t_stF[ti]])
            S.op("dve", lambda e, yr_=yr_, nr=nr, col=col: e.scalar_tensor_tensor(
                out=yr_[0:nr, :], in0=yr_[0:nr, :], scalar=stat2[0:nr, col:col + 1], in1=fgb[0:nr, :], op0=ALU.mult, op1=ALU.mult),
                reads=[tyr, t_stF[ti], t_fgb], writes=[tyr])
            S.dma("sp", ysem[ti % 3], dst, yr_[0:nr, :], reads=[tyr])

        f_stageA(0)
        f_stageA2(0)
        for ti in range(len(ftiles)):
            if ti + 1 < len(ftiles):
                f_stageA(ti + 1)
            f_stageB1(ti)
            if ti + 1 < len(ftiles):
                f_stageA2(ti + 1)
            f_stageB(ti)
        S.barrier()
        S.finish(block)
    return nc


_CACHE = {}


def _get_program():
    if "nc" not in _CACHE:
        _CACHE["nc"] = build_program()
    return _CACHE["nc"]


def kernel(x_prompt, x_sample, cache_mem_k, cache_mem_v, state_lru_h, state_lru_conv, state_sconv, mem_prompt,
           norm_g, mem_norm_g, w_in, lru_conv_w, lru_conv_b, lru_wa, lru_ba, lru_wx, lru_bx, lru_lambda, lru_wo,
           sconv_w, sconv_wo, xa_wk, xa_wv, xa_wo, w_out, final_norm_g):
    f = lambda a: np.ascontiguousarray(np.asarray(a, dtype=np.float32))
    shared = {
        "norm_g": f(norm_g[0]), "mem_norm_g": f(mem_norm_g[0]), "w_in": f(w_in[0]),
        "lru_conv_w": f(lru_conv_w[0]), "lru_conv_b": f(lru_conv_b[0]), "lru_wa": f(lru_wa[0]),
        "lru_ba": f(lru_ba[0]), "lru_wx": f(lru_wx[0]), "lru_bx": f(lru_bx[0]), "lru_lambda": f(lru_lambda[0]),
        "lru_wo": f(lru_wo[0]), "sconv_w": f(sconv_w[0]), "sconv_wo": f(sconv_wo[0]), "xa_wk": f(xa_wk[0]),
        "xa_wv": f(xa_wv[0]), "xa_wo": f(xa_wo[0]), "w_out": f(w_out[0]), "final_norm_g": f(final_norm_g),
    }
    in_maps = []
    for c in range(NCORES):
        sl = slice(c * TS, (c + 1) * TS)
        m = dict(shared)
        m["xp"] = f(x_prompt[c])
        m["xs"] = f(np.asarray(x_sample)[sl, 0, :])
        m["memp"] = f(mem_prompt[c])
        m["ck"] = f(np.asarray(cache_mem_k)[0, sl].reshape(TS, NM, XW))
        m["cv"] = f(np.asarray(cache_mem_v)[0, sl].reshape(TS, NM, XW))
        m["st_h"] = f(np.asarray(state_lru_h)[0, sl])
        m["st_lc"] = f(np.asarray(state_lru_conv)[0, sl].reshape(TS, 3 * W))
        m["st_sc"] = f(np.asarray(state_sconv)[0, sl].reshape(TS, 2 * W))
        in_maps.append(m)
    nc = _get_program()
    res = run_bass_kernel_spmd(nc, in_maps, core_ids=list(range(NCORES)))
    rs = res.results
    cat = lambda k: np.concatenate([np.asarray(r[k]) for r in rs], axis=0)
    y_prompt = np.stack([np.asarray(r["y_p"]) for r in rs], axis=0).astype(np.float32)
    y_sample = cat("y_s").reshape(NCORES * TS, 1, D).astype(np.float32)
    p_mk = np.stack([np.asarray(r["o_pk"]) for r in rs], axis=0).reshape(1, NCORES, NM, 4, 128).astype(np.float32)
    p_mv = np.stack([np.asarray(r["o_pv"]) for r in rs], axis=0).reshape(1, NCORES, NM, 4, 128).astype(np.float32)
    p_h = cat("o_ph").reshape(1, NCORES, W).astype(np.float32)
    p_lc = np.stack([np.asarray(r["o_plc"]) for r in rs], axis=0).reshape(1, NCORES, 3, W).astype(np.float32)
    p_sc = np.stack([np.asarray(r["o_psc"]) for r in rs], axis=0).reshape(1, NCORES, 2, W).astype(np.float32)
    s_h = cat("o_sh").reshape(1, NCORES * TS, W).astype(np.float32)
    s_lc = cat("o_slc").reshape(1, NCORES * TS, 3, W).astype(np.float32)
    s_sc = cat("o_ssc").reshape(1, NCORES * TS, 2, W).astype(np.float32)
    return (y_prompt, y_sample, p_mk, p_mv, p_h, p_lc, p_sc, s_h, s_lc, s_sc)
```

```python
import math
from contextlib import ExitStack

import numpy as np
import concourse.bass as bass
import concourse.mybir as mybir
from concourse.bass_utils import run_bass_kernel_spmd

F32 = mybir.dt.float32
BF16 = mybir.dt.bfloat16
AF = mybir.ActivationFunctionType
ALU = mybir.AluOpType
AX = mybir.AxisListType

NCORES = 8
T = 2048
TS = 16
TT = T + TS
D = 1024
KC = 8
W = 768
NCH = 6
NM = 256
XW = 512
IN_COLS = 8704
EPS = 1e-6
SCALE = 1.0 / math.sqrt(128.0)
STOP_AFTER = None

C_LX, C_LG, C_SB, C_SCG, C_SH, C_SG, C_Q, C_QG, C_MG = 0, 768, 1536, 2304, 3072, 3840, 4608, 5120, 5632


class Tok:
    __slots__ = ("name", "w", "r")

    def __init__(self, name=""):
        self.name = name
        self.w = None
        self.r = {}


class Stream:
    def __init__(self, key, sem):
        self.key = key
        self.sem = sem
        self.cnt = 0
        self.seen = {}
        self.ops = []
        self.pending = False


class Sched:
    def __init__(self, nc):
        self.nc = nc
        self.sems = {}
        self.streams = {}
        self.dma_cnt = {}

    def add_stream(self, key, sem):
        self.sems[key] = sem
        self.streams[key] = Stream(key, sem)

    def add_dma_sem(self, key, sem):
        self.sems[key] = sem
        self.dma_cnt[key] = 0

    def _needs(self, st, reads, writes):
        needs = {}

        def need(ev):
            if ev is None:
                return
            k, v = ev
            if needs.get(k, 0) < v:
                needs[k] = v
        for t in reads:
            need(t.w)
        for t in writes:
            need(t.w)
            for k, v in t.r.items():
                if k == st.key:
                    continue
                need((k, v))
        out = []
        for k, v in needs.items():
            if k == st.key and k == "pe":
                continue
            if st.seen.get(k, 0) < v:
                st.seen[k] = v
                out.append((k, v))
        return out

    def op(self, key, fn, reads=(), writes=(), inc=True):
        st = self.streams[key]
        waits = self._needs(st, reads, writes)
        if inc:
            st.cnt += 1
            st.pending = False
            ev = (key, st.cnt)
        else:
            st.pending = True
            ev = (key, st.cnt + 1)
        sems = self.sems
        sem = st.sem

        def run(eng, waits=waits, fn=fn, inc=inc):
            for k, v in waits:
                eng.wait_ge(sems[k], v)
            ins = fn(eng)
            if inc:
                ins.then_inc(sem, 1)
        st.ops.append(run)
        for t in writes:
            t.w = ev
            t.r = {}
        for t in reads:
            if t.r.get(key, 0) < ev[1]:
                t.r[key] = ev[1]

    def dma(self, key, semkey, out, in_, reads=(), writes=(), nowait=False, **kw):
        st = self.streams[key]
        waits = [] if nowait else self._needs(st, reads, writes)
        self.dma_cnt[semkey] += 16
        ev = (semkey, self.dma_cnt[semkey])
        sems = self.sems

        def run(eng, waits=waits):
            for k, v in waits:
                eng.wait_ge(sems[k], v)
            eng.dma_start(out=out, in_=in_, **kw).then_inc(sems[semkey], 16)
        st.ops.append(run)
        for t in writes:
            t.w = ev
            t.r = {}
        for t in reads:
            if t.r.get(semkey, 0) < ev[1]:
                t.r[semkey] = ev[1]

    def wait_dma(self, keys, semkey):
        v = self.dma_cnt[semkey]
        sems = self.sems
        for k in keys:
            st = self.streams[k]
            if v and st.seen.get(semkey, 0) < v:
                st.seen[semkey] = v
                st.ops.append(lambda eng, v=v: eng.wait_ge(sems[semkey], v))

    def barrier(self, skip=(), no_wait=()):
        targets = {}
        for k, st in self.streams.items():
            assert not st.pending
            if st.cnt:
                targets[k] = st.cnt
        for k, v in self.dma_cnt.items():
            if v and k not in skip:
                targets[k] = v
        sems = self.sems
        for k, st in self.streams.items():
            if k in no_wait:
                continue
            waits = []
            for tk, tv in targets.items():
                if st.seen.get(tk, 0) < tv:
                    st.seen[tk] = tv
                    waits.append((tk, tv))

            def run(eng, waits=waits):
                for kk, v in waits:
                    eng.wait_ge(sems[kk], v)
            st.ops.append(run)

    def finish(self, block):
        for key, st in self.streams.items():
            assert not st.pending, key
        ss = self.streams

        def mk(key):
            def body(eng):
                for o in ss[key].ops:
                    o(eng)
            return body
        block.gpsimd(mk("pool"))
        block.tensor(mk("pe"))
        block.scalar(mk("act"))
        block.vector(mk("dve"))
        block.sync(mk("sp"))


def build_program():
    nc = bass.Bass("TRN2", target_bir_lowering=False)

    def din(name, shape):
        return nc.dram_tensor(name, shape, F32, kind="ExternalInput").ap()

    def dout(name, shape):
        return nc.dram_tensor(name, shape, F32, kind="ExternalOutput").ap()

    xp = din("xp", [T, D])
    xs = din("xs", [TS, D])
    memp = din("memp", [NM, D])
    ck = din("ck", [TS, NM, XW])
    cv = din("cv", [TS, NM, XW])
    st_h = din("st_h", [TS, W])
    st_lc = din("st_lc", [TS, 3 * W])
    st_sc = din("st_sc", [TS, 2 * W])
    norm_g = din("norm_g", [D])
    mem_norm_g = din("mem_norm_g", [D])
    w_in = din("w_in", [D, IN_COLS])
    lru_conv_w = din("lru_conv_w", [4, W])
    lru_conv_b = din("lru_conv_b", [W])
    lru_wa = din("lru_wa", [NCH, 128, 128])
    lru_ba = din("lru_ba", [W])
    lru_wx = din("lru_wx", [NCH, 128, 128])
    lru_bx = din("lru_bx", [W])
    lru_lambda = din("lru_lambda", [W])
    lru_wo = din("lru_wo", [W, D])
    sconv_w = din("sconv_w", [3, W])
    sconv_wo = din("sconv_wo", [W, D])
    xa_wk = din("xa_wk", [D, XW])
    xa_wv = din("xa_wv", [D, XW])
    xa_wo = din("xa_wo", [XW, D])
    w_out = din("w_out", [D, D])
    final_norm_g = din("final_norm_g", [D])

    y_p = dout("y_p", [T, D])
    y_s = dout("y_s", [TS, D])
    o_pk = dout("o_pk", [NM, XW])
    o_pv = dout("o_pv", [NM, XW])
    o_ph = dout("o_ph", [1, W])
    o_plc = dout("o_plc", [3, W])
    o_psc = dout("o_psc", [2, W])
    o_sh = dout("o_sh", [TS, W])
    o_slc = dout("o_slc", [TS, 3 * W])
    o_ssc = dout("o_ssc", [TS, 2 * W])

    dbg = dout("dbg", [128, 4096]) if STOP_AFTER else None
    es = ExitStack()
    with es:
        def sb(name, shape, dt):
            return es.enter_context(nc.sbuf_tensor(name, shape, dt))

        uT_t = sb("uT", [128, KC * TT], BF16)
        uT = uT_t[:].rearrange("p (k t) -> p k t", k=KC)
        gA_t = sb("gA", [128, NCH * TT], BF16)
        gA = gA_t[:].rearrange("p (k t) -> p k t", k=NCH)
        gB_t = sb("gB", [128, NCH * TT], BF16)
        gB = gB_t[:].rearrange("p (k t) -> p k t", k=NCH)
        og_t = sb("og", [128, 4 * TT], BF16)
        og = og_t[:].rearrange("p (k t) -> p k t", k=4)
        NSLOT = 5
        SLOT_E = 3072
        slots = [sb(f"wslot{i}", [128, SLOT_E], BF16) for i in range(NSLOT)]
        ident = sb("ident", [128, 128], F32)
        identb = sb("identb", [128, 128], BF16)
        onesb = sb("onesb", [128, 128], BF16)
        pT = sb("pT", [128, 82], F32)
        dv = sb("dv", [128, 24], F32)
        SIT_t = sb("SIT", [128, 36 * TS], F32)
        SIT = SIT_t[:].rearrange("p (a b) -> p a b", a=36)
        SO_t = sb("SO", [128, NCH * 54], F32)
        SO = SO_t[:].rearrange("p (a b) -> p a b", a=NCH)
        stat = sb("stat", [128, 64], F32)
        stat2 = sb("stat2", [128, 64], F32)
        epst = sb("epst", [128, 1], F32)
        q25 = sb("q25", [128, 1], F32)
        RW = 18816
        R = sb("R", [128, RW], F32)

        def carve(off_b, nelem, dt):
            assert off_b % 4 == 0
            if dt == F32:
                assert off_b // 4 + nelem <= RW, (off_b, nelem)
                return R[:, off_b // 4: off_b // 4 + nelem]
            assert nelem % 2 == 0 and off_b // 4 + nelem // 2 <= RW, (off_b, nelem)
            return R[:, off_b // 4: off_b // 4 + nelem // 2].bitcast(BF16)

        PP = [es.enter_context(nc.psum_tensor(f"PP{i}", [128, 1024], F32)) for i in range(3)]
        PS = es.enter_context(nc.psum_tensor("PS", [128, 512], F32))
        PX = es.enter_context(nc.psum_tensor("PX", [128, 512], F32))
        tPP = [Tok(f"PP{i}") for i in range(3)]
        tPS = [Tok(f"PS{i}") for i in range(8)]
        tPX = Tok("PX")
        PSb = PS[:].bitcast(BF16)
        PXb = PX[:].bitcast(BF16)

        S = Sched(nc)
        for k in ["pe", "act", "dve", "pool", "sp"]:
            S.add_stream(k, es.enter_context(nc.semaphore("s_" + k)))
        dsem_names = (["cst", "cs2", "cs3", "cs4", "cs5", "sva", "svb", "svc", "svd", "sve", "svf", "fxa", "fxb", "out", "xa", "xb", "xc", "ya", "yb", "ka", "kb", "va", "vb"]
                      + [f"ws{i}" for i in range(NSLOT)])
        for k in dsem_names:
            S.add_dma_sem(k, es.enter_context(nc.semaphore("d_" + k)))
        block = es.enter_context(nc.Block())

        state = {"pp": 0, "ps": 0}

        def next_pp():
            i = state["pp"] % 3
            state["pp"] += 1
            return PP[i], tPP[i]

        def next_ps():
            i = state["ps"] % 8
            state["ps"] += 1
            return i, tPS[i]

        wtiles = []
        wstate = {"issued": 0}
        tslot = [Tok(f"slot{i}") for i in range(NSLOT)]

        def w_in_cols(c0, ncols):
            return w_in.rearrange("(k p) c -> p k c", p=128)[:, :, c0:c0 + ncols]

        def add_wtile(pieces):
            off = 0
            lst = []
            for ap in pieces:
                kc, ncols = ap.shape[1], ap.shape[2]
                lst.append((ap, off, kc, ncols))
                off += kc * ncols
            assert off <= SLOT_E, off
            wtiles.append(lst)
            return len(wtiles) - 1

        def w_issue_upto(i):
            while wstate["issued"] <= min(i, len(wtiles) - 1):
                j = wstate["issued"]
                sl = j % NSLOT
                for pi, (ap, off, kc, ncols) in enumerate(wtiles[j]):
                    dst = slots[sl][:, off:off + kc * ncols].rearrange("p (k c) -> p k c", k=kc)
                    S.dma("pool", f"ws{sl}", dst, ap, writes=[tslot[sl]], nowait=(pi > 0))
                wstate["issued"] += 1

        def w_get(i):
            w_issue_upto(i + NSLOT - 2)
            sl = i % NSLOT
            views = []
            for (ap, off, kc, ncols) in wtiles[i]:
                views.append(slots[sl][:, off:off + kc * ncols].rearrange("p (k c) -> p k c", k=kc))
            return views, tslot[sl]

        WT = {}
        WT["wk0"] = add_wtile([xa_wk.rearrange("(k p) c -> p k c", p=128)[:, :, 0:256]])
        WT["wk1"] = add_wtile([xa_wk.rearrange("(k p) c -> p k c", p=128)[:, :, 256:512]])
        WT["wv0"] = add_wtile([xa_wv.rearrange("(k p) c -> p k c", p=128)[:, :, 0:256]])
        WT["wv1"] = add_wtile([xa_wv.rearrange("(k p) c -> p k c", p=128)[:, :, 256:512]])
        for hh in range(2):
            WT[f"q{hh}"] = add_wtile([w_in_cols(C_Q + hh * 256, 256)])
        for hh in range(2):
            WT[f"qg{hh}"] = add_wtile([w_in_cols(C_QG + hh * 256, 256)])
        for n in range(NCH + 1):
            if n < NCH:
                WT[f"A{n}"] = add_wtile([w_in_cols(C_LG + n * 128, 128), w_in_cols(C_LX + n * 128, 128)])
                WT[f"B{n}a"] = add_wtile([w_in_cols(C_SG + n * 128, 128), w_in_cols(C_SB + n * 128, 128)])
            if n >= 1:
                m_ = n - 1
                WT[f"B{m_}b"] = add_wtile([w_in_cols(C_SCG + m_ * 128, 128), w_in_cols(C_SH + m_ * 128, 128)])
        lwo = lru_wo.rearrange("(k p) c -> p k c", p=128)
        swo = sconv_wo.rearrange("(k p) c -> p k c", p=128)
        xwo = xa_wo.rearrange("(k p) c -> p k c", p=128)
        for j in range(8):
            WT[f"Mg{j}"] = add_wtile([w_in_cols(C_MG + x * 1024 + j * 128, 128) for x in range(3)])
            WT[f"Mo{j}"] = add_wtile([lwo[:, :, j * 128:(j + 1) * 128], swo[:, :, j * 128:(j + 1) * 128],
                                      xwo[:, :, j * 128:(j + 1) * 128]])

        def mm_group(out_ap, pairs, reads, wtok, last_inc=True):
            n = len(pairs)
            for i, (l, r) in enumerate(pairs):
                S.op("pe", lambda e, l=l, r=r, i=i: e.matmul(out_ap, l, r, start=(i == 0), stop=(i == n - 1)),
                     reads=reads, writes=[wtok], inc=(last_inc and i == n - 1))

        def unit_half(lhs_list, rhs_arr, lo, reads):
            pp, tk = next_pp()
            nk = len(lhs_list)
            for b in range(2):
                pairs = [(lhs_list[k], rhs_arr[:, k, lo + b * 512: lo + (b + 1) * 512]) for k in range(nk)]
                mm_group(pp[:, b * 512:(b + 1) * 512], pairs, reads, tk, last_inc=(b == 1))
            return pp, tk

        tPSb = Tok("PSbank")
        tPS2 = [tPSb, tPX]

        def unit_samp(lhs_list, rhs_arr, reads):
            i = state.setdefault("ps2", 0) % 2
            state["ps2"] += 1
            tk = tPS2[i]
            nk = len(lhs_list)
            bank = PS if i == 0 else PX
            out_ap = bank[:, 0:TS]
            pairs = [(lhs_list[k], rhs_arr[:, k, T:TT]) for k in range(nk)]
            mm_group(out_ap, pairs, reads, tk)
            return out_ap, tk

        HALVES = [(0, 1024), (1024, 2048)]

        def dump(items, off_b=56320):
            dstage = carve(off_b, 4096, F32)
            S.barrier()
            S.op("dve", lambda e: e.memset(dstage[:], 0.0))
            S.barrier()
            off = 0
            for ap in items:
                P_, n_ = ap.shape[0], ap.shape[1]
                S.op("dve", lambda e, ap=ap, off=off, P_=P_, n_=n_: e.tensor_copy(dstage[0:P_, off:off + n_], ap))
                off += n_
            S.barrier()
            S.dma("sp", "out", dbg[:, :], dstage[:])
            S.barrier()

        t_ident = Tok("ident")
        S.op("pool", lambda e: e.memset(ident[:], 0.0), writes=[t_ident])
        S.op("pool", lambda e: e.affine_select(ident[:], ident[:], pattern=[[-1, 128]], compare_op=ALU.not_equal,
                                               fill=1.0, base=0, channel_multiplier=1),
             reads=[t_ident], writes=[t_ident])
        t_identb = Tok("identb")
        S.op("dve", lambda e: e.tensor_copy(identb[:], ident[:]), reads=[t_ident], writes=[t_identb])
        t_stat = Tok("stat")
        t_sm = Tok("sm")
        t_st3 = t_sm
        S.op("pool", lambda e: e.memset(stat[:], 0.0), writes=[t_stat])
        S.op("pool", lambda e: e.memset(stat2[:], 0.0), writes=[t_sm])
        t_ones = Tok("ones")
        S.op("pool", lambda e: e.memset(onesb[:], 1.0), writes=[t_ones])
        t_eps = Tok("eps")
        S.op("pool", lambda e: e.memset(epst[:], EPS), writes=[t_eps])
        S.op("pool", lambda e: e.memset(q25[:], 0.25), writes=[t_eps])

        prt = carve(0, 128, F32)[0:82, :]
        sin = carve(512, 4608, F32)
        NXB = 8
        xbuf = [carve(18944 + i * 4096, 1024, F32) for i in range(3)] + [carve(53760 + i * 4096, 1024, F32) for i in range(5)]
        xnb = [carve(31232 + i * 2048, 1024, BF16) for i in range(2)]
        junk = carve(35328, 1024, BF16)
        mnT_t = carve(37376, KC * NM, BF16)
        mnT = mnT_t.rearrange("p (k t) -> p k t", k=KC)
        kT_t = carve(41472, 4 * NM, BF16)
        kT = kT_t.rearrange("p (k t) -> p k t", k=4)
        vb_t = carve(43520, 2 * XW, BF16)
        vb = vb_t.rearrange("p (k t) -> p k t", k=2)
        kvout = [carve(45568 + i * 4096, 2 * XW, F32).rearrange("p (k t) -> p k t", k=2) for i in range(2)]

        t_prt = Tok("prt")

        def rows(v, r):
            return v.rearrange("(r c) -> r c", c=128)

        plist = [(norm_g, 0, 8, None), (mem_norm_g, 8, 8, None),
                 (lru_conv_w, 16, 24, "k (n c) -> (k n) c"), (lru_conv_b, 40, 6, None), (lru_ba, 46, 6, None),
                 (lru_bx, 52, 6, None), (lru_lambda, 58, 6, None), (sconv_w, 64, 18, "k (n c) -> (k n) c")]
        for (v, r0, nr, pat) in plist:
            src = v.rearrange(pat, c=128) if pat else v.rearrange("(r c) -> r c", c=128)
            S.dma("sp", "cst", prt[r0:r0 + nr, :], src, writes=[t_prt], nowait=True)
        if STOP_AFTER == "P0dma":
            S.barrier()
            S.finish(block)
            return nc
        t_pT = Tok("pT")
        S.op("pe", lambda e: e.transpose(PX[:, 0:82], prt, ident[0:82, 0:82]), reads=[t_prt, t_ident], writes=[tPX])
        S.op("act", lambda e: e.copy(out=pT[:], in_=PX[:, 0:82]), reads=[tPX], writes=[t_pT])
        if STOP_AFTER == "P0b":
            S.barrier()
            S.finish(block)
            return nc
        t_x = [Tok(f"x{i}") for i in range(NXB)]
        t_xn = [Tok(f"xn{i}") for i in range(2)]
        t_junk = Tok("junk")
        t_uT = Tok("uT")
        t_mnT = Tok("mnT")
        tiles = [(memp[i * 128:(i + 1) * 128, :], 128, "m", i * 128) for i in range(2)]
        tiles += [(xp[i * 128:(i + 1) * 128, :], 128, "u", i * 128) for i in range(16)]
        tiles.append((xs[:, :], TS, "u", T))
        xsem = ["xa", "xb", "xc", "ya", "yb", "ka", "kb", "va"]
        tPXh = [Tok("PXh0"), Tok("PXh1")]
        t_st0 = [Tok(f"st0_{i}") for i in range(len(tiles))]

        def p0_stageA(ti):
            src, nr, kind, c0 = tiles[ti]
            xb_, tx = xbuf[ti % NXB], t_x[ti % NXB]
            S.dma("sp", xsem[ti % NXB], xb_[0:nr, :], src, writes=[tx])
            S.op("act", lambda e, xb_=xb_, nr=nr, ti=ti: e.activation(out=junk[0:nr, :], in_=xb_[0:nr, :], func=AF.Square,
                                                                      accum_out=stat[0:nr, ti:ti + 1]),
                 reads=[tx, t_stat], writes=[t_st0[ti], t_junk])

        def p0_stageB(ti):
            src, nr, kind, c0 = tiles[ti]
            xb_, tx = xbuf[ti % NXB], t_x[ti % NXB]
            xn_, txn = xnb[ti % 2], t_xn[ti % 2]
            S.op("act", lambda e, nr=nr, ti=ti: e.activation(out=stat[0:nr, ti:ti + 1], in_=stat[0:nr, ti:ti + 1], func=AF.Sqrt,
                                                             scale=1.0 / D, bias=epst[0:nr, 0:1]),
                 reads=[t_st0[ti], t_eps], writes=[t_st0[ti]])
            S.op("dve", lambda e, nr=nr, ti=ti: e.reciprocal(out=stat[0:nr, ti:ti + 1], in_=stat[0:nr, ti:ti + 1]),
                 reads=[t_st0[ti]], writes=[t_st0[ti]])
            S.op("dve", lambda e, xb_=xb_, xn_=xn_, nr=nr, ti=ti: e.tensor_scalar(out=xn_[0:nr, :], in0=xb_[0:nr, :],
                                                                                scalar1=stat[0:nr, ti:ti + 1], scalar2=None,
                                                                                op0=ALU.mult),
                 reads=[tx, t_st0[ti]], writes=[txn])

        def p0_stageC(ti):
            src, nr, kind, c0 = tiles[ti]
            xn_, txn = xnb[ti % 2], t_xn[ti % 2]
            bankb, tbank = (PXb, tPX) if ti % 2 == 0 else (PSb, tPSb)
            for kc in range(KC):
                S.op("pe", lambda e, xn_=xn_, nr=nr, kc=kc, bankb=bankb: e.transpose(bankb[:, kc * 128: kc * 128 + nr],
                                                                        xn_[0:nr, kc * 128:(kc + 1) * 128],
                                                                        identb[0:nr, 0:nr]),
                     reads=[txn, t_identb], writes=[tbank], inc=(kc == KC - 1))
            pview = bankb.rearrange("p (k t) -> p k t", k=KC)[:, :, 0:nr]
            if kind == "u":
                dst = uT[:, :, c0:c0 + nr]
                gcols = pT[:, 0:8]
                tdst = t_uTi[ti]
            else:
                dst = mnT[:, :, c0:c0 + nr]
                gcols = pT[:, 8:16]
                tdst = t_mnT
            S.op("dve", lambda e, dst=dst, pview=pview, gcols=gcols, nr=nr: e.tensor_tensor(
                out=dst, in0=pview, in1=gcols.unsqueeze(2).to_broadcast([128, KC, nr]), op=ALU.mult),
                reads=[tbank, t_pT], writes=[tdst])

        t_kT = Tok("kT")
        t_vb = Tok("vb")
        t_kvout = [Tok("kvo0"), Tok("kvo1")]
        kvst = {}

        def kv_mm():
            for which in range(2):
                wv_, wt_ = [], []
                for hh in range(2):
                    vws, tk = w_get(WT[("wk" if which == 0 else "wv") + str(hh)])
                    wv_.append(vws[0])
                    wt_.append(tk)
                pp, tk = next_pp()
                for mc in range(2):
                    for hh in range(2):
                        pairs = [(mnT[:, k, mc * 128:(mc + 1) * 128], wv_[hh][:, k, :]) for k in range(KC)]
                        mm_group(pp[:, mc * 512 + hh * 256: mc * 512 + (hh + 1) * 256], pairs, [t_mnT, wt_[hh]], tk,
                                 last_inc=(mc == 1 and hh == 1))
                kvst[which] = (pp, tk)
                if which == 0:
                    pp2, tk2 = next_pp()
                    for dc in range(4):
                        hh, off = dc // 2, (dc % 2) * 128
                        pairs = [(wv_[hh][:, k, off:off + 128], mnT[:, k, :]) for k in range(KC)]
                        mm_group(pp2[:, dc * 256:(dc + 1) * 256], pairs, [t_mnT, wt_[hh]], tk2, last_inc=(dc == 3))
                    kvst["kT"] = (pp2, tk2)

        def kv_evac():
            for which in range(2):
                pp, tk = kvst[which]
                ko = kvout[which]
                S.op("act", lambda e, ko=ko, pp=pp: e.copy(out=ko, in_=pp[:].rearrange("p (k t) -> p k t", k=2)),
                     reads=[tk], writes=[t_kvout[which]])
                dsto = (o_pk if which == 0 else o_pv).rearrange("(k p) c -> p k c", p=128)
                S.dma("sp", "out", dsto, ko, reads=[t_kvout[which]])
            pp, tk = kvst[1]
            S.op("act", lambda e, pp=pp: e.copy(out=vb, in_=pp[:].rearrange("p (k t) -> p k t", k=2)),
                 reads=[tk], writes=[t_vb])
            pp2, tk2 = kvst["kT"]
            S.op("dve", lambda e, pp2=pp2: e.tensor_copy(kT, pp2[:].rearrange("p (k t) -> p k t", k=4)),
                 reads=[tk2], writes=[t_kT])

        t_uTi = [Tok(f"uT{i}") for i in range(len(tiles))]
        PD = 5
        p0_stageA(0)
        p0_stageB(0)
        for ti in range(1, PD):
            p0_stageA(ti)
        t_sin = Tok("sin")
        S.dma("sp", "cs2", sin[0:TS, 0:2304], st_lc[:, :], writes=[t_sin], nowait=True)
        S.dma("sp", "cs2", sin[0:TS, 2304:3072], st_h[:, :], writes=[t_sin], nowait=True)
        S.dma("sp", "cs2", sin[0:TS, 3072:4608], st_sc[:, :], writes=[t_sin], nowait=True)
        t_out = Tok("out")
        o_slc3 = o_slc.rearrange("b (k c) -> b k c", k=3)
        st_lc3 = st_lc.rearrange("b (k c) -> b k c", k=3)
        S.dma("sp", "out", o_slc3[:, 0:2, :], st_lc3[:, 1:3, :])
        o_ssc3 = o_ssc.rearrange("b (k c) -> b k c", k=2)
        st_sc3 = st_sc.rearrange("b (k c) -> b k c", k=2)
        S.dma("sp", "out", o_ssc3[:, 0:1, :], st_sc3[:, 1:2, :])

        t_dv = Tok("dv")
        t_SIT = Tok("SIT")
        qT_t = carve(0, 4 * T, BF16)
        qT = qT_t.rearrange("p (k t) -> p k t", k=4)
        t_qT = [Tok(f"qT{h}") for h in range(4)]

        def sit():
            blocks_ = []
            for n in range(NCH):
                for k in range(3):
                    blocks_.append((n * 6 + k, k * W + n * 128))
                blocks_.append((n * 6 + 3, 2304 + n * 128))
                for k in range(2):
                    blocks_.append((n * 6 + 4 + k, 3072 + k * W + n * 128))
            for g0 in range(0, 36, 18):
                grp = blocks_[g0:g0 + 18]
                for gi, (slot_i, c0) in enumerate(grp):
                    S.op("pe", lambda e, gi=gi, c0=c0: e.transpose(PX[:, gi * TS:(gi + 1) * TS], sin[0:TS, c0:c0 + 128],
                                                                  ident[0:TS, 0:TS]),
                         reads=[t_sin, t_ident], writes=[tPX], inc=(gi == len(grp) - 1))
                assert [b_[0] for b_ in grp] == list(range(g0, g0 + 18))
                S.op("act", lambda e, g0=g0: e.copy(out=SIT[:, g0:g0 + 18, :],
                                                    in_=PX[:, 0:18 * TS].rearrange("p (a b) -> p a b", a=18)),
                     reads=[tPX], writes=[t_SIT])


        def dv_derive():
            S.op("act", lambda e: e.activation(out=dv[:, 0:6], in_=pT[:, 58:64], func=AF.Exp, scale=-1.0),
                 reads=[t_pT], writes=[t_dv])
            S.op("act", lambda e: e.activation(out=dv[:, 6:12], in_=dv[:, 0:6], func=AF.Ln, bias=1.0, scale=1.0),
                 reads=[t_dv], writes=[t_dv])
            S.op("dve", lambda e: e.tensor_scalar(out=dv[:, 0:6], in0=dv[:, 6:12], scalar1=-4.0, scalar2=None, op0=ALU.mult),
                 reads=[t_dv], writes=[t_dv])
            S.op("dve", lambda e: e.tensor_scalar(out=dv[:, 6:12], in0=dv[:, 6:12], scalar1=-8.0, scalar2=None, op0=ALU.mult),
                 reads=[t_dv], writes=[t_dv])
            S.op("dve", lambda e: e.tensor_scalar(out=dv[:, 12:24], in0=pT[:, 46:58], scalar1=0.5, scalar2=None, op0=ALU.mult),
                 reads=[t_pT, t_dv], writes=[t_dv])

        eqst = {}

        def eq_bank(h, b):
            rd_u = [t_uTi[i] for i in range(2, 10)]
            hh, hl = h // 2, h % 2
            vws, wtk = w_get(WT[f"q{hh}"])
            wq = vws[0]
            if b == 0:
                eqst[h] = next_pp()
            pp, tk = eqst[h]
            pairs = [(wq[:, k, hl * 128:(hl + 1) * 128], uT[:, k, b * 512:(b + 1) * 512]) for k in range(KC)]
            mm_group(pp[:, b * 512:(b + 1) * 512], pairs, [wtk] + rd_u, tk, last_inc=True)

        def eq_evac(heads):
            for h in heads:
                pp, tk = eqst[h]
                S.op("act", lambda e, pp=pp, h=h: e.copy(out=qT[:, h, 0:1024], in_=pp[:]),
                     reads=[tk], writes=[t_qT[h], t_sin, t_prt])

        for ti in range(len(tiles)):
            if ti + PD < len(tiles):
                p0_stageA(ti + PD)
            if ti + 1 < len(tiles):
                p0_stageB(ti + 1)
            p0_stageC(ti)
            if ti == 1:
                kv_mm()
            if ti == 5:
                kv_evac()
            if ti == 6:
                sit()
            if 9 <= ti <= 16:
                eq_bank((ti - 9) // 2, (ti - 9) % 2)
            if ti in (12, 14, 16):
                eq_evac([(ti - 12) // 2])
            if ti == 18:
                eq_evac([3])

        dv_derive()
        if STOP_AFTER == "P0c":
            dump([pT[:], stat[:, 0:19], uT[:, 0, 0:256], uT[:, 7, T - 128:TT], mnT[:, 0, :], mnT[:, 7, :]])
            S.barrier()
            S.finish(block)
            return nc
        S.barrier(skip=("out",))
        if STOP_AFTER == "KV":
            S.finish(block)
            return nc

        qT_t = carve(0, 4 * T, BF16)
        qT = qT_t.rearrange("p (k t) -> p k t", k=4)
        qs_tok = carve(16384, XW, BF16)
        sqg_tok = carve(17408, XW, F32)
        pTe = [carve(19456 + i * 2048, 1024, BF16).rearrange("p (k t) -> p k t", k=2) for i in range(2)]
        rden = [carve(23552 + i * 2048, 512, F32) for i in range(2)]
        o1b = [carve(27648 + i * 2048, 512, F32) for i in range(2)]
        selb_t = carve(31744, TS * 128, BF16)
        selb = selb_t.rearrange("p (b c) -> p b c", b=TS)
        eye16_t = carve(35840, 256, F32)
        eye16 = eye16_t.rearrange("p (a b) -> p a b", a=16)
        Sall_t = carve(45568, 128, F32)
        Sall = Sall_t.rearrange("p (c b h) -> p c b h", c=2, b=TS)
        Esm = carve(46080, 256, F32)
        Psm = carve(47104, 256, F32)
        Mk_t = carve(48128, 2 * 4 * 16 * 16, BF16)
        Mk = Mk_t.rearrange("p (c h b q) -> p c h b q", c=2, h=4, b=16)
        qb_sb = [carve(56320 + i * 2048, 512, F32) for i in range(2)]
        Kb = [carve(60416 + i * 4096, 1024, F32).rearrange("p (c f) -> p c f", c=2) for i in range(2)]
        prod = carve(68608, 1024, F32).rearrange("p (c f) -> p c f", c=2)

        t_og = [Tok(f"og{h}") for h in range(4)]
        t_qs = Tok("qs")
        t_sqg = Tok("sqg")
        def next_pp_c():
            while True:
                i = state["pp"] % 3
                state["pp"] += 1
                if i != state.get("pp_excl", -1):
                    return PP[i], tPP[i]

        def unit_half_c(lhs_list, rhs_arr, lo, reads):
            pp, tk = next_pp_c()
            nk = len(lhs_list)
            for b in range(2):
                pairs = [(lhs_list[k], rhs_arr[:, k, lo + b * 512: lo + (b + 1) * 512]) for k in range(nk)]
                mm_group(pp[:, b * 512:(b + 1) * 512], pairs, reads, tk, last_inc=(b == 1))
            return pp, tk

        t_pTe = [Tok("pTe0"), Tok("pTe1")]
        t_rden = [Tok("rden0"), Tok("rden1")]
        t_o1 = [Tok("o10"), Tok("o11")]

        def gen_C_prompt():
            for hh in range(2):
                vws, wtk = w_get(WT[f"q{hh}"])
                wq = vws[0]
                pairs = [(uT[:, k, T:TT], wq[:, k, :]) for k in range(KC)]
                mm_group(PS[0:TS, 0:256], pairs, [wtk], tPSb)
                S.op("act", lambda e, hh=hh: e.copy(out=qs_tok[0:TS, hh * 256:(hh + 1) * 256], in_=PS[0:TS, 0:256]),
                     reads=[tPSb], writes=[t_qs])
                yield
            for hh in range(2):
                vws, wtk = w_get(WT[f"q{hh}"])
                wq = vws[0]
                for hl in range(2):
                    h = hh * 2 + hl
                    lhs = [wq[:, k, hl * 128:(hl + 1) * 128] for k in range(KC)]
                    for (lo, hi) in HALVES[1:]:
                        pp, tk = unit_half_c(lhs, uT, lo, [wtk])
                        S.op("act", lambda e, pp=pp, h=h, lo=lo, hi=hi: e.copy(out=qT[:, h, lo:hi], in_=pp[:]),
                             reads=[tk], writes=[t_qT[h]])
                        yield
            for hh in range(2):
                vws, wtk = w_get(WT[f"qg{hh}"])
                wq = vws[0]
                for hl in range(2):
                    h = hh * 2 + hl
                    lhs = [wq[:, k, hl * 128:(hl + 1) * 128] for k in range(KC)]
                    for (lo, hi) in HALVES:
                        pp, tk = unit_half_c(lhs, uT, lo, [wtk])
                        S.op("act", lambda e, pp=pp, h=h, lo=lo, hi=hi: e.activation(out=og[:, h, lo:hi], in_=pp[:], func=AF.Silu),
                             reads=[tk], writes=[t_og[h]])
                        yield
                pairs = [(uT[:, k, T:TT], wq[:, k, :]) for k in range(KC)]
                mm_group(PS[0:TS, 0:256], pairs, [wtk], tPSb)
                S.op("act", lambda e, hh=hh: e.activation(out=sqg_tok[0:TS, hh * 256:(hh + 1) * 256],
                                                         in_=PS[0:TS, 0:256], func=AF.Silu),
                     reads=[tPSb], writes=[t_sqg])
                yield
            iters = [(tb, h) for tb in range(4) for h in range(4)]
            stage1 = {}

            def att_s1(i):
                tb, h = iters[i]
                c0 = tb * 512
                pe_, tpe = pTe[i % 2], t_pTe[i % 2]
                pp, tk = next_pp_c()
                for mc in range(2):
                    mm_group(pp[:, mc * 512:(mc + 1) * 512], [(kT[:, h, mc * 128:(mc + 1) * 128], qT[:, h, c0:c0 + 512])],
                             [t_kT, t_qT[h]], tk, last_inc=(mc == 1))
                S.op("act", lambda e, pp=pp, pe_=pe_: e.activation(out=pe_, in_=pp[:].rearrange("p (k t) -> p k t", k=2),
                                                                  func=AF.Exp, scale=SCALE),
                     reads=[tk], writes=[tpe])

            def att_s2(i):
                tb, h = iters[i]
                c0 = tb * 512
                pe_, tpe = pTe[i % 2], t_pTe[i % 2]
                rd, trd = rden[i % 2], t_rden[i % 2]
                o1, to1 = o1b[i % 2], t_o1[i % 2]
                pp2, tk2 = next_pp_c()
                mm_group(pp2[:, 0:512], [(vb[:, mc, h * 128:(h + 1) * 128], pe_[:, mc, :]) for mc in range(2)],
                         [t_vb, tpe], tk2, last_inc=False)
                mm_group(pp2[:, 512:1024], [(onesb[:], pe_[:, mc, :]) for mc in range(2)], [t_ones, tpe], tk2)
                S.op("act", lambda e, pp2=pp2, rd=rd: e.activation(out=rd, in_=pp2[:, 512:1024], func=AF.Ln), reads=[tk2], writes=[trd])
                S.op("act", lambda e, rd=rd: e.activation(out=rd, in_=rd, func=AF.Exp, scale=-1.0), reads=[trd], writes=[trd])
                S.op("dve", lambda e, pp2=pp2, rd=rd, o1=o1: e.tensor_tensor(out=o1, in0=pp2[:, 0:512], in1=rd, op=ALU.mult),
                     reads=[tk2, trd], writes=[to1])
                S.op("dve", lambda e, o1=o1, h=h, c0=c0: e.tensor_tensor(out=og[:, h, c0:c0 + 512], in0=o1,
                                                                        in1=og[:, h, c0:c0 + 512], op=ALU.mult),
                     reads=[to1, t_og[h]], writes=[t_og[h]])

            att_s1(0)
            for i in range(len(iters)):
                if i + 1 < len(iters):
                    att_s1(i + 1)
                att_s2(i)
                yield

        t_selb = Tok("selb")
        t_eye = Tok("eye")
        t_qb = [Tok("qb0"), Tok("qb1")]
        t_Kb = [Tok("Kb0"), Tok("Kb1")]
        t_prod = Tok("prod")
        t_Sall = Tok("Sall")
        t_E = Tok("E")
        t_P = Tok("P")
        t_Mk = Tok("Mk")
        ksem = ["ka", "kb"]
        vsem = ["sva", "svb", "svc", "svd", "sve", "svf"]

        def gen_C_sample():
            S.wait_dma(["dve", "act", "pool", "pe"], "out")
            S.op("dve", lambda e: e.tensor_copy(selb[0:TS, :, :], identb[0:TS, 0:TS].unsqueeze(2).to_broadcast([TS, TS, 128])),
                 reads=[t_identb], writes=[t_selb])
            S.op("pool", lambda e: e.memset(eye16_t, 0.0), writes=[t_eye])
            S.op("pool", lambda e: e.affine_select(eye16, eye16, pattern=[[1, 16], [-1, 16]], compare_op=ALU.not_equal,
                                                   fill=1.0, base=0, channel_multiplier=0),
                 reads=[t_eye], writes=[t_eye])
            yield
            for b in range(TS):
                kb_, tkb = Kb[b % 2], t_Kb[b % 2]
                qb_, tqb = qb_sb[b % 2], t_qb[b % 2]
                S.dma("sp", ksem[b % 2], kb_, ck[b].rearrange("(c m) f -> m c f", c=2), writes=[tkb])
                S.op("pe", lambda e, b=b: e.matmul(PX[:, 0:512], selb[0:TS, b, :], qs_tok[0:TS, :], start=True, stop=True),
                     reads=[t_selb, t_qs], writes=[tPX])
                S.op("act", lambda e, qb_=qb_: e.copy(out=qb_, in_=PX[:, 0:512]), reads=[tPX], writes=[tqb])
                S.op("pool", lambda e, kb_=kb_, qb_=qb_: e.tensor_tensor(out=prod, in0=kb_,
                                                                         in1=qb_.unsqueeze(1).to_broadcast([128, 2, 512]), op=ALU.mult),
                     reads=[tkb, tqb], writes=[t_prod])
                S.op("dve", lambda e, b=b: e.tensor_reduce(out=Sall[:, :, b, :],
                                                           in_=prod.rearrange("p c (h d) -> p c h d", h=4), axis=AX.X, op=ALU.add),
                     reads=[t_prod], writes=[t_Sall])
                yield
            for c in range(2):
                S.op("pe", lambda e, c=c: e.transpose(PX[0:64, c * 128:(c + 1) * 128],
                                                      Sall_t[:, c * 64:(c + 1) * 64], ident[:, :]),
                     reads=[t_Sall, t_ident], writes=[tPX], inc=(c == 1))
            S.op("dve", lambda e: e.tensor_reduce(out=stat2[0:64, 0:1], in_=PX[0:64, 0:256], axis=AX.X, op=ALU.max),
                 reads=[tPX], writes=[t_sm])
            S.op("dve", lambda e: e.tensor_scalar(out=stat2[0:64, 0:1], in0=stat2[0:64, 0:1], scalar1=-SCALE, scalar2=None, op0=ALU.mult),
                 reads=[t_sm], writes=[t_sm])
            S.op("act", lambda e: e.activation(out=Esm[0:64, :], in_=PX[0:64, 0:256], func=AF.Exp, scale=SCALE,
                                               bias=stat2[0:64, 0:1], accum_out=stat2[0:64, 1:2]),
                 reads=[tPX, t_sm], writes=[t_E, t_sm])
            S.op("dve", lambda e: e.reciprocal(out=stat2[0:64, 1:2], in_=stat2[0:64, 1:2]), reads=[t_sm], writes=[t_sm])
            S.op("dve", lambda e: e.tensor_scalar(out=Psm[0:64, :], in0=Esm[0:64, :], scalar1=stat2[0:64, 1:2], scalar2=None, op0=ALU.mult),
                 reads=[t_E, t_sm], writes=[t_P])
            for c in range(2):
                S.op("pe", lambda e, c=c: e.transpose(PX[:, c * 64:(c + 1) * 64], Psm[0:64, c * 128:(c + 1) * 128], ident[0:64, 0:64]),
                     reads=[t_P, t_ident], writes=[tPX], inc=(c == 1))
            for c in range(2):
                S.op("dve", lambda e, c=c: e.tensor_tensor(
                    out=Mk[:, c],
                    in0=PX[:, c * 64:(c + 1) * 64].rearrange("p (b h) -> p h b", h=4).unsqueeze(3).to_broadcast([128, 4, 16, 16]),
                    in1=eye16.unsqueeze(1).to_broadcast([128, 4, 16, 16]), op=ALU.mult),
                    reads=[tPX, t_eye], writes=[t_Mk])
            yield
            ppa, tka = next_pp_c()
            state["pp_excl"] = PP.index(ppa)
            hbank = [(PS, 0, tPSb), (PX, 0, tPX), (ppa, 0, tka), (ppa, 512, tka)]
            NVB = 6
            Vb = ([carve(68608 + i * 2048, 1024, BF16).rearrange("p (c f) -> p c f", c=2) for i in range(2)]
                  + [carve(60416 + i * 2048, 1024, BF16).rearrange("p (c f) -> p c f", c=2) for i in range(4)])
            t_Vb = [Tok(f"Vb{i}") for i in range(NVB)]
            valias = [[t_prod], [t_prod], [t_Kb[0]], [t_Kb[0]], [t_Kb[1]], [t_Kb[1]]]

            def v_load(b):
                S.dma("pool", vsem[b % NVB], Vb[b % NVB], cv[b].rearrange("(c m) f -> m c f", c=2),
                      writes=[t_Vb[b % NVB]] + (valias[b] if b < NVB else []))

            for b in range(NVB - 1):
                v_load(b)
            for b in range(TS):
                vb_, tvb = Vb[b % NVB], t_Vb[b % NVB]
                if b + NVB - 1 < TS:
                    v_load(b + NVB - 1)
                for h in range(4):
                    pph, coff, tkh = hbank[h]
                    for c in range(2):
                        first = (b == 0 and c == 0)
                        last = (b == TS - 1 and c == 1)
                        S.op("pe", lambda e, vb_=vb_, b=b, h=h, c=c, first=first, last=last, pph=pph, coff=coff: e.matmul(
                            pph[0:TS, coff:coff + 128], Mk[:, c, h, b, :], vb_[:, c, h * 128:(h + 1) * 128],
                            start=first, stop=last, skip_group_check=True),
                            reads=[t_Mk, tvb], writes=[tkh], inc=(h == 3 and c == 1))
                yield
            ogs_tok = qs_tok
            for h in range(4):
                pph, coff, tkh = hbank[h]
                S.op("dve", lambda e, h=h, pph=pph, coff=coff: e.tensor_tensor(
                    out=ogs_tok[0:TS, h * 128:(h + 1) * 128], in0=pph[0:TS, coff:coff + 128],
                    in1=sqg_tok[0:TS, h * 128:(h + 1) * 128], op=ALU.mult),
                    reads=[tkh, t_sqg, t_qs], writes=[t_qs])
            state["pp_excl"] = -1
            for h in range(4):
                S.op("pe", lambda e, h=h: e.transpose(PXb[:, h * TS:(h + 1) * TS], ogs_tok[0:TS, h * 128:(h + 1) * 128],
                                                      identb[0:TS, 0:TS]),
                     reads=[t_qs, t_identb], writes=[tPX], inc=(h == 3))
            S.op("act", lambda e: e.copy(out=og[:, :, T:TT], in_=PXb[:, 0:4 * TS].rearrange("p (h b) -> p h b", h=4)),
                 reads=[tPX], writes=t_og)
            yield

        gp, gs = gen_C_prompt(), gen_C_sample()
        for _ in range(2):
            next(gp)
        alive = [gp, gs]
        while alive:
            for g in list(alive):
                try:
                    next(g)
                except StopIteration:
                    alive.remove(g)
        if STOP_AFTER == "C":
            dump([og[:, 0, 0:512], og[:, 3, T - 512:T], og[:, 0, T:TT], og[:, 1, T:TT], og[:, 2, T:TT], og[:, 3, T:TT], qT[:, 0, 0:256]], off_b=0)
        S.barrier(no_wait=("pe",))
        if STOP_AFTER == "C":
            S.finish(block)
            return nc

        wab_t = carve(0, 2 * NCH * 128, BF16)
        wab = wab_t.rearrange("p (a n d) -> p a n d", a=2, n=NCH)
        lxp = carve(3072, 3 + T, F32)
        lxs_t = carve(11280, 4 * TS, F32)
        lxs = lxs_t.rearrange("p (k b) -> p k b", k=4)
        xc = carve(11536, TT, F32)
        xcb = carve(19792, TT, BF16)
        thr = carve(23920, TT, F32)
        a2b = carve(32176, TT, F32)
        thi = carve(40432, TT, F32)
        scg_sb = carve(48752, TT, F32)
        cy = scg_sb
        cinp = carve(57008, 2 + T, F32)
        cins_t = carve(65208, 3 * TS, F32)
        cins = cins_t.rearrange("p (k b) -> p k b", k=3)
        so_tok = carve(65400, W, F32)
        t_wab = Tok("wab")
        S.dma("pool", "cs3", wab[:, 0], lru_wa.rearrange("n c d -> c n d"), writes=[t_wab])
        S.dma("pool", "cs3", wab[:, 1], lru_wx.rearrange("n c d -> c n d"), writes=[t_wab], nowait=True)
        t_lxp, t_lxs, t_xc, t_xcb, t_thr, t_a2, t_thi = (Tok("lxp"), Tok("lxs"), Tok("xc"), Tok("xcb"), Tok("thr"),
                                                        Tok("a2"), Tok("thi"))
        t_gA = [Tok(f"gA{n}") for n in range(NCH)]
        t_SO = Tok("SO")
        t_scg, t_cinp, t_cins = Tok("scg"), Tok("cinp"), Tok("cins")
        t_cy = t_scg
        t_gB = [Tok(f"gB{n}") for n in range(NCH)]
        S.op("pool", lambda e: e.memset(lxp[:, 0:3], 0.0), writes=[t_lxp])
        S.op("pool", lambda e: e.memset(cinp[:, 0:2], 0.0), writes=[t_cinp])

        def gen_A(n):
            vws, wtk = w_get(WT[f"A{n}"])
            wlg, wlx = vws
            lhs_lg = [wlg[:, k, :] for k in range(KC)]
            lhs_lx = [wlx[:, k, :] for k in range(KC)]
            for (lo, hi) in HALVES:
                pp, tk = unit_half(lhs_lg, uT, lo, [wtk])
                S.op("act", lambda e, pp=pp, n=n, lo=lo, hi=hi: e.activation(out=gA[:, n, lo:hi], in_=pp[:], func=AF.Silu),
                     reads=[tk], writes=[t_gA[n]])
            sp_, tk = unit_samp(lhs_lg, uT, [wtk])
            S.op("act", lambda e, sp_=sp_, n=n: e.activation(out=gA[:, n, T:TT], in_=sp_, func=AF.Silu),
                 reads=[tk], writes=[t_gA[n]])
            yield
            for (lo, hi) in HALVES:
                pp, tk = unit_half(lhs_lx, uT, lo, [wtk])
                S.op("dve", lambda e, pp=pp, lo=lo, hi=hi: e.tensor_copy(lxp[:, 3 + lo:3 + hi], pp[:]),
                     reads=[tk], writes=[t_lxp])
            sp_, tk = unit_samp(lhs_lx, uT, [wtk])
            S.op("dve", lambda e, n=n: e.tensor_copy(lxs[:, 0:3, :], SIT[:, n * 6:n * 6 + 3, :]), reads=[t_SIT], writes=[t_lxs])
            S.op("dve", lambda e, sp_=sp_: e.tensor_copy(lxs[:, 3, :], sp_), reads=[tk], writes=[t_lxs])
            yield
            cw = lambda k, n=n: pT[:, 16 + k * 6 + n: 17 + k * 6 + n]
            cbias = pT[:, 40 + n:41 + n]
            S.op("act", lambda e, cw=cw, cbias=cbias: e.activation(out=xc[:, 0:T], in_=lxp[:, 0:T], func=AF.Identity,
                                                                   scale=cw(0), bias=cbias),
                 reads=[t_lxp, t_pT], writes=[t_xc])
            S.op("act", lambda e, cw=cw, cbias=cbias: e.activation(out=xc[:, T:TT], in_=lxs[:, 0, :], func=AF.Identity,
                                                                   scale=cw(0), bias=cbias),
                 reads=[t_lxs, t_pT], writes=[t_xc])
            for k in range(1, 4):
                S.op("dve", lambda e, k=k, cw=cw: e.scalar_tensor_tensor(out=xc[:, 0:T], in0=lxp[:, k:k + T], scalar=cw(k),
                                                                         in1=xc[:, 0:T], op0=ALU.mult, op1=ALU.add),
                     reads=[t_lxp, t_xc], writes=[t_xc])
                S.op("dve", lambda e, k=k, cw=cw: e.scalar_tensor_tensor(out=xc[:, T:TT], in0=lxs[:, k, :], scalar=cw(k),
                                                                         in1=xc[:, T:TT], op0=ALU.mult, op1=ALU.add),
                     reads=[t_lxs, t_xc], writes=[t_xc])
            S.op("pool", lambda e, n=n: e.tensor_copy(SO[:, n, 0:3], lxp[:, T:T + 3]), reads=[t_lxp], writes=[t_SO])
            S.op("pool", lambda e, n=n: e.tensor_copy(SO[:, n, 6:22], lxs[:, 3, :]), reads=[t_lxs], writes=[t_SO])
            yield
            S.op("act", lambda e: e.copy(out=xcb, in_=xc), reads=[t_xc], writes=[t_xcb])
            yield
            for gi, (dst, tdst, bcol) in enumerate([(thr, t_thr, 12 + n), (thi, t_thi, 18 + n)]):
                lhs = [wab[:, gi, n, :]]
                xcb3 = xcb.unsqueeze(1)
                for (lo, hi) in HALVES:
                    pp, tk = unit_half(lhs, xcb3, lo, [t_wab, t_xcb])
                    S.op("act", lambda e, pp=pp, dst=dst, lo=lo, hi=hi, bcol=bcol: e.activation(
                        out=dst[:, lo:hi], in_=pp[:], func=AF.Tanh, scale=0.5, bias=dv[:, bcol:bcol + 1]),
                        reads=[tk, t_dv], writes=[tdst])
                sp_, tk = unit_samp(lhs, xcb3, [t_wab, t_xcb])
                S.op("act", lambda e, sp_=sp_, dst=dst, bcol=bcol: e.activation(
                    out=dst[:, T:TT], in_=sp_, func=AF.Tanh, scale=0.5, bias=dv[:, bcol:bcol + 1]),
                    reads=[tk, t_dv], writes=[tdst])
                yield
            S.op("act", lambda e, n=n: e.activation(out=a2b, in_=thr, func=AF.Exp, scale=dv[:, 6 + n:7 + n], bias=dv[:, 6 + n:7 + n]),
                 reads=[t_thr, t_dv], writes=[t_a2])
            S.op("act", lambda e, n=n: e.activation(out=thr, in_=thr, func=AF.Exp, scale=dv[:, n:n + 1], bias=dv[:, n:n + 1]),
                 reads=[t_thr, t_dv], writes=[t_thr])
            S.op("dve", lambda e: e.tensor_scalar(out=a2b, in0=a2b, scalar1=1.0, scalar2=-1.0, op0=ALU.min, op1=ALU.mult),
                 reads=[t_a2], writes=[t_a2])
            yield
            S.op("act", lambda e: e.activation(out=a2b, in_=a2b, func=AF.Sqrt, bias=1.0, scale=1.0), reads=[t_a2], writes=[t_a2])
            S.op("dve", lambda e: e.scalar_tensor_tensor(out=a2b, in0=a2b, scalar=0.5, in1=xc, op0=ALU.mult, op1=ALU.mult),
                 reads=[t_a2, t_xc], writes=[t_a2])
            S.op("dve", lambda e: e.scalar_tensor_tensor(out=thi, in0=thi, scalar=1.0, in1=a2b, op0=ALU.add, op1=ALU.mult),
                 reads=[t_thi, t_a2], writes=[t_thi])
            yield
            S.op("dve", lambda e: e.tensor_tensor_scan(out=a2b[:, 0:T], data0=thr[:, 0:T], data1=thi[:, 0:T], initial=0.0,
                                                       op0=ALU.mult, op1=ALU.add),
                 reads=[t_thr, t_thi], writes=[t_a2])
            S.op("dve", lambda e, n=n: e.tensor_tensor(out=a2b[:, T:TT], in0=thr[:, T:TT], in1=SIT[:, n * 6 + 3, :], op=ALU.mult),
                 reads=[t_thr, t_SIT], writes=[t_a2])
            S.op("dve", lambda e: e.tensor_tensor(out=a2b[:, T:TT], in0=a2b[:, T:TT], in1=thi[:, T:TT], op=ALU.add),
                 reads=[t_a2, t_thi], writes=[t_a2])
            yield
            S.op("pool", lambda e, n=n: e.tensor_copy(SO[:, n, 3:4], a2b[:, T - 1:T]), reads=[t_a2], writes=[t_SO])
            S.op("pool", lambda e, n=n: e.tensor_copy(SO[:, n, 22:38], a2b[:, T:TT]), reads=[t_a2], writes=[t_SO])
            S.op("dve", lambda e, n=n: e.tensor_tensor(out=gA[:, n, :], in0=a2b, in1=gA[:, n, :], op=ALU.mult),
                 reads=[t_a2, t_gA[n]], writes=[t_gA[n]])
            yield

        def gen_B(n):
            vws, wtk = w_get(WT[f"B{n}a"])
            wsg, wsb = vws
            lhs_sg = [wsg[:, k, :] for k in range(KC)]
            lhs_sb = [wsb[:, k, :] for k in range(KC)]
            for (lo, hi) in HALVES:
                pp, tk = unit_half(lhs_sg, uT, lo, [wtk])
                S.op("act", lambda e, pp=pp, n=n, lo=lo, hi=hi: e.activation(out=gB[:, n, lo:hi], in_=pp[:], func=AF.Silu),
                     reads=[tk], writes=[t_gB[n]])
            sp_, tk = unit_samp(lhs_sg, uT, [wtk])
            S.op("act", lambda e, sp_=sp_, n=n: e.activation(out=gB[:, n, T:TT], in_=sp_, func=AF.Silu),
                 reads=[tk], writes=[t_gB[n]])
            yield
            for (lo, hi) in HALVES:
                pp, tk = unit_half(lhs_sb, uT, lo, [wtk])
                S.op("dve", lambda e, pp=pp, n=n, lo=lo, hi=hi: e.tensor_tensor(out=gB[:, n, lo:hi], in0=pp[:], in1=gB[:, n, lo:hi],
                                                                                op=ALU.mult),
                     reads=[tk, t_gB[n]], writes=[t_gB[n]])
            sp_, tk = unit_samp(lhs_sb, uT, [wtk])
            S.op("dve", lambda e, sp_=sp_, n=n: e.tensor_tensor(out=gB[:, n, T:TT], in0=sp_, in1=gB[:, n, T:TT], op=ALU.mult),
                 reads=[tk, t_gB[n]], writes=[t_gB[n]])
            yield
            vws, wtk = w_get(WT[f"B{n}b"])
            wscg, wsh = vws
            lhs_scg = [wscg[:, k, :] for k in range(KC)]
            lhs_sh = [wsh[:, k, :] for k in range(KC)]
            for (lo, hi) in HALVES:
                pp, tk = unit_half(lhs_scg, uT, lo, [wtk])
                S.op("act", lambda e, pp=pp, lo=lo, hi=hi: e.copy(out=scg_sb[:, lo:hi], in_=pp[:]), reads=[tk], writes=[t_scg])
            sp_, tk = unit_samp(lhs_scg, uT, [wtk])
            S.op("act", lambda e, sp_=sp_: e.copy(out=scg_sb[:, T:TT], in_=sp_), reads=[tk], writes=[t_scg])
            yield
            for (lo, hi) in HALVES:
                pp, tk = unit_half(lhs_sh, uT, lo, [wtk])
                S.op("dve", lambda e, pp=pp, lo=lo, hi=hi: e.tensor_tensor(out=cinp[:, 2 + lo:2 + hi], in0=pp[:],
                                                                           in1=scg_sb[:, lo:hi], op=ALU.mult),
                     reads=[tk, t_scg], writes=[t_cinp])
            sp_, tk = unit_samp(lhs_sh, uT, [wtk])
            S.op("dve", lambda e, n=n: e.tensor_copy(cins[:, 0:2, :], SIT[:, n * 6 + 4:n * 6 + 6, :]), reads=[t_SIT], writes=[t_cins])
            S.op("dve", lambda e, sp_=sp_: e.tensor_tensor(out=cins[:, 2, :], in0=sp_, in1=scg_sb[:, T:TT], op=ALU.mult),
                 reads=[tk, t_scg], writes=[t_cins])
            yield
            sw = lambda k, n=n: pT[:, 64 + k * 6 + n: 65 + k * 6 + n]
            S.op("act", lambda e, sw=sw: e.activation(out=cy[:, 0:T], in_=cinp[:, 0:T], func=AF.Identity, scale=sw(0)),
                 reads=[t_cinp, t_pT], writes=[t_cy])
            S.op("act", lambda e, sw=sw: e.activation(out=cy[:, T:TT], in_=cins[:, 0, :], func=AF.Identity, scale=sw(0)),
                 reads=[t_cins, t_pT], writes=[t_cy])
            for k in range(1, 3):
                S.op("dve", lambda e, k=k, sw=sw: e.scalar_tensor_tensor(out=cy[:, 0:T], in0=cinp[:, k:k + T], scalar=sw(k),
                                                                         in1=cy[:, 0:T], op0=ALU.mult, op1=ALU.add),
                     reads=[t_cinp, t_cy], writes=[t_cy])
                S.op("dve", lambda e, k=k, sw=sw: e.scalar_tensor_tensor(out=cy[:, T:TT], in0=cins[:, k, :], scalar=sw(k),
                                                                         in1=cy[:, T:TT], op0=ALU.mult, op1=ALU.add),
                     reads=[t_cins, t_cy], writes=[t_cy])
            S.op("pool", lambda e, n=n: e.tensor_copy(SO[:, n, 4:6], cinp[:, T:T + 2]), reads=[t_cinp], writes=[t_SO])
            S.op("pool", lambda e, n=n: e.tensor_copy(SO[:, n, 38:54], cins[:, 2, :]), reads=[t_cins], writes=[t_SO])
            S.op("dve", lambda e, n=n: e.tensor_tensor(out=gB[:, n, :], in0=cy, in1=gB[:, n, :], op=ALU.mult),
                 reads=[t_cy, t_gB[n]], writes=[t_gB[n]])
            yield

        def interleave(*gens):
            gens = list(gens)
            while gens:
                for g in list(gens):
                    try:
                        next(g)
                    except StopIteration:
                        gens.remove(g)

        gens = {}

        def adv(kind, n):
            g = gens.get((kind, n))
            if g is None:
                return
            try:
                next(g)
            except StopIteration:
                pass

        for n in range(NCH + 1):
            if n < NCH:
                gens[("A", n)] = gen_A(n)
                gens[("B", n)] = gen_B(n)
            adv("A", n)
            if n > 0:
                adv("B", n)
            adv("A", n - 1)
            adv("A", n)
            if n > 0:
                adv("B", n)
            adv("A", n)
            if n == 0:
                adv("B", n)
                adv("B", n)
            adv("A", n - 1)
            adv("B", n - 1)
            adv("B", n - 1)
            adv("A", n - 1)
            adv("A", n)
            adv("A", n)
            adv("A", n)
            adv("B", n - 1)
            adv("A", n)
        for g in gens.values():
            for _ in g:
                pass
        if STOP_AFTER == "B":
            dump([gB[:, 0, 0:512], gB[:, 5, T - 512:T], gB[:, 0, T:TT], gB[:, 5, T:TT], gB[:, 2, 1024:1536]], off_b=0)
        t_sot = Tok("sot")
        for g0 in range(0, NCH, 3):
            for n in range(g0, g0 + 3):
                S.op("pe", lambda e, n=n, g0=g0: e.transpose(PX[0:54, (n - g0) * 128:(n - g0 + 1) * 128], SO[:, n, :], ident[:, :]),
                     reads=[t_SO, t_ident], writes=[tPX], inc=(n == g0 + 2))
            S.op("act", lambda e, g0=g0: e.copy(out=so_tok[0:54, g0 * 128:(g0 + 3) * 128], in_=PX[0:54, 0:384]),
                 reads=[tPX], writes=[t_sot])
        S.dma("sp", "out", o_plc[:, :], so_tok[0:3, :], reads=[t_sot])
        S.dma("sp", "out", o_ph[:, :], so_tok[3:4, :], reads=[t_sot])
        S.dma("sp", "out", o_psc[:, :], so_tok[4:6, :], reads=[t_sot])
        S.dma("sp", "out", o_slc3[:, 2, :], so_tok[6:22, :], reads=[t_sot])
        S.dma("sp", "out", o_sh[:, :], so_tok[22:38, :], reads=[t_sot])
        S.dma("sp", "out", o_ssc3[:, 1, :], so_tok[38:54, :], reads=[t_sot])
        if STOP_AFTER == "B":
            dump([gB[:, 0, 0:512], gB[:, 5, T - 512:T], gB[:, 0, T:TT], gB[:, 5, T:TT], gB[:, 2, 1024:1536]], off_b=56320)
        S.barrier(no_wait=("pe",))
        if STOP_AFTER == "B":
            S.finish(block)
            return nc

        NTH = 3
        thb = [carve(i * 4096, 1024, F32) for i in range(NTH)]
        tacc = [carve(12288 + i * 4096, 1024, F32) for i in range(2)]
        mT_t = carve(20480, KC * TT, BF16)
        mT = mT_t.rearrange("p (k t) -> p k t", k=KC)
        wo_t = carve(53504, KC * D, BF16)
        wo = wo_t.rearrange("p (k c) -> p k c", k=KC)
        t_wo = Tok("wo")
        if STOP_AFTER == "M0":
            S.barrier()
            S.finish(block)
            return nc
        t_th = [Tok(f"th{i}") for i in range(NTH)]
        t_acc = [Tok("acc0"), Tok("acc1")]
        t_mT = Tok("mT")
        t_mTr = {0: Tok("mT_h0"), 1024: Tok("mT_h1"), T: Tok("mT_s")}
        thc = {"i": 0, "a": 0}
        gsrc = [(gA, NCH), (gB, NCH), (og, 4)]
        xorder = [2, 1, 0]
        gtok = [t_gA, t_gB, t_og]
        ths_f = carve(69888, 3 * TS, F32)
        tzs_f = carve(70080, 3 * TS, F32)
        accs_f = carve(70272, TS, F32)
        t_accs2 = Tok("accs2")
        t_ths = Tok("ths")
        t_accs = Tok("accs")
        for j in range(8):
            vg, wtkg = w_get(WT[f"Mg{j}"])
            vo, wtko = w_get(WT[f"Mo{j}"])
            S.dma("pool", "cs4", wo[:, j, :], w_out[j * 128:(j + 1) * 128, :], writes=[t_wo], nowait=True)
            for (lo, hi) in HALVES:
                acc, tacc_ = tacc[thc["a"] % 2], t_acc[thc["a"] % 2]
                thc["a"] += 1
                for xi, x in enumerate(xorder):
                    garr, nk = gsrc[x]
                    lhs_g = [vg[x][:, k, :] for k in range(KC)]
                    lhs_o = [vo[x][:, k, :] for k in range(nk)]
                    th_, tth = thb[thc["i"] % NTH], t_th[thc["i"] % NTH]
                    thc["i"] += 1
                    pp, tk = unit_half(lhs_g, uT, lo, [wtkg])
                    S.op("act", lambda e, pp=pp, th_=th_: e.activation(out=th_, in_=pp[:], func=AF.Tanh, scale=0.5),
                         reads=[tk], writes=[tth])
                    zp_, tkz = unit_half(lhs_o, garr, lo, [wtko] + gtok[x])
                    zp = zp_[:]
                    if xi == 0:
                        S.op("dve", lambda e, acc=acc, th_=th_, zp=zp: e.scalar_tensor_tensor(out=acc, in0=th_, scalar=1.0, in1=zp,
                                                                                           op0=ALU.add, op1=ALU.mult),
                             reads=[tth, tkz], writes=[tacc_])
                    else:
                        S.op("dve", lambda e, th_=th_, zp=zp: e.scalar_tensor_tensor(out=th_, in0=th_, scalar=1.0, in1=zp,
                                                                                    op0=ALU.add, op1=ALU.mult),
                             reads=[tth, tkz], writes=[tth])
                        if xi == 1:
                            S.op("pool", lambda e, acc=acc, th_=th_: e.tensor_tensor(out=acc, in0=acc, in1=th_, op=ALU.add),
                                 reads=[tth, tacc_], writes=[tacc_])
                        else:
                            S.op("pool", lambda e, acc=acc, th_=th_, j=j, lo=lo, hi=hi: e.tensor_tensor(
                                out=mT[:, j, lo:hi], in0=acc, in1=th_, op=ALU.add),
                                reads=[tth, tacc_], writes=[t_mTr[lo]])
            tMs_g, tMs_o = tPSb, tPX
            for x in xorder:
                lhs_g = [vg[x][:, k, :] for k in range(KC)]
                mm_group(PS[:, x * TS:(x + 1) * TS], [(lhs_g[k], uT[:, k, T:TT]) for k in range(KC)], [wtkg], tMs_g)
            S.op("act", lambda e: e.activation(out=ths_f, in_=PS[:, 0:3 * TS], func=AF.Tanh, scale=0.5),
                 reads=[tMs_g], writes=[t_ths])
            for x in xorder:
                garr, nk = gsrc[x]
                lhs_o = [vo[x][:, k, :] for k in range(nk)]
                mm_group(PX[:, x * TS:(x + 1) * TS], [(lhs_o[k], garr[:, k, T:TT]) for k in range(nk)], [wtko] + gtok[x], tMs_o)
            S.op("dve", lambda e: e.scalar_tensor_tensor(out=tzs_f, in0=ths_f, scalar=1.0, in1=PX[:, 0:3 * TS],
                                                         op0=ALU.add, op1=ALU.mult),
                 reads=[t_ths, tMs_o], writes=[t_accs])
            S.op("dve", lambda e: e.tensor_reduce(out=accs_f, in_=tzs_f.rearrange("p (x b) -> p b x", x=3),
                                                  axis=AX.X, op=ALU.add),
                 reads=[t_accs], writes=[t_accs2])
            S.op("dve", lambda e, j=j: e.tensor_copy(mT[:, j, T:TT], accs_f), reads=[t_accs2], writes=[t_mTr[T]])
            if STOP_AFTER == "M5":
                S.barrier()
                S.finish(block)
                return nc
        S.barrier(no_wait=("pe",))
        if STOP_AFTER == "M":
            S.finish(block)
            return nc

        xr = [carve(i * 4096, 1024, F32) for i in range(2)]
        yr = [carve(8192 + i * 4096, 1024, F32) for i in range(2)] + [carve(69888, 1024, F32)]
        fgb = carve(16384, 1024, F32)
        t_fgb = Tok("fgb")
        S.dma("sp", "cs5", fgb, final_norm_g.partition_broadcast(128), writes=[t_fgb])
        t_xr = [Tok("xr0"), Tok("xr1")]
        t_yr = [Tok("yr0"), Tok("yr1"), Tok("yr2")]
        ftiles = [(xp[i * 128:(i + 1) * 128, :], y_p[i * 128:(i + 1) * 128, :], 128, i * 128) for i in range(16)]
        ftiles.append((xs[:, :], y_s[:, :], TS, T))
        xsem2 = ["fxa", "fxb"]
        ysem = ["ya", "yb", "xc"]
        t_stF = [Tok(f"stF{i}") for i in range(len(ftiles))]

        def f_stageA(ti):
            src, dst, nr, c0 = ftiles[ti]
            xr_, txr = xr[ti % 2], t_xr[ti % 2]
            yr_, tyr = yr[ti % 3], t_yr[ti % 3]
            S.dma("pool", xsem2[ti % 2], xr_[0:nr, :], src, writes=[txr])
            pp, tk = next_pp()
            for b in range(2):
                pairs = [(mT[:, k, c0:c0 + nr], wo[:, k, b * 512:(b + 1) * 512]) for k in range(KC)]
                mm_group(pp[0:nr, b * 512:(b + 1) * 512], pairs, [t_mTr[0 if c0 < 1024 else (1024 if c0 < T else T)], t_wo], tk,
                         last_inc=(b == 1))
            S.op("dve", lambda e, pp=pp, yr_=yr_, xr_=xr_, nr=nr: e.scalar_tensor_tensor(
                out=yr_[0:nr, :], in0=pp[0:nr, :], scalar=0.5, in1=xr_[0:nr, :], op0=ALU.mult, op1=ALU.add),
                reads=[tk, txr], writes=[tyr])

        def f_stageA2(ti):
            src, dst, nr, c0 = ftiles[ti]
            xr_, txr = xr[ti % 2], t_xr[ti % 2]
            yr_, tyr = yr[ti % 3], t_yr[ti % 3]
            col = 32 + ti
            S.op("act", lambda e, xr_=xr_, yr_=yr_, nr=nr, col=col: e.activation(out=xr_[0:nr, :], in_=yr_[0:nr, :], func=AF.Square,
                                                                                accum_out=stat2[0:nr, col:col + 1]),
                 reads=[tyr, t_sm], writes=[txr, t_stF[ti]])

        def f_stageB1(ti):
            src, dst, nr, c0 = ftiles[ti]
            col = 32 + ti
            S.op("act", lambda e, nr=nr, col=col: e.activation(out=stat2[0:nr, col:col + 1], in_=stat2[0:nr, col:col + 1], func=AF.Sqrt,
                                                               scale=1.0 / D, bias=epst[0:nr, 0:1]),
                 reads=[t_stF[ti], t_eps], writes=[t_stF[ti]])

        def f_stageB(ti):
            src, dst, nr, c0 = ftiles[ti]
            yr_, tyr = yr[ti % 3], t_yr[ti % 3]
            col = 32 + ti
            S.op("dve", lambda e, nr=nr, col=col: e.reciprocal(out=stat2[0:nr, col:col + 1], in_=stat2[0:nr, col:col + 1]),
                 reads=[t_stF[ti]], writes=[t_stF[ti]])
            S.op("dve", lambda e, yr_=yr_, nr=nr, col=col: e.scalar_tensor_tensor(
                out=yr_[0:nr, :], in0=yr_[0:nr, :], scalar=stat2[0:nr, col:col + 1], in1=fgb[0:nr, :], op0=ALU.mult, op1=ALU.mult),
                reads=[tyr, t_stF[ti], t_fgb], writes=[tyr])
            S.dma("sp", ysem[ti % 3], dst, yr_[0:nr, :], reads=[tyr])

        f_stageA(0)
        f_stageA2(0)
        for ti in range(len(ftiles)):
            if ti + 1 < len(ftiles):
                f_stageA(ti + 1)
            f_stageB1(ti)
            if ti + 1 < len(ftiles):
                f_stageA2(ti + 1)
            f_stageB(ti)
        S.barrier()
        S.finish(block)
    return nc


_CACHE = {}


def _get_program():
    if "nc" not in _CACHE:
        _CACHE["nc"] = build_program()
    return _CACHE["nc"]


def kernel(x_prompt, x_sample, cache_mem_k, cache_mem_v, state_lru_h, state_lru_conv, state_sconv, mem_prompt,
           norm_g, mem_norm_g, w_in, lru_conv_w, lru_conv_b, lru_wa, lru_ba, lru_wx, lru_bx, lru_lambda, lru_wo,
           sconv_w, sconv_wo, xa_wk, xa_wv, xa_wo, w_out, final_norm_g):
    f = lambda a: np.ascontiguousarray(np.asarray(a, dtype=np.float32))
    shared = {
        "norm_g": f(norm_g[0]), "mem_norm_g": f(mem_norm_g[0]), "w_in": f(w_in[0]),
        "lru_conv_w": f(lru_conv_w[0]), "lru_conv_b": f(lru_conv_b[0]), "lru_wa": f(lru_wa[0]),
        "lru_ba": f(lru_ba[0]), "lru_wx": f(lru_wx[0]), "lru_bx": f(lru_bx[0]), "lru_lambda": f(lru_lambda[0]),
        "lru_wo": f(lru_wo[0]), "sconv_w": f(sconv_w[0]), "sconv_wo": f(sconv_wo[0]), "xa_wk": f(xa_wk[0]),
        "xa_wv": f(xa_wv[0]), "xa_wo": f(xa_wo[0]), "w_out": f(w_out[0]), "final_norm_g": f(final_norm_g),
    }
    in_maps = []
    for c in range(NCORES):
        sl = slice(c * TS, (c + 1) * TS)
        m = dict(shared)
        m["xp"] = f(x_prompt[c])
        m["xs"] = f(np.asarray(x_sample)[sl, 0, :])
        m["memp"] = f(mem_prompt[c])
        m["ck"] = f(np.asarray(cache_mem_k)[0, sl].reshape(TS, NM, XW))
        m["cv"] = f(np.asarray(cache_mem_v)[0, sl].reshape(TS, NM, XW))
        m["st_h"] = f(np.asarray(state_lru_h)[0, sl])
        m["st_lc"] = f(np.asarray(state_lru_conv)[0, sl].reshape(TS, 3 * W))
        m["st_sc"] = f(np.asarray(state_sconv)[0, sl].reshape(TS, 2 * W))
        in_maps.append(m)
    nc = _get_program()
    res = run_bass_kernel_spmd(nc, in_maps, core_ids=list(range(NCORES)))
    rs = res.results
    cat = lambda k: np.concatenate([np.asarray(r[k]) for r in rs], axis=0)
    y_prompt = np.stack([np.asarray(r["y_p"]) for r in rs], axis=0).astype(np.float32)
    y_sample = cat("y_s").reshape(NCORES * TS, 1, D).astype(np.float32)
    p_mk = np.stack([np.asarray(r["o_pk"]) for r in rs], axis=0).reshape(1, NCORES, NM, 4, 128).astype(np.float32)
    p_mv = np.stack([np.asarray(r["o_pv"]) for r in rs], axis=0).reshape(1, NCORES, NM, 4, 128).astype(np.float32)
    p_h = cat("o_ph").reshape(1, NCORES, W).astype(np.float32)
    p_lc = np.stack([np.asarray(r["o_plc"]) for r in rs], axis=0).reshape(1, NCORES, 3, W).astype(np.float32)
    p_sc = np.stack([np.asarray(r["o_psc"]) for r in rs], axis=0).reshape(1, NCORES, 2, W).astype(np.float32)
    s_h = cat("o_sh").reshape(1, NCORES * TS, W).astype(np.float32)
    s_lc = cat("o_slc").reshape(1, NCORES * TS, 3, W).astype(np.float32)
    s_sc = cat("o_ssc").reshape(1, NCORES * TS, 2, W).astype(np.float32)
    return (y_prompt, y_sample, p_mk, p_mv, p_h, p_lc, p_sc, s_h, s_lc, s_sc)
```
